# Optimizing a Trainium2 kernel written in Bass

```python
import math
import jax, jax.numpy as jnp
from jax import lax
import numpy as np

D_MODEL = 1024
BATCH = 4
SEQ = 4096
DEPTH = 4

GRID_W = 64
CTX_LEN = 256
N_MIXERS = 2
N_RET_LAYERS = (DEPTH + 1) // 2
N_DIFF_LAYERS = DEPTH // 2
RET_HEADS = 4
RET_DK = D_MODEL // RET_HEADS
RET_DV = 2 * RET_DK
RET_CHUNK = 128
RET_IN = 2 * RET_HEADS * RET_DK + 3 * RET_HEADS * RET_DV
DIFF_HEADS = 8
DIFF_D = D_MODEL // (2 * DIFF_HEADS)
DIFF_Q_BLOCK = 128
FFN_HIDDEN = ((8 * D_MODEL // 3 + 127) // 128) * 128
CONV_W = 3
ROPE_BASE = 10000.0
EPS = 1e-6

kernel_name = "hybrid_retention_diffattn_dit"


def rms_norm(x, g):
    x32 = x.astype(jnp.float32)
    y = x32 * lax.rsqrt(jnp.mean(x32 * x32, axis=-1, keepdims=True) + EPS)
    return (y * g.astype(jnp.float32)).astype(x.dtype)


def head_norm(x):
    x32 = x.astype(jnp.float32)
    return x32 * lax.rsqrt(jnp.mean(x32 * x32, axis=-1, keepdims=True) + EPS)


def ada_mod(cvec, w, b):
    m = jax.nn.silu(cvec) @ w + b
    return jnp.split(m[:, None, :], 6, axis=-1)


def modulate(xn, shift, scale):
    return xn * (1 + scale) + shift


def grid_positions(n):
    rows = n // GRID_W
    row = jnp.repeat(jnp.arange(rows, dtype=jnp.float32), GRID_W)
    col = jnp.tile(jnp.arange(GRID_W, dtype=jnp.float32), rows)
    return row, col


def rope_1d(x, pos):
    d2 = x.shape[-1]
    inv = ROPE_BASE ** (-jnp.arange(0, d2, 2, dtype=jnp.float32) / d2)
    ang = pos[:, None] * inv[None, :]
    cos, sin = jnp.cos(ang), jnp.sin(ang)
    x32 = x.astype(jnp.float32)
    x1, x2 = x32[..., : d2 // 2], x32[..., d2 // 2:]
    return jnp.concatenate([x1 * cos - x2 * sin, x1 * sin + x2 * cos], axis=-1).astype(x.dtype)


def rope_2d(x, row, col):
    half = x.shape[-1] // 2
    return jnp.concatenate([rope_1d(x[..., :half], row), rope_1d(x[..., half:], col)], axis=-1)


def split_heads(t, h, d):
    b, n = t.shape[0], t.shape[1]
    return t.reshape(b, n, h, d).transpose(0, 2, 1, 3)


def merge_heads(t):
    b, h, n, d = t.shape
    return t.transpose(0, 2, 1, 3).reshape(b, n, h * d)


def retention_scan(q, k, v, log_gamma, r0):
    b, h, n, dk = q.shape
    dv = v.shape[-1]
    nc = n // RET_CHUNK
    q, k, v = q.astype(jnp.float32), k.astype(jnp.float32), v.astype(jnp.float32)
    idx = jnp.arange(RET_CHUNK, dtype=jnp.float32)
    diff = idx[:, None] - idx[None, :]
    lg = log_gamma.astype(jnp.float32)
    dmask = jnp.where(diff >= 0, jnp.exp(lg[:, None, None] * jnp.maximum(diff, 0.0)), 0.0)
    xi = jnp.exp(lg[:, None] * (idx + 1.0))[None, :, :, None]
    zeta = jnp.exp(lg[:, None] * (RET_CHUNK - 1.0 - idx))[None, :, :, None]
    chunk_decay = jnp.exp(lg * RET_CHUNK)[None, :, None, None]

    def step(r, qkv):
        qc, kc, vc = qkv
        s = jnp.einsum('bhid,bhjd->bhij', qc, kc) * dmask
        o = jnp.einsum('bhij,bhjv->bhiv', s, vc) + jnp.einsum('bhid,bhdv->bhiv', qc, r) * xi
        r = r * chunk_decay + jnp.einsum('bhjd,bhjv->bhdv', kc * zeta, vc)
        return r, o

    def to_chunks(t):
        return jnp.moveaxis(t.reshape(b, h, nc, RET_CHUNK, t.shape[-1]), 2, 0)

    r, o = lax.scan(step, r0, (to_chunks(q), to_chunks(k), to_chunks(v)))
    return jnp.moveaxis(o, 0, 2).reshape(b, h, n, dv), r


def retention_mixer(hx, hc, w_in, w_out, logit_f, logit_b, need_ctx):
    hdk, hdv = RET_HEADS * RET_DK, RET_HEADS * RET_DV

    def proj(u):
        p = u @ w_in
        q, k, v, gf, gb = jnp.split(p, [hdk, 2 * hdk, 2 * hdk + hdv, 2 * hdk + 2 * hdv], axis=-1)
        return (split_heads(q, RET_HEADS, RET_DK), split_heads(k, RET_HEADS, RET_DK) * (RET_DK ** -0.5),
                split_heads(v, RET_HEADS, RET_DV), gf, gb)

    qx, kx, vx, gfx, gbx = proj(hx)
    qc, kc, vc, gfc, gbc = proj(hc)
    row, col = grid_positions(hx.shape[1])
    qx, kx = rope_2d(qx, row, col), rope_2d(kx, row, col)
    lg_f = jax.nn.log_sigmoid(logit_f.astype(jnp.float32))
    lg_b = jax.nn.log_sigmoid(logit_b.astype(jnp.float32))
    r0 = jnp.zeros((hx.shape[0], RET_HEADS, RET_DK, RET_DV), jnp.float32)
    flip = lambda t: jnp.flip(t, axis=2)
    oc_f, rc_f = retention_scan(qc, kc, vc, lg_f, r0)
    oc_b, rc_b = retention_scan(flip(qc), flip(kc), flip(vc), lg_b, r0)
    ox_f, _ = retention_scan(qx, kx, vx, lg_f, rc_f)
    ox_b, _ = retention_scan(flip(qx), flip(kx), flip(vx), lg_b, rc_b)

    def merge(of, ob, gf, gb, dt):
        y = (jax.nn.silu(gf.astype(jnp.float32)) * merge_heads(head_norm(of))
             + jax.nn.silu(gb.astype(jnp.float32)) * merge_heads(head_norm(ob)))
        return y.astype(dt) @ w_out

    yx = merge(ox_f, flip(ox_b), gfx, gbx, hx.dtype)
    yc = merge(oc_f, flip(oc_b), gfc, gbc, hc.dtype) if need_ctx else None
    return yx, yc


def diff_attend(q1, q2, k1, k2, v, lam):
    b, h, n, d = q1.shape
    nb = n // DIFF_Q_BLOCK
    scale = d ** -0.5

    def to_blocks(t):
        return jnp.moveaxis(t.reshape(b, h, nb, DIFF_Q_BLOCK, d), 2, 0)

    def one(qs):
        a, c = qs
        s1 = jnp.einsum('bhqd,bhkd->bhqk', a, k1).astype(jnp.float32) * scale
        s2 = jnp.einsum('bhqd,bhkd->bhqk', c, k2).astype(jnp.float32) * scale
        p = jax.nn.softmax(s1, axis=-1) - lam * jax.nn.softmax(s2, axis=-1)
        return jnp.einsum('bhqk,bhkv->bhqv', p.astype(v.dtype), v)

    o = lax.map(one, (to_blocks(q1), to_blocks(q2)))
    return jnp.moveaxis(o, 0, 2).reshape(b, h, n, v.shape[-1])


def diff_mixer(hx, hc, w_in, w_out, lq1, lk1, lq2, lk2, subln_g, lam_init, need_ctx):
    def proj(u):
        bsz, n = u.shape[0], u.shape[1]
        p = u @ w_in
        q, k, v = jnp.split(p, [D_MODEL, 2 * D_MODEL], axis=-1)
        q = q.reshape(bsz, n, DIFF_HEADS, 2, DIFF_D).transpose(0, 2, 3, 1, 4)
        k = k.reshape(bsz, n, DIFF_HEADS, 2, DIFF_D).transpose(0, 2, 3, 1, 4)
        v = split_heads(v, DIFF_HEADS, 2 * DIFF_D)
        return q[:, :, 0], q[:, :, 1], k[:, :, 0], k[:, :, 1], v

    q1x, q2x, k1x, k2x, vx = proj(hx)
    q1c, q2c, k1c, k2c, vc = proj(hc)
    row, col = grid_positions(hx.shape[1])
    q1x, q2x = rope_2d(q1x, row, col), rope_2d(q2x, row, col)
    k1x, k2x = rope_2d(k1x, row, col), rope_2d(k2x, row, col)
    lam = (jnp.exp(jnp.sum(lq1.astype(jnp.float32) * lk1.astype(jnp.float32)))
           - jnp.exp(jnp.sum(lq2.astype(jnp.float32) * lk2.astype(jnp.float32))) + lam_init)
    k1a = jnp.concatenate([k1x, k1c], axis=2)
    k2a = jnp.concatenate([k2x, k2c], axis=2)
    va = jnp.concatenate([vx, vc], axis=2)

    def post(o):
        return merge_heads(rms_norm(o, subln_g) * (1.0 - lam_init)) @ w_out

    yx = post(diff_attend(q1x, q2x, k1a, k2a, va, lam))
    yc = post(diff_attend(q1c, q2c, k1c, k2c, vc, lam)) if need_ctx else None
    return yx, yc


def dwconv3(x, w, b):
    xp = jnp.pad(x, ((0, 0), (1, 1), (0, 0)))
    return xp[:, :-2] * w[0] + xp[:, 1:-1] * w[1] + xp[:, 2:] * w[2] + b


def conv_ffn(u, w_up, conv_w, conv_b, w_down):
    val, gate = jnp.split(u @ w_up, 2, axis=-1)
    return (jax.nn.silu(dwconv3(gate, conv_w, conv_b)) * val) @ w_down


def setup_inputs(seed: int = 0) -> dict:
    key = jax.random.key(seed)
    ks = jax.random.split(key, 24)
    f32 = jnp.float32
    nrm = lambda k, s, sc: jax.random.normal(k, s, f32) * sc
    base_logit = jnp.log(2.0 ** (5.0 + jnp.arange(RET_HEADS, dtype=f32)) - 1.0)
    return {
        "x": nrm(ks[0], (BATCH, SEQ, D_MODEL), 1.0),
        "c": nrm(ks[1], (BATCH, D_MODEL), 1.0),
        "ctx": nrm(ks[2], (BATCH, CTX_LEN, D_MODEL), 1.0),
        "c_ctx": nrm(ks[3], (D_MODEL,), 1.0),
        "ada_w": nrm(ks[4], (DEPTH, D_MODEL, 6 * D_MODEL), 0.5 * D_MODEL ** -0.5),
        "ada_b": nrm(ks[5], (DEPTH, 6 * D_MODEL), 0.02),
        "norm_g": 1.0 + nrm(ks[6], (DEPTH, 4, D_MODEL), 0.1),
        "ret_w_in": nrm(ks[7], (N_RET_LAYERS, D_MODEL, RET_IN), D_MODEL ** -0.5),
        "ret_w_out": nrm(ks[8], (N_RET_LAYERS, RET_HEADS * RET_DV, D_MODEL), (RET_HEADS * RET_DV) ** -0.5),
        "ret_decay_f": base_logit + nrm(ks[9], (N_RET_LAYERS, RET_HEADS), 0.05),
        "ret_decay_b": base_logit + nrm(ks[10], (N_RET_LAYERS, RET_HEADS), 0.05),
        "diff_w_in": nrm(ks[11], (N_DIFF_LAYERS, D_MODEL, 3 * D_MODEL), D_MODEL ** -0.5),
        "diff_w_out": nrm(ks[12], (N_DIFF_LAYERS, D_MODEL, D_MODEL), D_MODEL ** -0.5),
        "diff_lq1": nrm(ks[13], (N_DIFF_LAYERS, DIFF_D), 0.1),
        "diff_lk1": nrm(ks[14], (N_DIFF_LAYERS, DIFF_D), 0.1),
        "diff_lq2": nrm(ks[15], (N_DIFF_LAYERS, DIFF_D), 0.1),
        "diff_lk2": nrm(ks[16], (N_DIFF_LAYERS, DIFF_D), 0.1),
        "diff_subln": 1.0 + nrm(ks[17], (N_DIFF_LAYERS, 2 * DIFF_D), 0.1),
        "ffn_w_up": nrm(ks[18], (DEPTH, D_MODEL, 2 * FFN_HIDDEN), D_MODEL ** -0.5),
        "ffn_conv_w": nrm(ks[19], (DEPTH, CONV_W, FFN_HIDDEN), CONV_W ** -0.5),
        "ffn_conv_b": nrm(ks[20], (DEPTH, FFN_HIDDEN), 0.02),
        "ffn_w_down": nrm(ks[21], (DEPTH, FFN_HIDDEN, D_MODEL), FFN_HIDDEN ** -0.5),
    }


def reference(x, c, ctx, c_ctx, ada_w, ada_b, norm_g, ret_w_in, ret_w_out, ret_decay_f, ret_decay_b,
              diff_w_in, diff_w_out, diff_lq1, diff_lk1, diff_lq2, diff_lk2, diff_subln,
              ffn_w_up, ffn_conv_w, ffn_conv_b, ffn_w_down):
    h_x, h_c = x, ctx
    for l in range(DEPTH):
        last = l == DEPTH - 1
        shm, scm, gm, shf, scf, gf = ada_mod(c, ada_w[l], ada_b[l])
        cshm, cscm, cgm, cshf, cscf, cgf = ada_mod(c_ctx[None, :], ada_w[l], ada_b[l])
        ux = modulate(rms_norm(h_x, norm_g[l, 0]), shm, scm)
        uc = modulate(rms_norm(h_c, norm_g[l, 0]), cshm, cscm)
        i = l // N_MIXERS
        if l % N_MIXERS == 0:
            yx, yc = retention_mixer(ux, uc, ret_w_in[i], ret_w_out[i], ret_decay_f[i], ret_decay_b[i],
                                     not last)
        else:
            lam_init = 0.8 - 0.6 * math.exp(-0.3 * l)
            yx, yc = diff_mixer(ux, uc, diff_w_in[i], diff_w_out[i], diff_lq1[i], diff_lk1[i],
                                diff_lq2[i], diff_lk2[i], diff_subln[i], lam_init, not last)
        h_x = h_x + gm * rms_norm(yx, norm_g[l, 1])
        fx = conv_ffn(modulate(rms_norm(h_x, norm_g[l, 2]), shf, scf),
                      ffn_w_up[l], ffn_conv_w[l], ffn_conv_b[l], ffn_w_down[l])
        h_x = h_x + gf * rms_norm(fx, norm_g[l, 3])
        if not last:
            h_c = h_c + cgm * rms_norm(yc, norm_g[l, 1])
            fc = conv_ffn(modulate(rms_norm(h_c, norm_g[l, 2]), cshf, cscf),
                          ffn_w_up[l], ffn_conv_w[l], ffn_conv_b[l], ffn_w_down[l])
            h_c = h_c + cgf * rms_norm(fc, norm_g[l, 3])
    return h_x
```

```python
import numpy as np
from contextlib import ExitStack
import concourse.bass as bass
import concourse.mybir as mybir
from concourse.bass_utils import run_bass_kernel_spmd

F32 = mybir.dt.float32
BF16 = mybir.dt.bfloat16
I32 = mybir.dt.int32
AF = mybir.ActivationFunctionType
ALU = mybir.AluOpType
AX = mybir.AxisListType

D = 1024
NL = 2048
NCX = 256
NT = NL + NCX
TL, TCX, TT = 16, 2, 18
DEPTH = 4
FH = 2816
NFC = 22
EPS = 1e-6
GROUPS = [[0, 1], [2, 3], [4, 5], [6, 7]]


class Prog:
    def __init__(self, nc, stack):
        self.nc = nc
        self.stack = stack
        self.E = dict(pe=nc.tensor, act=nc.scalar, dve=nc.vector, pool=nc.gpsimd, sp=nc.sync)
        self.phys = []
        self.pcnt = []
        self.key2p = {}
        self.free = []
        self.lastw = {}
        self.rd = {}
        self.dmalast = {}
        self.waited = {e: {} for e in self.E}
        self.nins = 0

    def _sem(self, k):
        if k not in self.key2p:
            if self.free:
                p = self.free.pop()
            else:
                p = len(self.phys)
                self.phys.append(self.stack.enter_context(self.nc.semaphore("s%d" % p)))
                self.pcnt.append(0)
            self.key2p[k] = p
        return self.key2p[k]

    @staticmethod
    def _rk(x):
        return x if isinstance(x, (str, tuple)) else ("T", x.name)

    def op(self, eng, fn, r=(), w=(), sig=True, dma=None, inc=None):
        r = [self._rk(x) for x in r]
        w = [self._rk(x) for x in w]
        if dma is not None:
            dma = self._rk(dma)
        kind = 'd' if dma is not None else 'c'
        p = self._sem(('d', dma) if dma is not None else eng)
        deps = {}

        def add(tok, raw):
            k, v, e, kd = tok
            if kd == 'c' and kind == 'c' and e == eng:
                if eng == 'pe' or not raw:
                    return
            if deps.get(k, 0) < v:
                deps[k] = v

        for x in r:
            if x in self.lastw:
                add(self.lastw[x], True)
        for x in w:
            if x in self.lastw:
                add(self.lastw[x], False)
            for tok in self.rd.get(x, {}).values():
                add(tok, False)
        if dma is not None and dma in self.dmalast:
            add(self.dmalast[dma], False)
        wd = self.waited[eng]
        for k, v in deps.items():
            if wd.get(k, 0) >= v:
                continue
            assert self.pcnt[k] >= v, ("wait on unsignaled op", k, v, self.pcnt[k])
            self.E[eng].wait_ge(self.phys[k], v)
            wd[k] = v
            self.nins += 1
        ins = fn()
        self.nins += 1
        if sig:
            if inc is None:
                inc = 16 if kind == 'd' else 1
            self.pcnt[p] += inc
            ins.then_inc(self.phys[p], inc)
            val = self.pcnt[p]
        else:
            val = self.pcnt[p] + 1
        tok = (p, val, eng, kind)
        for x in r:
            self.rd.setdefault(x, {})[p] = tok
        for x in w:
            self.lastw[x] = tok
            self.rd[x] = {}
        if dma is not None:
            self.dmalast[dma] = tok

    def dma(self, q, out, in_, key, r=(), w=(), **kw):
        self.op(q, lambda: self.E[q].dma_start(out=out, in_=in_, **kw), r=r, w=w, dma=key)

    def barrier(self):
        for e in self.E:
            wd = self.waited[e]
            for p, h in enumerate(self.phys):
                v = self.pcnt[p]
                if v > wd.get(p, 0):
                    self.E[e].wait_ge(h, v)
                    wd[p] = v
                    self.nins += 1
        self.lastw.clear()
        self.rd.clear()
        self.dmalast.clear()
        for k in [k for k in self.key2p if isinstance(k, tuple) and k[0] == 'd']:
            self.free.append(self.key2p.pop(k))


_UNIQ = [0]


def sbt(ph, nc, name, shape, dt):
    _UNIQ[0] += 1
    return ph.enter_context(nc.sbuf_tensor("%s_%d" % (name, _UNIQ[0]), list(shape), dt))


class Rot:
    def __init__(self, items):
        self.items = list(items)
        self.i = 0

    def next(self):
        x = self.items[self.i % len(self.items)]
        self.i += 1
        return x


class K:
    pass


def build(mode="full"):
    nc = bass.Bass("TRN2", target_bir_lowering=False)
    stack = ExitStack()
    with stack:
        _build(nc, stack, mode)
    return nc


def _dram_in(nc, name, shape, dt=F32):
    return nc.dram_tensor(name, list(shape), dt, kind="ExternalInput")


def _build(nc, stack, mode):
    P = Prog(nc, stack)
    k = K()
    k.nc, k.P, k.mode = nc, P, mode
    k.x = _dram_in(nc, "x", [NL, D])
    k.ctx = _dram_in(nc, "ctx", [NCX, D])
    k.c2 = _dram_in(nc, "c2", [2, D])
    k.sel = _dram_in(nc, "sel", [128, 2])
    k.pos = _dram_in(nc, "pos", [128, TL, 2])
    k.ada_w = _dram_in(nc, "ada_w", [DEPTH, D, 6 * D])
    k.ada_b = _dram_in(nc, "ada_b", [DEPTH, 6 * D])
    k.norm_g = _dram_in(nc, "norm_g", [DEPTH, 4, D])
    k.ret_w_in = _dram_in(nc, "ret_w_in", [2, D, 8192])
    k.ret_w_out = _dram_in(nc, "ret_w_out", [2, 2048, D])
    k.ret_decay = _dram_in(nc, "ret_decay", [2, 2, 4])
    k.diff_w_in = _dram_in(nc, "diff_w_in", [2, D, 3 * D])
    k.diff_w_out = _dram_in(nc, "diff_w_out", [2, D, D])
    k.diff_lam = _dram_in(nc, "diff_lam", [2, 4, 64])
    k.diff_subln = _dram_in(nc, "diff_subln", [2, 128])
    k.ffn_w_up = _dram_in(nc, "ffn_w_up", [DEPTH, D, 2 * FH])
    k.ffn_conv_w = _dram_in(nc, "ffn_conv_w", [DEPTH, 3, FH])
    k.ffn_conv_b = _dram_in(nc, "ffn_conv_b", [DEPTH, FH])
    k.ffn_w_down = _dram_in(nc, "ffn_w_down", [DEPTH, FH, D])
    k.out = nc.dram_tensor("out", [NT, D], F32, kind="ExternalOutput")
    k.h_d = nc.dram_tensor("h_d", [NT, D], F32)
    k.mod_d = nc.dram_tensor("mod_d", [DEPTH, 2, 6 * D], F32)
    k.hx_d = nc.dram_tensor("hx_d", [128, 8], BF16)
    k.hg_d = nc.dram_tensor("hg_d", [256, 8], BF16)

    def sb(name, shape, dt):
        return stack.enter_context(nc.sbuf_tensor(name, list(shape), dt))

    k.ident = sb("ident", [128, 128], BF16)
    k.iotaf = sb("iotaf", [128, 128], F32)
    k.iotap = sb("iotap", [128, 1], F32)
    k.neghalf = sb("neghalf", [128, 1], F32)
    k.selt = sb("selt", [128, 2], F32)
    k.junk = sb("junk", [128, 1024], BF16)
    k.psf = [stack.enter_context(nc.psum_tensor("psf%d" % i, [128, 512], F32)) for i in range(6)]
    k.psb = [stack.enter_context(nc.psum_tensor("psb%d" % i, [128, 1024], BF16)) for i in range(2)]
    k.psf_rot = Rot(k.psf)
    k.psb_rot = Rot(k.psb)

    consts(k)
    ada_all(k)
    init_h(k)
    if mode == "ffn0":
        ffn_layer(k, 0, False)
    if mode in ("ret0", "L0"):
        rope_tables(k)
        ret_layer(k, 0)
        if mode == "L0":
            ffn_layer(k, 0, False)
    if mode == "diff1":
        rope_tables(k)
        diff_layer(k, 1, False)
    if mode == "full":
        rope_tables(k)
        for l in range(DEPTH):
            last = l == DEPTH - 1
            if l % 2 == 0:
                ret_layer(k, l)
            else:
                diff_layer(k, l, last)
            ffn_layer(k, l, last)
    P.barrier()
    with ExitStack() as ph:
        ob = [sbt(ph, nc, "ob%d" % i, [128, D], F32) for i in range(3)]
        rot = Rot(ob)
        for t in range(TT):
            b = rot.next()
            P.dma('sp', b[:], k.h_d[t * 128:(t + 1) * 128, :], key=b, r=[("h", t)], w=[b])
            P.dma('sp', k.out[t * 128:(t + 1) * 128, :], b[:], key=b, r=[b], w=[("out", t)])
        P.barrier()
    print("instructions:", P.nins, "sems:", len(P.phys))


def consts(k):
    nc, P = k.nc, k.P
    with ExitStack() as ph:
        ii = sbt(ph, nc, "ii", [128, 128], I32)
        pp = sbt(ph, nc, "pp", [128, 1], I32)
        P.op('pool', lambda: nc.gpsimd.iota(ii[:], pattern=[[1, 128]], base=0, channel_multiplier=0), w=[ii])
        P.op('pool', lambda: nc.gpsimd.iota(pp[:], pattern=[[0, 1]], base=0, channel_multiplier=1), w=[pp])
        P.op('dve', lambda: nc.vector.tensor_copy(out=k.iotaf[:], in_=ii[:]), r=[ii], w=[k.iotaf])
        P.op('dve', lambda: nc.vector.tensor_copy(out=k.iotap[:], in_=pp[:]), r=[pp], w=[k.iotap])
        P.op('dve', lambda: nc.vector.tensor_scalar(out=k.ident[:], in0=k.iotaf[:], scalar1=k.iotap[:, 0:1],
                                                    scalar2=None, op0=ALU.is_equal),
             r=[k.iotaf, k.iotap], w=[k.ident])
        P.op('dve', lambda: nc.vector.memset(k.neghalf[:], -0.5), w=[k.neghalf])
        P.dma('sp', k.selt[:], k.sel[:, :], key=k.selt, w=[k.selt])
        P.barrier()


def init_h(k):
    P = k.P
    P.dma('sp', k.h_d[0:NL, :], k.x[:, :], key="inith0", w=[("h", t) for t in range(TL)])
    P.dma('sp', k.h_d[NL:NT, :], k.ctx[:, :], key="inith1", w=[("h", t) for t in range(TL, TT)])


def ada_all(k):
    nc, P = k.nc, k.P
    with ExitStack() as ph:
        cT = sbt(ph, nc, "cT", [128, 8, 2], F32)
        sT = sbt(ph, nc, "sT", [128, 8, 2], F32)
        wb = [sbt(ph, nc, "adaw%d" % i, [128, 8, 512], F32) for i in range(2)]
        bias = sbt(ph, nc, "adab", [2, 6 * D], F32)
        row = sbt(ph, nc, "adarow", [2, 6 * D], F32)
        wrot = Rot(wb)
        for r in range(2):
            P.dma('sp', cT[:, :, r], k.c2.ap()[r].rearrange("(k p) -> p k", p=128), key=cT, w=[cT],
                  allow_slow_non_contiguous=True)
        P.op('act', lambda: nc.scalar.activation(out=sT[:], in_=cT[:], func=AF.Silu), r=[cT], w=[sT])
        for l in range(DEPTH):
            P.dma('sp', bias[:], k.ada_b.ap()[l:l + 1, :].to_broadcast([2, 6 * D]), key=bias, w=[bias])
            for g in range(12):
                w = wrot.next()
                P.dma('sp', w[:], k.ada_w.ap()[l][:, g * 512:(g + 1) * 512].rearrange("(k p) n -> p k n", p=128),
                      key=w, w=[w])
                ps = k.psf_rot.next()
                for kk in range(8):
                    P.op('pe', lambda kk=kk, w=w, ps=ps: nc.tensor.matmul(ps[0:2, :], lhsT=sT[:, kk, :], rhs=w[:, kk, :],
                                                                     start=(kk == 0), stop=(kk == 7)),
                         r=[sT, w], w=[ps], sig=(kk == 7))
                P.op('dve', lambda g=g, ps=ps: nc.vector.tensor_tensor(out=row[:, g * 512:(g + 1) * 512], in0=ps[0:2, :],
                                                                    in1=bias[:, g * 512:(g + 1) * 512], op=ALU.add),
                     r=[ps, bias], w=[row])
            P.dma('sp', k.mod_d.ap()[l], row[:], key=row, r=[row], w=[("mod", l)])
        P.barrier()


def bcast_row(k, q, dst, src_row, key, r=(), w=()):
    n = dst.shape[-1]
    k.P.dma(q, dst, src_row.to_broadcast([128, n]), key=key, r=r, w=w)


def rstd_from_ss(k, ss, ms, rstd, n):
    nc, P = k.nc, k.P
    P.op('dve', lambda: nc.vector.tensor_scalar(out=ms[:], in0=ss[:], scalar1=1.0 / n, scalar2=EPS,
                                                op0=ALU.mult, op1=ALU.add), r=[ss], w=[ms])
    P.op('pool', lambda: nc.gpsimd.tensor_tensor(out=rstd[:], in0=ms[:], in1=k.neghalf[:], op=ALU.pow),
         r=[ms, k.neghalf], w=[rstd])


def prenorm(k, ph, l, gi, shi, sci, uT, tiles, hook=None):
    nc, P = k.nc, k.P

    def t_(name, shape, dt):
        return sbt(ph, nc, name, shape, dt)

    gb = t_("pn_g", [128, D], F32)
    gs = [t_("pn_gs%d" % r, [128, D], F32) for r in range(2)]
    sh = [t_("pn_sh%d" % r, [128, D], F32) for r in range(2)]
    hb = Rot([t_("pn_h%d" % i, [128, D], F32) for i in range(3)])
    tb = Rot([t_("pn_t%d" % i, [128, D], F32) for i in range(2)])
    ub = Rot([t_("pn_u%d" % i, [128, D], BF16) for i in range(2)])
    st = Rot([[t_("pn_s%d_%d" % (i, j), [128, 1], F32) for j in range(3)] for i in range(3)])
    bcast_row(k, 'sp', gb[:], k.norm_g.ap()[l, gi:gi + 1, :], key=gb, w=[gb])
    for r in range(2):
        bcast_row(k, 'sp', gs[r][:], k.mod_d.ap()[l, r:r + 1, sci * D:(sci + 1) * D], key=gs[r], r=[("mod", l)], w=[gs[r]])
        bcast_row(k, 'sp', sh[r][:], k.mod_d.ap()[l, r:r + 1, shi * D:(shi + 1) * D], key=sh[r], r=[("mod", l)], w=[sh[r]])
        P.op('dve', lambda r=r: nc.vector.scalar_tensor_tensor(out=gs[r][:], in0=gs[r][:], scalar=1.0, in1=gb[:],
                                                           op0=ALU.add, op1=ALU.mult), r=[gs[r], gb], w=[gs[r]])
    for t in tiles:
        r = 0 if t < TL else 1
        h = hb.next()
        tt = tb.next()
        u = ub.next()
        ss, ms, rstd = st.next()
        P.dma('sp', h[:], k.h_d[t * 128:(t + 1) * 128, :], key=h, r=[("h", t)], w=[h])
        P.op('act', lambda h=h, ss=ss: nc.scalar.activation(out=k.junk[:], in_=h[:], func=AF.Square, accum_out=ss[:]),
             r=[h], w=[ss])
        rstd_from_ss(k, ss, ms, rstd, D)
        P.op('dve', lambda h=h, tt=tt, rstd=rstd, r=r: nc.vector.scalar_tensor_tensor(
            out=tt[:], in0=h[:], scalar=rstd[:, 0:1], in1=gs[r][:], op0=ALU.mult, op1=ALU.mult),
            r=[h, rstd, gs[r]], w=[tt])
        P.op('dve', lambda tt=tt, u=u, r=r: nc.vector.tensor_tensor(out=u[:], in0=tt[:], in1=sh[r][:], op=ALU.add),
             r=[tt, sh[r]], w=[u])
        pb = k.psb_rot.next()
        for kk in range(8):
            P.op('pe', lambda kk=kk, u=u, pb=pb: nc.tensor.transpose(pb[:, kk * 128:(kk + 1) * 128], u[:, kk * 128:(kk + 1) * 128],
                                                                 k.ident[:]),
                 r=[u, k.ident], w=[pb], sig=(kk == 7))
        P.op('act', lambda t=t, pb=pb: nc.scalar.copy(out=uT[:, :, t * 128:(t + 1) * 128],
                                                    in_=pb[:].rearrange("p (k n) -> p k n", k=8)),
             r=[pb], w=[("uT", t)])
        if hook is not None:
            hook(t)


def uT_res(c, n):
    return [("uT", t) for t in range(c // 128, (c + n - 1) // 128 + 1)]


def ffn_layer(k, l, last):
    nc, P = k.nc, k.P
    tiles_all = list(range(TL)) + ([] if last else list(range(TL, TT)))
    with ExitStack() as ph:
        def t_(name, shape, dt):
            return sbt(ph, nc, name, shape, dt)

        uT = t_("f_uT", [128, 8, NT], BF16)
        Wd = t_("f_Wd", [128, NFC, D], BF16)
        cw = t_("f_cw", [128, 4, NFC], F32)
        uH = [t_("f_uH%d" % i, [128, 8, 2], BF16) for i in range(2)]
        hx = t_("f_hx", [128, 8], BF16)
        hg = t_("f_hg", [128, 2, 8], BF16)
        hgf = t_("f_hgf", [128, 8], F32)
        for half in range(2):
            f0, f1 = half * 11, (half + 1) * 11
            P.dma('pool', Wd[:, f0:f1, :], k.ffn_w_down.ap()[l][f0 * 128:f1 * 128, :].rearrange("(f p) n -> p f n", p=128),
                  key=("Wd", half), w=[("Wd", half)])
        for tap in range(3):
            P.dma('sp', cw[:, tap, :], k.ffn_conv_w.ap()[l, tap].rearrange("(f p) -> p f", p=128), key=cw, w=[cw],
                  allow_slow_non_contiguous=True)
        P.dma('sp', cw[:, 3, :], k.ffn_conv_b.ap()[l].rearrange("(f p) -> p f", p=128), key=cw, w=[cw],
              allow_slow_non_contiguous=True)
        with ExitStack() as ph1:
            order = [TL - 1] + [t for t in tiles_all if t != TL - 1]

            def hook(t):
                if t != TL - 1:
                    return
                P.op('dve', lambda: nc.vector.tensor_copy(out=hx[:], in_=uT[:, :, NL - 1]), r=[("uT", TL - 1)], w=[hx])
                P.dma('pool', k.hx_d[:, :], hx[:], key=hx, r=[hx], w=["hx_d"])
                P.op('pool', lambda: nc.gpsimd.collective_compute("AllGather", ALU.bypass, replica_groups=GROUPS,
                                                                  ins=[k.hx_d.ap().opt()], outs=[k.hg_d.ap().opt()]),
                     r=["hx_d"], w=["hg_d"], dma="cc", inc=1)

            prenorm(k, ph1, l, 2, 3, 4, uT, order, hook)
            P.dma('sp', hg[:], k.hg_d.ap().rearrange("(r p) k -> p r k", p=128), key=hg, r=["hg_d"], w=[hg])
            P.op('dve', lambda: nc.vector.tensor_scalar(out=hgf[:], in0=hg[:, 0, :], scalar1=k.selt[:, 0:1], scalar2=None,
                                                        op0=ALU.mult), r=[hg, k.selt], w=[hgf])
            P.op('dve', lambda: nc.vector.scalar_tensor_tensor(out=uH[1][:, :, 1], in0=hg[:, 1, :], scalar=k.selt[:, 1:2],
                                                               in1=hgf[:], op0=ALU.mult, op1=ALU.add),
                 r=[hg, hgf, k.selt], w=[uH[1]])
            P.op('dve', lambda: nc.vector.tensor_copy(out=uH[1][:, :, 0], in_=uT[:, :, NL // 2 - 1]), r=[("uT", 7)], w=[uH[1]])
            P.op('dve', lambda: nc.vector.tensor_copy(out=uH[0][:, :, 0], in_=uT[:, :, NL // 2]), r=[("uT", 8)], w=[uH[0]])
            P.op('dve', lambda: nc.vector.tensor_copy(out=uH[0][:, :, 1], in_=uT[:, :, NL // 2]), r=[("uT", 8)], w=[uH[0]])
            P.barrier()
        segs = [dict(tiles=list(range(0, 8)), subs=[(0, 1024, False, True)], uH=uH[0]),
                dict(tiles=list(range(8, 16)) + ([] if last else [16, 17]),
                     subs=[(1024, 1024, True, True)] + ([] if last else [(2048, 256, False, False)]), uH=uH[1])]
        with ExitStack() as ph2:
            def t2(name, shape, dt):
                return sbt(ph2, nc, name, shape, dt)

            NCOL = 1280
            aT = t2("f_aT", [128, NFC, NCOL], BF16)
            Wv = Rot([t2("f_Wv%d" % i, [128, 8, 256], BF16) for i in range(2)])
            Wg = Rot([t2("f_Wg%d" % i, [128, 8, 256], BF16) for i in range(2)])
            GW = NCOL + 4
            Gb = Rot([t2("f_G%d" % i, [128, GW], F32) for i in range(2)])
            Vb = Rot([t2("f_V%d" % i, [128, NCOL], F32) for i in range(2)])
            Tb = Rot([t2("f_T%d" % i, [128, NCOL], F32) for i in range(1)])
            gg = [t2("f_gg%d" % r, [128, D], F32) for r in range(2)]
            g3 = t2("f_g3", [128, D], F32)
            hb = Rot([t2("f_h%d" % i, [128, D], F32) for i in range(2)])
            ob = Rot([t2("f_o%d" % i, [128, D], F32) for i in range(2)])
            st = Rot([[t2("f_s%d_%d" % (i, j), [128, 1], F32) for j in range(4)] for i in range(2)])
            for G in Gb.items:
                P.op('pool', lambda G=G: nc.gpsimd.memset(G[:], 0.0), w=[G])
            bcast_row(k, 'sp', g3[:], k.norm_g.ap()[l, 3:4, :], key=g3, w=[g3])
            for r in range(2):
                bcast_row(k, 'sp', gg[r][:], k.mod_d.ap()[l, r:r + 1, 5 * D:6 * D], key=gg[r], r=[("mod", l)], w=[gg[r]])
                P.op('dve', lambda r=r: nc.vector.tensor_tensor(out=gg[r][:], in0=gg[r][:], in1=g3[:], op=ALU.mult),
                     r=[gg[r], g3], w=[gg[r]])
            wup = k.ffn_w_up.ap()[l]

            def load_w(fcg):
                wv, wg = Wv.next(), Wg.next()
                P.dma('pool', wv[:], wup[:, fcg * 256:(fcg + 1) * 256].rearrange("(k p) n -> p k n", p=128), key=wv, w=[wv])
                P.dma('pool', wg[:], wup[:, FH + fcg * 256:FH + (fcg + 1) * 256].rearrange("(k p) n -> p k n", p=128),
                      key=wg, w=[wg])
                return wv, wg

            for seg in segs:
                subs = seg["subs"]
                uHs = seg["uH"]
                offs, goffs = [], []
                o, go = 0, 0
                for (c0, n, lh, rh) in subs:
                    offs.append(o)
                    goffs.append(go)
                    o += n
                    go += n + 2
                nxt = load_w(0)
                for fcg in range(11):
                    wv, wg = nxt
                    if fcg + 1 < 11:
                        nxt = load_w(fcg + 1)
                    for j in range(2):
                        fc = fcg * 2 + j
                        G, V, T = Gb.next(), Vb.next(), Tb.next()
                        for si, (c0, n, lh, rh) in enumerate(subs):
                            for b0 in range(0, n, 512):
                                nb = min(512, n - b0)
                                for (wt, dst, doff) in ((wv, V, offs[si] + b0), (wg, G, goffs[si] + 1 + b0)):
                                    ps = k.psf_rot.next()
                                    for kk in range(8):
                                        P.op('pe', lambda kk=kk, wt=wt, ps=ps, c=c0 + b0, nb=nb, j=j: nc.tensor.matmul(
                                            ps[:, 0:nb], lhsT=wt[:, kk, j * 128:(j + 1) * 128], rhs=uT[:, kk, c:c + nb],
                                            start=(kk == 0), stop=(kk == 7)),
                                            r=[wt] + uT_res(c0 + b0, nb), w=[ps], sig=(kk == 7))
                                    P.op('act', lambda ps=ps, dst=dst, doff=doff, nb=nb: nc.scalar.copy(
                                        out=dst[:, doff:doff + nb], in_=ps[:, 0:nb]), r=[ps], w=[dst])
                        ps = k.psf_rot.next()
                        for kk in range(8):
                            P.op('pe', lambda kk=kk, ps=ps, j=j, wg=wg: nc.tensor.matmul(
                                ps[:, 0:2], lhsT=wg[:, kk, j * 128:(j + 1) * 128], rhs=uHs[:, kk, :],
                                start=(kk == 0), stop=(kk == 7)), r=[wg, uHs], w=[ps], sig=(kk == 7))
                        for si, (c0, n, lh, rh) in enumerate(subs):
                            go = goffs[si]
                            if lh:
                                P.op('act', lambda ps=ps, G=G, go=go: nc.scalar.copy(out=G[:, go:go + 1], in_=ps[:, 0:1]),
                                     r=[ps], w=[G])
                            if rh:
                                P.op('act', lambda ps=ps, G=G, go=go, n=n: nc.scalar.copy(
                                    out=G[:, go + n + 1:go + n + 2], in_=ps[:, 1:2]), r=[ps], w=[G])
                        for si, (c0, n, lh, rh) in enumerate(subs):
                            go, o = goffs[si], offs[si]
                            P.op('dve', lambda G=G, T=T, go=go, o=o, n=n, fc=fc: nc.vector.tensor_scalar(
                                out=T[:, o:o + n], in0=G[:, go + 1:go + 1 + n], scalar1=cw[:, 1, fc:fc + 1],
                                scalar2=cw[:, 3, fc:fc + 1], op0=ALU.mult, op1=ALU.add), r=[G, cw], w=[T])
                            P.op('dve', lambda G=G, T=T, go=go, o=o, n=n, fc=fc: nc.vector.scalar_tensor_tensor(
                                out=T[:, o:o + n], in0=G[:, go:go + n], scalar=cw[:, 0, fc:fc + 1], in1=T[:, o:o + n],
                                op0=ALU.mult, op1=ALU.add), r=[G, cw, T], w=[T])
                            P.op('dve', lambda G=G, T=T, go=go, o=o, n=n, fc=fc: nc.vector.scalar_tensor_tensor(
                                out=T[:, o:o + n], in0=G[:, go + 2:go + 2 + n], scalar=cw[:, 2, fc:fc + 1], in1=T[:, o:o + n],
                                op0=ALU.mult, op1=ALU.add), r=[G, cw, T], w=[T])
                            P.op('act', lambda T=T, o=o, n=n: nc.scalar.activation(out=T[:, o:o + n], in_=T[:, o:o + n],
                                                                                func=AF.Silu), r=[T], w=[T])
                            P.op('dve', lambda T=T, V=V, o=o, n=n, fc=fc: nc.vector.tensor_tensor(
                                out=aT[:, fc, o:o + n], in0=T[:, o:o + n], in1=V[:, o:o + n], op=ALU.mult),
                                r=[T, V], w=[("aT", fc)])
                col = 0
                for t in seg["tiles"]:
                    r = 0 if t < TL else 1
                    pss = [k.psf_rot.next(), k.psf_rot.next()]
                    for cg in range(2):
                        for fc in range(NFC):
                            P.op('pe', lambda fc=fc, cg=cg, col=col: nc.tensor.matmul(
                                pss[cg][:], lhsT=aT[:, fc, col:col + 128], rhs=Wd[:, fc, cg * 512:(cg + 1) * 512],
                                start=(fc == 0), stop=(fc == NFC - 1)),
                                r=[("aT", fc), ("Wd", fc // 11)], w=[pss[cg]], sig=(fc == NFC - 1))
                    h, o_ = hb.next(), ob.next()
                    s0, s1, ms, rstd = st.next()
                    P.dma('sp', h[:], k.h_d[t * 128:(t + 1) * 128, :], key=h, r=[("h", t)], w=[h])
                    for cg, s in ((0, s0), (1, s1)):
                        P.op('act', lambda cg=cg, s=s: nc.scalar.activation(out=k.junk[:, 0:512], in_=pss[cg][:], func=AF.Square,
                                                                        accum_out=s[:]), r=[pss[cg]], w=[s])
                    P.op('dve', lambda s0=s0, s1=s1: nc.vector.tensor_tensor(out=s0[:], in0=s0[:], in1=s1[:], op=ALU.add),
                         r=[s0, s1], w=[s0])
                    rstd_from_ss(k, s0, ms, rstd, D)
                    for cg in range(2):
                        P.op('dve', lambda cg=cg, o_=o_, rstd=rstd, r=r: nc.vector.scalar_tensor_tensor(
                            out=o_[:, cg * 512:(cg + 1) * 512], in0=pss[cg][:], scalar=rstd[:, 0:1],
                            in1=gg[r][:, cg * 512:(cg + 1) * 512], op0=ALU.mult, op1=ALU.mult),
                            r=[pss[cg], rstd, gg[r]], w=[o_])
                    P.op('dve', lambda o_=o_, h=h: nc.vector.tensor_tensor(out=o_[:], in0=o_[:], in1=h[:], op=ALU.add),
                         r=[o_, h], w=[o_])
                    P.dma('sp', k.h_d[t * 128:(t + 1) * 128, :], o_[:], key=o_, r=[o_], w=[("h", t)])
                    col += 128
            P.barrier()


def resid_epilogue(k, pss, t, ggr, h, o_, stt4):
    nc, P = k.nc, k.P
    s0, s1, ms, rstd = stt4
    P.dma('sp', h[:], k.h_d[t * 128:(t + 1) * 128, :], key=h, r=[("h", t)], w=[h])
    for cg, s_ in ((0, s0), (1, s1)):
        P.op('act', lambda cg=cg, s_=s_: nc.scalar.activation(out=k.junk[:, 0:512], in_=pss[cg][:], func=AF.Square,
                                                            accum_out=s_[:]), r=[pss[cg]], w=[s_])
    P.op('dve', lambda: nc.vector.tensor_tensor(out=s0[:], in0=s0[:], in1=s1[:], op=ALU.add), r=[s0, s1], w=[s0])
    rstd_from_ss(k, s0, ms, rstd, D)
    for cg in range(2):
        P.op('dve', lambda cg=cg: nc.vector.scalar_tensor_tensor(
            out=o_[:, cg * 512:(cg + 1) * 512], in0=pss[cg][:], scalar=rstd[:, 0:1],
            in1=ggr[:, cg * 512:(cg + 1) * 512], op0=ALU.mult, op1=ALU.mult), r=[pss[cg], rstd, ggr], w=[o_])
    P.op('dve', lambda: nc.vector.tensor_tensor(out=o_[:], in0=o_[:], in1=h[:], op=ALU.add), r=[o_, h], w=[o_])
    P.dma('pool', k.h_d[t * 128:(t + 1) * 128, :], o_[:], key=o_, r=[o_], w=[("h", t)])


def gate_tiles(k, ph, l, gi, gatei):
    nc, P = k.nc, k.P
    gg = [sbt(ph, nc, "gg%d" % r, [128, D], F32) for r in range(2)]
    g3 = sbt(ph, nc, "ggn", [128, D], F32)
    bcast_row(k, 'sp', g3[:], k.norm_g.ap()[l, gi:gi + 1, :], key=g3, w=[g3])
    for r in range(2):
        bcast_row(k, 'sp', gg[r][:], k.mod_d.ap()[l, r:r + 1, gatei * D:(gatei + 1) * D], key=gg[r], r=[("mod", l)],
                  w=[gg[r]])
        P.op('dve', lambda r=r: nc.vector.tensor_tensor(out=gg[r][:], in0=gg[r][:], in1=g3[:], op=ALU.mult),
             r=[gg[r], g3], w=[gg[r]])
    return gg


def proj_tm(k, ph, uT, w_ap, ngroups, tiles, epilogue):
    nc, P = k.nc, k.P
    Wb = Rot([sbt(ph, nc, "pj_w%d" % i, [128, 8, 512], BF16) for i in range(2)])

    def load(g):
        w = Wb.next()
        P.dma('pool', w[:], w_ap[:, g * 512:(g + 1) * 512].rearrange("(k p) n -> p k n", p=128), key=w, w=[w])
        return w

    nxt = load(0)
    for g in range(ngroups):
        w = nxt
        if g + 1 < ngroups:
            nxt = load(g + 1)
        for t in tiles:
            ps = k.psf_rot.next()
            for kk in range(8):
                P.op('pe', lambda kk=kk: nc.tensor.matmul(ps[:], lhsT=uT[:, kk, t * 128:(t + 1) * 128], rhs=w[:, kk, :],
                                                        start=(kk == 0), stop=(kk == 7)),
                     r=[w, ("uT", t)], w=[ps], sig=(kk == 7))
            epilogue(t, g, ps)


def rope_apply(k, ps, outb, cos, sin, nh, nf, scale, tmp):
    nc, P = k.nc, k.P
    n = nh * 4 * nf
    pv = ps[:, 0:n].rearrange("p (h r x f) -> p h r x f", h=nh, r=2, x=2)
    ov = outb[:, 0:n].rearrange("p (h r x f) -> p h r x f", h=nh, r=2, x=2)
    x1, x2 = pv[:, :, :, 0, :], pv[:, :, :, 1, :]
    cb = cos.unsqueeze(1).to_broadcast([128, nh, 2, nf])
    sb_ = sin.unsqueeze(1).to_broadcast([128, nh, 2, nf])
    tv = [t_[:, 0:nh * 2 * nf].rearrange("p (h r f) -> p h r f", h=nh, r=2) for t_ in tmp]
    for (dst, xin, tab) in ((0, x1, cb), (1, x2, sb_), (2, x1, sb_), (3, x2, cb)):
        P.op('dve', lambda dst=dst, xin=xin, tab=tab: nc.vector.scalar_tensor_tensor(
            out=tv[dst], in0=xin, scalar=float(scale), in1=tab, op0=ALU.mult, op1=ALU.mult),
            r=[ps, "rope_tab"], w=[tmp[dst]])
    P.op('dve', lambda: nc.vector.tensor_tensor(out=ov[:, :, :, 0, :], in0=tv[0], in1=tv[1], op=ALU.subtract),
         r=[tmp[0], tmp[1]], w=[outb])
    P.op('dve', lambda: nc.vector.tensor_tensor(out=ov[:, :, :, 1, :], in0=tv[2], in1=tv[3], op=ALU.add),
         r=[tmp[2], tmp[3]], w=[outb])


def rope_tables(k):
    nc, P = k.nc, k.P
    PI = float(np.pi)
    k.ropeR_d = nc.dram_tensor("ropeR_d", [2, 128, TL * 2 * 64], F32)
    k.ropeD_d = nc.dram_tensor("ropeD_d", [2, 128, TL * 2 * 16], F32)
    with ExitStack() as ph:
        pos = sbt(ph, nc, "pos", [128, TL, 2], F32)
        P.dma('sp', pos[:], k.pos[:, :, :], key=pos, w=[pos])
        for (nf, d2, dst) in ((64, 128, k.ropeR_d), (16, 32, k.ropeD_d)):
            n = TL * 2 * nf
            inv = sbt(ph, nc, "inv", [128, nf], F32)
            ang = sbt(ph, nc, "ang", [128, n], F32)
            xx = sbt(ph, nc, "xx", [128, n], F32)
            ki = sbt(ph, nc, "ki", [128, n], I32)
            kf = sbt(ph, nc, "kf", [128, n], F32)
            mm = sbt(ph, nc, "mm", [128, n], F32)
            res = sbt(ph, nc, "res", [128, n], F32)
            P.op('act', lambda: nc.scalar.activation(out=inv[:], in_=k.iotaf[:, 0:nf], func=AF.Exp,
                                                     scale=-float(np.log(10000.0)) * 2.0 / d2), r=[k.iotaf], w=[inv])
            for t in range(TL):
                for rc in range(2):
                    o = (t * 2 + rc) * nf
                    P.op('dve', lambda t=t, rc=rc, o=o: nc.vector.tensor_scalar(
                        out=ang[:, o:o + nf], in0=inv[:], scalar1=pos[:, t, rc:rc + 1], scalar2=None, op0=ALU.mult),
                        r=[inv, pos], w=[ang])
            for ci, off in ((0, PI / 2), (1, 0.0)):
                P.op('dve', lambda: nc.vector.tensor_scalar(out=xx[:], in0=ang[:], scalar1=float(off), scalar2=None,
                                                            op0=ALU.add), r=[ang], w=[xx])
                P.op('dve', lambda: nc.vector.tensor_scalar(out=ki[:], in0=xx[:], scalar1=float(1.0 / (2 * PI)),
                                                            scalar2=None, op0=ALU.mult), r=[xx], w=[ki])
                P.op('dve', lambda: nc.vector.tensor_copy(out=kf[:], in_=ki[:]), r=[ki], w=[kf])
                P.op('dve', lambda: nc.vector.scalar_tensor_tensor(out=xx[:], in0=kf[:], scalar=-2 * PI, in1=xx[:],
                                                                   op0=ALU.mult, op1=ALU.add), r=[kf, xx], w=[xx])
                P.op('dve', lambda: nc.vector.tensor_scalar(out=mm[:], in0=xx[:], scalar1=PI, scalar2=None,
                                                            op0=ALU.is_gt), r=[xx], w=[mm])
                P.op('dve', lambda: nc.vector.scalar_tensor_tensor(out=xx[:], in0=mm[:], scalar=-2 * PI, in1=xx[:],
                                                                   op0=ALU.mult, op1=ALU.add), r=[mm, xx], w=[xx])
                P.op('dve', lambda: nc.vector.tensor_scalar(out=mm[:], in0=xx[:], scalar1=-PI, scalar2=None,
                                                            op0=ALU.is_lt), r=[xx], w=[mm])
                P.op('dve', lambda: nc.vector.scalar_tensor_tensor(out=xx[:], in0=mm[:], scalar=2 * PI, in1=xx[:],
                                                                   op0=ALU.mult, op1=ALU.add), r=[mm, xx], w=[xx])
                P.op('act', lambda: nc.scalar.activation(out=res[:], in_=xx[:], func=AF.Sin), r=[xx], w=[res])
                P.dma('sp', dst.ap()[ci], res[:], key=res, r=[res], w=[("ropetab", nf, ci)])
        P.barrier()


def ret_layer(k, l):
    nc, P = k.nc, k.P
    i = l // 2
    H, DK, DV = 4, 256, 512
    if not hasattr(k, "qT_d"):
        k.qT_d = nc.dram_tensor("qT_d", [8, 128, NT], BF16)
        k.kT_d = nc.dram_tensor("kT_d", [8, 128, NT], BF16)
        k.k_d = nc.dram_tensor("k_d", [NT, 1024], BF16)
        k.v_d = nc.dram_tensor("v_d", [NT, 2048], BF16)
        k.g_d = [nc.dram_tensor("g%d_d" % p, [NT, 2048], BF16) for p in range(2)]
        k.y_d = nc.dram_tensor("y_d", [NT, 2048], F32)
        k.F_d = nc.dram_tensor("F_d", [1024, 512], F32)
        k.Fg_d = nc.dram_tensor("Fg_d", [2048, 512], F32)
    tiles_all = list(range(TT))
    with ExitStack() as ph:
        uT = sbt(ph, nc, "r_uT", [128, 8, NT], BF16)
        with ExitStack() as ph1:
            prenorm(k, ph1, l, 0, 0, 1, uT, tiles_all)
            P.barrier()
        cos = sbt(ph, nc, "r_cos", [128, TL, 2, 64], F32)
        sin = sbt(ph, nc, "r_sin", [128, TL, 2, 64], F32)
        P.dma('sp', cos[:].rearrange("p t r f -> p (t r f)"), k.ropeR_d.ap()[0], key=cos, w=["rope_tab"])
        P.dma('sp', sin[:].rearrange("p t r f -> p (t r f)"), k.ropeR_d.ap()[1], key=sin, w=["rope_tab"])
        tmp = [sbt(ph, nc, "r_tmp%d" % j, [128, 256], F32) for j in range(4)]
        ob = Rot([sbt(ph, nc, "r_ob%d" % j, [128, 512], BF16) for j in range(3)])
        tb = Rot([sbt(ph, nc, "r_tb%d" % j, [128, 4, 128], BF16) for j in range(3)])

        def epi(t, g, ps):
            o = ob.next()
            if g < 4:
                scale = 1.0 if g < 2 else DK ** -0.5
                if t < TL:
                    rope_apply(k, ps, o, cos[:, t], sin[:, t], 2, 64, scale, tmp)
                else:
                    P.op('act', lambda: nc.scalar.activation(out=o[:], in_=ps[:], func=AF.Copy, scale=float(scale)),
                         r=[ps], w=[o])
                if g >= 2:
                    P.dma('sp', k.k_d[t * 128:(t + 1) * 128, (g - 2) * 512:(g - 1) * 512], o[:], key=o, r=[o],
                          w=[("k_d", t)])
                pb = k.psb_rot.next()
                for j in range(4):
                    P.op('pe', lambda j=j: nc.tensor.transpose(pb[:, j * 128:(j + 1) * 128], o[:, j * 128:(j + 1) * 128],
                                                              k.ident[:]), r=[o, k.ident], w=[pb], sig=(j == 3))
                tt = tb.next()
                P.op('act', lambda: nc.scalar.copy(out=tt[:], in_=pb[:, 0:512].rearrange("p (j n) -> p j n", j=4)),
                     r=[pb], w=[tt])
                dst = k.qT_d if g < 2 else k.kT_d
                gg_ = g % 2
                P.dma('sp', dst.ap()[gg_ * 4:(gg_ + 1) * 4, :, t * 128:(t + 1) * 128].rearrange("j p n -> p j n"), tt[:],
                      key=tt, r=[tt], w=[("qkT_d", g, t)])
            elif g < 8:
                P.op('act', lambda: nc.scalar.copy(out=o[:], in_=ps[:]), r=[ps], w=[o])
                P.dma('sp', k.v_d[t * 128:(t + 1) * 128, (g - 4) * 512:(g - 3) * 512], o[:], key=o, r=[o], w=[("v_d", t)])
            else:
                p_ = (g - 8) // 4
                gc = (g - 8) % 4
                P.op('act', lambda: nc.scalar.activation(out=o[:], in_=ps[:], func=AF.Silu), r=[ps], w=[o])
                P.dma('sp', k.g_d[p_][t * 128:(t + 1) * 128, gc * 512:(gc + 1) * 512], o[:], key=o, r=[o],
                      w=[("g_d", p_, t)])

        proj_tm(k, ph, uT, k.ret_w_in.ap()[i], 16, tiles_all, epi)
        P.barrier()
    with ExitStack() as ph:
        def t_(name, shape, dt):
            return sbt(ph, nc, name, shape, dt)

        lg = t_("lg", [128, 8], F32)
        dm = t_("dm", [128, 128], F32)
        msk = [t_("msk%d" % j, [128, 128], F32) for j in range(8)]
        xi = [t_("xi%d" % j, [128, 128], BF16) for j in range(8)]
        zeta = t_("zeta", [128, 8], F32)
        cd = t_("cd", [128, 8], F32)
        c128 = t_("c128", [128, 1], F32)
        tA = t_("tA", [128, 128], F32)
        tB = t_("tB", [128, 128], F32)
        P.dma('sp', lg[:], k.ret_decay.ap()[i:i + 1].rearrange("o p h -> o (p h)").to_broadcast([128, 8]), key=lg, w=[lg])
        P.op('act', lambda: nc.scalar.activation(out=lg[:], in_=lg[:], func=AF.Exp, scale=-1.0), r=[lg], w=[lg])
        P.op('dve', lambda: nc.vector.tensor_scalar(out=lg[:], in0=lg[:], scalar1=1.0, scalar2=None, op0=ALU.add),
             r=[lg], w=[lg])
        P.op('act', lambda: nc.scalar.activation(out=lg[:], in_=lg[:], func=AF.Ln), r=[lg], w=[lg])
        P.op('dve', lambda: nc.vector.tensor_scalar(out=lg[:], in0=lg[:], scalar1=-1.0, scalar2=None, op0=ALU.mult),
             r=[lg], w=[lg])
        P.op('dve', lambda: nc.vector.memset(c128[:], 128.0), w=[c128])
        P.op('dve', lambda: nc.vector.tensor_scalar(out=dm[:], in0=k.iotaf[:], scalar1=k.iotap[:, 0:1], scalar2=None,
                                                    op0=ALU.subtract), r=[k.iotaf, k.iotap], w=[dm])
        for p_ in range(2):
            sgn = 1.0 if p_ == 0 else -1.0
            for h in range(H):
                j = p_ * 4 + h
                lgc = lg[:, j:j + 1]
                P.op('dve', lambda: nc.vector.tensor_scalar(out=tA[:], in0=dm[:], scalar1=sgn, scalar2=0.0, op0=ALU.mult,
                                                            op1=ALU.max), r=[dm], w=[tA])
                P.op('act', lambda lgc=lgc: nc.scalar.activation(out=tA[:], in_=tA[:], func=AF.Exp, scale=lgc),
                     r=[tA, lg], w=[tA])
                P.op('dve', lambda: nc.vector.tensor_scalar(out=tB[:], in0=dm[:], scalar1=sgn, scalar2=0.0, op0=ALU.mult,
                                                            op1=ALU.is_ge), r=[dm], w=[tB])
                P.op('dve', lambda j=j: nc.vector.tensor_tensor(out=msk[j][:], in0=tA[:], in1=tB[:], op=ALU.mult),
                     r=[tA, tB], w=[msk[j]])
                if p_ == 0:
                    P.op('dve', lambda: nc.vector.tensor_scalar(out=tA[:], in0=k.iotaf[:], scalar1=1.0, scalar2=None,
                                                                op0=ALU.add), r=[k.iotaf], w=[tA])
                else:
                    P.op('dve', lambda: nc.vector.tensor_scalar(out=tA[:], in0=k.iotaf[:], scalar1=-1.0, scalar2=128.0,
                                                                op0=ALU.mult, op1=ALU.add), r=[k.iotaf], w=[tA])
                P.op('act', lambda j=j, lgc=lgc: nc.scalar.activation(out=xi[j][:], in_=tA[:], func=AF.Exp, scale=lgc),
                     r=[tA, lg], w=[xi[j]])
                if p_ == 0:
                    P.op('dve', lambda: nc.vector.tensor_scalar(out=tB[:, 0:1], in0=k.iotap[:], scalar1=-1.0, scalar2=127.0,
                                                                op0=ALU.mult, op1=ALU.add), r=[k.iotap], w=[tB])
                else:
                    P.op('dve', lambda: nc.vector.tensor_copy(out=tB[:, 0:1], in_=k.iotap[:]), r=[k.iotap], w=[tB])
                P.op('act', lambda j=j, lgc=lgc: nc.scalar.activation(out=zeta[:, j:j + 1], in_=tB[:, 0:1], func=AF.Exp,
                                                                     scale=lgc), r=[tB, lg], w=[zeta])
                P.op('act', lambda j=j, lgc=lgc: nc.scalar.activation(out=cd[:, j:j + 1], in_=c128[:], func=AF.Exp,
                                                                     scale=lgc), r=[c128, lg], w=[cd])
        Wo = t_("r_Wo", [128, 16, D], BF16)
        for half in range(2):
            P.dma('pool', Wo[:, half * 8:(half + 1) * 8, :],
                  k.ret_w_out.ap()[i][half * 1024:(half + 1) * 1024, :].rearrange("(f p) n -> p f n", p=128),
                  key=("Wo", half), w=[("Wo", half)])
        gg = gate_tiles(k, ph, l, 1, 2)
        R = [t_("R%d" % h, [128, 2, DV], F32) for h in range(H)]
        Rb = [t_("Rb%d" % h, [128, 2, DV], BF16) for h in range(H)]
        Fg = t_("Fg", [128, 2, DV], F32)
        qTb = Rot([t_("qTc%d" % j, [128, 8, 128], BF16) for j in range(2)])
        kTb = Rot([t_("kTc%d" % j, [128, 8, 128], BF16) for j in range(2)])
        kb = Rot([t_("kc%d" % j, [128, 1024], BF16) for j in range(2)])
        vb = Rot([t_("vc%d" % j, [128, 2048], BF16) for j in range(2)])
        gb = Rot([t_("gc%d" % j, [128, 2048], BF16) for j in range(2)])
        yb = Rot([t_("y%d" % j, [128, 2048], F32) for j in range(2)])
        y1b = Rot([t_("y1_%d" % j, [128, 2048], F32) for j in range(2)])
        ybf = Rot([t_("ybf%d" % j, [128, 2048], BF16) for j in range(2)])
        yTb = Rot([t_("yT%d" % j, [128, 16, 128], BF16) for j in range(2)])
        sTb = Rot([t_("sT%d" % j, [128, 128], BF16) for j in range(3)])
        qxb = Rot([t_("qx%d" % j, [128, 2, 128], BF16) for j in range(3)])
        kzb = Rot([t_("kz%d" % j, [128, 256], BF16) for j in range(3)])
        hb = Rot([t_("r_h%d" % j, [128, D], F32) for j in range(2)])
        ob2 = Rot([t_("r_o%d" % j, [128, D], F32) for j in range(2)])
        st4 = Rot([[t_("r_s%d_%d" % (a, b), [128, 1], F32) for b in range(4)] for a in range(2)])
        st3 = Rot([[t_("r_n%d_%d" % (a, b), [128, 1], F32) for b in range(3)] for a in range(4)])

        def zero_state():
            for h in range(H):
                P.op('pool', lambda h=h: nc.gpsimd.memset(R[h][:], 0.0), w=[R[h]])
                P.op('pool', lambda h=h: nc.gpsimd.memset(Rb[h][:], 0.0), w=[Rb[h]])

        def load_chunk(p_, t):
            qTc, kTc, kc, vc, gc = qTb.next(), kTb.next(), kb.next(), vb.next(), gb.next()
            cs = slice(t * 128, (t + 1) * 128)
            P.dma('sp', qTc[:], k.qT_d.ap()[:, :, cs].rearrange("j p n -> p j n"), key=qTc,
                  r=[("qkT_d", g, t) for g in (0, 1)], w=[qTc])
            P.dma('sp', kTc[:], k.kT_d.ap()[:, :, cs].rearrange("j p n -> p j n"), key=kTc,
                  r=[("qkT_d", g, t) for g in (2, 3)], w=[kTc])
            P.dma('sp', kc[:], k.k_d[cs, :], key=kc, r=[("k_d", t)], w=[kc])
            P.dma('sp', vc[:], k.v_d[cs, :], key=vc, r=[("v_d", t)], w=[vc])
            P.dma('sp', gc[:], k.g_d[p_][cs, :], key=gc, r=[("g_d", p_, t)], w=[gc])
            y1 = None
            if p_ == 1:
                y1 = y1b.next()
                P.dma('sp', y1[:], k.y_d[cs, :], key=y1, r=[("y_d", t)], w=[y1])
            return qTc, kTc, kc, vc, gc, y1

        def chunk(p_, t, bufs):
            qTc, kTc, kc, vc, gc, y1 = bufs
            cs = slice(t * 128, (t + 1) * 128)
            y = yb.next()
            if p_ == 1:
                yf = ybf.next()
            for h in range(H):
                j = p_ * 4 + h
                sT, qx, kz = sTb.next(), qxb.next(), kzb.next()
                ss, ms, rstd = st3.next()
                ps_s = k.psf_rot.next()
                for c2 in range(2):
                    P.op('pe', lambda c2=c2: nc.tensor.matmul(ps_s[:, 0:128], lhsT=kTc[:, h * 2 + c2, :],
                                                            rhs=qTc[:, h * 2 + c2, :], start=(c2 == 0), stop=(c2 == 1)),
                         r=[kTc, qTc], w=[ps_s], sig=(c2 == 1))
                P.op('dve', lambda: nc.vector.tensor_tensor(out=sT[:], in0=ps_s[:, 0:128], in1=msk[j][:], op=ALU.mult),
                     r=[ps_s, msk[j]], w=[sT])
                P.op('pool', lambda: nc.gpsimd.tensor_tensor(
                    out=qx[:], in0=qTc[:, h * 2:h * 2 + 2, :], in1=xi[j][:].unsqueeze(1).to_broadcast([128, 2, 128]),
                    op=ALU.mult), r=[qTc, xi[j]], w=[qx])
                ps_o = k.psf_rot.next()
                P.op('pe', lambda: nc.tensor.matmul(ps_o[:], lhsT=sT[:], rhs=vc[:, h * DV:(h + 1) * DV], start=True,
                                                    stop=False), r=[sT, vc], w=[ps_o], sig=False)
                for c2 in range(2):
                    P.op('pe', lambda c2=c2: nc.tensor.matmul(ps_o[:], lhsT=qx[:, c2, :], rhs=Rb[h][:, c2, :], start=False,
                                                            stop=(c2 == 1)), r=[qx, Rb[h]], w=[ps_o], sig=(c2 == 1))
                P.op('act', lambda: nc.scalar.activation(out=k.junk[:, 0:512], in_=ps_o[:], func=AF.Square, accum_out=ss[:]),
                     r=[ps_o], w=[ss])
                rstd_from_ss(k, ss, ms, rstd, DV)
                hs = slice(h * DV, (h + 1) * DV)
                P.op('dve', lambda: nc.vector.scalar_tensor_tensor(out=y[:, hs], in0=ps_o[:], scalar=rstd[:, 0:1],
                                                                   in1=gc[:, hs], op0=ALU.mult, op1=ALU.mult),
                     r=[ps_o, rstd, gc], w=[y])
                if p_ == 1:
                    P.op('pool', lambda: nc.gpsimd.tensor_tensor(out=yf[:, hs], in0=y[:, hs], in1=y1[:, hs], op=ALU.add),
                         r=[y, y1], w=[yf])
                P.op('act', lambda: nc.scalar.activation(out=kz[:], in_=kc[:, h * DK:(h + 1) * DK], func=AF.Copy,
                                                         scale=zeta[:, j:j + 1]), r=[kc, zeta], w=[kz])
                for c2 in range(2):
                    ps_k = k.psf_rot.next()
                    P.op('pe', lambda c2=c2, ps_k=ps_k: nc.tensor.matmul(ps_k[:], lhsT=kz[:, c2 * 128:(c2 + 1) * 128],
                                                                       rhs=vc[:, hs], start=True, stop=True),
                         r=[kz, vc], w=[ps_k])
                    P.op('dve', lambda c2=c2, ps_k=ps_k: nc.vector.scalar_tensor_tensor(
                        out=R[h][:, c2, :], in0=R[h][:, c2, :], scalar=cd[:, j:j + 1], in1=ps_k[:], op0=ALU.mult,
                        op1=ALU.add), r=[R[h], cd, ps_k], w=[R[h]])
                P.op('act', lambda: nc.scalar.copy(out=Rb[h][:], in_=R[h][:]), r=[R[h]], w=[Rb[h]])
            if p_ == 0:
                P.dma('pool', k.y_d[cs, :], y[:], key=y, r=[y], w=[("y_d", t)])
            else:
                yT = yTb.next()
                for half in range(2):
                    pb = k.psb_rot.next()
                    for jj in range(8):
                        kk = half * 8 + jj
                        P.op('pe', lambda jj=jj, kk=kk, pb=pb: nc.tensor.transpose(pb[:, jj * 128:(jj + 1) * 128],
                                                                                  yf[:, kk * 128:(kk + 1) * 128], k.ident[:]),
                             r=[yf, k.ident], w=[pb], sig=(jj == 7))
                    P.op('act', lambda half=half, pb=pb: nc.scalar.copy(out=yT[:, half * 8:(half + 1) * 8, :],
                                                                     in_=pb[:].rearrange("p (j n) -> p j n", j=8)),
                         r=[pb], w=[yT])
                pss = [k.psf_rot.next(), k.psf_rot.next()]
                for cg in range(2):
                    for kk in range(16):
                        P.op('pe', lambda cg=cg, kk=kk: nc.tensor.matmul(pss[cg][:], lhsT=yT[:, kk, :],
                                                                       rhs=Wo[:, kk, cg * 512:(cg + 1) * 512],
                                                                       start=(kk == 0), stop=(kk == 15)),
                             r=[yT, ("Wo", kk // 8)], w=[pss[cg]], sig=(kk == 15))
                resid_epilogue(k, pss, t, gg[0 if t < TL else 1], hb.next(), ob2.next(), st4.next())

        def run_chunks(p_, ts):
            nxt = load_chunk(p_, ts[0])
            for n_, t in enumerate(ts):
                cur = nxt
                if n_ + 1 < len(ts):
                    nxt = load_chunk(p_, ts[n_ + 1])
                chunk(p_, t, cur)

        zero_state()
        run_chunks(0, [TL, TL + 1] + list(range(TL)))
        for h in range(H):
            P.dma('sp', k.F_d.ap()[h * 256:(h + 1) * 256, :].rearrange("(c p) n -> p c n", p=128), R[h][:], key=R[h],
                  r=[R[h]], w=["F_d"])
        P.op('pool', lambda: nc.gpsimd.collective_compute("AllGather", ALU.bypass, replica_groups=GROUPS,
                                                          ins=[k.F_d.ap().opt()], outs=[k.Fg_d.ap().opt()]),
             r=["F_d"], w=["Fg_d"], dma="cc", inc=1)
        zero_state()
        run_chunks(1, [TL + 1, TL])
        for h in range(H):
            for rk_ in range(2):
                P.dma('sp', Fg[:], k.Fg_d.ap()[rk_ * 1024 + h * 256:rk_ * 1024 + (h + 1) * 256, :].rearrange(
                    "(c p) n -> p c n", p=128), key=Fg, r=["Fg_d"], w=[Fg])
                if rk_ == 0:
                    P.op('dve', lambda h=h: nc.vector.tensor_scalar(out=R[h][:], in0=Fg[:], scalar1=k.selt[:, 0:1],
                                                                  scalar2=None, op0=ALU.mult), r=[Fg, k.selt], w=[R[h]])
                else:
                    P.op('dve', lambda h=h: nc.vector.scalar_tensor_tensor(out=R[h][:], in0=Fg[:], scalar=k.selt[:, 1:2],
                                                                         in1=R[h][:], op0=ALU.mult, op1=ALU.add),
                         r=[Fg, k.selt, R[h]], w=[R[h]])
            P.op('act', lambda h=h: nc.scalar.copy(out=Rb[h][:], in_=R[h][:]), r=[R[h]], w=[Rb[h]])
        run_chunks(1, list(range(TL - 1, -1, -1)))
        P.barrier()


def diff_layer(k, l, last):
    import math
    nc, P = k.nc, k.P
    i = l // 2
    lam_init = 0.8 - 0.6 * math.exp(-0.3 * l)
    NH = 8
    if not hasattr(k, "qT_d"):
        k.qT_d = nc.dram_tensor("qT_d", [8, 128, NT], BF16)
    if not hasattr(k, "kTl_d"):
        k.kTl_d = [nc.dram_tensor("kTl%d_d" % g, [512, NL], BF16) for g in range(2)]
        k.kTc_d = nc.dram_tensor("kTc_d", [1024, NCX], BF16)
        k.kTg_d = [nc.dram_tensor("kTg%d_d" % g, [1024, NL], BF16) for g in range(2)]
        k.vl_d = [nc.dram_tensor("vl%d_d" % g, [NL, 512], BF16) for g in range(2)]
        k.vc_d = nc.dram_tensor("vc_d", [NCX, 1024], BF16)
        k.vg_d = [nc.dram_tensor("vg%d_d" % g, [2 * NL, 512], BF16) for g in range(2)]
    tiles_all = list(range(TT))
    q_tiles = list(range(TL)) + ([] if last else [TL, TL + 1])
    with ExitStack() as ph:
        uT = sbt(ph, nc, "d_uT", [128, 8, NT], BF16)
        with ExitStack() as ph1:
            prenorm(k, ph1, l, 0, 0, 1, uT, tiles_all)
            P.barrier()
        cos = sbt(ph, nc, "d_cos", [128, TL, 2, 16], F32)
        sin = sbt(ph, nc, "d_sin", [128, TL, 2, 16], F32)
        P.dma('sp', cos[:].rearrange("p t r f -> p (t r f)"), k.ropeD_d.ap()[0], key=cos, w=["rope_tab"])
        P.dma('sp', sin[:].rearrange("p t r f -> p (t r f)"), k.ropeD_d.ap()[1], key=sin, w=["rope_tab"])
        tmp = [sbt(ph, nc, "d_tmp%d" % j, [128, 256], F32) for j in range(4)]
        ob = Rot([sbt(ph, nc, "d_ob%d" % j, [128, 512], BF16) for j in range(3)])
        tb = Rot([sbt(ph, nc, "d_tb%d" % j, [128, 4, 128], BF16) for j in range(3)])

        def epi_qk(isq):
            def epi(t, g, ps):
                if isq and t not in q_tiles:
                    return
                o = ob.next()
                scale = 0.125 if isq else 1.0
                if t < TL:
                    rope_apply(k, ps, o, cos[:, t], sin[:, t], 8, 16, scale, tmp)
                else:
                    P.op('act', lambda: nc.scalar.activation(out=o[:], in_=ps[:], func=AF.Copy, scale=float(scale)),
                         r=[ps], w=[o])
                pb = k.psb_rot.next()
                for j in range(4):
                    P.op('pe', lambda j=j: nc.tensor.transpose(pb[:, j * 128:(j + 1) * 128], o[:, j * 128:(j + 1) * 128],
                                                              k.ident[:]), r=[o, k.ident], w=[pb], sig=(j == 3))
                tt = tb.next()
                P.op('act', lambda: nc.scalar.copy(out=tt[:], in_=pb[:, 0:512].rearrange("p (j n) -> p j n", j=4)),
                     r=[pb], w=[tt])
                if isq:
                    P.dma('sp', k.qT_d.ap()[g * 4:(g + 1) * 4, :, t * 128:(t + 1) * 128].rearrange("j p n -> p j n"), tt[:],
                          key=tt, r=[tt], w=[("qT_d", g, t)])
                elif t < TL:
                    P.dma('sp', k.kTl_d[g].ap()[:, t * 128:(t + 1) * 128].rearrange("(j p) n -> p j n", p=128),
                          tt[:], key=tt, r=[tt], w=[("kTl_d", g, t)])
                    if t == TL - 1:
                        P.op('pool', lambda: nc.gpsimd.collective_compute(
                            "AllGather", ALU.bypass, replica_groups=GROUPS, ins=[k.kTl_d[g].ap().opt()],
                            outs=[k.kTg_d[g].ap().opt()]), r=[("kTl_d", g, t_) for t_ in range(TL)], w=[("kTg_d", g)],
                            dma="cc", inc=1)
                else:
                    P.dma('sp', k.kTc_d.ap()[g * 512:(g + 1) * 512, (t - TL) * 128:(t - TL + 1) * 128].rearrange(
                        "(j p) n -> p j n", p=128), tt[:], key=tt, r=[tt], w=[("kTc_d", g, t)])
            return epi

        def epi_v(t, g, ps):
            o = ob.next()
            P.op('act', lambda: nc.scalar.copy(out=o[:], in_=ps[:]), r=[ps], w=[o])
            if t < TL:
                P.dma('sp', k.vl_d[g][t * 128:(t + 1) * 128, :], o[:], key=o, r=[o], w=[("vl_d", g, t)])
                if t == TL - 1:
                    P.op('pool', lambda: nc.gpsimd.collective_compute(
                        "AllGather", ALU.bypass, replica_groups=GROUPS, ins=[k.vl_d[g].ap().opt()],
                        outs=[k.vg_d[g].ap().opt()]), r=[("vl_d", g, t_) for t_ in range(TL)], w=[("vg_d", g)],
                        dma="cc", inc=1)
            else:
                P.dma('sp', k.vc_d[(t - TL) * 128:(t - TL + 1) * 128, g * 512:(g + 1) * 512], o[:], key=o, r=[o],
                      w=[("vc_d", g, t)])

        wi = k.diff_w_in.ap()[i]
        proj_tm(k, ph, uT, wi[:, 1024:2048], 2, tiles_all, epi_qk(False))
        proj_tm(k, ph, uT, wi[:, 2048:3072], 2, tiles_all, epi_v)
        proj_tm(k, ph, uT, wi[:, 0:1024], 2, tiles_all, epi_qk(True))
        P.barrier()
    with ExitStack() as ph:
        def t_(name, shape, dt):
            return sbt(ph, nc, name, shape, dt)

        NK = 2 * NL + NCX
        NKT = NK // 128
        yT = t_("d_yT", [128, NH, NT], BF16)
        Wo = t_("d_Wo", [128, 8, D], BF16)
        P.dma('pool', Wo[:], k.diff_w_out.ap()[i].rearrange("(f p) n -> p f n", p=128), key=Wo, w=[Wo])
        gg = gate_tiles(k, ph, l, 1, 2)
        ones = t_("d_ones", [128, 128], BF16)
        P.op('dve', lambda: nc.vector.memset(ones[:], 1.0), w=[ones])
        nh512 = t_("d_nh", [128, 512], F32)
        P.op('dve', lambda: nc.vector.memset(nh512[:], -0.5), w=[nh512])
        lv = t_("d_lv", [128, 4, 64], F32)
        lp = t_("d_lp", [128, 2, 64], F32)
        ls = t_("d_ls", [128, 2], F32)
        nlam = t_("d_nlam", [128, 1], F32)
        gsub = t_("d_gsub", [128, 1], F32)
        P.dma('sp', lv[:].rearrange("p a b -> p (a b)"),
              k.diff_lam.ap()[i:i + 1].rearrange("o a b -> o (a b)").to_broadcast([128, 256]), key=lv, w=[lv])
        for a in range(2):
            P.op('dve', lambda a=a: nc.vector.tensor_tensor(out=lp[:, a, :], in0=lv[:, 2 * a, :], in1=lv[:, 2 * a + 1, :],
                                                          op=ALU.mult), r=[lv], w=[lp])
            P.op('dve', lambda a=a: nc.vector.reduce_sum(out=ls[:, a:a + 1], in_=lp[:, a, :], axis=AX.X), r=[lp], w=[ls])
        P.op('act', lambda: nc.scalar.activation(out=ls[:], in_=ls[:], func=AF.Exp), r=[ls], w=[ls])
        P.op('dve', lambda: nc.vector.scalar_tensor_tensor(out=nlam[:], in0=ls[:, 1:2], scalar=-float(lam_init),
                                                           in1=ls[:, 0:1], op0=ALU.add, op1=ALU.subtract),
             r=[ls], w=[nlam])
        P.dma('sp', gsub[:], k.diff_subln.ap()[i].rearrange("(p o) -> p o", o=1), key=gsub, w=[gsub],
              allow_slow_non_contiguous=True)
        P.op('dve', lambda: nc.vector.tensor_scalar(out=gsub[:], in0=gsub[:], scalar1=float(1.0 - lam_init), scalar2=None,
                                                    op0=ALU.mult), r=[gsub], w=[gsub])
        kTb = Rot([t_("d_kT%d" % j, [128, NK], BF16) for j in range(2)])
        Vb = Rot([t_("d_V%d" % j, [128, NKT, 128], BF16) for j in range(2)])
        qTb = Rot([t_("d_qT%d" % j, [128, NT], BF16) for j in range(2)])
        eb = Rot([t_("d_e%d" % j, [128, 512], BF16) for j in range(4)])
        rzb = Rot([t_("d_rz%d" % j, [128, 512], F32) for j in range(2)])
        Onb = [t_("d_On%d" % j, [128, 512], F32) for j in range(2)]
        OT = t_("d_OT", [128, 512], F32)
        sq = t_("d_sq", [128, 512], BF16)
        rs = t_("d_rs", [128, 512], F32)
        hb = Rot([t_("d_h%d" % j, [128, D], F32) for j in range(2)])
        ob2 = Rot([t_("d_o%d" % j, [128, D], F32) for j in range(2)])
        st4 = Rot([[t_("d_s%d_%d" % (a, b), [128, 1], F32) for b in range(4)] for a in range(2)])
        ps_s = Rot(k.psf[0:2])
        ps_oz = Rot([(k.psf[2], k.psf[3]), (k.psf[4], k.psf[5])])

        def load_head(h):
            kT, V, qT = kTb.next(), Vb.next(), qTb.next()
            for rk_ in range(2):
                hg_, hh = h // 4, h % 4
                P.dma('sp', kT[:, rk_ * NL:(rk_ + 1) * NL],
                      k.kTg_d[hg_][rk_ * 512 + hh * 128:rk_ * 512 + (hh + 1) * 128, :],
                      key=("kT", kT.name, rk_), r=[("kTg_d", hg_)], w=[("kTp", kT.name, rk_)])
                P.dma('sp', V[:, rk_ * TL:(rk_ + 1) * TL, :],
                      k.vg_d[hg_].ap()[rk_ * NL:(rk_ + 1) * NL, hh * 128:(hh + 1) * 128].rearrange("(j p) d -> p j d", p=128),
                      key=("V", V.name, rk_), r=[("vg_d", hg_)], w=[("Vp", V.name, rk_)])
            P.dma('sp', kT[:, 2 * NL:NK], k.kTc_d[h * 128:(h + 1) * 128, :], key=("kT", kT.name, 2), r=[("kTc_d", h // 4, t_) for t_ in (TL, TL + 1)], w=[("kTp", kT.name, 2)])
            P.dma('sp', V[:, 2 * TL:NKT, :], k.vc_d.ap()[:, h * 128:(h + 1) * 128].rearrange("(j p) d -> p j d", p=128),
                  key=("V", V.name, 2), r=[("vc_d", h // 4, t_) for t_ in (TL, TL + 1)], w=[("Vp", V.name, 2)])
            P.dma('sp', qT[:], k.qT_d.ap()[h], key=qT, r=[("qT_d", h // 4, t) for t in q_tiles], w=[qT])
            return kT, V, qT

        blocks = [(qb * 512, 512, list(range(NKT))) for qb in range(4)]
        if not last:
            blocks.append((NL, NCX, [NKT - 2, NKT - 1]))
        nxt = load_head(0)
        for h in range(NH):
            kT, V, qT = nxt
            if h + 1 < NH:
                nxt = load_head(h + 1)
            for (q0, nq, ktiles) in blocks:
                for w_ in range(2):
                    rows = slice(w_ * 64, (w_ + 1) * 64)
                    pO, pZ = ps_oz.next()
                    for jn, jt in enumerate(ktiles):
                        ps = ps_s.next()
                        e = eb.next()
                        P.op('pe', lambda: nc.tensor.matmul(ps[:, 0:nq], lhsT=kT[rows, jt * 128:(jt + 1) * 128],
                                                            rhs=qT[rows, q0:q0 + nq], start=True, stop=True),
                             r=[("kTp", kT.name, min(jt // TL, 2)), qT], w=[ps])
                        P.op('act', lambda: nc.scalar.activation(out=e[:, 0:nq], in_=ps[:, 0:nq], func=AF.Exp),
                             r=[ps], w=[e])
                        first, lastk = jn == 0, jn == len(ktiles) - 1
                        P.op('pe', lambda: nc.tensor.matmul(pO[:, 0:nq], lhsT=V[:, jt, :], rhs=e[:, 0:nq], start=first,
                                                            stop=lastk), r=[("Vp", V.name, min(jt // TL, 2)), e], w=[pO], sig=lastk)
                        P.op('pe', lambda: nc.tensor.matmul(pZ[:, 0:nq], lhsT=ones[:], rhs=e[:, 0:nq], start=first,
                                                            stop=lastk), r=[ones, e], w=[pZ], sig=True)
                    rz = rzb.next()
                    P.op('dve', lambda: nc.vector.reciprocal(out=rz[:, 0:nq], in_=pZ[:, 0:nq]), r=[pZ], w=[rz])
                    P.op('dve', lambda: nc.vector.tensor_tensor(out=Onb[w_][:, 0:nq], in0=pO[:, 0:nq], in1=rz[:, 0:nq],
                                                                op=ALU.mult), r=[pO, rz], w=[Onb[w_]])
                P.op('dve', lambda: nc.vector.scalar_tensor_tensor(out=OT[:, 0:nq], in0=Onb[1][:, 0:nq], scalar=nlam[:, 0:1],
                                                                   in1=Onb[0][:, 0:nq], op0=ALU.mult, op1=ALU.add),
                     r=[Onb[0], Onb[1], nlam], w=[OT])
                P.op('act', lambda: nc.scalar.activation(out=sq[:, 0:nq], in_=OT[:, 0:nq], func=AF.Square), r=[OT], w=[sq])
                pS = ps_s.next()
                P.op('pe', lambda: nc.tensor.matmul(pS[:, 0:nq], lhsT=ones[:], rhs=sq[:, 0:nq], start=True, stop=True),
                     r=[ones, sq], w=[pS])
                P.op('dve', lambda: nc.vector.tensor_scalar(out=rs[:, 0:nq], in0=pS[:, 0:nq], scalar1=1.0 / 128, scalar2=EPS,
                                                            op0=ALU.mult, op1=ALU.add), r=[pS], w=[rs])
                P.op('pool', lambda: nc.gpsimd.tensor_tensor(out=rs[:, 0:nq], in0=rs[:, 0:nq], in1=nh512[:, 0:nq], op=ALU.pow),
                     r=[rs, nh512], w=[rs])
                P.op('dve', lambda: nc.vector.scalar_tensor_tensor(out=yT[:, h, q0:q0 + nq], in0=OT[:, 0:nq],
                                                                   scalar=gsub[:, 0:1], in1=rs[:, 0:nq], op0=ALU.mult,
                                                                   op1=ALU.mult), r=[OT, gsub, rs], w=[("yT", h, q0)])
        for t in q_tiles:
            pss = [k.psf_rot.next(), k.psf_rot.next()]
            for cg in range(2):
                for kk in range(8):
                    P.op('pe', lambda cg=cg, kk=kk: nc.tensor.matmul(pss[cg][:], lhsT=yT[:, kk, t * 128:(t + 1) * 128],
                                                                   rhs=Wo[:, kk, cg * 512:(cg + 1) * 512],
                                                                   start=(kk == 0), stop=(kk == 7)),
                         r=[("yT", kk, (t * 128 // 512) * 512 if t < TL else NL), Wo], w=[pss[cg]], sig=(kk == 7))
            resid_epilogue(k, pss, t, gg[0 if t < TL else 1], hb.next(), ob2.next(), st4.next())
        P.barrier()

def prep_inputs(inp):
    f = lambda a: np.ascontiguousarray(np.asarray(a, dtype=np.float32))
    x, c, ctx, c_ctx = f(inp["x"]), f(inp["c"]), f(inp["ctx"]), f(inp["c_ctx"])
    shared = {n: f(inp[n]) for n in ("ada_w", "ada_b", "norm_g", "ret_w_out", "diff_w_in", "diff_w_out",
                                     "diff_subln", "ffn_w_up", "ffn_conv_b", "ffn_w_down")}
    shared["diff_lam"] = np.ascontiguousarray(np.stack([f(inp["diff_lq1"]), f(inp["diff_lk1"]), f(inp["diff_lq2"]),
                                                        f(inp["diff_lk2"])], axis=1))
    rw = f(inp["ret_w_in"])
    rw_sw = np.ascontiguousarray(np.concatenate([rw[:, :, :4096], rw[:, :, 6144:8192], rw[:, :, 4096:6144]], axis=2))
    dec = np.stack([f(inp["ret_decay_f"]), f(inp["ret_decay_b"])], axis=1)
    dec_sw = np.ascontiguousarray(dec[:, ::-1, :])
    cwv = f(inp["ffn_conv_w"])
    cw_sw = np.ascontiguousarray(cwv[:, ::-1, :])
    maps = []
    for core in range(8):
        b, half = core // 2, core % 2
        xs = x[b, half * NL:(half + 1) * NL]
        cs = ctx[b]
        pos = np.arange(half * NL, (half + 1) * NL)
        if half:
            xs, cs, pos = xs[::-1], cs[::-1], pos[::-1]
        posrc = np.stack([pos // 64, pos % 64], axis=-1).astype(np.float32)
        m = dict(shared)
        m["x"] = np.ascontiguousarray(xs)
        m["ctx"] = np.ascontiguousarray(cs)
        m["c2"] = np.ascontiguousarray(np.stack([c[b], c_ctx]))
        sel = np.zeros((128, 2), np.float32)
        sel[:, 1 - half] = 1.0
        m["sel"] = sel
        m["pos"] = np.ascontiguousarray(posrc.reshape(TL, 128, 2).transpose(1, 0, 2))
        m["ret_w_in"] = rw_sw if half else rw
        m["ret_decay"] = dec_sw if half else np.ascontiguousarray(dec)
        m["ffn_conv_w"] = cw_sw if half else cwv
        maps.append(m)
    return maps


def gather_out(results):
    out = np.empty((4, 2 * NL, D), np.float32)
    for core in range(8):
        b, half = core // 2, core % 2
        o = np.asarray(results[core]["out"])[:NL]
        out[b, half * NL:(half + 1) * NL] = o[::-1] if half else o
    return out


def kernel(**inputs):
    nc = build("full")
    maps = prep_inputs(inputs)
    res = run_bass_kernel_spmd(nc, maps, core_ids=list(range(8)))
    return gather_out(res.results)
```

```python
import numpy as np
from contextlib import ExitStack
import concourse.bass as bass
import concourse.mybir as mybir
from concourse.bass_utils import run_bass_kernel_spmd

F32 = mybir.dt.float32
BF16 = mybir.dt.bfloat16
I32 = mybir.dt.int32
AF = mybir.ActivationFunctionType
ALU = mybir.AluOpType
AX = mybir.AxisListType

D = 1024
NL = 2048
NCX = 256
NT = NL + NCX
TL, TCX, TT = 16, 2, 18
DEPTH = 4
FH = 2816
NFC = 22
EPS = 1e-6
GROUPS = [[0, 1], [2, 3], [4, 5], [6, 7]]


class Prog:
    def __init__(self, nc, stack):
        self.nc = nc
        self.stack = stack
        self.E = dict(pe=nc.tensor, act=nc.scalar, dve=nc.vector, pool=nc.gpsimd, sp=nc.sync)
        self.phys = []
        self.pcnt = []
        self.key2p = {}
        self.free = []
        self.lastw = {}
        self.rd = {}
        self.dmalast = {}
        self.waited = {e: {} for e in self.E}
        self.nins = 0

    def _sem(self, k):
        if k not in self.key2p:
            if self.free:
                p = self.free.pop()
            else:
                p = len(self.phys)
                self.phys.append(self.stack.enter_context(self.nc.semaphore("s%d" % p)))
                self.pcnt.append(0)
            self.key2p[k] = p
        return self.key2p[k]

    @staticmethod
    def _rk(x):
        return x if isinstance(x, (str, tuple)) else ("T", x.name)

    def op(self, eng, fn, r=(), w=(), sig=True, dma=None, inc=None):
        r = [self._rk(x) for x in r]
        w = [self._rk(x) for x in w]
        if dma is not None:
            dma = self._rk(dma)
        kind = 'd' if dma is not None else 'c'
        p = self._sem(('d', dma) if dma is not None else eng)
        deps = {}

        def add(tok, raw):
            k, v, e, kd = tok
            if kd == 'c' and kind == 'c' and e == eng:
                if eng == 'pe' or not raw:
                    return
            if deps.get(k, 0) < v:
                deps[k] = v

        for x in r:
            if x in self.lastw:
                add(self.lastw[x], True)
        for x in w:
            if x in self.lastw:
                add(self.lastw[x], False)
            for tok in self.rd.get(x, {}).values():
                add(tok, False)
        if dma is not None and dma in self.dmalast:
            add(self.dmalast[dma], False)
        wd = self.waited[eng]
        for k, v in deps.items():
            if wd.get(k, 0) >= v:
                continue
            assert self.pcnt[k] >= v, ("wait on unsignaled op", k, v, self.pcnt[k])
            self.E[eng].wait_ge(self.phys[k], v)
            wd[k] = v
            self.nins += 1
        ins = fn()
        self.nins += 1
        if sig:
            if inc is None:
                inc = 16 if kind == 'd' else 1
            self.pcnt[p] += inc
            ins.then_inc(self.phys[p], inc)
            val = self.pcnt[p]
        else:
            val = self.pcnt[p] + 1
        tok = (p, val, eng, kind)
        for x in r:
            self.rd.setdefault(x, {})[p] = tok
        for x in w:
            self.lastw[x] = tok
            self.rd[x] = {}
        if dma is not None:
            self.dmalast[dma] = tok

    def dma(self, q, out, in_, key, r=(), w=(), **kw):
        self.op(q, lambda: self.E[q].dma_start(out=out, in_=in_, **kw), r=r, w=w, dma=key)

    def barrier(self):
        for e in self.E:
            wd = self.waited[e]
            for p, h in enumerate(self.phys):
                v = self.pcnt[p]
                if v > wd.get(p, 0):
                    self.E[e].wait_ge(h, v)
                    wd[p] = v
                    self.nins += 1
        self.lastw.clear()
        self.rd.clear()
        self.dmalast.clear()
        for k in [k for k in self.key2p if isinstance(k, tuple) and k[0] == 'd']:
            self.free.append(self.key2p.pop(k))


_UNIQ = [0]


def sbt(ph, nc, name, shape, dt):
    _UNIQ[0] += 1
    return ph.enter_context(nc.sbuf_tensor("%s_%d" % (name, _UNIQ[0]), list(shape), dt))


class Rot:
    def __init__(self, items):
        self.items = list(items)
        self.i = 0

    def next(self):
        x = self.items[self.i % len(self.items)]
        self.i += 1
        return x


class K:
    pass


def build(mode="full"):
    nc = bass.Bass("TRN2", target_bir_lowering=False)
    stack = ExitStack()
    with stack:
        _build(nc, stack, mode)
    return nc


def _dram_in(nc, name, shape, dt=F32):
    return nc.dram_tensor(name, list(shape), dt, kind="ExternalInput")


def _build(nc, stack, mode):
    P = Prog(nc, stack)
    k = K()
    k.nc, k.P, k.mode = nc, P, mode
    k.x = _dram_in(nc, "x", [NL, D])
    k.ctx = _dram_in(nc, "ctx", [NCX, D])
    k.c2 = _dram_in(nc, "c2", [2, D])
    k.sel = _dram_in(nc, "sel", [128, 2])
    k.pos = _dram_in(nc, "pos", [128, TL, 2])
    k.ada_w = _dram_in(nc, "ada_w", [DEPTH, D, 6 * D])
    k.ada_b = _dram_in(nc, "ada_b", [DEPTH, 6 * D])
    k.norm_g = _dram_in(nc, "norm_g", [DEPTH, 4, D])
    k.ret_w_in = _dram_in(nc, "ret_w_in", [2, D, 8192])
    k.ret_w_out = _dram_in(nc, "ret_w_out", [2, 2048, D])
    k.ret_decay = _dram_in(nc, "ret_decay", [2, 2, 4])
    k.diff_w_in = _dram_in(nc, "diff_w_in", [2, D, 3 * D])
    k.diff_w_out = _dram_in(nc, "diff_w_out", [2, D, D])
    k.diff_lam = _dram_in(nc, "diff_lam", [2, 4, 64])
    k.diff_subln = _dram_in(nc, "diff_subln", [2, 128])
    k.ffn_w_up = _dram_in(nc, "ffn_w_up", [DEPTH, D, 2 * FH])
    k.ffn_conv_w = _dram_in(nc, "ffn_conv_w", [DEPTH, 3, FH])
    k.ffn_conv_b = _dram_in(nc, "ffn_conv_b", [DEPTH, FH])
    k.ffn_w_down = _dram_in(nc, "ffn_w_down", [DEPTH, FH, D])
    k.out = nc.dram_tensor("out", [NT, D], F32, kind="ExternalOutput")
    k.h_d = nc.dram_tensor("h_d", [NT, D], F32)
    k.mod_d = nc.dram_tensor("mod_d", [DEPTH, 2, 6 * D], F32)
    k.hx_d = nc.dram_tensor("hx_d", [128, 8], BF16)
    k.hg_d = nc.dram_tensor("hg_d", [256, 8], BF16)

    def sb(name, shape, dt):
        return stack.enter_context(nc.sbuf_tensor(name, list(shape), dt))

    k.ident = sb("ident", [128, 128], BF16)
    k.iotaf = sb("iotaf", [128, 128], F32)
    k.iotap = sb("iotap", [128, 1], F32)
    k.neghalf = sb("neghalf", [128, 1], F32)
    k.epsb = sb("epsb", [128, 1], F32)
    k.selt = sb("selt", [128, 2], F32)
    k.junk = sb("junk", [128, 1024], BF16)
    k.psf = [stack.enter_context(nc.psum_tensor("psf%d" % i, [128, 512], F32)) for i in range(6)]
    k.psb = [stack.enter_context(nc.psum_tensor("psb%d" % i, [128, 1024], BF16)) for i in range(2)]
    k.psf_rot = Rot(k.psf)
    k.psb_rot = Rot(k.psb)

    consts(k)
    ada_all(k)
    init_h(k)
    if mode == "ffn0":
        ffn_layer(k, 0, False)
    if mode in ("ret0", "L0"):
        rope_tables(k)
        ret_layer(k, 0)
        if mode == "L0":
            ffn_layer(k, 0, False)
    if mode == "diff1":
        rope_tables(k)
        diff_layer(k, 1, False)
    if mode == "full":
        rope_tables(k)
        for l in range(DEPTH):
            last = l == DEPTH - 1
            if l % 2 == 0:
                ret_layer(k, l)
            else:
                diff_layer(k, l, last)
            ffn_layer(k, l, last)
    P.barrier()
    with ExitStack() as ph:
        ob = [sbt(ph, nc, "ob%d" % i, [128, D], F32) for i in range(3)]
        rot = Rot(ob)
        for t in range(TT):
            b = rot.next()
            P.dma('sp', b[:], k.h_d[t * 128:(t + 1) * 128, :], key=b, r=[("h", t)], w=[b])
            P.dma('sp', k.out[t * 128:(t + 1) * 128, :], b[:], key=b, r=[b], w=[("out", t)])
        P.barrier()
    print("instructions:", P.nins, "sems:", len(P.phys))


def consts(k):
    nc, P = k.nc, k.P
    with ExitStack() as ph:
        ii = sbt(ph, nc, "ii", [128, 128], I32)
        pp = sbt(ph, nc, "pp", [128, 1], I32)
        P.op('pool', lambda: nc.gpsimd.iota(ii[:], pattern=[[1, 128]], base=0, channel_multiplier=0), w=[ii])
        P.op('pool', lambda: nc.gpsimd.iota(pp[:], pattern=[[0, 1]], base=0, channel_multiplier=1), w=[pp])
        P.op('dve', lambda: nc.vector.tensor_copy(out=k.iotaf[:], in_=ii[:]), r=[ii], w=[k.iotaf])
        P.op('dve', lambda: nc.vector.tensor_copy(out=k.iotap[:], in_=pp[:]), r=[pp], w=[k.iotap])
        P.op('dve', lambda: nc.vector.tensor_scalar(out=k.ident[:], in0=k.iotaf[:], scalar1=k.iotap[:, 0:1],
                                                    scalar2=None, op0=ALU.is_equal),
             r=[k.iotaf, k.iotap], w=[k.ident])
        P.op('dve', lambda: nc.vector.memset(k.neghalf[:], -0.5), w=[k.neghalf])
        P.op('dve', lambda: nc.vector.memset(k.epsb[:], EPS), w=[k.epsb])
        P.dma('sp', k.selt[:], k.sel[:, :], key=k.selt, w=[k.selt])
        P.barrier()


def init_h(k):
    P = k.P
    P.dma('sp', k.h_d[0:NL, :], k.x[:, :], key="inith0", w=[("h", t) for t in range(TL)])
    P.dma('sp', k.h_d[NL:NT, :], k.ctx[:, :], key="inith1", w=[("h", t) for t in range(TL, TT)])


def ada_all(k):
    nc, P = k.nc, k.P
    with ExitStack() as ph:
        cT = sbt(ph, nc, "cT", [128, 8, 2], F32)
        sT = sbt(ph, nc, "sT", [128, 8, 2], F32)
        wb = [sbt(ph, nc, "adaw%d" % i, [128, 8, 512], F32) for i in range(2)]
        bias = sbt(ph, nc, "adab", [2, 6 * D], F32)
        row = sbt(ph, nc, "adarow", [2, 6 * D], F32)
        wrot = Rot(wb)
        for r in range(2):
            P.dma('sp', cT[:, :, r], k.c2.ap()[r].rearrange("(k p) -> p k", p=128), key=cT, w=[cT],
                  allow_slow_non_contiguous=True)
        P.op('act', lambda: nc.scalar.activation(out=sT[:], in_=cT[:], func=AF.Silu), r=[cT], w=[sT])
        for l in range(DEPTH):
            P.dma('sp', bias[:], k.ada_b.ap()[l:l + 1, :].to_broadcast([2, 6 * D]), key=bias, w=[bias])
            for g in range(12):
                w = wrot.next()
                P.dma('sp', w[:], k.ada_w.ap()[l][:, g * 512:(g + 1) * 512].rearrange("(k p) n -> p k n", p=128),
                      key=w, w=[w])
                ps = k.psf_rot.next()
                for kk in range(8):
                    P.op('pe', lambda kk=kk, w=w, ps=ps: nc.tensor.matmul(ps[0:2, :], lhsT=sT[:, kk, :], rhs=w[:, kk, :],
                                                                     start=(kk == 0), stop=(kk == 7)),
                         r=[sT, w], w=[ps], sig=(kk == 7))
                P.op('dve', lambda g=g, ps=ps: nc.vector.tensor_tensor(out=row[:, g * 512:(g + 1) * 512], in0=ps[0:2, :],
                                                                    in1=bias[:, g * 512:(g + 1) * 512], op=ALU.add),
                     r=[ps, bias], w=[row])
            P.dma('sp', k.mod_d.ap()[l], row[:], key=row, r=[row], w=[("mod", l)])
        P.barrier()


def bcast_row(k, q, dst, src_row, key, r=(), w=()):
    n = dst.shape[-1]
    k.P.dma(q, dst, src_row.to_broadcast([128, n]), key=key, r=r, w=w)


def rstd_from_ss(k, ss, ms, rstd, n):
    nc, P = k.nc, k.P
    P.op('act', lambda: nc.scalar.activation(out=ms[:], in_=ss[:], func=AF.Ln, scale=1.0 / n, bias=k.epsb[:, 0:1]),
         r=[ss, k.epsb], w=[ms])
    P.op('act', lambda: nc.scalar.activation(out=rstd[:], in_=ms[:], func=AF.Exp, scale=-0.5), r=[ms], w=[rstd])


def prenorm(k, ph, l, gi, shi, sci, uT, tiles, hook=None):
    nc, P = k.nc, k.P

    def t_(name, shape, dt):
        return sbt(ph, nc, name, shape, dt)

    gb = t_("pn_g", [128, D], F32)
    gs = [t_("pn_gs%d" % r, [128, D], F32) for r in range(2)]
    sh = [t_("pn_sh%d" % r, [128, D], F32) for r in range(2)]
    hb = Rot([t_("pn_h%d" % i, [128, D], F32) for i in range(3)])
    tb = Rot([t_("pn_t%d" % i, [128, D], F32) for i in range(2)])
    ub = Rot([t_("pn_u%d" % i, [128, D], BF16) for i in range(2)])
    st = Rot([[t_("pn_s%d_%d" % (i, j), [128, 1], F32) for j in range(3)] for i in range(3)])
    bcast_row(k, 'sp', gb[:], k.norm_g.ap()[l, gi:gi + 1, :], key=gb, w=[gb])
    for r in range(2):
        bcast_row(k, 'sp', gs[r][:], k.mod_d.ap()[l, r:r + 1, sci * D:(sci + 1) * D], key=gs[r], r=[("mod", l)], w=[gs[r]])
        bcast_row(k, 'sp', sh[r][:], k.mod_d.ap()[l, r:r + 1, shi * D:(shi + 1) * D], key=sh[r], r=[("mod", l)], w=[sh[r]])
        P.op('dve', lambda r=r: nc.vector.scalar_tensor_tensor(out=gs[r][:], in0=gs[r][:], scalar=1.0, in1=gb[:],
                                                           op0=ALU.add, op1=ALU.mult), r=[gs[r], gb], w=[gs[r]])
    for t in tiles:
        r = 0 if t < TL else 1
        h = hb.next()
        tt = tb.next()
        u = ub.next()
        ss, ms, rstd = st.next()
        P.dma('sp', h[:], k.h_d[t * 128:(t + 1) * 128, :], key=h, r=[("h", t)], w=[h])
        P.op('act', lambda h=h, ss=ss: nc.scalar.activation(out=k.junk[:], in_=h[:], func=AF.Square, accum_out=ss[:]),
             r=[h], w=[ss])
        rstd_from_ss(k, ss, ms, rstd, D)
        P.op('dve', lambda h=h, tt=tt, rstd=rstd, r=r: nc.vector.scalar_tensor_tensor(
            out=tt[:], in0=h[:], scalar=rstd[:, 0:1], in1=gs[r][:], op0=ALU.mult, op1=ALU.mult),
            r=[h, rstd, gs[r]], w=[tt])
        P.op('dve', lambda tt=tt, u=u, r=r: nc.vector.tensor_tensor(out=u[:], in0=tt[:], in1=sh[r][:], op=ALU.add),
             r=[tt, sh[r]], w=[u])
        pb = k.psb_rot.next()
        for kk in range(8):
            P.op('pe', lambda kk=kk, u=u, pb=pb: nc.tensor.transpose(pb[:, kk * 128:(kk + 1) * 128], u[:, kk * 128:(kk + 1) * 128],
                                                                 k.ident[:]),
                 r=[u, k.ident], w=[pb], sig=(kk == 7))
        P.op('act', lambda t=t, pb=pb: nc.scalar.copy(out=uT[:, :, t * 128:(t + 1) * 128],
                                                    in_=pb[:].rearrange("p (k n) -> p k n", k=8)),
             r=[pb], w=[("uT", t)])
        if hook is not None:
            hook(t)


def uT_res(c, n):
    return [("uT", t) for t in range(c // 128, (c + n - 1) // 128 + 1)]


def ffn_layer(k, l, last):
    nc, P = k.nc, k.P
    tiles_all = list(range(TL)) + ([] if last else list(range(TL, TT)))
    with ExitStack() as ph:
        def t_(name, shape, dt):
            return sbt(ph, nc, name, shape, dt)

        uT = t_("f_uT", [128, 8, NT], BF16)
        Wd = t_("f_Wd", [128, NFC, D], BF16)
        cw = t_("f_cw", [128, 4, NFC], F32)
        uH = [t_("f_uH%d" % i, [128, 8, 2], BF16) for i in range(2)]
        hx = t_("f_hx", [128, 8], BF16)
        hg = t_("f_hg", [128, 2, 8], BF16)
        hgf = t_("f_hgf", [128, 8], F32)
        for half in range(2):
            f0, f1 = half * 11, (half + 1) * 11
            P.dma('pool', Wd[:, f0:f1, :], k.ffn_w_down.ap()[l][f0 * 128:f1 * 128, :].rearrange("(f p) n -> p f n", p=128),
                  key=("Wd", half), w=[("Wd", half)])
        for tap in range(3):
            P.dma('sp', cw[:, tap, :], k.ffn_conv_w.ap()[l, tap].rearrange("(f p) -> p f", p=128), key=cw, w=[cw],
                  allow_slow_non_contiguous=True)
        P.dma('sp', cw[:, 3, :], k.ffn_conv_b.ap()[l].rearrange("(f p) -> p f", p=128), key=cw, w=[cw],
              allow_slow_non_contiguous=True)
        with ExitStack() as ph1:
            order = [TL - 1] + [t for t in tiles_all if t != TL - 1]

            def hook(t):
                if t != TL - 1:
                    return
                P.op('dve', lambda: nc.vector.tensor_copy(out=hx[:], in_=uT[:, :, NL - 1]), r=[("uT", TL - 1)], w=[hx])
                P.dma('pool', k.hx_d[:, :], hx[:], key=hx, r=[hx], w=["hx_d"])
                P.op('pool', lambda: nc.gpsimd.collective_compute("AllGather", ALU.bypass, replica_groups=GROUPS,
                                                                  ins=[k.hx_d.ap().opt()], outs=[k.hg_d.ap().opt()]),
                     r=["hx_d"], w=["hg_d"], dma="cc", inc=1)

            prenorm(k, ph1, l, 2, 3, 4, uT, order, hook)
            P.dma('sp', hg[:], k.hg_d.ap().rearrange("(r p) k -> p r k", p=128), key=hg, r=["hg_d"], w=[hg])
            P.op('dve', lambda: nc.vector.tensor_scalar(out=hgf[:], in0=hg[:, 0, :], scalar1=k.selt[:, 0:1], scalar2=None,
                                                        op0=ALU.mult), r=[hg, k.selt], w=[hgf])
            P.op('dve', lambda: nc.vector.scalar_tensor_tensor(out=uH[1][:, :, 1], in0=hg[:, 1, :], scalar=k.selt[:, 1:2],
                                                               in1=hgf[:], op0=ALU.mult, op1=ALU.add),
                 r=[hg, hgf, k.selt], w=[uH[1]])
            P.op('dve', lambda: nc.vector.tensor_copy(out=uH[1][:, :, 0], in_=uT[:, :, NL // 2 - 1]), r=[("uT", 7)], w=[uH[1]])
            P.op('dve', lambda: nc.vector.tensor_copy(out=uH[0][:, :, 0], in_=uT[:, :, NL // 2]), r=[("uT", 8)], w=[uH[0]])
            P.op('dve', lambda: nc.vector.tensor_copy(out=uH[0][:, :, 1], in_=uT[:, :, NL // 2]), r=[("uT", 8)], w=[uH[0]])
            P.barrier()
        segs = [dict(tiles=list(range(0, 8)), subs=[(0, 1024, False, True)], uH=uH[0]),
                dict(tiles=list(range(8, 16)) + ([] if last else [16, 17]),
                     subs=[(1024, 1024, True, True)] + ([] if last else [(2048, 256, False, False)]), uH=uH[1])]
        with ExitStack() as ph2:
            def t2(name, shape, dt):
                return sbt(ph2, nc, name, shape, dt)

            NCOL = 1280
            aT = t2("f_aT", [128, NFC, NCOL], BF16)
            Wv = Rot([t2("f_Wv%d" % i, [128, 8, 256], BF16) for i in range(2)])
            Wg = Rot([t2("f_Wg%d" % i, [128, 8, 256], BF16) for i in range(2)])
            GW = NCOL + 4
            Gb = Rot([t2("f_G%d" % i, [128, GW], F32) for i in range(2)])
            Vb = Rot([t2("f_V%d" % i, [128, NCOL], F32) for i in range(2)])
            Tb = Rot([t2("f_T%d" % i, [128, NCOL], F32) for i in range(1)])
            gg = [t2("f_gg%d" % r, [128, D], F32) for r in range(2)]
            g3 = t2("f_g3", [128, D], F32)
            hb = Rot([t2("f_h%d" % i, [128, D], F32) for i in range(2)])
            ob = Rot([t2("f_o%d" % i, [128, D], F32) for i in range(2)])
            st = Rot([[t2("f_s%d_%d" % (i, j), [128, 1], F32) for j in range(4)] for i in range(2)])
            for G in Gb.items:
                P.op('pool', lambda G=G: nc.gpsimd.memset(G[:], 0.0), w=[G])
            bcast_row(k, 'sp', g3[:], k.norm_g.ap()[l, 3:4, :], key=g3, w=[g3])
            for r in range(2):
                bcast_row(k, 'sp', gg[r][:], k.mod_d.ap()[l, r:r + 1, 5 * D:6 * D], key=gg[r], r=[("mod", l)], w=[gg[r]])
                P.op('dve', lambda r=r: nc.vector.tensor_tensor(out=gg[r][:], in0=gg[r][:], in1=g3[:], op=ALU.mult),
                     r=[gg[r], g3], w=[gg[r]])
            wup = k.ffn_w_up.ap()[l]

            def load_w(fcg):
                wv, wg = Wv.next(), Wg.next()
                P.dma('pool', wv[:], wup[:, fcg * 256:(fcg + 1) * 256].rearrange("(k p) n -> p k n", p=128), key=wv, w=[wv])
                P.dma('pool', wg[:], wup[:, FH + fcg * 256:FH + (fcg + 1) * 256].rearrange("(k p) n -> p k n", p=128),
                      key=wg, w=[wg])
                return wv, wg

            for seg in segs:
                subs = seg["subs"]
                uHs = seg["uH"]
                offs, goffs = [], []
                o, go = 0, 0
                for (c0, n, lh, rh) in subs:
                    offs.append(o)
                    goffs.append(go)
                    o += n
                    go += n + 2
                nxt = load_w(0)
                for fcg in range(11):
                    wv, wg = nxt
                    if fcg + 1 < 11:
                        nxt = load_w(fcg + 1)
                    for j in range(2):
                        fc = fcg * 2 + j
                        G, V, T = Gb.next(), Vb.next(), Tb.next()
                        for si, (c0, n, lh, rh) in enumerate(subs):
                            for b0 in range(0, n, 512):
                                nb = min(512, n - b0)
                                for (wt, dst, doff) in ((wv, V, offs[si] + b0), (wg, G, goffs[si] + 1 + b0)):
                                    ps = k.psf_rot.next()
                                    for kk in range(8):
                                        P.op('pe', lambda kk=kk, wt=wt, ps=ps, c=c0 + b0, nb=nb, j=j: nc.tensor.matmul(
                                            ps[:, 0:nb], lhsT=wt[:, kk, j * 128:(j + 1) * 128], rhs=uT[:, kk, c:c + nb],
                                            start=(kk == 0), stop=(kk == 7)),
                                            r=[wt] + uT_res(c0 + b0, nb), w=[ps], sig=(kk == 7))
                                    P.op('act', lambda ps=ps, dst=dst, doff=doff, nb=nb: nc.scalar.copy(
                                        out=dst[:, doff:doff + nb], in_=ps[:, 0:nb]), r=[ps], w=[dst])
                        ps = k.psf_rot.next()
                        for kk in range(8):
                            P.op('pe', lambda kk=kk, ps=ps, j=j, wg=wg: nc.tensor.matmul(
                                ps[:, 0:2], lhsT=wg[:, kk, j * 128:(j + 1) * 128], rhs=uHs[:, kk, :],
                                start=(kk == 0), stop=(kk == 7)), r=[wg, uHs], w=[ps], sig=(kk == 7))
                        for si, (c0, n, lh, rh) in enumerate(subs):
                            go = goffs[si]
                            if lh:
                                P.op('act', lambda ps=ps, G=G, go=go: nc.scalar.copy(out=G[:, go:go + 1], in_=ps[:, 0:1]),
                                     r=[ps], w=[G])
                            if rh:
                                P.op('act', lambda ps=ps, G=G, go=go, n=n: nc.scalar.copy(
                                    out=G[:, go + n + 1:go + n + 2], in_=ps[:, 1:2]), r=[ps], w=[G])
                        for si, (c0, n, lh, rh) in enumerate(subs):
                            go, o = goffs[si], offs[si]
                            P.op('dve', lambda G=G, T=T, go=go, o=o, n=n, fc=fc: nc.vector.tensor_scalar(
                                out=T[:, o:o + n], in0=G[:, go + 1:go + 1 + n], scalar1=cw[:, 1, fc:fc + 1],
                                scalar2=cw[:, 3, fc:fc + 1], op0=ALU.mult, op1=ALU.add), r=[G, cw], w=[T])
                            P.op('dve', lambda G=G, T=T, go=go, o=o, n=n, fc=fc: nc.vector.scalar_tensor_tensor(
                                out=T[:, o:o + n], in0=G[:, go:go + n], scalar=cw[:, 0, fc:fc + 1], in1=T[:, o:o + n],
                                op0=ALU.mult, op1=ALU.add), r=[G, cw, T], w=[T])
                            P.op('dve', lambda G=G, T=T, go=go, o=o, n=n, fc=fc: nc.vector.scalar_tensor_tensor(
                                out=T[:, o:o + n], in0=G[:, go + 2:go + 2 + n], scalar=cw[:, 2, fc:fc + 1], in1=T[:, o:o + n],
                                op0=ALU.mult, op1=ALU.add), r=[G, cw, T], w=[T])
                            P.op('act', lambda T=T, o=o, n=n: nc.scalar.activation(out=T[:, o:o + n], in_=T[:, o:o + n],
                                                                                func=AF.Silu), r=[T], w=[T])
                            P.op('dve', lambda T=T, V=V, o=o, n=n, fc=fc: nc.vector.tensor_tensor(
                                out=aT[:, fc, o:o + n], in0=T[:, o:o + n], in1=V[:, o:o + n], op=ALU.mult),
                                r=[T, V], w=[("aT", fc)])
                col = 0
                for t in seg["tiles"]:
                    r = 0 if t < TL else 1
                    pss = [k.psf_rot.next(), k.psf_rot.next()]
                    for cg in range(2):
                        for fc in range(NFC):
                            P.op('pe', lambda fc=fc, cg=cg, col=col: nc.tensor.matmul(
                                pss[cg][:], lhsT=aT[:, fc, col:col + 128], rhs=Wd[:, fc, cg * 512:(cg + 1) * 512],
                                start=(fc == 0), stop=(fc == NFC - 1)),
                                r=[("aT", fc), ("Wd", fc // 11)], w=[pss[cg]], sig=(fc == NFC - 1))
                    h, o_ = hb.next(), ob.next()
                    s0, s1, ms, rstd = st.next()
                    P.dma('sp', h[:], k.h_d[t * 128:(t + 1) * 128, :], key=h, r=[("h", t)], w=[h])
                    for cg, s in ((0, s0), (1, s1)):
                        P.op('act', lambda cg=cg, s=s: nc.scalar.activation(out=k.junk[:, 0:512], in_=pss[cg][:], func=AF.Square,
                                                                        accum_out=s[:]), r=[pss[cg]], w=[s])
                    P.op('dve', lambda s0=s0, s1=s1: nc.vector.tensor_tensor(out=s0[:], in0=s0[:], in1=s1[:], op=ALU.add),
                         r=[s0, s1], w=[s0])
                    rstd_from_ss(k, s0, ms, rstd, D)
                    for cg in range(2):
                        P.op('dve', lambda cg=cg, o_=o_, rstd=rstd, r=r: nc.vector.scalar_tensor_tensor(
                            out=o_[:, cg * 512:(cg + 1) * 512], in0=pss[cg][:], scalar=rstd[:, 0:1],
                            in1=gg[r][:, cg * 512:(cg + 1) * 512], op0=ALU.mult, op1=ALU.mult),
                            r=[pss[cg], rstd, gg[r]], w=[o_])
                    P.op('dve', lambda o_=o_, h=h: nc.vector.tensor_tensor(out=o_[:], in0=o_[:], in1=h[:], op=ALU.add),
                         r=[o_, h], w=[o_])
                    P.dma('sp', k.h_d[t * 128:(t + 1) * 128, :], o_[:], key=o_, r=[o_], w=[("h", t)])
                    col += 128
            P.barrier()


def resid_epilogue(k, pss, t, ggr, h, o_, stt4):
    nc, P = k.nc, k.P
    s0, s1, ms, rstd = stt4
    P.dma('sp', h[:], k.h_d[t * 128:(t + 1) * 128, :], key=h, r=[("h", t)], w=[h])
    for cg, s_ in ((0, s0), (1, s1)):
        P.op('act', lambda cg=cg, s_=s_: nc.scalar.activation(out=k.junk[:, 0:512], in_=pss[cg][:], func=AF.Square,
                                                            accum_out=s_[:]), r=[pss[cg]], w=[s_])
    P.op('dve', lambda: nc.vector.tensor_tensor(out=s0[:], in0=s0[:], in1=s1[:], op=ALU.add), r=[s0, s1], w=[s0])
    rstd_from_ss(k, s0, ms, rstd, D)
    for cg in range(2):
        P.op('dve', lambda cg=cg: nc.vector.scalar_tensor_tensor(
            out=o_[:, cg * 512:(cg + 1) * 512], in0=pss[cg][:], scalar=rstd[:, 0:1],
            in1=ggr[:, cg * 512:(cg + 1) * 512], op0=ALU.mult, op1=ALU.mult), r=[pss[cg], rstd, ggr], w=[o_])
    P.op('dve', lambda: nc.vector.tensor_tensor(out=o_[:], in0=o_[:], in1=h[:], op=ALU.add), r=[o_, h], w=[o_])
    P.dma('pool', k.h_d[t * 128:(t + 1) * 128, :], o_[:], key=o_, r=[o_], w=[("h", t)])


def gate_tiles(k, ph, l, gi, gatei):
    nc, P = k.nc, k.P
    gg = [sbt(ph, nc, "gg%d" % r, [128, D], F32) for r in range(2)]
    g3 = sbt(ph, nc, "ggn", [128, D], F32)
    bcast_row(k, 'sp', g3[:], k.norm_g.ap()[l, gi:gi + 1, :], key=g3, w=[g3])
    for r in range(2):
        bcast_row(k, 'sp', gg[r][:], k.mod_d.ap()[l, r:r + 1, gatei * D:(gatei + 1) * D], key=gg[r], r=[("mod", l)],
                  w=[gg[r]])
        P.op('dve', lambda r=r: nc.vector.tensor_tensor(out=gg[r][:], in0=gg[r][:], in1=g3[:], op=ALU.mult),
             r=[gg[r], g3], w=[gg[r]])
    return gg


def proj_tm(k, ph, uT, w_ap, ngroups, tiles, epilogue):
    nc, P = k.nc, k.P
    Wb = Rot([sbt(ph, nc, "pj_w%d" % i, [128, 8, 512], BF16) for i in range(2)])

    def load(g):
        w = Wb.next()
        P.dma('pool', w[:], w_ap[:, g * 512:(g + 1) * 512].rearrange("(k p) n -> p k n", p=128), key=w, w=[w])
        return w

    nxt = load(0)
    for g in range(ngroups):
        w = nxt
        if g + 1 < ngroups:
            nxt = load(g + 1)
        for t in tiles:
            ps = k.psf_rot.next()
            for kk in range(8):
                P.op('pe', lambda kk=kk: nc.tensor.matmul(ps[:], lhsT=uT[:, kk, t * 128:(t + 1) * 128], rhs=w[:, kk, :],
                                                        start=(kk == 0), stop=(kk == 7)),
                     r=[w, ("uT", t)], w=[ps], sig=(kk == 7))
            epilogue(t, g, ps)


def rope_apply(k, ps, outb, cos, sin, nh, nf, scale, tmp):
    nc, P = k.nc, k.P
    n = nh * 4 * nf
    pv = ps[:, 0:n].rearrange("p (h r x f) -> p h r x f", h=nh, r=2, x=2)
    ov = outb[:, 0:n].rearrange("p (h r x f) -> p h r x f", h=nh, r=2, x=2)
    x1, x2 = pv[:, :, :, 0, :], pv[:, :, :, 1, :]
    cb = cos.unsqueeze(1).to_broadcast([128, nh, 2, nf])
    sb_ = sin.unsqueeze(1).to_broadcast([128, nh, 2, nf])
    tv = [t_[:, 0:nh * 2 * nf].rearrange("p (h r f) -> p h r f", h=nh, r=2) for t_ in tmp]
    for (dst, xin, tab) in ((0, x1, cb), (1, x2, sb_), (2, x1, sb_), (3, x2, cb)):
        P.op('dve', lambda dst=dst, xin=xin, tab=tab: nc.vector.scalar_tensor_tensor(
            out=tv[dst], in0=xin, scalar=float(scale), in1=tab, op0=ALU.mult, op1=ALU.mult),
            r=[ps, "rope_tab"], w=[tmp[dst]])
    P.op('dve', lambda: nc.vector.tensor_tensor(out=ov[:, :, :, 0, :], in0=tv[0], in1=tv[1], op=ALU.subtract),
         r=[tmp[0], tmp[1]], w=[outb])
    P.op('dve', lambda: nc.vector.tensor_tensor(out=ov[:, :, :, 1, :], in0=tv[2], in1=tv[3], op=ALU.add),
         r=[tmp[2], tmp[3]], w=[outb])


def rope_tables(k):
    nc, P = k.nc, k.P
    PI = float(np.pi)
    k.ropeR_d = nc.dram_tensor("ropeR_d", [2, 128, TL * 2 * 64], F32)
    k.ropeD_d = nc.dram_tensor("ropeD_d", [2, 128, TL * 2 * 16], F32)
    with ExitStack() as ph:
        pos = sbt(ph, nc, "pos", [128, TL, 2], F32)
        P.dma('sp', pos[:], k.pos[:, :, :], key=pos, w=[pos])
        for (nf, d2, dst) in ((64, 128, k.ropeR_d), (16, 32, k.ropeD_d)):
            n = TL * 2 * nf
            inv = sbt(ph, nc, "inv", [128, nf], F32)
            ang = sbt(ph, nc, "ang", [128, n], F32)
            xx = sbt(ph, nc, "xx", [128, n], F32)
            ki = sbt(ph, nc, "ki", [128, n], I32)
            kf = sbt(ph, nc, "kf", [128, n], F32)
            mm = sbt(ph, nc, "mm", [128, n], F32)
            res = sbt(ph, nc, "res", [128, n], F32)
            P.op('act', lambda: nc.scalar.activation(out=inv[:], in_=k.iotaf[:, 0:nf], func=AF.Exp,
                                                     scale=-float(np.log(10000.0)) * 2.0 / d2), r=[k.iotaf], w=[inv])
            for t in range(TL):
                for rc in range(2):
                    o = (t * 2 + rc) * nf
                    P.op('dve', lambda t=t, rc=rc, o=o: nc.vector.tensor_scalar(
                        out=ang[:, o:o + nf], in0=inv[:], scalar1=pos[:, t, rc:rc + 1], scalar2=None, op0=ALU.mult),
                        r=[inv, pos], w=[ang])
            for ci, off in ((0, PI / 2), (1, 0.0)):
                P.op('dve', lambda: nc.vector.tensor_scalar(out=xx[:], in0=ang[:], scalar1=float(off), scalar2=None,
                                                            op0=ALU.add), r=[ang], w=[xx])
                P.op('dve', lambda: nc.vector.tensor_scalar(out=ki[:], in0=xx[:], scalar1=float(1.0 / (2 * PI)),
                                                            scalar2=None, op0=ALU.mult), r=[xx], w=[ki])
                P.op('dve', lambda: nc.vector.tensor_copy(out=kf[:], in_=ki[:]), r=[ki], w=[kf])
                P.op('dve', lambda: nc.vector.scalar_tensor_tensor(out=xx[:], in0=kf[:], scalar=-2 * PI, in1=xx[:],
                                                                   op0=ALU.mult, op1=ALU.add), r=[kf, xx], w=[xx])
                P.op('dve', lambda: nc.vector.tensor_scalar(out=mm[:], in0=xx[:], scalar1=PI, scalar2=None,
                                                            op0=ALU.is_gt), r=[xx], w=[mm])
                P.op('dve', lambda: nc.vector.scalar_tensor_tensor(out=xx[:], in0=mm[:], scalar=-2 * PI, in1=xx[:],
                                                                   op0=ALU.mult, op1=ALU.add), r=[mm, xx], w=[xx])
                P.op('dve', lambda: nc.vector.tensor_scalar(out=mm[:], in0=xx[:], scalar1=-PI, scalar2=None,
                                                            op0=ALU.is_lt), r=[xx], w=[mm])
                P.op('dve', lambda: nc.vector.scalar_tensor_tensor(out=xx[:], in0=mm[:], scalar=2 * PI, in1=xx[:],
                                                                   op0=ALU.mult, op1=ALU.add), r=[mm, xx], w=[xx])
                P.op('act', lambda: nc.scalar.activation(out=res[:], in_=xx[:], func=AF.Sin), r=[xx], w=[res])
                P.dma('sp', dst.ap()[ci], res[:], key=res, r=[res], w=[("ropetab", nf, ci)])
        P.barrier()


def ret_layer(k, l):
    nc, P = k.nc, k.P
    i = l // 2
    H, DK, DV = 4, 256, 512
    if not hasattr(k, "qT_d"):
        k.qT_d = nc.dram_tensor("qT_d", [8, 128, NT], BF16)
        k.kT_d = nc.dram_tensor("kT_d", [8, 128, NT], BF16)
        k.k_d = nc.dram_tensor("k_d", [NT, 1024], BF16)
        k.v_d = nc.dram_tensor("v_d", [NT, 2048], BF16)
        k.g_d = [nc.dram_tensor("g%d_d" % p, [NT, 2048], BF16) for p in range(2)]
        k.y_d = nc.dram_tensor("y_d", [NT, 2048], F32)
        k.F_d = nc.dram_tensor("F_d", [1024, 512], F32)
        k.Fg_d = nc.dram_tensor("Fg_d", [2048, 512], F32)
    tiles_all = list(range(TT))
    with ExitStack() as ph:
        uT = sbt(ph, nc, "r_uT", [128, 8, NT], BF16)
        with ExitStack() as ph1:
            prenorm(k, ph1, l, 0, 0, 1, uT, tiles_all)
            P.barrier()
        cos = sbt(ph, nc, "r_cos", [128, TL, 2, 64], F32)
        sin = sbt(ph, nc, "r_sin", [128, TL, 2, 64], F32)
        P.dma('sp', cos[:].rearrange("p t r f -> p (t r f)"), k.ropeR_d.ap()[0], key=cos, w=["rope_tab"])
        P.dma('sp', sin[:].rearrange("p t r f -> p (t r f)"), k.ropeR_d.ap()[1], key=sin, w=["rope_tab"])
        tmp = [sbt(ph, nc, "r_tmp%d" % j, [128, 256], F32) for j in range(4)]
        ob = Rot([sbt(ph, nc, "r_ob%d" % j, [128, 512], BF16) for j in range(3)])
        tb = Rot([sbt(ph, nc, "r_tb%d" % j, [128, 4, 128], BF16) for j in range(3)])

        def epi(t, g, ps):
            o = ob.next()
            if g < 4:
                scale = 1.0 if g < 2 else DK ** -0.5
                if t < TL:
                    rope_apply(k, ps, o, cos[:, t], sin[:, t], 2, 64, scale, tmp)
                else:
                    P.op('act', lambda: nc.scalar.activation(out=o[:], in_=ps[:], func=AF.Copy, scale=float(scale)),
                         r=[ps], w=[o])
                if g >= 2:
                    P.dma('sp', k.k_d[t * 128:(t + 1) * 128, (g - 2) * 512:(g - 1) * 512], o[:], key=o, r=[o],
                          w=[("k_d", t)])
                pb = k.psb_rot.next()
                for j in range(4):
                    P.op('pe', lambda j=j: nc.tensor.transpose(pb[:, j * 128:(j + 1) * 128], o[:, j * 128:(j + 1) * 128],
                                                              k.ident[:]), r=[o, k.ident], w=[pb], sig=(j == 3))
                tt = tb.next()
                P.op('act', lambda: nc.scalar.copy(out=tt[:], in_=pb[:, 0:512].rearrange("p (j n) -> p j n", j=4)),
                     r=[pb], w=[tt])
                dst = k.qT_d if g < 2 else k.kT_d
                gg_ = g % 2
                P.dma('sp', dst.ap()[gg_ * 4:(gg_ + 1) * 4, :, t * 128:(t + 1) * 128].rearrange("j p n -> p j n"), tt[:],
                      key=tt, r=[tt], w=[("qkT_d", g, t)])
            elif g < 8:
                P.op('act', lambda: nc.scalar.copy(out=o[:], in_=ps[:]), r=[ps], w=[o])
                P.dma('sp', k.v_d[t * 128:(t + 1) * 128, (g - 4) * 512:(g - 3) * 512], o[:], key=o, r=[o], w=[("v_d", t)])
            else:
                p_ = (g - 8) // 4
                gc = (g - 8) % 4
                P.op('act', lambda: nc.scalar.activation(out=o[:], in_=ps[:], func=AF.Silu), r=[ps], w=[o])
                P.dma('sp', k.g_d[p_][t * 128:(t + 1) * 128, gc * 512:(gc + 1) * 512], o[:], key=o, r=[o],
                      w=[("g_d", p_, t)])

        proj_tm(k, ph, uT, k.ret_w_in.ap()[i], 16, tiles_all, epi)
        P.barrier()
    with ExitStack() as ph:
        def t_(name, shape, dt):
            return sbt(ph, nc, name, shape, dt)

        lg = t_("lg", [128, 8], F32)
        dm = t_("dm", [128, 128], F32)
        msk = [t_("msk%d" % j, [128, 128], F32) for j in range(8)]
        xi = [t_("xi%d" % j, [128, 128], BF16) for j in range(8)]
        zeta = t_("zeta", [128, 8], F32)
        cd = t_("cd", [128, 8], F32)
        c128 = t_("c128", [128, 1], F32)
        tA = t_("tA", [128, 128], F32)
        tB = t_("tB", [128, 128], F32)
        P.dma('sp', lg[:], k.ret_decay.ap()[i:i + 1].rearrange("o p h -> o (p h)").to_broadcast([128, 8]), key=lg, w=[lg])
        P.op('act', lambda: nc.scalar.activation(out=lg[:], in_=lg[:], func=AF.Exp, scale=-1.0), r=[lg], w=[lg])
        P.op('dve', lambda: nc.vector.tensor_scalar(out=lg[:], in0=lg[:], scalar1=1.0, scalar2=None, op0=ALU.add),
             r=[lg], w=[lg])
        P.op('act', lambda: nc.scalar.activation(out=lg[:], in_=lg[:], func=AF.Ln), r=[lg], w=[lg])
        P.op('dve', lambda: nc.vector.tensor_scalar(out=lg[:], in0=lg[:], scalar1=-1.0, scalar2=None, op0=ALU.mult),
             r=[lg], w=[lg])
        P.op('dve', lambda: nc.vector.memset(c128[:], 128.0), w=[c128])
        P.op('dve', lambda: nc.vector.tensor_scalar(out=dm[:], in0=k.iotaf[:], scalar1=k.iotap[:, 0:1], scalar2=None,
                                                    op0=ALU.subtract), r=[k.iotaf, k.iotap], w=[dm])
        for p_ in range(2):
            sgn = 1.0 if p_ == 0 else -1.0
            for h in range(H):
                j = p_ * 4 + h
                lgc = lg[:, j:j + 1]
                P.op('dve', lambda: nc.vector.tensor_scalar(out=tA[:], in0=dm[:], scalar1=sgn, scalar2=0.0, op0=ALU.mult,
                                                            op1=ALU.max), r=[dm], w=[tA])
                P.op('act', lambda lgc=lgc: nc.scalar.activation(out=tA[:], in_=tA[:], func=AF.Exp, scale=lgc),
                     r=[tA, lg], w=[tA])
                P.op('dve', lambda: nc.vector.tensor_scalar(out=tB[:], in0=dm[:], scalar1=sgn, scalar2=0.0, op0=ALU.mult,
                                                            op1=ALU.is_ge), r=[dm], w=[tB])
                P.op('dve', lambda j=j: nc.vector.tensor_tensor(out=msk[j][:], in0=tA[:], in1=tB[:], op=ALU.mult),
                     r=[tA, tB], w=[msk[j]])
                if p_ == 0:
                    P.op('dve', lambda: nc.vector.tensor_scalar(out=tA[:], in0=k.iotaf[:], scalar1=1.0, scalar2=None,
                                                                op0=ALU.add), r=[k.iotaf], w=[tA])
                else:
                    P.op('dve', lambda: nc.vector.tensor_scalar(out=tA[:], in0=k.iotaf[:], scalar1=-1.0, scalar2=128.0,
                                                                op0=ALU.mult, op1=ALU.add), r=[k.iotaf], w=[tA])
                P.op('act', lambda j=j, lgc=lgc: nc.scalar.activation(out=xi[j][:], in_=tA[:], func=AF.Exp, scale=lgc),
                     r=[tA, lg], w=[xi[j]])
                if p_ == 0:
                    P.op('dve', lambda: nc.vector.tensor_scalar(out=tB[:, 0:1], in0=k.iotap[:], scalar1=-1.0, scalar2=127.0,
                                                                op0=ALU.mult, op1=ALU.add), r=[k.iotap], w=[tB])
                else:
                    P.op('dve', lambda: nc.vector.tensor_copy(out=tB[:, 0:1], in_=k.iotap[:]), r=[k.iotap], w=[tB])
                P.op('act', lambda j=j, lgc=lgc: nc.scalar.activation(out=zeta[:, j:j + 1], in_=tB[:, 0:1], func=AF.Exp,
                                                                     scale=lgc), r=[tB, lg], w=[zeta])
                P.op('act', lambda j=j, lgc=lgc: nc.scalar.activation(out=cd[:, j:j + 1], in_=c128[:], func=AF.Exp,
                                                                     scale=lgc), r=[c128, lg], w=[cd])
        Wo = t_("r_Wo", [128, 16, D], BF16)
        for half in range(2):
            P.dma('pool', Wo[:, half * 8:(half + 1) * 8, :],
                  k.ret_w_out.ap()[i][half * 1024:(half + 1) * 1024, :].rearrange("(f p) n -> p f n", p=128),
                  key=("Wo", half), w=[("Wo", half)])
        gg = gate_tiles(k, ph, l, 1, 2)
        R = [t_("R%d" % h, [128, 2, DV], F32) for h in range(H)]
        Rb = [t_("Rb%d" % h, [128, 2, DV], BF16) for h in range(H)]
        Fg = t_("Fg", [128, 2, DV], F32)
        qTb = Rot([t_("qTc%d" % j, [128, 8, 128], BF16) for j in range(2)])
        kTb = Rot([t_("kTc%d" % j, [128, 8, 128], BF16) for j in range(2)])
        kb = Rot([t_("kc%d" % j, [128, 1024], BF16) for j in range(2)])
        vb = Rot([t_("vc%d" % j, [128, 2048], BF16) for j in range(2)])
        gb = Rot([t_("gc%d" % j, [128, 2048], BF16) for j in range(2)])
        yb = Rot([t_("y%d" % j, [128, 2048], F32) for j in range(2)])
        y1b = Rot([t_("y1_%d" % j, [128, 2048], F32) for j in range(2)])
        ybf = Rot([t_("ybf%d" % j, [128, 2048], BF16) for j in range(2)])
        yTb = Rot([t_("yT%d" % j, [128, 16, 128], BF16) for j in range(2)])
        sTb = Rot([t_("sT%d" % j, [128, 128], BF16) for j in range(3)])
        qxb = Rot([t_("qx%d" % j, [128, 2, 128], BF16) for j in range(3)])
        kzb = Rot([t_("kz%d" % j, [128, 256], BF16) for j in range(3)])
        hb = Rot([t_("r_h%d" % j, [128, D], F32) for j in range(2)])
        ob2 = Rot([t_("r_o%d" % j, [128, D], F32) for j in range(2)])
        st4 = Rot([[t_("r_s%d_%d" % (a, b), [128, 1], F32) for b in range(4)] for a in range(2)])
        st3 = Rot([[t_("r_n%d_%d" % (a, b), [128, 1], F32) for b in range(3)] for a in range(4)])

        def zero_state():
            for h in range(H):
                P.op('pool', lambda h=h: nc.gpsimd.memset(R[h][:], 0.0), w=[R[h]])
                P.op('pool', lambda h=h: nc.gpsimd.memset(Rb[h][:], 0.0), w=[Rb[h]])

        def load_chunk(p_, t):
            qTc, kTc, kc, vc, gc = qTb.next(), kTb.next(), kb.next(), vb.next(), gb.next()
            cs = slice(t * 128, (t + 1) * 128)
            P.dma('sp', qTc[:], k.qT_d.ap()[:, :, cs].rearrange("j p n -> p j n"), key=qTc,
                  r=[("qkT_d", g, t) for g in (0, 1)], w=[qTc])
            P.dma('sp', kTc[:], k.kT_d.ap()[:, :, cs].rearrange("j p n -> p j n"), key=kTc,
                  r=[("qkT_d", g, t) for g in (2, 3)], w=[kTc])
            P.dma('sp', kc[:], k.k_d[cs, :], key=kc, r=[("k_d", t)], w=[kc])
            P.dma('sp', vc[:], k.v_d[cs, :], key=vc, r=[("v_d", t)], w=[vc])
            P.dma('sp', gc[:], k.g_d[p_][cs, :], key=gc, r=[("g_d", p_, t)], w=[gc])
            y1 = None
            if p_ == 1:
                y1 = y1b.next()
                P.dma('sp', y1[:], k.y_d[cs, :], key=y1, r=[("y_d", t)], w=[y1])
            return qTc, kTc, kc, vc, gc, y1

        def chunk(p_, t, bufs):
            qTc, kTc, kc, vc, gc, y1 = bufs
            cs = slice(t * 128, (t + 1) * 128)
            y = yb.next()
            if p_ == 1:
                yf = ybf.next()
            for h in range(H):
                j = p_ * 4 + h
                sT, qx, kz = sTb.next(), qxb.next(), kzb.next()
                ss, ms, rstd = st3.next()
                ps_s = k.psf_rot.next()
                for c2 in range(2):
                    P.op('pe', lambda c2=c2: nc.tensor.matmul(ps_s[:, 0:128], lhsT=kTc[:, h * 2 + c2, :],
                                                            rhs=qTc[:, h * 2 + c2, :], start=(c2 == 0), stop=(c2 == 1)),
                         r=[kTc, qTc], w=[ps_s], sig=(c2 == 1))
                P.op('dve', lambda: nc.vector.tensor_tensor(out=sT[:], in0=ps_s[:, 0:128], in1=msk[j][:], op=ALU.mult),
                     r=[ps_s, msk[j]], w=[sT])
                P.op('pool', lambda: nc.gpsimd.tensor_tensor(
                    out=qx[:], in0=qTc[:, h * 2:h * 2 + 2, :], in1=xi[j][:].unsqueeze(1).to_broadcast([128, 2, 128]),
                    op=ALU.mult), r=[qTc, xi[j]], w=[qx])
                ps_o = k.psf_rot.next()
                P.op('pe', lambda: nc.tensor.matmul(ps_o[:], lhsT=sT[:], rhs=vc[:, h * DV:(h + 1) * DV], start=True,
                                                    stop=False), r=[sT, vc], w=[ps_o], sig=False)
                for c2 in range(2):
                    P.op('pe', lambda c2=c2: nc.tensor.matmul(ps_o[:], lhsT=qx[:, c2, :], rhs=Rb[h][:, c2, :], start=False,
                                                            stop=(c2 == 1)), r=[qx, Rb[h]], w=[ps_o], sig=(c2 == 1))
                P.op('act', lambda: nc.scalar.activation(out=k.junk[:, 0:512], in_=ps_o[:], func=AF.Square, accum_out=ss[:]),
                     r=[ps_o], w=[ss])
                rstd_from_ss(k, ss, ms, rstd, DV)
                hs = slice(h * DV, (h + 1) * DV)
                P.op('dve', lambda: nc.vector.scalar_tensor_tensor(out=y[:, hs], in0=ps_o[:], scalar=rstd[:, 0:1],
                                                                   in1=gc[:, hs], op0=ALU.mult, op1=ALU.mult),
                     r=[ps_o, rstd, gc], w=[y])
                if p_ == 1:
                    P.op('pool', lambda: nc.gpsimd.tensor_tensor(out=yf[:, hs], in0=y[:, hs], in1=y1[:, hs], op=ALU.add),
                         r=[y, y1], w=[yf])
                P.op('act', lambda: nc.scalar.activation(out=kz[:], in_=kc[:, h * DK:(h + 1) * DK], func=AF.Copy,
                                                         scale=zeta[:, j:j + 1]), r=[kc, zeta], w=[kz])
                for c2 in range(2):
                    ps_k = k.psf_rot.next()
                    P.op('pe', lambda c2=c2, ps_k=ps_k: nc.tensor.matmul(ps_k[:], lhsT=kz[:, c2 * 128:(c2 + 1) * 128],
                                                                       rhs=vc[:, hs], start=True, stop=True),
                         r=[kz, vc], w=[ps_k])
                    P.op('dve', lambda c2=c2, ps_k=ps_k: nc.vector.scalar_tensor_tensor(
                        out=R[h][:, c2, :], in0=R[h][:, c2, :], scalar=cd[:, j:j + 1], in1=ps_k[:], op0=ALU.mult,
                        op1=ALU.add), r=[R[h], cd, ps_k], w=[R[h]])
                P.op('act', lambda: nc.scalar.copy(out=Rb[h][:], in_=R[h][:]), r=[R[h]], w=[Rb[h]])
            if p_ == 0:
                P.dma('pool', k.y_d[cs, :], y[:], key=y, r=[y], w=[("y_d", t)])
            else:
                yT = yTb.next()
                for half in range(2):
                    pb = k.psb_rot.next()
                    for jj in range(8):
                        kk = half * 8 + jj
                        P.op('pe', lambda jj=jj, kk=kk, pb=pb: nc.tensor.transpose(pb[:, jj * 128:(jj + 1) * 128],
                                                                                  yf[:, kk * 128:(kk + 1) * 128], k.ident[:]),
                             r=[yf, k.ident], w=[pb], sig=(jj == 7))
                    P.op('act', lambda half=half, pb=pb: nc.scalar.copy(out=yT[:, half * 8:(half + 1) * 8, :],
                                                                     in_=pb[:].rearrange("p (j n) -> p j n", j=8)),
                         r=[pb], w=[yT])
                pss = [k.psf_rot.next(), k.psf_rot.next()]
                for cg in range(2):
                    for kk in range(16):
                        P.op('pe', lambda cg=cg, kk=kk: nc.tensor.matmul(pss[cg][:], lhsT=yT[:, kk, :],
                                                                       rhs=Wo[:, kk, cg * 512:(cg + 1) * 512],
                                                                       start=(kk == 0), stop=(kk == 15)),
                             r=[yT, ("Wo", kk // 8)], w=[pss[cg]], sig=(kk == 15))
                resid_epilogue(k, pss, t, gg[0 if t < TL else 1], hb.next(), ob2.next(), st4.next())

        def run_chunks(p_, ts):
            nxt = load_chunk(p_, ts[0])
            for n_, t in enumerate(ts):
                cur = nxt
                if n_ + 1 < len(ts):
                    nxt = load_chunk(p_, ts[n_ + 1])
                chunk(p_, t, cur)

        zero_state()
        run_chunks(0, [TL, TL + 1] + list(range(TL)))
        for h in range(H):
            P.dma('sp', k.F_d.ap()[h * 256:(h + 1) * 256, :].rearrange("(c p) n -> p c n", p=128), R[h][:], key=R[h],
                  r=[R[h]], w=["F_d"])
        P.op('pool', lambda: nc.gpsimd.collective_compute("AllGather", ALU.bypass, replica_groups=GROUPS,
                                                          ins=[k.F_d.ap().opt()], outs=[k.Fg_d.ap().opt()]),
             r=["F_d"], w=["Fg_d"], dma="cc", inc=1)
        zero_state()
        run_chunks(1, [TL + 1, TL])
        for h in range(H):
            for rk_ in range(2):
                P.dma('sp', Fg[:], k.Fg_d.ap()[rk_ * 1024 + h * 256:rk_ * 1024 + (h + 1) * 256, :].rearrange(
                    "(c p) n -> p c n", p=128), key=Fg, r=["Fg_d"], w=[Fg])
                if rk_ == 0:
                    P.op('dve', lambda h=h: nc.vector.tensor_scalar(out=R[h][:], in0=Fg[:], scalar1=k.selt[:, 0:1],
                                                                  scalar2=None, op0=ALU.mult), r=[Fg, k.selt], w=[R[h]])
                else:
                    P.op('dve', lambda h=h: nc.vector.scalar_tensor_tensor(out=R[h][:], in0=Fg[:], scalar=k.selt[:, 1:2],
                                                                         in1=R[h][:], op0=ALU.mult, op1=ALU.add),
                         r=[Fg, k.selt, R[h]], w=[R[h]])
            P.op('act', lambda h=h: nc.scalar.copy(out=Rb[h][:], in_=R[h][:]), r=[R[h]], w=[Rb[h]])
        run_chunks(1, list(range(TL - 1, -1, -1)))
        P.barrier()


def diff_layer(k, l, last):
    import math
    nc, P = k.nc, k.P
    i = l // 2
    lam_init = 0.8 - 0.6 * math.exp(-0.3 * l)
    NH = 8
    if not hasattr(k, "qT_d"):
        k.qT_d = nc.dram_tensor("qT_d", [8, 128, NT], BF16)
    if not hasattr(k, "kTl_d"):
        k.kTl_d = [nc.dram_tensor("kTl%d_d" % g, [512, NL], BF16) for g in range(2)]
        k.kTc_d = nc.dram_tensor("kTc_d", [1024, NCX], BF16)
        k.kTg_d = [nc.dram_tensor("kTg%d_d" % g, [1024, NL], BF16) for g in range(2)]
        k.vl_d = [nc.dram_tensor("vl%d_d" % g, [NL, 512], BF16) for g in range(2)]
        k.vc_d = nc.dram_tensor("vc_d", [NCX, 1024], BF16)
        k.vg_d = [nc.dram_tensor("vg%d_d" % g, [2 * NL, 512], BF16) for g in range(2)]
    tiles_all = list(range(TT))
    q_tiles = list(range(TL)) + ([] if last else [TL, TL + 1])
    with ExitStack() as ph:
        uT = sbt(ph, nc, "d_uT", [128, 8, NT], BF16)
        with ExitStack() as ph1:
            prenorm(k, ph1, l, 0, 0, 1, uT, tiles_all)
            P.barrier()
        cos = sbt(ph, nc, "d_cos", [128, TL, 2, 16], F32)
        sin = sbt(ph, nc, "d_sin", [128, TL, 2, 16], F32)
        P.dma('sp', cos[:].rearrange("p t r f -> p (t r f)"), k.ropeD_d.ap()[0], key=cos, w=["rope_tab"])
        P.dma('sp', sin[:].rearrange("p t r f -> p (t r f)"), k.ropeD_d.ap()[1], key=sin, w=["rope_tab"])
        tmp = [sbt(ph, nc, "d_tmp%d" % j, [128, 256], F32) for j in range(4)]
        ob = Rot([sbt(ph, nc, "d_ob%d" % j, [128, 512], BF16) for j in range(3)])
        tb = Rot([sbt(ph, nc, "d_tb%d" % j, [128, 4, 128], BF16) for j in range(3)])

        def epi_qk(isq):
            def epi(t, g, ps):
                if isq and t not in q_tiles:
                    return
                o = ob.next()
                scale = 0.125 if isq else 1.0
                if t < TL:
                    rope_apply(k, ps, o, cos[:, t], sin[:, t], 8, 16, scale, tmp)
                else:
                    P.op('act', lambda: nc.scalar.activation(out=o[:], in_=ps[:], func=AF.Copy, scale=float(scale)),
                         r=[ps], w=[o])
                pb = k.psb_rot.next()
                for j in range(4):
                    P.op('pe', lambda j=j: nc.tensor.transpose(pb[:, j * 128:(j + 1) * 128], o[:, j * 128:(j + 1) * 128],
                                                              k.ident[:]), r=[o, k.ident], w=[pb], sig=(j == 3))
                tt = tb.next()
                P.op('act', lambda: nc.scalar.copy(out=tt[:], in_=pb[:, 0:512].rearrange("p (j n) -> p j n", j=4)),
                     r=[pb], w=[tt])
                if isq:
                    P.dma('sp', k.qT_d.ap()[g * 4:(g + 1) * 4, :, t * 128:(t + 1) * 128].rearrange("j p n -> p j n"), tt[:],
                          key=tt, r=[tt], w=[("qT_d", g, t)])
                elif t < TL:
                    P.dma('sp', k.kTl_d[g].ap()[:, t * 128:(t + 1) * 128].rearrange("(j p) n -> p j n", p=128),
                          tt[:], key=tt, r=[tt], w=[("kTl_d", g, t)])
                    if t == TL - 1:
                        P.op('pool', lambda: nc.gpsimd.collective_compute(
                            "AllGather", ALU.bypass, replica_groups=GROUPS, ins=[k.kTl_d[g].ap().opt()],
                            outs=[k.kTg_d[g].ap().opt()]), r=[("kTl_d", g, t_) for t_ in range(TL)], w=[("kTg_d", g)],
                            dma="cc", inc=1)
                else:
                    P.dma('sp', k.kTc_d.ap()[g * 512:(g + 1) * 512, (t - TL) * 128:(t - TL + 1) * 128].rearrange(
                        "(j p) n -> p j n", p=128), tt[:], key=tt, r=[tt], w=[("kTc_d", g, t)])
            return epi

        def epi_v(t, g, ps):
            o = ob.next()
            P.op('act', lambda: nc.scalar.copy(out=o[:], in_=ps[:]), r=[ps], w=[o])
            if t < TL:
                P.dma('sp', k.vl_d[g][t * 128:(t + 1) * 128, :], o[:], key=o, r=[o], w=[("vl_d", g, t)])
                if t == TL - 1:
                    P.op('pool', lambda: nc.gpsimd.collective_compute(
                        "AllGather", ALU.bypass, replica_groups=GROUPS, ins=[k.vl_d[g].ap().opt()],
                        outs=[k.vg_d[g].ap().opt()]), r=[("vl_d", g, t_) for t_ in range(TL)], w=[("vg_d", g)],
                        dma="cc", inc=1)
            else:
                P.dma('sp', k.vc_d[(t - TL) * 128:(t - TL + 1) * 128, g * 512:(g + 1) * 512], o[:], key=o, r=[o],
                      w=[("vc_d", g, t)])

        wi = k.diff_w_in.ap()[i]
        proj_tm(k, ph, uT, wi[:, 1024:2048], 2, tiles_all, epi_qk(False))
        proj_tm(k, ph, uT, wi[:, 2048:3072], 2, tiles_all, epi_v)
        proj_tm(k, ph, uT, wi[:, 0:1024], 2, tiles_all, epi_qk(True))
        P.barrier()
    with ExitStack() as ph:
        def t_(name, shape, dt):
            return sbt(ph, nc, name, shape, dt)

        NK = 2 * NL + NCX
        NKT = NK // 128
        yT = t_("d_yT", [128, NH, NT], BF16)
        Wo = t_("d_Wo", [128, 8, D], BF16)
        P.dma('pool', Wo[:], k.diff_w_out.ap()[i].rearrange("(f p) n -> p f n", p=128), key=Wo, w=[Wo])
        gg = gate_tiles(k, ph, l, 1, 2)
        ones = t_("d_ones", [128, 128], BF16)
        P.op('dve', lambda: nc.vector.memset(ones[:], 1.0), w=[ones])
        lv = t_("d_lv", [128, 4, 64], F32)
        lp = t_("d_lp", [128, 2, 64], F32)
        ls = t_("d_ls", [128, 2], F32)
        nlam = t_("d_nlam", [128, 1], F32)
        gsub = t_("d_gsub", [128, 1], F32)
        P.dma('sp', lv[:].rearrange("p a b -> p (a b)"),
              k.diff_lam.ap()[i:i + 1].rearrange("o a b -> o (a b)").to_broadcast([128, 256]), key=lv, w=[lv])
        for a in range(2):
            P.op('dve', lambda a=a: nc.vector.tensor_tensor(out=lp[:, a, :], in0=lv[:, 2 * a, :], in1=lv[:, 2 * a + 1, :],
                                                          op=ALU.mult), r=[lv], w=[lp])
            P.op('dve', lambda a=a: nc.vector.reduce_sum(out=ls[:, a:a + 1], in_=lp[:, a, :], axis=AX.X), r=[lp], w=[ls])
        P.op('act', lambda: nc.scalar.activation(out=ls[:], in_=ls[:], func=AF.Exp), r=[ls], w=[ls])
        P.op('dve', lambda: nc.vector.scalar_tensor_tensor(out=nlam[:], in0=ls[:, 1:2], scalar=-float(lam_init),
                                                           in1=ls[:, 0:1], op0=ALU.add, op1=ALU.subtract),
             r=[ls], w=[nlam])
        P.dma('sp', gsub[:], k.diff_subln.ap()[i].rearrange("(p o) -> p o", o=1), key=gsub, w=[gsub],
              allow_slow_non_contiguous=True)
        P.op('dve', lambda: nc.vector.tensor_scalar(out=gsub[:], in0=gsub[:], scalar1=float(1.0 - lam_init), scalar2=None,
                                                    op0=ALU.mult), r=[gsub], w=[gsub])
        kTb = Rot([t_("d_kT%d" % j, [128, NK], BF16) for j in range(2)])
        Vb = Rot([t_("d_V%d" % j, [128, NKT, 128], BF16) for j in range(2)])
        qAb = Rot([t_("d_qA%d" % j, [128, NT], BF16) for j in range(2)])
        qBb = Rot([t_("d_qB%d" % j, [128, NT], BF16) for j in range(2)])
        for qa in qAb.items:
            P.op('pool', lambda qa=qa: nc.gpsimd.memset(qa[64:128, :], 0.0), w=[("qz", qa.name)])
        for qb_ in qBb.items:
            P.op('pool', lambda qb_=qb_: nc.gpsimd.memset(qb_[0:64, :], 0.0), w=[("qz", qb_.name)])
        eb = Rot([t_("d_e%d" % j, [128, 512], BF16) for j in range(6)])
        accD = Rot([t_("d_aD%d" % j, [128, 512], F32) for j in range(2)])
        accP = Rot([t_("d_aP%d" % j, [128, 512], F32) for j in range(2)])
        zhi = Rot([t_("d_zh%d" % j, [128, 512], BF16) for j in range(2)])
        zlo = Rot([t_("d_zl%d" % j, [128, 512], BF16) for j in range(2)])
        rzb = Rot([t_("d_rz%d" % j, [128, 512], F32) for j in range(2)])
        Onb = [Rot([t_("d_On%d_%d" % (w_, j), [128, 512], F32) for j in range(2)]) for w_ in range(2)]
        OTb = Rot([t_("d_OT%d" % j, [128, 512], F32) for j in range(2)])
        sqb = Rot([t_("d_sq%d" % j, [128, 512], BF16) for j in range(2)])
        rsb = Rot([t_("d_rs%d" % j, [128, 512], F32) for j in range(2)])
        lnb = Rot([t_("d_ln%d" % j, [128, 512], F32) for j in range(2)])
        hb = Rot([t_("d_h%d" % j, [128, D], F32) for j in range(2)])
        ob2 = Rot([t_("d_o%d" % j, [128, D], F32) for j in range(2)])
        st4 = Rot([[t_("d_s%d_%d" % (a, b), [128, 1], F32) for b in range(4)] for a in range(2)])
        ps_s = Rot(k.psf[0:2])
        ps_o = Rot(k.psf[2:4])
        ps_z = Rot(k.psf[4:6])

        def load_head(h):
            kT, V, qA, qB = kTb.next(), Vb.next(), qAb.next(), qBb.next()
            hg_, hh = h // 4, h % 4
            for rk_ in range(2):
                P.dma('sp', kT[:, rk_ * NL:(rk_ + 1) * NL],
                      k.kTg_d[hg_][rk_ * 512 + hh * 128:rk_ * 512 + (hh + 1) * 128, :],
                      key=("kT", kT.name, rk_), r=[("kTg_d", hg_)], w=[("kTp", kT.name, rk_)])
                P.dma('sp', V[:, rk_ * TL:(rk_ + 1) * TL, :],
                      k.vg_d[hg_].ap()[rk_ * NL:(rk_ + 1) * NL, hh * 128:(hh + 1) * 128].rearrange("(j p) d -> p j d", p=128),
                      key=("V", V.name, rk_), r=[("vg_d", hg_)], w=[("Vp", V.name, rk_)])
            P.dma('sp', kT[:, 2 * NL:NK], k.kTc_d[h * 128:(h + 1) * 128, :], key=("kT", kT.name, 2),
                  r=[("kTc_d", h // 4, t_) for t_ in (TL, TL + 1)], w=[("kTp", kT.name, 2)])
            P.dma('sp', V[:, 2 * TL:NKT, :], k.vc_d.ap()[:, h * 128:(h + 1) * 128].rearrange("(j p) d -> p j d", p=128),
                  key=("V", V.name, 2), r=[("vc_d", h // 4, t_) for t_ in (TL, TL + 1)], w=[("Vp", V.name, 2)])
            qr = [("qT_d", h // 4, t) for t in q_tiles]
            P.dma('sp', qA[0:64, :], k.qT_d.ap()[h, 0:64, :], key=qA, r=qr, w=[qA])
            P.dma('sp', qB[64:128, :], k.qT_d.ap()[h, 64:128, :], key=qB, r=qr, w=[qB])
            return kT, V, qA, qB

        blocks = [(qb * 512, 512, list(range(NKT))) for qb in range(4)]
        if not last:
            blocks.append((NL, NCX, [NKT - 2, NKT - 1]))
        items = []
        for h in range(NH):
            for (q0, nq, ktiles) in blocks:
                for w_ in range(2):
                    for jn, jt in enumerate(ktiles):
                        items.append(dict(h=h, q0=q0, nq=nq, w=w_, jn=jn, jt=jt, first=(jn == 0),
                                          last=(jn == len(ktiles) - 1)))
        heads = {}
        state = {}

        def emit_S(it):
            h = it["h"]
            if h not in heads:
                assert h == 0
                heads[h] = load_head(h)
            kT, V, qA, qB = heads[h]
            q = qA if it["w"] == 0 else qB
            nq, q0, jt = it["nq"], it["q0"], it["jt"]
            ps = ps_s.next()
            e = eb.next()
            it["e"] = e
            P.op('pe', lambda: nc.tensor.matmul(ps[:, 0:nq], lhsT=kT[:, jt * 128:(jt + 1) * 128], rhs=q[:, q0:q0 + nq],
                                                start=True, stop=True),
                 r=[("kTp", kT.name, min(jt // TL, 2)), q, ("qz", q.name)], w=[ps])
            P.op('act', lambda: nc.scalar.activation(out=e[:, 0:nq], in_=ps[:, 0:nq], func=AF.Exp), r=[ps], w=[e])

        def emit_PV(it):
            h, w_, nq, q0, jt, jn = it["h"], it["w"], it["nq"], it["q0"], it["jt"], it["jn"]
            kT, V, qA, qB = heads[h]
            e = it["e"]
            if it["first"] and w_ == 0 and q0 == 0 and h + 1 < NH:
                heads[h + 1] = load_head(h + 1)
            if it["first"]:
                state["pO"] = ps_o.next()
                state["aD"], state["aP"] = accD.next(), accP.next()
                state["usedP"] = False
            pO, aD, aP = state["pO"], state["aD"], state["aP"]
            P.op('pe', lambda: nc.tensor.matmul(pO[:, 0:nq], lhsT=V[:, jt, :], rhs=e[:, 0:nq], start=it["first"],
                                                stop=it["last"]),
                 r=[("Vp", V.name, min(jt // TL, 2)), e], w=[pO], sig=it["last"])
            if jn % 3 == 2:
                if not state["usedP"]:
                    P.op('pool', lambda: nc.gpsimd.tensor_copy(out=aP[:, 0:nq], in_=e[:, 0:nq]), r=[e], w=[aP])
                    state["usedP"] = True
                else:
                    P.op('pool', lambda: nc.gpsimd.tensor_tensor(out=aP[:, 0:nq], in0=aP[:, 0:nq], in1=e[:, 0:nq],
                                                                 op=ALU.add), r=[e, aP], w=[aP])
            elif jn == 0:
                P.op('dve', lambda: nc.vector.tensor_copy(out=aD[:, 0:nq], in_=e[:, 0:nq]), r=[e], w=[aD])
            else:
                P.op('dve', lambda: nc.vector.tensor_tensor(out=aD[:, 0:nq], in0=aD[:, 0:nq], in1=e[:, 0:nq], op=ALU.add),
                     r=[e, aD], w=[aD])
            if not it["last"]:
                return
            if state["usedP"]:
                P.op('dve', lambda: nc.vector.tensor_tensor(out=aD[:, 0:nq], in0=aD[:, 0:nq], in1=aP[:, 0:nq], op=ALU.add),
                     r=[aD, aP], w=[aD])
            zh, zl, rz = zhi.next(), zlo.next(), rzb.next()
            P.op('dve', lambda: nc.vector.tensor_copy(out=zh[:, 0:nq], in_=aD[:, 0:nq]), r=[aD], w=[zh])
            P.op('dve', lambda: nc.vector.tensor_tensor(out=zl[:, 0:nq], in0=aD[:, 0:nq], in1=zh[:, 0:nq], op=ALU.subtract),
                 r=[aD, zh], w=[zl])
            pZ = ps_z.next()
            P.op('pe', lambda: nc.tensor.matmul(pZ[:, 0:nq], lhsT=ones[:], rhs=zh[:, 0:nq], start=True, stop=False),
                 r=[ones, zh], w=[pZ], sig=False)
            P.op('pe', lambda: nc.tensor.matmul(pZ[:, 0:nq], lhsT=ones[:], rhs=zl[:, 0:nq], start=False, stop=True),
                 r=[ones, zl], w=[pZ])
            On = Onb[w_].next()
            P.op('dve', lambda: nc.vector.reciprocal(out=rz[:, 0:nq], in_=pZ[:, 0:nq]), r=[pZ], w=[rz])
            P.op('dve', lambda: nc.vector.tensor_tensor(out=On[:, 0:nq], in0=pO[:, 0:nq], in1=rz[:, 0:nq], op=ALU.mult),
                 r=[pO, rz], w=[On])
            state["On%d" % w_] = On
            if w_ == 0:
                return
            On0, On1 = state["On0"], state["On1"]
            OT, sq, rs, ln_ = OTb.next(), sqb.next(), rsb.next(), lnb.next()
            P.op('dve', lambda: nc.vector.scalar_tensor_tensor(out=OT[:, 0:nq], in0=On1[:, 0:nq], scalar=nlam[:, 0:1],
                                                               in1=On0[:, 0:nq], op0=ALU.mult, op1=ALU.add),
                 r=[On0, On1, nlam], w=[OT])
            P.op('act', lambda: nc.scalar.activation(out=sq[:, 0:nq], in_=OT[:, 0:nq], func=AF.Square), r=[OT], w=[sq])
            pS = ps_z.next()
            P.op('pe', lambda: nc.tensor.matmul(pS[:, 0:nq], lhsT=ones[:], rhs=sq[:, 0:nq], start=True, stop=True),
                 r=[ones, sq], w=[pS])
            P.op('act', lambda: nc.scalar.activation(out=ln_[:, 0:nq], in_=pS[:, 0:nq], func=AF.Ln, scale=1.0 / 128,
                                                     bias=k.epsb[:, 0:1]), r=[pS, k.epsb], w=[ln_])
            P.op('act', lambda: nc.scalar.activation(out=rs[:, 0:nq], in_=ln_[:, 0:nq], func=AF.Exp, scale=-0.5),
                 r=[ln_], w=[rs])
            P.op('dve', lambda: nc.vector.scalar_tensor_tensor(out=yT[:, h, q0:q0 + nq], in0=OT[:, 0:nq],
                                                               scalar=gsub[:, 0:1], in1=rs[:, 0:nq], op0=ALU.mult,
                                                               op1=ALU.mult), r=[OT, gsub, rs], w=[("yT", h, q0)])

        prev = None
        for it in items:
            emit_S(it)
            if prev is not None:
                emit_PV(prev)
            prev = it
        emit_PV(prev)
        for t in q_tiles:
            pss = [k.psf_rot.next(), k.psf_rot.next()]
            for cg in range(2):
                for kk in range(8):
                    P.op('pe', lambda cg=cg, kk=kk: nc.tensor.matmul(pss[cg][:], lhsT=yT[:, kk, t * 128:(t + 1) * 128],
                                                                   rhs=Wo[:, kk, cg * 512:(cg + 1) * 512],
                                                                   start=(kk == 0), stop=(kk == 7)),
                         r=[("yT", kk, (t * 128 // 512) * 512 if t < TL else NL), Wo], w=[pss[cg]], sig=(kk == 7))
            resid_epilogue(k, pss, t, gg[0 if t < TL else 1], hb.next(), ob2.next(), st4.next())
        P.barrier()

def prep_inputs(inp):
    f = lambda a: np.ascontiguousarray(np.asarray(a, dtype=np.float32))
    x, c, ctx, c_ctx = f(inp["x"]), f(inp["c"]), f(inp["ctx"]), f(inp["c_ctx"])
    shared = {n: f(inp[n]) for n in ("ada_w", "ada_b", "norm_g", "ret_w_out", "diff_w_in", "diff_w_out",
                                     "diff_subln", "ffn_w_up", "ffn_conv_b", "ffn_w_down")}
    shared["diff_lam"] = np.ascontiguousarray(np.stack([f(inp["diff_lq1"]), f(inp["diff_lk1"]), f(inp["diff_lq2"]),
                                                        f(inp["diff_lk2"])], axis=1))
    rw = f(inp["ret_w_in"])
    rw_sw = np.ascontiguousarray(np.concatenate([rw[:, :, :4096], rw[:, :, 6144:8192], rw[:, :, 4096:6144]], axis=2))
    dec = np.stack([f(inp["ret_decay_f"]), f(inp["ret_decay_b"])], axis=1)
    dec_sw = np.ascontiguousarray(dec[:, ::-1, :])
    cwv = f(inp["ffn_conv_w"])
    cw_sw = np.ascontiguousarray(cwv[:, ::-1, :])
    maps = []
    for core in range(8):
        b, half = core // 2, core % 2
        xs = x[b, half * NL:(half + 1) * NL]
        cs = ctx[b]
        pos = np.arange(half * NL, (half + 1) * NL)
        if half:
            xs, cs, pos = xs[::-1], cs[::-1], pos[::-1]
        posrc = np.stack([pos // 64, pos % 64], axis=-1).astype(np.float32)
        m = dict(shared)
        m["x"] = np.ascontiguousarray(xs)
        m["ctx"] = np.ascontiguousarray(cs)
        m["c2"] = np.ascontiguousarray(np.stack([c[b], c_ctx]))
        sel = np.zeros((128, 2), np.float32)
        sel[:, 1 - half] = 1.0
        m["sel"] = sel
        m["pos"] = np.ascontiguousarray(posrc.reshape(TL, 128, 2).transpose(1, 0, 2))
        m["ret_w_in"] = rw_sw if half else rw
        m["ret_decay"] = dec_sw if half else np.ascontiguousarray(dec)
        m["ffn_conv_w"] = cw_sw if half else cwv
        maps.append(m)
    return maps


def gather_out(results):
    out = np.empty((4, 2 * NL, D), np.float32)
    for core in range(8):
        b, half = core // 2, core % 2
        o = np.asarray(results[core]["out"])[:NL]
        out[b, half * NL:(half + 1) * NL] = o[::-1] if half else o
    return out


def kernel(**inputs):
    nc = build("full")
    maps = prep_inputs(inputs)
    res = run_bass_kernel_spmd(nc, maps, core_ids=list(range(8)))
    return gather_out(res.results)
```

```python
import numpy as np
from contextlib import ExitStack
import concourse.bass as bass
import concourse.mybir as mybir
from concourse.bass_utils import run_bass_kernel_spmd

F32 = mybir.dt.float32
BF16 = mybir.dt.bfloat16
I32 = mybir.dt.int32
AF = mybir.ActivationFunctionType
ALU = mybir.AluOpType
AX = mybir.AxisListType

D = 1024
NL = 2048
NCX = 256
NT = NL + NCX
TL, TCX, TT = 16, 2, 18
DEPTH = 4
FH = 2816
NFC = 22
EPS = 1e-6
GROUPS = [[0, 1], [2, 3], [4, 5], [6, 7]]


class Prog:
    def __init__(self, nc, stack):
        self.nc = nc
        self.stack = stack
        self.E = dict(pe=nc.tensor, act=nc.scalar, dve=nc.vector, pool=nc.gpsimd, sp=nc.sync)
        self.phys = []
        self.pcnt = []
        self.key2p = {}
        self.free = {}
        self.pclass = {}
        self.lastw = {}
        self.rd = {}
        self.dmalast = {}
        self.waited = {e: {} for e in self.E}
        self.nins = 0

    def _sem(self, k, cls="c"):
        if k not in self.key2p:
            fl = self.free.setdefault(cls, [])
            if fl:
                p = fl.pop()
            else:
                p = len(self.phys)
                self.phys.append(self.stack.enter_context(self.nc.semaphore("s%d" % p)))
                self.pcnt.append(0)
                self.pclass[p] = cls
            self.key2p[k] = p
        p = self.key2p[k]
        assert self.pclass[p] == cls, (k, cls, self.pclass[p])
        return p

    @staticmethod
    def _rk(x):
        return x if isinstance(x, (str, tuple)) else ("T", x.name)

    def op(self, eng, fn, r=(), w=(), sig=True, dma=None, inc=None):
        r = [self._rk(x) for x in r]
        w = [self._rk(x) for x in w]
        if dma is not None:
            dma = self._rk(dma)
        kind = 'd' if dma is not None else 'c'
        if dma is None:
            p = self._sem(eng)
        else:
            p = self._sem(('d', dma), "cc" if inc == 1 else ("sw" if eng == "pool" else "hw"))
        deps = {}

        def add(tok, raw):
            k, v, e, kd = tok
            if kd == 'c' and kind == 'c' and e == eng:
                if eng == 'pe' or not raw:
                    return
            if deps.get(k, 0) < v:
                deps[k] = v

        for x in r:
            if x in self.lastw:
                add(self.lastw[x], True)
        for x in w:
            if x in self.lastw:
                add(self.lastw[x], False)
            for tok in self.rd.get(x, {}).values():
                add(tok, False)
        if dma is not None and dma in self.dmalast:
            add(self.dmalast[dma], False)
        wd = self.waited[eng]
        for k, v in deps.items():
            if wd.get(k, 0) >= v:
                continue
            assert self.pcnt[k] >= v, ("wait on unsignaled op", k, v, self.pcnt[k])
            self.E[eng].wait_ge(self.phys[k], v)
            wd[k] = v
            self.nins += 1
        ins = fn()
        self.nins += 1
        if sig:
            if inc is None:
                inc = 16 if kind == 'd' else 1
            self.pcnt[p] += inc
            ins.then_inc(self.phys[p], inc)
            val = self.pcnt[p]
        else:
            val = self.pcnt[p] + 1
        tok = (p, val, eng, kind)
        for x in r:
            self.rd.setdefault(x, {})[p] = tok
        for x in w:
            self.lastw[x] = tok
            self.rd[x] = {}
        if dma is not None:
            self.dmalast[dma] = tok

    def dma(self, q, out, in_, key, r=(), w=(), **kw):
        self.op(q, lambda: self.E[q].dma_start(out=out, in_=in_, **kw), r=r, w=w, dma=key)

    def barrier(self):
        for e in self.E:
            wd = self.waited[e]
            for p, h in enumerate(self.phys):
                v = self.pcnt[p]
                if v > wd.get(p, 0):
                    self.E[e].wait_ge(h, v)
                    wd[p] = v
                    self.nins += 1
        self.lastw.clear()
        self.rd.clear()
        self.dmalast.clear()
        for k in [k for k in self.key2p if isinstance(k, tuple) and k[0] == 'd']:
            p = self.key2p.pop(k)
            self.free[self.pclass[p]].append(p)


_UNIQ = [0]


def sbt(ph, nc, name, shape, dt):
    _UNIQ[0] += 1
    return ph.enter_context(nc.sbuf_tensor("%s_%d" % (name, _UNIQ[0]), list(shape), dt))


class Rot:
    def __init__(self, items):
        self.items = list(items)
        self.i = 0

    def next(self):
        x = self.items[self.i % len(self.items)]
        self.i += 1
        return x


class K:
    pass


def build(mode="full"):
    nc = bass.Bass("TRN2", target_bir_lowering=False)
    stack = ExitStack()
    with stack:
        _build(nc, stack, mode)
    return nc


def _dram_in(nc, name, shape, dt=F32):
    return nc.dram_tensor(name, list(shape), dt, kind="ExternalInput")


def _build(nc, stack, mode):
    P = Prog(nc, stack)
    k = K()
    k.nc, k.P, k.mode = nc, P, mode
    k.x = _dram_in(nc, "x", [NL, D])
    k.ctx = _dram_in(nc, "ctx", [NCX, D])
    k.c2 = _dram_in(nc, "c2", [2, D])
    k.sel = _dram_in(nc, "sel", [128, 2])
    k.pos = _dram_in(nc, "pos", [128, TL, 2])
    k.ada_w = _dram_in(nc, "ada_w", [DEPTH, D, 6 * D])
    k.ada_b = _dram_in(nc, "ada_b", [DEPTH, 6 * D])
    k.norm_g = _dram_in(nc, "norm_g", [DEPTH, 4, D])
    k.ret_w_in = _dram_in(nc, "ret_w_in", [2, D, 8192])
    k.ret_w_out = _dram_in(nc, "ret_w_out", [2, 2048, D])
    k.ret_decay = _dram_in(nc, "ret_decay", [2, 2, 4])
    k.diff_w_in = _dram_in(nc, "diff_w_in", [2, D, 3 * D])
    k.diff_w_out = _dram_in(nc, "diff_w_out", [2, D, D])
    k.diff_lam = _dram_in(nc, "diff_lam", [2, 4, 64])
    k.diff_subln = _dram_in(nc, "diff_subln", [2, 128])
    k.ffn_w_up = _dram_in(nc, "ffn_w_up", [DEPTH, D, 2 * FH])
    k.ffn_conv_w = _dram_in(nc, "ffn_conv_w", [DEPTH, 3, FH])
    k.ffn_conv_b = _dram_in(nc, "ffn_conv_b", [DEPTH, FH])
    k.ffn_w_down = _dram_in(nc, "ffn_w_down", [DEPTH, FH, D])
    k.out = nc.dram_tensor("out", [NT, D], F32, kind="ExternalOutput")
    k.h_d = nc.dram_tensor("h_d", [NT, D], F32)
    k.mod_d = nc.dram_tensor("mod_d", [DEPTH, 2, 6 * D], F32)
    k.hx_d = nc.dram_tensor("hx_d", [128, 8], BF16)
    k.hg_d = nc.dram_tensor("hg_d", [256, 8], BF16)

    def sb(name, shape, dt):
        return stack.enter_context(nc.sbuf_tensor(name, list(shape), dt))

    k.ident = sb("ident", [128, 128], BF16)
    k.iotaf = sb("iotaf", [128, 128], F32)
    k.iotap = sb("iotap", [128, 1], F32)
    k.neghalf = sb("neghalf", [128, 1], F32)
    k.epsb = sb("epsb", [128, 1], F32)
    k.selt = sb("selt", [128, 2], F32)
    k.junk = sb("junk", [128, 1024], BF16)
    k.psf = [stack.enter_context(nc.psum_tensor("psf%d" % i, [128, 512], F32)) for i in range(6)]
    k.psb = [stack.enter_context(nc.psum_tensor("psb%d" % i, [128, 1024], BF16)) for i in range(2)]
    k.psf_rot = Rot(k.psf)
    k.psb_rot = Rot(k.psb)

    consts(k)
    ada_all(k)
    init_h(k)
    if mode == "ffn0":
        ffn_layer(k, 0, False)
    if mode in ("ret0", "L0"):
        rope_tables(k)
        ret_layer(k, 0)
        if mode == "L0":
            ffn_layer(k, 0, False)
    if mode == "diff1":
        rope_tables(k)
        diff_layer(k, 1, False)
    if mode == "full":
        rope_tables(k)
        for l in range(DEPTH):
            last = l == DEPTH - 1
            if l % 2 == 0:
                ret_layer(k, l)
            else:
                diff_layer(k, l, last)
            ffn_layer(k, l, last)
    P.barrier()
    with ExitStack() as ph:
        ob = [sbt(ph, nc, "ob%d" % i, [128, D], F32) for i in range(3)]
        rot = Rot(ob)
        for t in range(TT):
            b = rot.next()
            P.dma('sp', b[:], k.h_d[t * 128:(t + 1) * 128, :], key=b, r=[("h", t)], w=[b])
            P.dma('sp', k.out[t * 128:(t + 1) * 128, :], b[:], key=b, r=[b], w=[("out", t)])
        P.barrier()
    print("instructions:", P.nins, "sems:", len(P.phys))


def consts(k):
    nc, P = k.nc, k.P
    with ExitStack() as ph:
        ii = sbt(ph, nc, "ii", [128, 128], I32)
        pp = sbt(ph, nc, "pp", [128, 1], I32)
        P.op('pool', lambda: nc.gpsimd.iota(ii[:], pattern=[[1, 128]], base=0, channel_multiplier=0), w=[ii])
        P.op('pool', lambda: nc.gpsimd.iota(pp[:], pattern=[[0, 1]], base=0, channel_multiplier=1), w=[pp])
        P.op('dve', lambda: nc.vector.tensor_copy(out=k.iotaf[:], in_=ii[:]), r=[ii], w=[k.iotaf])
        P.op('dve', lambda: nc.vector.tensor_copy(out=k.iotap[:], in_=pp[:]), r=[pp], w=[k.iotap])
        P.op('dve', lambda: nc.vector.tensor_scalar(out=k.ident[:], in0=k.iotaf[:], scalar1=k.iotap[:, 0:1],
                                                    scalar2=None, op0=ALU.is_equal),
             r=[k.iotaf, k.iotap], w=[k.ident])
        P.op('dve', lambda: nc.vector.memset(k.neghalf[:], -0.5), w=[k.neghalf])
        P.op('dve', lambda: nc.vector.memset(k.epsb[:], EPS), w=[k.epsb])
        P.dma('sp', k.selt[:], k.sel[:, :], key=k.selt, w=[k.selt])
        P.barrier()


def init_h(k):
    P = k.P
    P.dma('sp', k.h_d[0:NL, :], k.x[:, :], key="inith0", w=[("h", t) for t in range(TL)])
    P.dma('sp', k.h_d[NL:NT, :], k.ctx[:, :], key="inith1", w=[("h", t) for t in range(TL, TT)])


def ada_all(k):
    nc, P = k.nc, k.P
    with ExitStack() as ph:
        cT = sbt(ph, nc, "cT", [128, 8, 2], F32)
        sT = sbt(ph, nc, "sT", [128, 8, 2], F32)
        wb = [sbt(ph, nc, "adaw%d" % i, [128, 8, 512], F32) for i in range(2)]
        bias = sbt(ph, nc, "adab", [2, 6 * D], F32)
        row = sbt(ph, nc, "adarow", [2, 6 * D], F32)
        wrot = Rot(wb)
        for r in range(2):
            P.dma('sp', cT[:, :, r], k.c2.ap()[r].rearrange("(k p) -> p k", p=128), key=cT, w=[cT],
                  allow_slow_non_contiguous=True)
        P.op('act', lambda: nc.scalar.activation(out=sT[:], in_=cT[:], func=AF.Silu), r=[cT], w=[sT])
        for l in range(DEPTH):
            P.dma('sp', bias[:], k.ada_b.ap()[l:l + 1, :].to_broadcast([2, 6 * D]), key=bias, w=[bias])
            for g in range(12):
                w = wrot.next()
                P.dma('sp', w[:], k.ada_w.ap()[l][:, g * 512:(g + 1) * 512].rearrange("(k p) n -> p k n", p=128),
                      key=w, w=[w])
                ps = k.psf_rot.next()
                for kk in range(8):
                    P.op('pe', lambda kk=kk, w=w, ps=ps: nc.tensor.matmul(ps[0:2, :], lhsT=sT[:, kk, :], rhs=w[:, kk, :],
                                                                     start=(kk == 0), stop=(kk == 7)),
                         r=[sT, w], w=[ps], sig=(kk == 7))
                P.op('dve', lambda g=g, ps=ps: nc.vector.tensor_tensor(out=row[:, g * 512:(g + 1) * 512], in0=ps[0:2, :],
                                                                    in1=bias[:, g * 512:(g + 1) * 512], op=ALU.add),
                     r=[ps, bias], w=[row])
            P.dma('sp', k.mod_d.ap()[l], row[:], key=row, r=[row], w=[("mod", l)])
        P.barrier()


def bcast_row(k, q, dst, src_row, key, r=(), w=()):
    n = dst.shape[-1]
    k.P.dma(q, dst, src_row.to_broadcast([128, n]), key=key, r=r, w=w)


def rstd_from_ss(k, ss, ms, rstd, n):
    nc, P = k.nc, k.P
    P.op('act', lambda: nc.scalar.activation(out=ms[:], in_=ss[:], func=AF.Ln, scale=1.0 / n, bias=k.epsb[:, 0:1]),
         r=[ss, k.epsb], w=[ms])
    P.op('act', lambda: nc.scalar.activation(out=rstd[:], in_=ms[:], func=AF.Exp, scale=-0.5), r=[ms], w=[rstd])


def prenorm(k, ph, l, gi, shi, sci, uT, tiles, hook=None):
    nc, P = k.nc, k.P

    def t_(name, shape, dt):
        return sbt(ph, nc, name, shape, dt)

    gb = t_("pn_g", [128, D], F32)
    gs = [t_("pn_gs%d" % r, [128, D], F32) for r in range(2)]
    sh = [t_("pn_sh%d" % r, [128, D], F32) for r in range(2)]
    hb = Rot([t_("pn_h%d" % i, [128, D], F32) for i in range(3)])
    tb = Rot([t_("pn_t%d" % i, [128, D], F32) for i in range(2)])
    ub = Rot([t_("pn_u%d" % i, [128, D], BF16) for i in range(2)])
    st = Rot([[t_("pn_s%d_%d" % (i, j), [128, 1], F32) for j in range(3)] for i in range(3)])
    bcast_row(k, 'sp', gb[:], k.norm_g.ap()[l, gi:gi + 1, :], key=gb, w=[gb])
    for r in range(2):
        bcast_row(k, 'sp', gs[r][:], k.mod_d.ap()[l, r:r + 1, sci * D:(sci + 1) * D], key=gs[r], r=[("mod", l)], w=[gs[r]])
        bcast_row(k, 'sp', sh[r][:], k.mod_d.ap()[l, r:r + 1, shi * D:(shi + 1) * D], key=sh[r], r=[("mod", l)], w=[sh[r]])
        P.op('dve', lambda r=r: nc.vector.scalar_tensor_tensor(out=gs[r][:], in0=gs[r][:], scalar=1.0, in1=gb[:],
                                                           op0=ALU.add, op1=ALU.mult), r=[gs[r], gb], w=[gs[r]])
    def stage_a(t):
        h = hb.next()
        ss, ms, rstd = st.next()
        P.dma('sp', h[:], k.h_d[t * 128:(t + 1) * 128, :], key=h, r=[("h", t)], w=[h])
        P.op('act', lambda: nc.scalar.activation(out=k.junk[:], in_=h[:], func=AF.Square, accum_out=ss[:]),
             r=[h], w=[ss])
        rstd_from_ss(k, ss, ms, rstd, D)
        return h, rstd

    def stage_b(t, h, rstd):
        r = 0 if t < TL else 1
        tt = tb.next()
        u = ub.next()
        P.op('dve', lambda: nc.vector.scalar_tensor_tensor(
            out=tt[:], in0=h[:], scalar=rstd[:, 0:1], in1=gs[r][:], op0=ALU.mult, op1=ALU.mult),
            r=[h, rstd, gs[r]], w=[tt])
        P.op('dve', lambda: nc.vector.tensor_tensor(out=u[:], in0=tt[:], in1=sh[r][:], op=ALU.add),
             r=[tt, sh[r]], w=[u])
        pb = k.psb_rot.next()
        for kk in range(8):
            P.op('pe', lambda kk=kk: nc.tensor.transpose(pb[:, kk * 128:(kk + 1) * 128], u[:, kk * 128:(kk + 1) * 128],
                                                        k.ident[:]),
                 r=[u, k.ident], w=[pb], sig=(kk == 7))
        P.op('act', lambda: nc.scalar.copy(out=uT[:, :, t * 128:(t + 1) * 128],
                                           in_=pb[:].rearrange("p (k n) -> p k n", k=8)),
             r=[pb], w=[("uT", t)])
        if hook is not None:
            hook(t)

    prev = None
    for t in tiles:
        a = stage_a(t)
        if prev is not None:
            stage_b(*prev)
        prev = (t,) + a
    stage_b(*prev)


def uT_res(c, n):
    return [("uT", t) for t in range(c // 128, (c + n - 1) // 128 + 1)]


def ffn_layer(k, l, last):
    nc, P = k.nc, k.P
    tiles_all = list(range(TL)) + ([] if last else list(range(TL, TT)))
    with ExitStack() as ph:
        def t_(name, shape, dt):
            return sbt(ph, nc, name, shape, dt)

        uT = t_("f_uT", [128, 8, NT], BF16)
        Wd = t_("f_Wd", [128, NFC, D], BF16)
        cw = t_("f_cw", [128, 4, NFC], F32)
        uH = [t_("f_uH%d" % i, [128, 8, 2], BF16) for i in range(2)]
        hx = t_("f_hx", [128, 8], BF16)
        hg = t_("f_hg", [128, 2, 8], BF16)
        hgf = t_("f_hgf", [128, 8], F32)
        for half in range(2):
            f0, f1 = half * 11, (half + 1) * 11
            P.dma('pool', Wd[:, f0:f1, :], k.ffn_w_down.ap()[l][f0 * 128:f1 * 128, :].rearrange("(f p) n -> p f n", p=128),
                  key=("Wd", half), w=[("Wd", half)])
        for tap in range(3):
            P.dma('sp', cw[:, tap, :], k.ffn_conv_w.ap()[l, tap].rearrange("(f p) -> p f", p=128), key=cw, w=[cw],
                  allow_slow_non_contiguous=True)
        P.dma('sp', cw[:, 3, :], k.ffn_conv_b.ap()[l].rearrange("(f p) -> p f", p=128), key=cw, w=[cw],
              allow_slow_non_contiguous=True)
        with ExitStack() as ph1:
            order = [TL - 1] + [t for t in tiles_all if t != TL - 1]

            def hook(t):
                if t != TL - 1:
                    return
                P.op('dve', lambda: nc.vector.tensor_copy(out=hx[:], in_=uT[:, :, NL - 1]), r=[("uT", TL - 1)], w=[hx])
                P.dma('pool', k.hx_d[:, :], hx[:], key=hx, r=[hx], w=["hx_d"])
                P.op('pool', lambda: nc.gpsimd.collective_compute("AllGather", ALU.bypass, replica_groups=GROUPS,
                                                                  ins=[k.hx_d.ap().opt()], outs=[k.hg_d.ap().opt()]),
                     r=["hx_d"], w=["hg_d"], dma="cc", inc=1)

            prenorm(k, ph1, l, 2, 3, 4, uT, order, hook)
            P.dma('sp', hg[:], k.hg_d.ap().rearrange("(r p) k -> p r k", p=128), key=hg, r=["hg_d"], w=[hg])
            P.op('dve', lambda: nc.vector.tensor_scalar(out=hgf[:], in0=hg[:, 0, :], scalar1=k.selt[:, 0:1], scalar2=None,
                                                        op0=ALU.mult), r=[hg, k.selt], w=[hgf])
            P.op('dve', lambda: nc.vector.scalar_tensor_tensor(out=uH[1][:, :, 1], in0=hg[:, 1, :], scalar=k.selt[:, 1:2],
                                                               in1=hgf[:], op0=ALU.mult, op1=ALU.add),
                 r=[hg, hgf, k.selt], w=[uH[1]])
            P.op('dve', lambda: nc.vector.tensor_copy(out=uH[1][:, :, 0], in_=uT[:, :, NL // 2 - 1]), r=[("uT", 7)], w=[uH[1]])
            P.op('dve', lambda: nc.vector.tensor_copy(out=uH[0][:, :, 0], in_=uT[:, :, NL // 2]), r=[("uT", 8)], w=[uH[0]])
            P.op('dve', lambda: nc.vector.tensor_copy(out=uH[0][:, :, 1], in_=uT[:, :, NL // 2]), r=[("uT", 8)], w=[uH[0]])
            P.barrier()
        segs = [dict(tiles=list(range(0, 8)), subs=[(0, 1024, False, True)], uH=uH[0]),
                dict(tiles=list(range(8, 16)) + ([] if last else [16, 17]),
                     subs=[(1024, 1024, True, True)] + ([] if last else [(2048, 256, False, False)]), uH=uH[1])]
        with ExitStack() as ph2:
            def t2(name, shape, dt):
                return sbt(ph2, nc, name, shape, dt)

            NCOL = 1280
            aT = t2("f_aT", [128, NFC, NCOL], BF16)
            Wv = Rot([t2("f_Wv%d" % i, [128, 8, 256], BF16) for i in range(2)])
            Wg = Rot([t2("f_Wg%d" % i, [128, 8, 256], BF16) for i in range(2)])
            GW = NCOL + 4
            Gb = Rot([t2("f_G%d" % i, [128, GW], F32) for i in range(2)])
            Vb = Rot([t2("f_V%d" % i, [128, NCOL], F32) for i in range(2)])
            Tb = Rot([t2("f_T%d" % i, [128, NCOL], F32) for i in range(1)])
            gg = [t2("f_gg%d" % r, [128, D], F32) for r in range(2)]
            g3 = t2("f_g3", [128, D], F32)
            hb = Rot([t2("f_h%d" % i, [128, D], F32) for i in range(2)])
            ob = Rot([t2("f_o%d" % i, [128, D], F32) for i in range(2)])
            st = Rot([[t2("f_s%d_%d" % (i, j), [128, 1], F32) for j in range(4)] for i in range(2)])
            for G in Gb.items:
                P.op('pool', lambda G=G: nc.gpsimd.memset(G[:], 0.0), w=[G])
            bcast_row(k, 'sp', g3[:], k.norm_g.ap()[l, 3:4, :], key=g3, w=[g3])
            for r in range(2):
                bcast_row(k, 'sp', gg[r][:], k.mod_d.ap()[l, r:r + 1, 5 * D:6 * D], key=gg[r], r=[("mod", l)], w=[gg[r]])
                P.op('dve', lambda r=r: nc.vector.tensor_tensor(out=gg[r][:], in0=gg[r][:], in1=g3[:], op=ALU.mult),
                     r=[gg[r], g3], w=[gg[r]])
            wup = k.ffn_w_up.ap()[l]

            def load_w(fcg):
                wv, wg = Wv.next(), Wg.next()
                P.dma('pool', wv[:], wup[:, fcg * 256:(fcg + 1) * 256].rearrange("(k p) n -> p k n", p=128), key=wv, w=[wv])
                P.dma('pool', wg[:], wup[:, FH + fcg * 256:FH + (fcg + 1) * 256].rearrange("(k p) n -> p k n", p=128),
                      key=wg, w=[wg])
                return wv, wg

            for seg in segs:
                subs = seg["subs"]
                uHs = seg["uH"]
                offs, goffs = [], []
                o, go = 0, 0
                for (c0, n, lh, rh) in subs:
                    offs.append(o)
                    goffs.append(go)
                    o += n
                    go += n + 2
                nxt = load_w(0)
                for fcg in range(11):
                    wv, wg = nxt
                    if fcg + 1 < 11:
                        nxt = load_w(fcg + 1)
                    for j in range(2):
                        fc = fcg * 2 + j
                        G, V, T = Gb.next(), Vb.next(), Tb.next()
                        for si, (c0, n, lh, rh) in enumerate(subs):
                            for b0 in range(0, n, 512):
                                nb = min(512, n - b0)
                                for (wt, dst, doff) in ((wv, V, offs[si] + b0), (wg, G, goffs[si] + 1 + b0)):
                                    ps = k.psf_rot.next()
                                    for kk in range(8):
                                        P.op('pe', lambda kk=kk, wt=wt, ps=ps, c=c0 + b0, nb=nb, j=j: nc.tensor.matmul(
                                            ps[:, 0:nb], lhsT=wt[:, kk, j * 128:(j + 1) * 128], rhs=uT[:, kk, c:c + nb],
                                            start=(kk == 0), stop=(kk == 7)),
                                            r=[wt] + uT_res(c0 + b0, nb), w=[ps], sig=(kk == 7))
                                    P.op('act', lambda ps=ps, dst=dst, doff=doff, nb=nb: nc.scalar.copy(
                                        out=dst[:, doff:doff + nb], in_=ps[:, 0:nb]), r=[ps], w=[dst])
                        ps = k.psf_rot.next()
                        for kk in range(8):
                            P.op('pe', lambda kk=kk, ps=ps, j=j, wg=wg: nc.tensor.matmul(
                                ps[:, 0:2], lhsT=wg[:, kk, j * 128:(j + 1) * 128], rhs=uHs[:, kk, :],
                                start=(kk == 0), stop=(kk == 7)), r=[wg, uHs], w=[ps], sig=(kk == 7))
                        for si, (c0, n, lh, rh) in enumerate(subs):
                            go = goffs[si]
                            if lh:
                                P.op('act', lambda ps=ps, G=G, go=go: nc.scalar.copy(out=G[:, go:go + 1], in_=ps[:, 0:1]),
                                     r=[ps], w=[G])
                            if rh:
                                P.op('act', lambda ps=ps, G=G, go=go, n=n: nc.scalar.copy(
                                    out=G[:, go + n + 1:go + n + 2], in_=ps[:, 1:2]), r=[ps], w=[G])
                        for si, (c0, n, lh, rh) in enumerate(subs):
                            go, o = goffs[si], offs[si]
                            P.op('dve', lambda G=G, T=T, go=go, o=o, n=n, fc=fc: nc.vector.tensor_scalar(
                                out=T[:, o:o + n], in0=G[:, go + 1:go + 1 + n], scalar1=cw[:, 1, fc:fc + 1],
                                scalar2=cw[:, 3, fc:fc + 1], op0=ALU.mult, op1=ALU.add), r=[G, cw], w=[T])
                            P.op('dve', lambda G=G, T=T, go=go, o=o, n=n, fc=fc: nc.vector.scalar_tensor_tensor(
                                out=T[:, o:o + n], in0=G[:, go:go + n], scalar=cw[:, 0, fc:fc + 1], in1=T[:, o:o + n],
                                op0=ALU.mult, op1=ALU.add), r=[G, cw, T], w=[T])
                            P.op('dve', lambda G=G, T=T, go=go, o=o, n=n, fc=fc: nc.vector.scalar_tensor_tensor(
                                out=T[:, o:o + n], in0=G[:, go + 2:go + 2 + n], scalar=cw[:, 2, fc:fc + 1], in1=T[:, o:o + n],
                                op0=ALU.mult, op1=ALU.add), r=[G, cw, T], w=[T])
                            P.op('act', lambda T=T, o=o, n=n: nc.scalar.activation(out=T[:, o:o + n], in_=T[:, o:o + n],
                                                                                func=AF.Silu), r=[T], w=[T])
                            P.op('dve', lambda T=T, V=V, o=o, n=n, fc=fc: nc.vector.tensor_tensor(
                                out=aT[:, fc, o:o + n], in0=T[:, o:o + n], in1=V[:, o:o + n], op=ALU.mult),
                                r=[T, V], w=[("aT", fc)])
                col = 0
                for t in seg["tiles"]:
                    r = 0 if t < TL else 1
                    pss = [k.psf_rot.next(), k.psf_rot.next()]
                    for cg in range(2):
                        for fc in range(NFC):
                            P.op('pe', lambda fc=fc, cg=cg, col=col: nc.tensor.matmul(
                                pss[cg][:], lhsT=aT[:, fc, col:col + 128], rhs=Wd[:, fc, cg * 512:(cg + 1) * 512],
                                start=(fc == 0), stop=(fc == NFC - 1)),
                                r=[("aT", fc), ("Wd", fc // 11)], w=[pss[cg]], sig=(fc == NFC - 1))
                    h, o_ = hb.next(), ob.next()
                    s0, s1, ms, rstd = st.next()
                    P.dma('sp', h[:], k.h_d[t * 128:(t + 1) * 128, :], key=h, r=[("h", t)], w=[h])
                    for cg, s in ((0, s0), (1, s1)):
                        P.op('act', lambda cg=cg, s=s: nc.scalar.activation(out=k.junk[:, 0:512], in_=pss[cg][:], func=AF.Square,
                                                                        accum_out=s[:]), r=[pss[cg]], w=[s])
                    P.op('dve', lambda s0=s0, s1=s1: nc.vector.tensor_tensor(out=s0[:], in0=s0[:], in1=s1[:], op=ALU.add),
                         r=[s0, s1], w=[s0])
                    rstd_from_ss(k, s0, ms, rstd, D)
                    for cg in range(2):
                        P.op('dve', lambda cg=cg, o_=o_, rstd=rstd, r=r: nc.vector.scalar_tensor_tensor(
                            out=o_[:, cg * 512:(cg + 1) * 512], in0=pss[cg][:], scalar=rstd[:, 0:1],
                            in1=gg[r][:, cg * 512:(cg + 1) * 512], op0=ALU.mult, op1=ALU.mult),
                            r=[pss[cg], rstd, gg[r]], w=[o_])
                    P.op('dve', lambda o_=o_, h=h: nc.vector.tensor_tensor(out=o_[:], in0=o_[:], in1=h[:], op=ALU.add),
                         r=[o_, h], w=[o_])
                    P.dma('sp', k.h_d[t * 128:(t + 1) * 128, :], o_[:], key=o_, r=[o_], w=[("h", t)])
                    col += 128
            P.barrier()


def resid_epilogue(k, pss, t, ggr, h, o_, stt4):
    nc, P = k.nc, k.P
    s0, s1, ms, rstd = stt4
    P.dma('sp', h[:], k.h_d[t * 128:(t + 1) * 128, :], key=h, r=[("h", t)], w=[h])
    for cg, s_ in ((0, s0), (1, s1)):
        P.op('act', lambda cg=cg, s_=s_: nc.scalar.activation(out=k.junk[:, 0:512], in_=pss[cg][:], func=AF.Square,
                                                            accum_out=s_[:]), r=[pss[cg]], w=[s_])
    P.op('dve', lambda: nc.vector.tensor_tensor(out=s0[:], in0=s0[:], in1=s1[:], op=ALU.add), r=[s0, s1], w=[s0])
    rstd_from_ss(k, s0, ms, rstd, D)
    for cg in range(2):
        P.op('dve', lambda cg=cg: nc.vector.scalar_tensor_tensor(
            out=o_[:, cg * 512:(cg + 1) * 512], in0=pss[cg][:], scalar=rstd[:, 0:1],
            in1=ggr[:, cg * 512:(cg + 1) * 512], op0=ALU.mult, op1=ALU.mult), r=[pss[cg], rstd, ggr], w=[o_])
    P.op('dve', lambda: nc.vector.tensor_tensor(out=o_[:], in0=o_[:], in1=h[:], op=ALU.add), r=[o_, h], w=[o_])
    P.dma('pool', k.h_d[t * 128:(t + 1) * 128, :], o_[:], key=o_, r=[o_], w=[("h", t)])


def gate_tiles(k, ph, l, gi, gatei):
    nc, P = k.nc, k.P
    gg = [sbt(ph, nc, "gg%d" % r, [128, D], F32) for r in range(2)]
    g3 = sbt(ph, nc, "ggn", [128, D], F32)
    bcast_row(k, 'sp', g3[:], k.norm_g.ap()[l, gi:gi + 1, :], key=g3, w=[g3])
    for r in range(2):
        bcast_row(k, 'sp', gg[r][:], k.mod_d.ap()[l, r:r + 1, gatei * D:(gatei + 1) * D], key=gg[r], r=[("mod", l)],
                  w=[gg[r]])
        P.op('dve', lambda r=r: nc.vector.tensor_tensor(out=gg[r][:], in0=gg[r][:], in1=g3[:], op=ALU.mult),
             r=[gg[r], g3], w=[gg[r]])
    return gg


def proj_tm(k, ph, uT, w_ap, ngroups, tiles, epilogue):
    nc, P = k.nc, k.P
    Wb = Rot([sbt(ph, nc, "pj_w%d" % i, [128, 8, 512], BF16) for i in range(2)])

    def load(g):
        w = Wb.next()
        P.dma('pool', w[:], w_ap[:, g * 512:(g + 1) * 512].rearrange("(k p) n -> p k n", p=128), key=w, w=[w])
        return w

    nxt = load(0)
    for g in range(ngroups):
        w = nxt
        if g + 1 < ngroups:
            nxt = load(g + 1)
        for t in tiles:
            ps = k.psf_rot.next()
            for kk in range(8):
                P.op('pe', lambda kk=kk: nc.tensor.matmul(ps[:], lhsT=uT[:, kk, t * 128:(t + 1) * 128], rhs=w[:, kk, :],
                                                        start=(kk == 0), stop=(kk == 7)),
                     r=[w, ("uT", t)], w=[ps], sig=(kk == 7))
            epilogue(t, g, ps)


def rope_apply(k, ps, outb, cos, sin, nh, nf, scale, tmp):
    nc, P = k.nc, k.P
    n = nh * 4 * nf
    pv = ps[:, 0:n].rearrange("p (h r x f) -> p h r x f", h=nh, r=2, x=2)
    ov = outb[:, 0:n].rearrange("p (h r x f) -> p h r x f", h=nh, r=2, x=2)
    x1, x2 = pv[:, :, :, 0, :], pv[:, :, :, 1, :]
    cb = cos.unsqueeze(1).to_broadcast([128, nh, 2, nf])
    sb_ = sin.unsqueeze(1).to_broadcast([128, nh, 2, nf])
    tv = [t_[:, 0:nh * 2 * nf].rearrange("p (h r f) -> p h r f", h=nh, r=2) for t_ in tmp]
    for (dst, xin, tab) in ((0, x1, cb), (1, x2, sb_), (2, x1, sb_), (3, x2, cb)):
        P.op('dve', lambda dst=dst, xin=xin, tab=tab: nc.vector.scalar_tensor_tensor(
            out=tv[dst], in0=xin, scalar=float(scale), in1=tab, op0=ALU.mult, op1=ALU.mult),
            r=[ps, "rope_tab"], w=[tmp[dst]])
    P.op('dve', lambda: nc.vector.tensor_tensor(out=ov[:, :, :, 0, :], in0=tv[0], in1=tv[1], op=ALU.subtract),
         r=[tmp[0], tmp[1]], w=[outb])
    P.op('dve', lambda: nc.vector.tensor_tensor(out=ov[:, :, :, 1, :], in0=tv[2], in1=tv[3], op=ALU.add),
         r=[tmp[2], tmp[3]], w=[outb])


def rope_tables(k):
    nc, P = k.nc, k.P
    PI = float(np.pi)
    k.ropeR_d = nc.dram_tensor("ropeR_d", [2, 128, TL * 2 * 64], F32)
    k.ropeD_d = nc.dram_tensor("ropeD_d", [2, 128, TL * 2 * 16], F32)
    with ExitStack() as ph:
        pos = sbt(ph, nc, "pos", [128, TL, 2], F32)
        P.dma('sp', pos[:], k.pos[:, :, :], key=pos, w=[pos])
        for (nf, d2, dst) in ((64, 128, k.ropeR_d), (16, 32, k.ropeD_d)):
            n = TL * 2 * nf
            inv = sbt(ph, nc, "inv", [128, nf], F32)
            ang = sbt(ph, nc, "ang", [128, n], F32)
            xx = sbt(ph, nc, "xx", [128, n], F32)
            ki = sbt(ph, nc, "ki", [128, n], I32)
            kf = sbt(ph, nc, "kf", [128, n], F32)
            mm = sbt(ph, nc, "mm", [128, n], F32)
            res = sbt(ph, nc, "res", [128, n], F32)
            P.op('act', lambda: nc.scalar.activation(out=inv[:], in_=k.iotaf[:, 0:nf], func=AF.Exp,
                                                     scale=-float(np.log(10000.0)) * 2.0 / d2), r=[k.iotaf], w=[inv])
            for t in range(TL):
                for rc in range(2):
                    o = (t * 2 + rc) * nf
                    P.op('dve', lambda t=t, rc=rc, o=o: nc.vector.tensor_scalar(
                        out=ang[:, o:o + nf], in0=inv[:], scalar1=pos[:, t, rc:rc + 1], scalar2=None, op0=ALU.mult),
                        r=[inv, pos], w=[ang])
            for ci, off in ((0, PI / 2), (1, 0.0)):
                P.op('dve', lambda: nc.vector.tensor_scalar(out=xx[:], in0=ang[:], scalar1=float(off), scalar2=None,
                                                            op0=ALU.add), r=[ang], w=[xx])
                P.op('dve', lambda: nc.vector.tensor_scalar(out=ki[:], in0=xx[:], scalar1=float(1.0 / (2 * PI)),
                                                            scalar2=None, op0=ALU.mult), r=[xx], w=[ki])
                P.op('dve', lambda: nc.vector.tensor_copy(out=kf[:], in_=ki[:]), r=[ki], w=[kf])
                P.op('dve', lambda: nc.vector.scalar_tensor_tensor(out=xx[:], in0=kf[:], scalar=-2 * PI, in1=xx[:],
                                                                   op0=ALU.mult, op1=ALU.add), r=[kf, xx], w=[xx])
                P.op('dve', lambda: nc.vector.tensor_scalar(out=mm[:], in0=xx[:], scalar1=PI, scalar2=None,
                                                            op0=ALU.is_gt), r=[xx], w=[mm])
                P.op('dve', lambda: nc.vector.scalar_tensor_tensor(out=xx[:], in0=mm[:], scalar=-2 * PI, in1=xx[:],
                                                                   op0=ALU.mult, op1=ALU.add), r=[mm, xx], w=[xx])
                P.op('dve', lambda: nc.vector.tensor_scalar(out=mm[:], in0=xx[:], scalar1=-PI, scalar2=None,
                                                            op0=ALU.is_lt), r=[xx], w=[mm])
                P.op('dve', lambda: nc.vector.scalar_tensor_tensor(out=xx[:], in0=mm[:], scalar=2 * PI, in1=xx[:],
                                                                   op0=ALU.mult, op1=ALU.add), r=[mm, xx], w=[xx])
                P.op('act', lambda: nc.scalar.activation(out=res[:], in_=xx[:], func=AF.Sin), r=[xx], w=[res])
                P.dma('sp', dst.ap()[ci], res[:], key=res, r=[res], w=[("ropetab", nf, ci)])
        P.barrier()


def ret_layer(k, l):
    nc, P = k.nc, k.P
    i = l // 2
    H, DK, DV = 4, 256, 512
    if not hasattr(k, "qT_d"):
        k.qT_d = nc.dram_tensor("qT_d", [8, 128, NT], BF16)
        k.kT_d = nc.dram_tensor("kT_d", [8, 128, NT], BF16)
        k.k_d = nc.dram_tensor("k_d", [NT, 1024], BF16)
        k.v_d = nc.dram_tensor("v_d", [NT, 2048], BF16)
        k.g_d = [nc.dram_tensor("g%d_d" % p, [NT, 2048], BF16) for p in range(2)]
        k.y_d = nc.dram_tensor("y_d", [NT, 2048], F32)
        k.F_d = nc.dram_tensor("F_d", [1024, 512], F32)
        k.Fg_d = nc.dram_tensor("Fg_d", [2048, 512], F32)
    tiles_all = list(range(TT))
    with ExitStack() as ph:
        uT = sbt(ph, nc, "r_uT", [128, 8, NT], BF16)
        with ExitStack() as ph1:
            prenorm(k, ph1, l, 0, 0, 1, uT, tiles_all)
            P.barrier()
        cos = sbt(ph, nc, "r_cos", [128, TL, 2, 64], F32)
        sin = sbt(ph, nc, "r_sin", [128, TL, 2, 64], F32)
        P.dma('sp', cos[:].rearrange("p t r f -> p (t r f)"), k.ropeR_d.ap()[0], key=cos, w=["rope_tab"])
        P.dma('sp', sin[:].rearrange("p t r f -> p (t r f)"), k.ropeR_d.ap()[1], key=sin, w=["rope_tab"])
        tmp = [sbt(ph, nc, "r_tmp%d" % j, [128, 256], F32) for j in range(4)]
        ob = Rot([sbt(ph, nc, "r_ob%d" % j, [128, 512], BF16) for j in range(3)])
        tb = Rot([sbt(ph, nc, "r_tb%d" % j, [128, 4, 128], BF16) for j in range(3)])

        def epi(t, g, ps):
            o = ob.next()
            if g < 4:
                scale = 1.0 if g < 2 else DK ** -0.5
                if t < TL:
                    rope_apply(k, ps, o, cos[:, t], sin[:, t], 2, 64, scale, tmp)
                else:
                    P.op('act', lambda: nc.scalar.activation(out=o[:], in_=ps[:], func=AF.Copy, scale=float(scale)),
                         r=[ps], w=[o])
                if g >= 2:
                    P.dma('sp', k.k_d[t * 128:(t + 1) * 128, (g - 2) * 512:(g - 1) * 512], o[:], key=o, r=[o],
                          w=[("k_d", t)])
                pb = k.psb_rot.next()
                for j in range(4):
                    P.op('pe', lambda j=j: nc.tensor.transpose(pb[:, j * 128:(j + 1) * 128], o[:, j * 128:(j + 1) * 128],
                                                              k.ident[:]), r=[o, k.ident], w=[pb], sig=(j == 3))
                tt = tb.next()
                P.op('act', lambda: nc.scalar.copy(out=tt[:], in_=pb[:, 0:512].rearrange("p (j n) -> p j n", j=4)),
                     r=[pb], w=[tt])
                dst = k.qT_d if g < 2 else k.kT_d
                gg_ = g % 2
                P.dma('sp', dst.ap()[gg_ * 4:(gg_ + 1) * 4, :, t * 128:(t + 1) * 128].rearrange("j p n -> p j n"), tt[:],
                      key=tt, r=[tt], w=[("qkT_d", g, t)])
            elif g < 8:
                P.op('act', lambda: nc.scalar.copy(out=o[:], in_=ps[:]), r=[ps], w=[o])
                P.dma('sp', k.v_d[t * 128:(t + 1) * 128, (g - 4) * 512:(g - 3) * 512], o[:], key=o, r=[o], w=[("v_d", t)])
            else:
                p_ = (g - 8) // 4
                gc = (g - 8) % 4
                P.op('act', lambda: nc.scalar.activation(out=o[:], in_=ps[:], func=AF.Silu), r=[ps], w=[o])
                P.dma('sp', k.g_d[p_][t * 128:(t + 1) * 128, gc * 512:(gc + 1) * 512], o[:], key=o, r=[o],
                      w=[("g_d", p_, t)])

        proj_tm(k, ph, uT, k.ret_w_in.ap()[i], 16, tiles_all, epi)
        P.barrier()
    with ExitStack() as ph:
        def t_(name, shape, dt):
            return sbt(ph, nc, name, shape, dt)

        lg = t_("lg", [128, 8], F32)
        dm = t_("dm", [128, 128], F32)
        msk = [t_("msk%d" % j, [128, 128], F32) for j in range(8)]
        xi = [t_("xi%d" % j, [128, 128], BF16) for j in range(8)]
        zeta = t_("zeta", [128, 8], F32)
        cd = t_("cd", [128, 8], F32)
        c128 = t_("c128", [128, 1], F32)
        tA = t_("tA", [128, 128], F32)
        tB = t_("tB", [128, 128], F32)
        P.dma('sp', lg[:], k.ret_decay.ap()[i:i + 1].rearrange("o p h -> o (p h)").to_broadcast([128, 8]), key=lg, w=[lg])
        P.op('act', lambda: nc.scalar.activation(out=lg[:], in_=lg[:], func=AF.Exp, scale=-1.0), r=[lg], w=[lg])
        P.op('dve', lambda: nc.vector.tensor_scalar(out=lg[:], in0=lg[:], scalar1=1.0, scalar2=None, op0=ALU.add),
             r=[lg], w=[lg])
        P.op('act', lambda: nc.scalar.activation(out=lg[:], in_=lg[:], func=AF.Ln), r=[lg], w=[lg])
        P.op('dve', lambda: nc.vector.tensor_scalar(out=lg[:], in0=lg[:], scalar1=-1.0, scalar2=None, op0=ALU.mult),
             r=[lg], w=[lg])
        P.op('dve', lambda: nc.vector.memset(c128[:], 128.0), w=[c128])
        P.op('dve', lambda: nc.vector.tensor_scalar(out=dm[:], in0=k.iotaf[:], scalar1=k.iotap[:, 0:1], scalar2=None,
                                                    op0=ALU.subtract), r=[k.iotaf, k.iotap], w=[dm])
        for p_ in range(2):
            sgn = 1.0 if p_ == 0 else -1.0
            for h in range(H):
                j = p_ * 4 + h
                lgc = lg[:, j:j + 1]
                P.op('dve', lambda: nc.vector.tensor_scalar(out=tA[:], in0=dm[:], scalar1=sgn, scalar2=0.0, op0=ALU.mult,
                                                            op1=ALU.max), r=[dm], w=[tA])
                P.op('act', lambda lgc=lgc: nc.scalar.activation(out=tA[:], in_=tA[:], func=AF.Exp, scale=lgc),
                     r=[tA, lg], w=[tA])
                P.op('dve', lambda: nc.vector.tensor_scalar(out=tB[:], in0=dm[:], scalar1=sgn, scalar2=0.0, op0=ALU.mult,
                                                            op1=ALU.is_ge), r=[dm], w=[tB])
                P.op('dve', lambda j=j: nc.vector.tensor_tensor(out=msk[j][:], in0=tA[:], in1=tB[:], op=ALU.mult),
                     r=[tA, tB], w=[msk[j]])
                if p_ == 0:
                    P.op('dve', lambda: nc.vector.tensor_scalar(out=tA[:], in0=k.iotaf[:], scalar1=1.0, scalar2=None,
                                                                op0=ALU.add), r=[k.iotaf], w=[tA])
                else:
                    P.op('dve', lambda: nc.vector.tensor_scalar(out=tA[:], in0=k.iotaf[:], scalar1=-1.0, scalar2=128.0,
                                                                op0=ALU.mult, op1=ALU.add), r=[k.iotaf], w=[tA])
                P.op('act', lambda j=j, lgc=lgc: nc.scalar.activation(out=xi[j][:], in_=tA[:], func=AF.Exp, scale=lgc),
                     r=[tA, lg], w=[xi[j]])
                if p_ == 0:
                    P.op('dve', lambda: nc.vector.tensor_scalar(out=tB[:, 0:1], in0=k.iotap[:], scalar1=-1.0, scalar2=127.0,
                                                                op0=ALU.mult, op1=ALU.add), r=[k.iotap], w=[tB])
                else:
                    P.op('dve', lambda: nc.vector.tensor_copy(out=tB[:, 0:1], in_=k.iotap[:]), r=[k.iotap], w=[tB])
                P.op('act', lambda j=j, lgc=lgc: nc.scalar.activation(out=zeta[:, j:j + 1], in_=tB[:, 0:1], func=AF.Exp,
                                                                     scale=lgc), r=[tB, lg], w=[zeta])
                P.op('act', lambda j=j, lgc=lgc: nc.scalar.activation(out=cd[:, j:j + 1], in_=c128[:], func=AF.Exp,
                                                                     scale=lgc), r=[c128, lg], w=[cd])
        Wo = t_("r_Wo", [128, 16, D], BF16)
        for half in range(2):
            P.dma('pool', Wo[:, half * 8:(half + 1) * 8, :],
                  k.ret_w_out.ap()[i][half * 1024:(half + 1) * 1024, :].rearrange("(f p) n -> p f n", p=128),
                  key=("Wo", half), w=[("Wo", half)])
        gg = gate_tiles(k, ph, l, 1, 2)
        R = [t_("R%d" % h, [128, 2, DV], F32) for h in range(H)]
        Rb = [t_("Rb%d" % h, [128, 2, DV], BF16) for h in range(H)]
        Fg = t_("Fg", [128, 2, DV], F32)
        qTb = Rot([t_("qTc%d" % j, [128, 8, 128], BF16) for j in range(2)])
        kTb = Rot([t_("kTc%d" % j, [128, 8, 128], BF16) for j in range(2)])
        kb = Rot([t_("kc%d" % j, [128, 1024], BF16) for j in range(2)])
        vb = Rot([t_("vc%d" % j, [128, 2048], BF16) for j in range(2)])
        gb = Rot([t_("gc%d" % j, [128, 2048], BF16) for j in range(2)])
        yb = Rot([t_("y%d" % j, [128, 2048], F32) for j in range(2)])
        y1b = Rot([t_("y1_%d" % j, [128, 2048], F32) for j in range(2)])
        ybf = Rot([t_("ybf%d" % j, [128, 2048], BF16) for j in range(2)])
        yTb = Rot([t_("yT%d" % j, [128, 16, 128], BF16) for j in range(2)])
        sTb = Rot([t_("sT%d" % j, [128, 128], BF16) for j in range(3)])
        qxb = Rot([t_("qx%d" % j, [128, 2, 128], BF16) for j in range(3)])
        kzb = Rot([t_("kz%d" % j, [128, 256], BF16) for j in range(3)])
        hb = Rot([t_("r_h%d" % j, [128, D], F32) for j in range(2)])
        ob2 = Rot([t_("r_o%d" % j, [128, D], F32) for j in range(2)])
        st4 = Rot([[t_("r_s%d_%d" % (a, b), [128, 1], F32) for b in range(4)] for a in range(2)])
        st3 = Rot([[t_("r_n%d_%d" % (a, b), [128, 1], F32) for b in range(3)] for a in range(4)])

        def zero_state():
            for h in range(H):
                P.op('pool', lambda h=h: nc.gpsimd.memset(R[h][:], 0.0), w=[R[h]])
                P.op('pool', lambda h=h: nc.gpsimd.memset(Rb[h][:], 0.0), w=[Rb[h]])

        def load_chunk(p_, t):
            qTc, kTc, kc, vc, gc = qTb.next(), kTb.next(), kb.next(), vb.next(), gb.next()
            cs = slice(t * 128, (t + 1) * 128)
            P.dma('sp', qTc[:], k.qT_d.ap()[:, :, cs].rearrange("j p n -> p j n"), key=qTc,
                  r=[("qkT_d", g, t) for g in (0, 1)], w=[qTc])
            P.dma('sp', kTc[:], k.kT_d.ap()[:, :, cs].rearrange("j p n -> p j n"), key=kTc,
                  r=[("qkT_d", g, t) for g in (2, 3)], w=[kTc])
            P.dma('sp', kc[:], k.k_d[cs, :], key=kc, r=[("k_d", t)], w=[kc])
            P.dma('sp', vc[:], k.v_d[cs, :], key=vc, r=[("v_d", t)], w=[vc])
            P.dma('sp', gc[:], k.g_d[p_][cs, :], key=gc, r=[("g_d", p_, t)], w=[gc])
            y1 = None
            if p_ == 1:
                y1 = y1b.next()
                P.dma('sp', y1[:], k.y_d[cs, :], key=y1, r=[("y_d", t)], w=[y1])
            return qTc, kTc, kc, vc, gc, y1

        def chunk(p_, t, bufs):
            qTc, kTc, kc, vc, gc, y1 = bufs
            cs = slice(t * 128, (t + 1) * 128)
            y = yb.next()
            if p_ == 1:
                yf = ybf.next()
            for h in range(H):
                j = p_ * 4 + h
                sT, qx, kz = sTb.next(), qxb.next(), kzb.next()
                ss, ms, rstd = st3.next()
                ps_s = k.psf_rot.next()
                for c2 in range(2):
                    P.op('pe', lambda c2=c2: nc.tensor.matmul(ps_s[:, 0:128], lhsT=kTc[:, h * 2 + c2, :],
                                                            rhs=qTc[:, h * 2 + c2, :], start=(c2 == 0), stop=(c2 == 1)),
                         r=[kTc, qTc], w=[ps_s], sig=(c2 == 1))
                P.op('dve', lambda: nc.vector.tensor_tensor(out=sT[:], in0=ps_s[:, 0:128], in1=msk[j][:], op=ALU.mult),
                     r=[ps_s, msk[j]], w=[sT])
                P.op('pool', lambda: nc.gpsimd.tensor_tensor(
                    out=qx[:], in0=qTc[:, h * 2:h * 2 + 2, :], in1=xi[j][:].unsqueeze(1).to_broadcast([128, 2, 128]),
                    op=ALU.mult), r=[qTc, xi[j]], w=[qx])
                ps_o = k.psf_rot.next()
                P.op('pe', lambda: nc.tensor.matmul(ps_o[:], lhsT=sT[:], rhs=vc[:, h * DV:(h + 1) * DV], start=True,
                                                    stop=False), r=[sT, vc], w=[ps_o], sig=False)
                for c2 in range(2):
                    P.op('pe', lambda c2=c2: nc.tensor.matmul(ps_o[:], lhsT=qx[:, c2, :], rhs=Rb[h][:, c2, :], start=False,
                                                            stop=(c2 == 1)), r=[qx, Rb[h]], w=[ps_o], sig=(c2 == 1))
                P.op('act', lambda: nc.scalar.activation(out=k.junk[:, 0:512], in_=ps_o[:], func=AF.Square, accum_out=ss[:]),
                     r=[ps_o], w=[ss])
                rstd_from_ss(k, ss, ms, rstd, DV)
                hs = slice(h * DV, (h + 1) * DV)
                P.op('dve', lambda: nc.vector.scalar_tensor_tensor(out=y[:, hs], in0=ps_o[:], scalar=rstd[:, 0:1],
                                                                   in1=gc[:, hs], op0=ALU.mult, op1=ALU.mult),
                     r=[ps_o, rstd, gc], w=[y])
                if p_ == 1:
                    P.op('pool', lambda: nc.gpsimd.tensor_tensor(out=yf[:, hs], in0=y[:, hs], in1=y1[:, hs], op=ALU.add),
                         r=[y, y1], w=[yf])
                P.op('act', lambda: nc.scalar.activation(out=kz[:], in_=kc[:, h * DK:(h + 1) * DK], func=AF.Copy,
                                                         scale=zeta[:, j:j + 1]), r=[kc, zeta], w=[kz])
                for c2 in range(2):
                    ps_k = k.psf_rot.next()
                    P.op('pe', lambda c2=c2, ps_k=ps_k: nc.tensor.matmul(ps_k[:], lhsT=kz[:, c2 * 128:(c2 + 1) * 128],
                                                                       rhs=vc[:, hs], start=True, stop=True),
                         r=[kz, vc], w=[ps_k])
                    P.op('dve', lambda c2=c2, ps_k=ps_k: nc.vector.scalar_tensor_tensor(
                        out=R[h][:, c2, :], in0=R[h][:, c2, :], scalar=cd[:, j:j + 1], in1=ps_k[:], op0=ALU.mult,
                        op1=ALU.add), r=[R[h], cd, ps_k], w=[R[h]])
                P.op('act', lambda: nc.scalar.copy(out=Rb[h][:], in_=R[h][:]), r=[R[h]], w=[Rb[h]])
            if p_ == 0:
                P.dma('pool', k.y_d[cs, :], y[:], key=y, r=[y], w=[("y_d", t)])
            else:
                yT = yTb.next()
                for half in range(2):
                    pb = k.psb_rot.next()
                    for jj in range(8):
                        kk = half * 8 + jj
                        P.op('pe', lambda jj=jj, kk=kk, pb=pb: nc.tensor.transpose(pb[:, jj * 128:(jj + 1) * 128],
                                                                                  yf[:, kk * 128:(kk + 1) * 128], k.ident[:]),
                             r=[yf, k.ident], w=[pb], sig=(jj == 7))
                    P.op('act', lambda half=half, pb=pb: nc.scalar.copy(out=yT[:, half * 8:(half + 1) * 8, :],
                                                                     in_=pb[:].rearrange("p (j n) -> p j n", j=8)),
                         r=[pb], w=[yT])
                pss = [k.psf_rot.next(), k.psf_rot.next()]
                for cg in range(2):
                    for kk in range(16):
                        P.op('pe', lambda cg=cg, kk=kk: nc.tensor.matmul(pss[cg][:], lhsT=yT[:, kk, :],
                                                                       rhs=Wo[:, kk, cg * 512:(cg + 1) * 512],
                                                                       start=(kk == 0), stop=(kk == 15)),
                             r=[yT, ("Wo", kk // 8)], w=[pss[cg]], sig=(kk == 15))
                resid_epilogue(k, pss, t, gg[0 if t < TL else 1], hb.next(), ob2.next(), st4.next())

        def run_chunks(p_, ts):
            nxt = load_chunk(p_, ts[0])
            for n_, t in enumerate(ts):
                cur = nxt
                if n_ + 1 < len(ts):
                    nxt = load_chunk(p_, ts[n_ + 1])
                chunk(p_, t, cur)

        zero_state()
        run_chunks(0, [TL, TL + 1] + list(range(TL)))
        for h in range(H):
            P.dma('sp', k.F_d.ap()[h * 256:(h + 1) * 256, :].rearrange("(c p) n -> p c n", p=128), R[h][:], key=R[h],
                  r=[R[h]], w=["F_d"])
        P.op('pool', lambda: nc.gpsimd.collective_compute("AllGather", ALU.bypass, replica_groups=GROUPS,
                                                          ins=[k.F_d.ap().opt()], outs=[k.Fg_d.ap().opt()]),
             r=["F_d"], w=["Fg_d"], dma="cc", inc=1)
        zero_state()
        run_chunks(1, [TL + 1, TL])
        for h in range(H):
            for rk_ in range(2):
                P.dma('sp', Fg[:], k.Fg_d.ap()[rk_ * 1024 + h * 256:rk_ * 1024 + (h + 1) * 256, :].rearrange(
                    "(c p) n -> p c n", p=128), key=Fg, r=["Fg_d"], w=[Fg])
                if rk_ == 0:
                    P.op('dve', lambda h=h: nc.vector.tensor_scalar(out=R[h][:], in0=Fg[:], scalar1=k.selt[:, 0:1],
                                                                  scalar2=None, op0=ALU.mult), r=[Fg, k.selt], w=[R[h]])
                else:
                    P.op('dve', lambda h=h: nc.vector.scalar_tensor_tensor(out=R[h][:], in0=Fg[:], scalar=k.selt[:, 1:2],
                                                                         in1=R[h][:], op0=ALU.mult, op1=ALU.add),
                         r=[Fg, k.selt, R[h]], w=[R[h]])
            P.op('act', lambda h=h: nc.scalar.copy(out=Rb[h][:], in_=R[h][:]), r=[R[h]], w=[Rb[h]])
        run_chunks(1, list(range(TL - 1, -1, -1)))
        P.barrier()


def diff_layer(k, l, last):
    import math
    nc, P = k.nc, k.P
    i = l // 2
    lam_init = 0.8 - 0.6 * math.exp(-0.3 * l)
    NH = 8
    if not hasattr(k, "qT_d"):
        k.qT_d = nc.dram_tensor("qT_d", [8, 128, NT], BF16)
    if not hasattr(k, "kTl_d"):
        k.kTl_d = [nc.dram_tensor("kTl%d_d" % g, [512, NL], BF16) for g in range(2)]
        k.kTc_d = nc.dram_tensor("kTc_d", [1024, NCX], BF16)
        k.kTg_d = [nc.dram_tensor("kTg%d_d" % g, [1024, NL], BF16) for g in range(2)]
        k.vl_d = [nc.dram_tensor("vl%d_d" % g, [NL, 512], BF16) for g in range(2)]
        k.vc_d = nc.dram_tensor("vc_d", [NCX, 1024], BF16)
        k.vg_d = [nc.dram_tensor("vg%d_d" % g, [2 * NL, 512], BF16) for g in range(2)]
    tiles_all = list(range(TT))
    q_tiles = list(range(TL)) + ([] if last else [TL, TL + 1])
    with ExitStack() as ph:
        uT = sbt(ph, nc, "d_uT", [128, 8, NT], BF16)
        with ExitStack() as ph1:
            prenorm(k, ph1, l, 0, 0, 1, uT, tiles_all)
            P.barrier()
        cos = sbt(ph, nc, "d_cos", [128, TL, 2, 16], F32)
        sin = sbt(ph, nc, "d_sin", [128, TL, 2, 16], F32)
        P.dma('sp', cos[:].rearrange("p t r f -> p (t r f)"), k.ropeD_d.ap()[0], key=cos, w=["rope_tab"])
        P.dma('sp', sin[:].rearrange("p t r f -> p (t r f)"), k.ropeD_d.ap()[1], key=sin, w=["rope_tab"])
        tmp = [sbt(ph, nc, "d_tmp%d" % j, [128, 256], F32) for j in range(4)]
        ob = Rot([sbt(ph, nc, "d_ob%d" % j, [128, 512], BF16) for j in range(3)])
        tb = Rot([sbt(ph, nc, "d_tb%d" % j, [128, 4, 128], BF16) for j in range(3)])

        def epi_qk(isq):
            def epi(t, g, ps):
                if isq and t not in q_tiles:
                    return
                o = ob.next()
                scale = 0.125 if isq else 1.0
                if t < TL:
                    rope_apply(k, ps, o, cos[:, t], sin[:, t], 8, 16, scale, tmp)
                else:
                    P.op('act', lambda: nc.scalar.activation(out=o[:], in_=ps[:], func=AF.Copy, scale=float(scale)),
                         r=[ps], w=[o])
                pb = k.psb_rot.next()
                for j in range(4):
                    P.op('pe', lambda j=j: nc.tensor.transpose(pb[:, j * 128:(j + 1) * 128], o[:, j * 128:(j + 1) * 128],
                                                              k.ident[:]), r=[o, k.ident], w=[pb], sig=(j == 3))
                tt = tb.next()
                P.op('act', lambda: nc.scalar.copy(out=tt[:], in_=pb[:, 0:512].rearrange("p (j n) -> p j n", j=4)),
                     r=[pb], w=[tt])
                if isq:
                    P.dma('sp', k.qT_d.ap()[g * 4:(g + 1) * 4, :, t * 128:(t + 1) * 128].rearrange("j p n -> p j n"), tt[:],
                          key=tt, r=[tt], w=[("qT_d", g, t)])
                elif t < TL:
                    P.dma('sp', k.kTl_d[g].ap()[:, t * 128:(t + 1) * 128].rearrange("(j p) n -> p j n", p=128),
                          tt[:], key=tt, r=[tt], w=[("kTl_d", g, t)])
                    if t == TL - 1:
                        P.op('pool', lambda: nc.gpsimd.collective_compute(
                            "AllGather", ALU.bypass, replica_groups=GROUPS, ins=[k.kTl_d[g].ap().opt()],
                            outs=[k.kTg_d[g].ap().opt()]), r=[("kTl_d", g, t_) for t_ in range(TL)], w=[("kTg_d", g)],
                            dma="cc", inc=1)
                else:
                    P.dma('sp', k.kTc_d.ap()[g * 512:(g + 1) * 512, (t - TL) * 128:(t - TL + 1) * 128].rearrange(
                        "(j p) n -> p j n", p=128), tt[:], key=tt, r=[tt], w=[("kTc_d", g, t)])
            return epi

        def epi_v(t, g, ps):
            o = ob.next()
            P.op('act', lambda: nc.scalar.copy(out=o[:], in_=ps[:]), r=[ps], w=[o])
            if t < TL:
                P.dma('sp', k.vl_d[g][t * 128:(t + 1) * 128, :], o[:], key=o, r=[o], w=[("vl_d", g, t)])
                if t == TL - 1:
                    P.op('pool', lambda: nc.gpsimd.collective_compute(
                        "AllGather", ALU.bypass, replica_groups=GROUPS, ins=[k.vl_d[g].ap().opt()],
                        outs=[k.vg_d[g].ap().opt()]), r=[("vl_d", g, t_) for t_ in range(TL)], w=[("vg_d", g)],
                        dma="cc", inc=1)
            else:
                P.dma('sp', k.vc_d[(t - TL) * 128:(t - TL + 1) * 128, g * 512:(g + 1) * 512], o[:], key=o, r=[o],
                      w=[("vc_d", g, t)])

        wi = k.diff_w_in.ap()[i]
        proj_tm(k, ph, uT, wi[:, 1024:2048], 2, tiles_all, epi_qk(False))
        proj_tm(k, ph, uT, wi[:, 2048:3072], 2, tiles_all, epi_v)
        proj_tm(k, ph, uT, wi[:, 0:1024], 2, tiles_all, epi_qk(True))
        P.barrier()
    with ExitStack() as ph:
        def t_(name, shape, dt):
            return sbt(ph, nc, name, shape, dt)

        NK = 2 * NL + NCX
        NKT = NK // 128
        yT = t_("d_yT", [128, NH, NT], BF16)
        Wo = t_("d_Wo", [128, 8, D], BF16)
        P.dma('pool', Wo[:], k.diff_w_out.ap()[i].rearrange("(f p) n -> p f n", p=128), key=Wo, w=[Wo])
        gg = gate_tiles(k, ph, l, 1, 2)
        ones = t_("d_ones", [128, 128], BF16)
        P.op('dve', lambda: nc.vector.memset(ones[:], 1.0), w=[ones])
        lv = t_("d_lv", [128, 4, 64], F32)
        lp = t_("d_lp", [128, 2, 64], F32)
        ls = t_("d_ls", [128, 2], F32)
        nlam = t_("d_nlam", [128, 1], F32)
        gsub = t_("d_gsub", [128, 1], F32)
        P.dma('sp', lv[:].rearrange("p a b -> p (a b)"),
              k.diff_lam.ap()[i:i + 1].rearrange("o a b -> o (a b)").to_broadcast([128, 256]), key=lv, w=[lv])
        for a in range(2):
            P.op('dve', lambda a=a: nc.vector.tensor_tensor(out=lp[:, a, :], in0=lv[:, 2 * a, :], in1=lv[:, 2 * a + 1, :],
                                                          op=ALU.mult), r=[lv], w=[lp])
            P.op('dve', lambda a=a: nc.vector.reduce_sum(out=ls[:, a:a + 1], in_=lp[:, a, :], axis=AX.X), r=[lp], w=[ls])
        P.op('act', lambda: nc.scalar.activation(out=ls[:], in_=ls[:], func=AF.Exp), r=[ls], w=[ls])
        P.op('dve', lambda: nc.vector.scalar_tensor_tensor(out=nlam[:], in0=ls[:, 1:2], scalar=-float(lam_init),
                                                           in1=ls[:, 0:1], op0=ALU.add, op1=ALU.subtract),
             r=[ls], w=[nlam])
        P.dma('sp', gsub[:], k.diff_subln.ap()[i].rearrange("(p o) -> p o", o=1), key=gsub, w=[gsub],
              allow_slow_non_contiguous=True)
        P.op('dve', lambda: nc.vector.tensor_scalar(out=gsub[:], in0=gsub[:], scalar1=float(1.0 - lam_init), scalar2=None,
                                                    op0=ALU.mult), r=[gsub], w=[gsub])
        kTb = Rot([t_("d_kT%d" % j, [128, NK], BF16) for j in range(2)])
        Vb = Rot([t_("d_V%d" % j, [128, NKT, 128], BF16) for j in range(2)])
        qAb = Rot([t_("d_qA%d" % j, [128, NT], BF16) for j in range(2)])
        qBb = Rot([t_("d_qB%d" % j, [128, NT], BF16) for j in range(2)])
        for qa in qAb.items:
            P.op('pool', lambda qa=qa: nc.gpsimd.memset(qa[64:128, :], 0.0), w=[("qz", qa.name)])
        for qb_ in qBb.items:
            P.op('pool', lambda qb_=qb_: nc.gpsimd.memset(qb_[0:64, :], 0.0), w=[("qz", qb_.name)])
        eb = Rot([t_("d_e%d" % j, [128, 512], BF16) for j in range(6)])
        accD = Rot([t_("d_aD%d" % j, [128, 512], F32) for j in range(2)])
        accP = Rot([t_("d_aP%d" % j, [128, 512], F32) for j in range(2)])
        zhi = Rot([t_("d_zh%d" % j, [128, 512], BF16) for j in range(2)])
        zlo = Rot([t_("d_zl%d" % j, [128, 512], BF16) for j in range(2)])
        rzb = Rot([t_("d_rz%d" % j, [128, 512], F32) for j in range(2)])
        Onb = [Rot([t_("d_On%d_%d" % (w_, j), [128, 512], F32) for j in range(2)]) for w_ in range(2)]
        OTb = Rot([t_("d_OT%d" % j, [128, 512], F32) for j in range(2)])
        sqb = Rot([t_("d_sq%d" % j, [128, 512], BF16) for j in range(2)])
        rsb = Rot([t_("d_rs%d" % j, [128, 512], F32) for j in range(2)])
        lnb = Rot([t_("d_ln%d" % j, [128, 512], F32) for j in range(2)])
        hb = Rot([t_("d_h%d" % j, [128, D], F32) for j in range(2)])
        ob2 = Rot([t_("d_o%d" % j, [128, D], F32) for j in range(2)])
        st4 = Rot([[t_("d_s%d_%d" % (a, b), [128, 1], F32) for b in range(4)] for a in range(2)])
        ps_s = Rot(k.psf[0:2])
        ps_o = Rot(k.psf[2:4])
        ps_z = Rot(k.psf[4:6])

        def load_head(h):
            kT, V, qA, qB = kTb.next(), Vb.next(), qAb.next(), qBb.next()
            hg_, hh = h // 4, h % 4
            for rk_ in range(2):
                P.dma('sp', kT[:, rk_ * NL:(rk_ + 1) * NL],
                      k.kTg_d[hg_][rk_ * 512 + hh * 128:rk_ * 512 + (hh + 1) * 128, :],
                      key=("kT", kT.name, rk_), r=[("kTg_d", hg_)], w=[("kTp", kT.name, rk_)])
                P.dma('sp', V[:, rk_ * TL:(rk_ + 1) * TL, :],
                      k.vg_d[hg_].ap()[rk_ * NL:(rk_ + 1) * NL, hh * 128:(hh + 1) * 128].rearrange("(j p) d -> p j d", p=128),
                      key=("V", V.name, rk_), r=[("vg_d", hg_)], w=[("Vp", V.name, rk_)])
            P.dma('sp', kT[:, 2 * NL:NK], k.kTc_d[h * 128:(h + 1) * 128, :], key=("kT", kT.name, 2),
                  r=[("kTc_d", h // 4, t_) for t_ in (TL, TL + 1)], w=[("kTp", kT.name, 2)])
            P.dma('sp', V[:, 2 * TL:NKT, :], k.vc_d.ap()[:, h * 128:(h + 1) * 128].rearrange("(j p) d -> p j d", p=128),
                  key=("V", V.name, 2), r=[("vc_d", h // 4, t_) for t_ in (TL, TL + 1)], w=[("Vp", V.name, 2)])
            qr = [("qT_d", h // 4, t) for t in q_tiles]
            P.dma('sp', qA[0:64, :], k.qT_d.ap()[h, 0:64, :], key=qA, r=qr, w=[qA])
            P.dma('sp', qB[64:128, :], k.qT_d.ap()[h, 64:128, :], key=qB, r=qr, w=[qB])
            return kT, V, qA, qB

        blocks = [(qb * 512, 512, list(range(NKT))) for qb in range(4)]
        if not last:
            blocks.append((NL, NCX, [NKT - 2, NKT - 1]))
        items = []
        for h in range(NH):
            for (q0, nq, ktiles) in blocks:
                for w_ in range(2):
                    for jn, jt in enumerate(ktiles):
                        items.append(dict(h=h, q0=q0, nq=nq, w=w_, jn=jn, jt=jt, first=(jn == 0),
                                          last=(jn == len(ktiles) - 1)))
        heads = {}
        state = {}

        def emit_S(it):
            h = it["h"]
            if h not in heads:
                assert h == 0
                heads[h] = load_head(h)
            kT, V, qA, qB = heads[h]
            q = qA if it["w"] == 0 else qB
            nq, q0, jt = it["nq"], it["q0"], it["jt"]
            ps = ps_s.next()
            e = eb.next()
            it["e"] = e
            P.op('pe', lambda: nc.tensor.matmul(ps[:, 0:nq], lhsT=kT[:, jt * 128:(jt + 1) * 128], rhs=q[:, q0:q0 + nq],
                                                start=True, stop=True),
                 r=[("kTp", kT.name, min(jt // TL, 2)), q, ("qz", q.name)], w=[ps])
            P.op('act', lambda: nc.scalar.activation(out=e[:, 0:nq], in_=ps[:, 0:nq], func=AF.Exp), r=[ps], w=[e])

        def emit_PV(it):
            h, w_, nq, q0, jt, jn = it["h"], it["w"], it["nq"], it["q0"], it["jt"], it["jn"]
            kT, V, qA, qB = heads[h]
            e = it["e"]
            if it["first"] and w_ == 0 and q0 == 0 and h + 1 < NH:
                heads[h + 1] = load_head(h + 1)
            if it["first"]:
                state["pO"] = ps_o.next()
                state["aD"], state["aP"] = accD.next(), accP.next()
                state["usedP"] = False
            pO, aD, aP = state["pO"], state["aD"], state["aP"]
            P.op('pe', lambda: nc.tensor.matmul(pO[:, 0:nq], lhsT=V[:, jt, :], rhs=e[:, 0:nq], start=it["first"],
                                                stop=it["last"]),
                 r=[("Vp", V.name, min(jt // TL, 2)), e], w=[pO], sig=it["last"])
            if it["first"]:
                state["pZ"] = ps_z.next()
                state["zstart"] = True
                state["usedD"] = False
            pZ = state["pZ"]
            who = ("dve", "pe", "pool", "pe", "dve")[jn % 5]
            if who == "pe":
                P.op('pe', lambda: nc.tensor.matmul(pZ[:, 0:nq], lhsT=ones[:], rhs=e[:, 0:nq], start=state["zstart"],
                                                    stop=False), r=[ones, e], w=[pZ], sig=False)
                state["zstart"] = False
            elif who == "pool":
                if not state["usedP"]:
                    P.op('pool', lambda: nc.gpsimd.tensor_copy(out=aP[:, 0:nq], in_=e[:, 0:nq]), r=[e], w=[aP])
                    state["usedP"] = True
                else:
                    P.op('pool', lambda: nc.gpsimd.tensor_tensor(out=aP[:, 0:nq], in0=aP[:, 0:nq], in1=e[:, 0:nq],
                                                                 op=ALU.add), r=[e, aP], w=[aP])
            elif not state["usedD"]:
                P.op('dve', lambda: nc.vector.tensor_copy(out=aD[:, 0:nq], in_=e[:, 0:nq]), r=[e], w=[aD])
                state["usedD"] = True
            else:
                P.op('dve', lambda: nc.vector.tensor_tensor(out=aD[:, 0:nq], in0=aD[:, 0:nq], in1=e[:, 0:nq], op=ALU.add),
                     r=[e, aD], w=[aD])
            if not it["last"]:
                return
            if state["usedP"]:
                P.op('dve', lambda: nc.vector.tensor_tensor(out=aD[:, 0:nq], in0=aD[:, 0:nq], in1=aP[:, 0:nq], op=ALU.add),
                     r=[aD, aP], w=[aD])
            zh, zl, rz = zhi.next(), zlo.next(), rzb.next()
            P.op('dve', lambda: nc.vector.tensor_copy(out=zh[:, 0:nq], in_=aD[:, 0:nq]), r=[aD], w=[zh])
            P.op('dve', lambda: nc.vector.tensor_tensor(out=zl[:, 0:nq], in0=aD[:, 0:nq], in1=zh[:, 0:nq], op=ALU.subtract),
                 r=[aD, zh], w=[zl])
            P.op('pe', lambda: nc.tensor.matmul(pZ[:, 0:nq], lhsT=ones[:], rhs=zh[:, 0:nq], start=state["zstart"],
                                                stop=False), r=[ones, zh], w=[pZ], sig=False)
            P.op('pe', lambda: nc.tensor.matmul(pZ[:, 0:nq], lhsT=ones[:], rhs=zl[:, 0:nq], start=False, stop=True),
                 r=[ones, zl], w=[pZ])
            On = Onb[w_].next()
            P.op('dve', lambda: nc.vector.reciprocal(out=rz[:, 0:nq], in_=pZ[:, 0:nq]), r=[pZ], w=[rz])
            P.op('dve', lambda: nc.vector.tensor_tensor(out=On[:, 0:nq], in0=pO[:, 0:nq], in1=rz[:, 0:nq], op=ALU.mult),
                 r=[pO, rz], w=[On])
            state["On%d" % w_] = On
            if w_ == 0:
                return
            On0, On1 = state["On0"], state["On1"]
            OT, sq, rs, ln_ = OTb.next(), sqb.next(), rsb.next(), lnb.next()
            P.op('dve', lambda: nc.vector.scalar_tensor_tensor(out=OT[:, 0:nq], in0=On1[:, 0:nq], scalar=nlam[:, 0:1],
                                                               in1=On0[:, 0:nq], op0=ALU.mult, op1=ALU.add),
                 r=[On0, On1, nlam], w=[OT])
            P.op('act', lambda: nc.scalar.activation(out=sq[:, 0:nq], in_=OT[:, 0:nq], func=AF.Square), r=[OT], w=[sq])
            pS = ps_z.next()
            P.op('pe', lambda: nc.tensor.matmul(pS[:, 0:nq], lhsT=ones[:], rhs=sq[:, 0:nq], start=True, stop=True),
                 r=[ones, sq], w=[pS])
            P.op('act', lambda: nc.scalar.activation(out=ln_[:, 0:nq], in_=pS[:, 0:nq], func=AF.Ln, scale=1.0 / 128,
                                                     bias=k.epsb[:, 0:1]), r=[pS, k.epsb], w=[ln_])
            P.op('act', lambda: nc.scalar.activation(out=rs[:, 0:nq], in_=ln_[:, 0:nq], func=AF.Exp, scale=-0.5),
                 r=[ln_], w=[rs])
            P.op('dve', lambda: nc.vector.scalar_tensor_tensor(out=yT[:, h, q0:q0 + nq], in0=OT[:, 0:nq],
                                                               scalar=gsub[:, 0:1], in1=rs[:, 0:nq], op0=ALU.mult,
                                                               op1=ALU.mult), r=[OT, gsub, rs], w=[("yT", h, q0)])

        prev = None
        for it in items:
            emit_S(it)
            if prev is not None:
                emit_PV(prev)
            prev = it
        emit_PV(prev)
        for t in q_tiles:
            pss = [k.psf_rot.next(), k.psf_rot.next()]
            for cg in range(2):
                for kk in range(8):
                    P.op('pe', lambda cg=cg, kk=kk: nc.tensor.matmul(pss[cg][:], lhsT=yT[:, kk, t * 128:(t + 1) * 128],
                                                                   rhs=Wo[:, kk, cg * 512:(cg + 1) * 512],
                                                                   start=(kk == 0), stop=(kk == 7)),
                         r=[("yT", kk, (t * 128 // 512) * 512 if t < TL else NL), Wo], w=[pss[cg]], sig=(kk == 7))
            resid_epilogue(k, pss, t, gg[0 if t < TL else 1], hb.next(), ob2.next(), st4.next())
        P.barrier()

def prep_inputs(inp):
    f = lambda a: np.ascontiguousarray(np.asarray(a, dtype=np.float32))
    x, c, ctx, c_ctx = f(inp["x"]), f(inp["c"]), f(inp["ctx"]), f(inp["c_ctx"])
    shared = {n: f(inp[n]) for n in ("ada_w", "ada_b", "norm_g", "ret_w_out", "diff_w_in", "diff_w_out",
                                     "diff_subln", "ffn_w_up", "ffn_conv_b", "ffn_w_down")}
    shared["diff_lam"] = np.ascontiguousarray(np.stack([f(inp["diff_lq1"]), f(inp["diff_lk1"]), f(inp["diff_lq2"]),
                                                        f(inp["diff_lk2"])], axis=1))
    rw = f(inp["ret_w_in"])
    rw_sw = np.ascontiguousarray(np.concatenate([rw[:, :, :4096], rw[:, :, 6144:8192], rw[:, :, 4096:6144]], axis=2))
    dec = np.stack([f(inp["ret_decay_f"]), f(inp["ret_decay_b"])], axis=1)
    dec_sw = np.ascontiguousarray(dec[:, ::-1, :])
    cwv = f(inp["ffn_conv_w"])
    cw_sw = np.ascontiguousarray(cwv[:, ::-1, :])
    maps = []
    for core in range(8):
        b, half = core // 2, core % 2
        xs = x[b, half * NL:(half + 1) * NL]
        cs = ctx[b]
        pos = np.arange(half * NL, (half + 1) * NL)
        if half:
            xs, cs, pos = xs[::-1], cs[::-1], pos[::-1]
        posrc = np.stack([pos // 64, pos % 64], axis=-1).astype(np.float32)
        m = dict(shared)
        m["x"] = np.ascontiguousarray(xs)
        m["ctx"] = np.ascontiguousarray(cs)
        m["c2"] = np.ascontiguousarray(np.stack([c[b], c_ctx]))
        sel = np.zeros((128, 2), np.float32)
        sel[:, 1 - half] = 1.0
        m["sel"] = sel
        m["pos"] = np.ascontiguousarray(posrc.reshape(TL, 128, 2).transpose(1, 0, 2))
        m["ret_w_in"] = rw_sw if half else rw
        m["ret_decay"] = dec_sw if half else np.ascontiguousarray(dec)
        m["ffn_conv_w"] = cw_sw if half else cwv
        maps.append(m)
    return maps


def gather_out(results):
    out = np.empty((4, 2 * NL, D), np.float32)
    for core in range(8):
        b, half = core // 2, core % 2
        o = np.asarray(results[core]["out"])[:NL]
        out[b, half * NL:(half + 1) * NL] = o[::-1] if half else o
    return out


def kernel(**inputs):
    nc = build("full")
    maps = prep_inputs(inputs)
    res = run_bass_kernel_spmd(nc, maps, core_ids=list(range(8)))
    return gather_out(res.results)
```

```python
import numpy as np
from contextlib import ExitStack
import concourse.bass as bass
import concourse.mybir as mybir
from concourse.bass_utils import run_bass_kernel_spmd

F32 = mybir.dt.float32
BF16 = mybir.dt.bfloat16
I32 = mybir.dt.int32
AF = mybir.ActivationFunctionType
ALU = mybir.AluOpType
AX = mybir.AxisListType

D = 1024
NL = 2048
NCX = 256
NT = NL + NCX
TL, TCX, TT = 16, 2, 18
DEPTH = 4
FH = 2816
NFC = 22
EPS = 1e-6
GROUPS = [[0, 1], [2, 3], [4, 5], [6, 7]]


class Prog:
    def __init__(self, nc, stack):
        self.nc = nc
        self.stack = stack
        self.E = dict(pe=nc.tensor, act=nc.scalar, dve=nc.vector, pool=nc.gpsimd, sp=nc.sync)
        self.phys = []
        self.pcnt = []
        self.key2p = {}
        self.free = {}
        self.pclass = {}
        self.lastw = {}
        self.rd = {}
        self.dmalast = {}
        self.waited = {e: {} for e in self.E}
        self.nins = 0

    def _sem(self, k, cls="c"):
        if k not in self.key2p:
            fl = self.free.setdefault(cls, [])
            if fl:
                p = fl.pop()
            else:
                p = len(self.phys)
                self.phys.append(self.stack.enter_context(self.nc.semaphore("s%d" % p)))
                self.pcnt.append(0)
                self.pclass[p] = cls
            self.key2p[k] = p
        p = self.key2p[k]
        assert self.pclass[p] == cls, (k, cls, self.pclass[p])
        return p

    @staticmethod
    def _rk(x):
        return x if isinstance(x, (str, tuple)) else ("T", x.name)

    def op(self, eng, fn, r=(), w=(), sig=True, dma=None, inc=None):
        r = [self._rk(x) for x in r]
        w = [self._rk(x) for x in w]
        if dma is not None:
            dma = self._rk(dma)
        kind = 'd' if dma is not None else 'c'
        if dma is None:
            p = self._sem(eng)
        else:
            p = self._sem(('d', dma), "cc" if inc == 1 else ("sw" if eng == "pool" else "hw"))
        deps = {}

        def add(tok, raw):
            k, v, e, kd = tok
            if kd == 'c' and kind == 'c' and e == eng:
                if eng == 'pe' or not raw:
                    return
            if deps.get(k, 0) < v:
                deps[k] = v

        for x in r:
            if x in self.lastw:
                add(self.lastw[x], True)
        for x in w:
            if x in self.lastw:
                add(self.lastw[x], False)
            for tok in self.rd.get(x, {}).values():
                add(tok, False)
        if dma is not None and dma in self.dmalast:
            add(self.dmalast[dma], False)
        wd = self.waited[eng]
        for k, v in deps.items():
            if wd.get(k, 0) >= v:
                continue
            assert self.pcnt[k] >= v, ("wait on unsignaled op", k, v, self.pcnt[k])
            self.E[eng].wait_ge(self.phys[k], v)
            wd[k] = v
            self.nins += 1
        ins = fn()
        self.nins += 1
        if sig:
            if inc is None:
                inc = 16 if kind == 'd' else 1
            self.pcnt[p] += inc
            ins.then_inc(self.phys[p], inc)
            val = self.pcnt[p]
        else:
            val = self.pcnt[p] + 1
        tok = (p, val, eng, kind)
        for x in r:
            self.rd.setdefault(x, {})[p] = tok
        for x in w:
            self.lastw[x] = tok
            self.rd[x] = {}
        if dma is not None:
            self.dmalast[dma] = tok

    def dma(self, q, out, in_, key, r=(), w=(), **kw):
        self.op(q, lambda: self.E[q].dma_start(out=out, in_=in_, **kw), r=r, w=w, dma=key)

    def barrier(self):
        for e in self.E:
            wd = self.waited[e]
            for p, h in enumerate(self.phys):
                v = self.pcnt[p]
                if v > wd.get(p, 0):
                    self.E[e].wait_ge(h, v)
                    wd[p] = v
                    self.nins += 1
        self.lastw.clear()
        self.rd.clear()
        self.dmalast.clear()
        for k in [k for k in self.key2p if isinstance(k, tuple) and k[0] == 'd']:
            p = self.key2p.pop(k)
            self.free[self.pclass[p]].append(p)


_UNIQ = [0]


def sbt(ph, nc, name, shape, dt):
    _UNIQ[0] += 1
    return ph.enter_context(nc.sbuf_tensor("%s_%d" % (name, _UNIQ[0]), list(shape), dt))


class Rot:
    def __init__(self, items):
        self.items = list(items)
        self.i = 0

    def next(self):
        x = self.items[self.i % len(self.items)]
        self.i += 1
        return x


class K:
    pass


def build(mode="full"):
    nc = bass.Bass("TRN2", target_bir_lowering=False)
    stack = ExitStack()
    with stack:
        _build(nc, stack, mode)
    return nc


def _dram_in(nc, name, shape, dt=F32):
    return nc.dram_tensor(name, list(shape), dt, kind="ExternalInput")


def _build(nc, stack, mode):
    P = Prog(nc, stack)
    k = K()
    k.nc, k.P, k.mode = nc, P, mode
    k.x = _dram_in(nc, "x", [NL, D])
    k.ctx = _dram_in(nc, "ctx", [NCX, D])
    k.c2 = _dram_in(nc, "c2", [2, D])
    k.sel = _dram_in(nc, "sel", [128, 2])
    k.pos = _dram_in(nc, "pos", [128, TL, 2])
    k.ada_w = _dram_in(nc, "ada_w", [DEPTH, D, 6 * D])
    k.ada_b = _dram_in(nc, "ada_b", [DEPTH, 6 * D])
    k.norm_g = _dram_in(nc, "norm_g", [DEPTH, 4, D])
    k.ret_w_in = _dram_in(nc, "ret_w_in", [2, D, 8192])
    k.ret_w_out = _dram_in(nc, "ret_w_out", [2, 2048, D])
    k.ret_decay = _dram_in(nc, "ret_decay", [2, 2, 4])
    k.diff_w_in = _dram_in(nc, "diff_w_in", [2, D, 3 * D])
    k.diff_w_out = _dram_in(nc, "diff_w_out", [2, D, D])
    k.diff_lam = _dram_in(nc, "diff_lam", [2, 4, 64])
    k.diff_subln = _dram_in(nc, "diff_subln", [2, 128])
    k.ffn_w_up = _dram_in(nc, "ffn_w_up", [DEPTH, D, 2 * FH])
    k.ffn_conv_w = _dram_in(nc, "ffn_conv_w", [DEPTH, 3, FH])
    k.ffn_conv_b = _dram_in(nc, "ffn_conv_b", [DEPTH, FH])
    k.ffn_w_down = _dram_in(nc, "ffn_w_down", [DEPTH, FH, D])
    k.out = nc.dram_tensor("out", [NT, D], F32, kind="ExternalOutput")
    k.h_d = nc.dram_tensor("h_d", [NT, D], F32)
    k.mod_d = nc.dram_tensor("mod_d", [DEPTH, 2, 6 * D], F32)
    k.hx_d = nc.dram_tensor("hx_d", [128, 8], BF16)
    k.hg_d = nc.dram_tensor("hg_d", [256, 8], BF16)

    def sb(name, shape, dt):
        return stack.enter_context(nc.sbuf_tensor(name, list(shape), dt))

    k.ident = sb("ident", [128, 128], BF16)
    k.iotaf = sb("iotaf", [128, 128], F32)
    k.iotap = sb("iotap", [128, 1], F32)
    k.neghalf = sb("neghalf", [128, 1], F32)
    k.epsb = sb("epsb", [128, 1], F32)
    k.selt = sb("selt", [128, 2], F32)
    k.junk = sb("junk", [128, 1024], BF16)
    k.psf = [stack.enter_context(nc.psum_tensor("psf%d" % i, [128, 512], F32)) for i in range(6)]
    k.psb = [stack.enter_context(nc.psum_tensor("psb%d" % i, [128, 1024], BF16)) for i in range(2)]
    k.psf_rot = Rot(k.psf)
    k.psb_rot = Rot(k.psb)

    consts(k)
    ada_all(k)
    init_h(k)
    if mode == "ffn0":
        ffn_layer(k, 0, False)
    if mode in ("ret0", "L0"):
        rope_tables(k)
        ret_layer(k, 0)
        if mode == "L0":
            ffn_layer(k, 0, False)
    if mode == "diff1":
        rope_tables(k)
        diff_layer(k, 1, False)
    if mode == "full":
        rope_tables(k)
        for l in range(DEPTH):
            last = l == DEPTH - 1
            if l % 2 == 0:
                ret_layer(k, l)
            else:
                diff_layer(k, l, last)
            ffn_layer(k, l, last)
    P.barrier()
    with ExitStack() as ph:
        ob = [sbt(ph, nc, "ob%d" % i, [128, D], F32) for i in range(3)]
        rot = Rot(ob)
        for t in range(TT):
            b = rot.next()
            P.dma('sp', b[:], k.h_d[t * 128:(t + 1) * 128, :], key=b, r=[("h", t)], w=[b])
            P.dma('sp', k.out[t * 128:(t + 1) * 128, :], b[:], key=b, r=[b], w=[("out", t)])
        P.barrier()
    print("instructions:", P.nins, "sems:", len(P.phys))


def consts(k):
    nc, P = k.nc, k.P
    with ExitStack() as ph:
        ii = sbt(ph, nc, "ii", [128, 128], I32)
        pp = sbt(ph, nc, "pp", [128, 1], I32)
        P.op('pool', lambda: nc.gpsimd.iota(ii[:], pattern=[[1, 128]], base=0, channel_multiplier=0), w=[ii])
        P.op('pool', lambda: nc.gpsimd.iota(pp[:], pattern=[[0, 1]], base=0, channel_multiplier=1), w=[pp])
        P.op('dve', lambda: nc.vector.tensor_copy(out=k.iotaf[:], in_=ii[:]), r=[ii], w=[k.iotaf])
        P.op('dve', lambda: nc.vector.tensor_copy(out=k.iotap[:], in_=pp[:]), r=[pp], w=[k.iotap])
        P.op('dve', lambda: nc.vector.tensor_scalar(out=k.ident[:], in0=k.iotaf[:], scalar1=k.iotap[:, 0:1],
                                                    scalar2=None, op0=ALU.is_equal),
             r=[k.iotaf, k.iotap], w=[k.ident])
        P.op('dve', lambda: nc.vector.memset(k.neghalf[:], -0.5), w=[k.neghalf])
        P.op('dve', lambda: nc.vector.memset(k.epsb[:], EPS), w=[k.epsb])
        P.dma('sp', k.selt[:], k.sel[:, :], key=k.selt, w=[k.selt])
        P.barrier()


def init_h(k):
    P = k.P
    P.dma('sp', k.h_d[0:NL, :], k.x[:, :], key="inith0", w=[("h", t) for t in range(TL)])
    P.dma('sp', k.h_d[NL:NT, :], k.ctx[:, :], key="inith1", w=[("h", t) for t in range(TL, TT)])


def ada_all(k):
    nc, P = k.nc, k.P
    with ExitStack() as ph:
        cT = sbt(ph, nc, "cT", [128, 8, 2], F32)
        sT = sbt(ph, nc, "sT", [128, 8, 2], F32)
        wb = [sbt(ph, nc, "adaw%d" % i, [128, 8, 512], F32) for i in range(2)]
        bias = sbt(ph, nc, "adab", [2, 6 * D], F32)
        row = sbt(ph, nc, "adarow", [2, 6 * D], F32)
        wrot = Rot(wb)
        for r in range(2):
            P.dma('sp', cT[:, :, r], k.c2.ap()[r].rearrange("(k p) -> p k", p=128), key=cT, w=[cT],
                  allow_slow_non_contiguous=True)
        P.op('act', lambda: nc.scalar.activation(out=sT[:], in_=cT[:], func=AF.Silu), r=[cT], w=[sT])
        for l in range(DEPTH):
            P.dma('sp', bias[:], k.ada_b.ap()[l:l + 1, :].to_broadcast([2, 6 * D]), key=bias, w=[bias])
            for g in range(12):
                w = wrot.next()
                P.dma('sp', w[:], k.ada_w.ap()[l][:, g * 512:(g + 1) * 512].rearrange("(k p) n -> p k n", p=128),
                      key=w, w=[w])
                ps = k.psf_rot.next()
                for kk in range(8):
                    P.op('pe', lambda kk=kk, w=w, ps=ps: nc.tensor.matmul(ps[0:2, :], lhsT=sT[:, kk, :], rhs=w[:, kk, :],
                                                                     start=(kk == 0), stop=(kk == 7)),
                         r=[sT, w], w=[ps], sig=(kk == 7))
                P.op('dve', lambda g=g, ps=ps: nc.vector.tensor_tensor(out=row[:, g * 512:(g + 1) * 512], in0=ps[0:2, :],
                                                                    in1=bias[:, g * 512:(g + 1) * 512], op=ALU.add),
                     r=[ps, bias], w=[row])
            P.dma('sp', k.mod_d.ap()[l], row[:], key=row, r=[row], w=[("mod", l)])
        P.barrier()


def bcast_row(k, q, dst, src_row, key, r=(), w=()):
    n = dst.shape[-1]
    k.P.dma(q, dst, src_row.to_broadcast([128, n]), key=key, r=r, w=w)


def rstd_from_ss(k, ss, ms, rstd, n):
    nc, P = k.nc, k.P
    P.op('act', lambda: nc.scalar.activation(out=ms[:], in_=ss[:], func=AF.Ln, scale=1.0 / n, bias=k.epsb[:, 0:1]),
         r=[ss, k.epsb], w=[ms])
    P.op('act', lambda: nc.scalar.activation(out=rstd[:], in_=ms[:], func=AF.Exp, scale=-0.5), r=[ms], w=[rstd])


def prenorm(k, ph, l, gi, shi, sci, uT, tiles, hook=None):
    nc, P = k.nc, k.P

    def t_(name, shape, dt):
        return sbt(ph, nc, name, shape, dt)

    gb = t_("pn_g", [128, D], F32)
    gs = [t_("pn_gs%d" % r, [128, D], F32) for r in range(2)]
    sh = [t_("pn_sh%d" % r, [128, D], F32) for r in range(2)]
    hb = Rot([t_("pn_h%d" % i, [128, D], F32) for i in range(3)])
    tb = Rot([t_("pn_t%d" % i, [128, D], F32) for i in range(2)])
    ub = Rot([t_("pn_u%d" % i, [128, D], BF16) for i in range(2)])
    st = Rot([[t_("pn_s%d_%d" % (i, j), [128, 1], F32) for j in range(3)] for i in range(3)])
    bcast_row(k, 'sp', gb[:], k.norm_g.ap()[l, gi:gi + 1, :], key=gb, w=[gb])
    for r in range(2):
        bcast_row(k, 'sp', gs[r][:], k.mod_d.ap()[l, r:r + 1, sci * D:(sci + 1) * D], key=gs[r], r=[("mod", l)], w=[gs[r]])
        bcast_row(k, 'sp', sh[r][:], k.mod_d.ap()[l, r:r + 1, shi * D:(shi + 1) * D], key=sh[r], r=[("mod", l)], w=[sh[r]])
        P.op('dve', lambda r=r: nc.vector.scalar_tensor_tensor(out=gs[r][:], in0=gs[r][:], scalar=1.0, in1=gb[:],
                                                           op0=ALU.add, op1=ALU.mult), r=[gs[r], gb], w=[gs[r]])
    def stage_a(t):
        h = hb.next()
        ss, ms, rstd = st.next()
        P.dma('sp', h[:], k.h_d[t * 128:(t + 1) * 128, :], key=h, r=[("h", t)], w=[h])
        P.op('act', lambda: nc.scalar.activation(out=k.junk[:], in_=h[:], func=AF.Square, accum_out=ss[:]),
             r=[h], w=[ss])
        rstd_from_ss(k, ss, ms, rstd, D)
        return h, rstd

    def stage_b(t, h, rstd):
        r = 0 if t < TL else 1
        tt = tb.next()
        u = ub.next()
        P.op('dve', lambda: nc.vector.scalar_tensor_tensor(
            out=tt[:], in0=h[:], scalar=rstd[:, 0:1], in1=gs[r][:], op0=ALU.mult, op1=ALU.mult),
            r=[h, rstd, gs[r]], w=[tt])
        P.op('dve', lambda: nc.vector.tensor_tensor(out=u[:], in0=tt[:], in1=sh[r][:], op=ALU.add),
             r=[tt, sh[r]], w=[u])
        pb = k.psb_rot.next()
        for kk in range(8):
            P.op('pe', lambda kk=kk: nc.tensor.transpose(pb[:, kk * 128:(kk + 1) * 128], u[:, kk * 128:(kk + 1) * 128],
                                                        k.ident[:]),
                 r=[u, k.ident], w=[pb], sig=(kk == 7))
        P.op('act', lambda: nc.scalar.copy(out=uT[:, :, t * 128:(t + 1) * 128],
                                           in_=pb[:].rearrange("p (k n) -> p k n", k=8)),
             r=[pb], w=[("uT", t)])
        if hook is not None:
            hook(t)

    prev = None
    for t in tiles:
        a = stage_a(t)
        if prev is not None:
            stage_b(*prev)
        prev = (t,) + a
    stage_b(*prev)


def uT_res(c, n):
    return [("uT", t) for t in range(c // 128, (c + n - 1) // 128 + 1)]


def ffn_layer(k, l, last):
    nc, P = k.nc, k.P
    tiles_all = list(range(TL)) + ([] if last else list(range(TL, TT)))
    with ExitStack() as ph:
        def t_(name, shape, dt):
            return sbt(ph, nc, name, shape, dt)

        uT = t_("f_uT", [128, 8, NT], BF16)
        Wd = t_("f_Wd", [128, NFC, D], BF16)
        cw = t_("f_cw", [128, 4, NFC], F32)
        uH = [t_("f_uH%d" % i, [128, 8, 2], BF16) for i in range(2)]
        hx = t_("f_hx", [128, 8], BF16)
        hg = t_("f_hg", [128, 2, 8], BF16)
        hgf = t_("f_hgf", [128, 8], F32)
        for half in range(2):
            f0, f1 = half * 11, (half + 1) * 11
            P.dma('pool', Wd[:, f0:f1, :], k.ffn_w_down.ap()[l][f0 * 128:f1 * 128, :].rearrange("(f p) n -> p f n", p=128),
                  key=("Wd", half), w=[("Wd", half)])
        for tap in range(3):
            P.dma('sp', cw[:, tap, :], k.ffn_conv_w.ap()[l, tap].rearrange("(f p) -> p f", p=128), key=cw, w=[cw],
                  allow_slow_non_contiguous=True)
        P.dma('sp', cw[:, 3, :], k.ffn_conv_b.ap()[l].rearrange("(f p) -> p f", p=128), key=cw, w=[cw],
              allow_slow_non_contiguous=True)
        with ExitStack() as ph1:
            order = [TL - 1] + [t for t in tiles_all if t != TL - 1]

            def hook(t):
                if t != TL - 1:
                    return
                P.op('dve', lambda: nc.vector.tensor_copy(out=hx[:], in_=uT[:, :, NL - 1]), r=[("uT", TL - 1)], w=[hx])
                P.dma('pool', k.hx_d[:, :], hx[:], key=hx, r=[hx], w=["hx_d"])
                P.op('pool', lambda: nc.gpsimd.collective_compute("AllGather", ALU.bypass, replica_groups=GROUPS,
                                                                  ins=[k.hx_d.ap().opt()], outs=[k.hg_d.ap().opt()]),
                     r=["hx_d"], w=["hg_d"], dma="cc", inc=1)

            prenorm(k, ph1, l, 2, 3, 4, uT, order, hook)
            P.dma('sp', hg[:], k.hg_d.ap().rearrange("(r p) k -> p r k", p=128), key=hg, r=["hg_d"], w=[hg])
            P.op('dve', lambda: nc.vector.tensor_scalar(out=hgf[:], in0=hg[:, 0, :], scalar1=k.selt[:, 0:1], scalar2=None,
                                                        op0=ALU.mult), r=[hg, k.selt], w=[hgf])
            P.op('dve', lambda: nc.vector.scalar_tensor_tensor(out=uH[1][:, :, 1], in0=hg[:, 1, :], scalar=k.selt[:, 1:2],
                                                               in1=hgf[:], op0=ALU.mult, op1=ALU.add),
                 r=[hg, hgf, k.selt], w=[uH[1]])
            P.op('dve', lambda: nc.vector.tensor_copy(out=uH[1][:, :, 0], in_=uT[:, :, NL // 2 - 1]), r=[("uT", 7)], w=[uH[1]])
            P.op('dve', lambda: nc.vector.tensor_copy(out=uH[0][:, :, 0], in_=uT[:, :, NL // 2]), r=[("uT", 8)], w=[uH[0]])
            P.op('dve', lambda: nc.vector.tensor_copy(out=uH[0][:, :, 1], in_=uT[:, :, NL // 2]), r=[("uT", 8)], w=[uH[0]])
            P.barrier()
        segs = [dict(tiles=list(range(0, 8)), subs=[(0, 1024, False, True)], uH=uH[0]),
                dict(tiles=list(range(8, 16)) + ([] if last else [16, 17]),
                     subs=[(1024, 1024, True, True)] + ([] if last else [(2048, 256, False, False)]), uH=uH[1])]
        with ExitStack() as ph2:
            def t2(name, shape, dt):
                return sbt(ph2, nc, name, shape, dt)

            NCOL = 1280
            aT = t2("f_aT", [128, NFC, NCOL], BF16)
            Wv = Rot([t2("f_Wv%d" % i, [128, 8, 256], BF16) for i in range(2)])
            Wg = Rot([t2("f_Wg%d" % i, [128, 8, 256], BF16) for i in range(2)])
            GW = NCOL + 4
            Gb = Rot([t2("f_G%d" % i, [128, GW], F32) for i in range(2)])
            Vb = Rot([t2("f_V%d" % i, [128, NCOL], F32) for i in range(2)])
            Tb = Rot([t2("f_T%d" % i, [128, NCOL], F32) for i in range(1)])
            gg = [t2("f_gg%d" % r, [128, D], F32) for r in range(2)]
            g3 = t2("f_g3", [128, D], F32)
            hb = Rot([t2("f_h%d" % i, [128, D], F32) for i in range(2)])
            ob = Rot([t2("f_o%d" % i, [128, D], F32) for i in range(2)])
            st = Rot([[t2("f_s%d_%d" % (i, j), [128, 1], F32) for j in range(4)] for i in range(2)])
            for G in Gb.items:
                P.op('pool', lambda G=G: nc.gpsimd.memset(G[:], 0.0), w=[G])
            bcast_row(k, 'sp', g3[:], k.norm_g.ap()[l, 3:4, :], key=g3, w=[g3])
            for r in range(2):
                bcast_row(k, 'sp', gg[r][:], k.mod_d.ap()[l, r:r + 1, 5 * D:6 * D], key=gg[r], r=[("mod", l)], w=[gg[r]])
                P.op('dve', lambda r=r: nc.vector.tensor_tensor(out=gg[r][:], in0=gg[r][:], in1=g3[:], op=ALU.mult),
                     r=[gg[r], g3], w=[gg[r]])
            wup = k.ffn_w_up.ap()[l]

            def load_w(fcg):
                wv, wg = Wv.next(), Wg.next()
                P.dma('pool', wv[:], wup[:, fcg * 256:(fcg + 1) * 256].rearrange("(k p) n -> p k n", p=128), key=wv, w=[wv])
                P.dma('pool', wg[:], wup[:, FH + fcg * 256:FH + (fcg + 1) * 256].rearrange("(k p) n -> p k n", p=128),
                      key=wg, w=[wg])
                return wv, wg

            for seg in segs:
                subs = seg["subs"]
                uHs = seg["uH"]
                offs, goffs = [], []
                o, go = 0, 0
                for (c0, n, lh, rh) in subs:
                    offs.append(o)
                    goffs.append(go)
                    o += n
                    go += n + 2
                nxt = load_w(0)
                for fcg in range(11):
                    wv, wg = nxt
                    if fcg + 1 < 11:
                        nxt = load_w(fcg + 1)
                    for j in range(2):
                        fc = fcg * 2 + j
                        G, V, T = Gb.next(), Vb.next(), Tb.next()
                        for si, (c0, n, lh, rh) in enumerate(subs):
                            for b0 in range(0, n, 512):
                                nb = min(512, n - b0)
                                for (wt, dst, doff) in ((wv, V, offs[si] + b0), (wg, G, goffs[si] + 1 + b0)):
                                    ps = k.psf_rot.next()
                                    for kk in range(8):
                                        P.op('pe', lambda kk=kk, wt=wt, ps=ps, c=c0 + b0, nb=nb, j=j: nc.tensor.matmul(
                                            ps[:, 0:nb], lhsT=wt[:, kk, j * 128:(j + 1) * 128], rhs=uT[:, kk, c:c + nb],
                                            start=(kk == 0), stop=(kk == 7)),
                                            r=[wt] + uT_res(c0 + b0, nb), w=[ps], sig=(kk == 7))
                                    P.op('act', lambda ps=ps, dst=dst, doff=doff, nb=nb: nc.scalar.copy(
                                        out=dst[:, doff:doff + nb], in_=ps[:, 0:nb]), r=[ps], w=[dst])
                        ps = k.psf_rot.next()
                        for kk in range(8):
                            P.op('pe', lambda kk=kk, ps=ps, j=j, wg=wg: nc.tensor.matmul(
                                ps[:, 0:2], lhsT=wg[:, kk, j * 128:(j + 1) * 128], rhs=uHs[:, kk, :],
                                start=(kk == 0), stop=(kk == 7)), r=[wg, uHs], w=[ps], sig=(kk == 7))
                        for si, (c0, n, lh, rh) in enumerate(subs):
                            go = goffs[si]
                            if lh:
                                P.op('act', lambda ps=ps, G=G, go=go: nc.scalar.copy(out=G[:, go:go + 1], in_=ps[:, 0:1]),
                                     r=[ps], w=[G])
                            if rh:
                                P.op('act', lambda ps=ps, G=G, go=go, n=n: nc.scalar.copy(
                                    out=G[:, go + n + 1:go + n + 2], in_=ps[:, 1:2]), r=[ps], w=[G])
                        for si, (c0, n, lh, rh) in enumerate(subs):
                            go, o = goffs[si], offs[si]
                            P.op('dve', lambda G=G, T=T, go=go, o=o, n=n, fc=fc: nc.vector.tensor_scalar(
                                out=T[:, o:o + n], in0=G[:, go + 1:go + 1 + n], scalar1=cw[:, 1, fc:fc + 1],
                                scalar2=cw[:, 3, fc:fc + 1], op0=ALU.mult, op1=ALU.add), r=[G, cw], w=[T])
                            P.op('dve', lambda G=G, T=T, go=go, o=o, n=n, fc=fc: nc.vector.scalar_tensor_tensor(
                                out=T[:, o:o + n], in0=G[:, go:go + n], scalar=cw[:, 0, fc:fc + 1], in1=T[:, o:o + n],
                                op0=ALU.mult, op1=ALU.add), r=[G, cw, T], w=[T])
                            P.op('dve', lambda G=G, T=T, go=go, o=o, n=n, fc=fc: nc.vector.scalar_tensor_tensor(
                                out=T[:, o:o + n], in0=G[:, go + 2:go + 2 + n], scalar=cw[:, 2, fc:fc + 1], in1=T[:, o:o + n],
                                op0=ALU.mult, op1=ALU.add), r=[G, cw, T], w=[T])
                            P.op('act', lambda T=T, o=o, n=n: nc.scalar.activation(out=T[:, o:o + n], in_=T[:, o:o + n],
                                                                                func=AF.Silu), r=[T], w=[T])
                            P.op('dve', lambda T=T, V=V, o=o, n=n, fc=fc: nc.vector.tensor_tensor(
                                out=aT[:, fc, o:o + n], in0=T[:, o:o + n], in1=V[:, o:o + n], op=ALU.mult),
                                r=[T, V], w=[("aT", fc)])
                col = 0
                for t in seg["tiles"]:
                    r = 0 if t < TL else 1
                    pss = [k.psf_rot.next(), k.psf_rot.next()]
                    for cg in range(2):
                        for fc in range(NFC):
                            P.op('pe', lambda fc=fc, cg=cg, col=col: nc.tensor.matmul(
                                pss[cg][:], lhsT=aT[:, fc, col:col + 128], rhs=Wd[:, fc, cg * 512:(cg + 1) * 512],
                                start=(fc == 0), stop=(fc == NFC - 1)),
                                r=[("aT", fc), ("Wd", fc // 11)], w=[pss[cg]], sig=(fc == NFC - 1))
                    h, o_ = hb.next(), ob.next()
                    s0, s1, ms, rstd = st.next()
                    P.dma('sp', h[:], k.h_d[t * 128:(t + 1) * 128, :], key=h, r=[("h", t)], w=[h])
                    for cg, s in ((0, s0), (1, s1)):
                        P.op('act', lambda cg=cg, s=s: nc.scalar.activation(out=k.junk[:, 0:512], in_=pss[cg][:], func=AF.Square,
                                                                        accum_out=s[:]), r=[pss[cg]], w=[s])
                    P.op('dve', lambda s0=s0, s1=s1: nc.vector.tensor_tensor(out=s0[:], in0=s0[:], in1=s1[:], op=ALU.add),
                         r=[s0, s1], w=[s0])
                    rstd_from_ss(k, s0, ms, rstd, D)
                    for cg in range(2):
                        P.op('dve', lambda cg=cg, o_=o_, rstd=rstd, r=r: nc.vector.scalar_tensor_tensor(
                            out=o_[:, cg * 512:(cg + 1) * 512], in0=pss[cg][:], scalar=rstd[:, 0:1],
                            in1=gg[r][:, cg * 512:(cg + 1) * 512], op0=ALU.mult, op1=ALU.mult),
                            r=[pss[cg], rstd, gg[r]], w=[o_])
                    P.op('dve', lambda o_=o_, h=h: nc.vector.tensor_tensor(out=o_[:], in0=o_[:], in1=h[:], op=ALU.add),
                         r=[o_, h], w=[o_])
                    P.dma('sp', k.h_d[t * 128:(t + 1) * 128, :], o_[:], key=o_, r=[o_], w=[("h", t)])
                    col += 128
            P.barrier()


def resid_epilogue(k, pss, t, ggr, h, o_, stt4):
    nc, P = k.nc, k.P
    s0, s1, ms, rstd = stt4
    P.dma('sp', h[:], k.h_d[t * 128:(t + 1) * 128, :], key=h, r=[("h", t)], w=[h])
    for cg, s_ in ((0, s0), (1, s1)):
        P.op('act', lambda cg=cg, s_=s_: nc.scalar.activation(out=k.junk[:, 0:512], in_=pss[cg][:], func=AF.Square,
                                                            accum_out=s_[:]), r=[pss[cg]], w=[s_])
    P.op('dve', lambda: nc.vector.tensor_tensor(out=s0[:], in0=s0[:], in1=s1[:], op=ALU.add), r=[s0, s1], w=[s0])
    rstd_from_ss(k, s0, ms, rstd, D)
    for cg in range(2):
        P.op('dve', lambda cg=cg: nc.vector.scalar_tensor_tensor(
            out=o_[:, cg * 512:(cg + 1) * 512], in0=pss[cg][:], scalar=rstd[:, 0:1],
            in1=ggr[:, cg * 512:(cg + 1) * 512], op0=ALU.mult, op1=ALU.mult), r=[pss[cg], rstd, ggr], w=[o_])
    P.op('dve', lambda: nc.vector.tensor_tensor(out=o_[:], in0=o_[:], in1=h[:], op=ALU.add), r=[o_, h], w=[o_])
    P.dma('pool', k.h_d[t * 128:(t + 1) * 128, :], o_[:], key=o_, r=[o_], w=[("h", t)])


def gate_tiles(k, ph, l, gi, gatei):
    nc, P = k.nc, k.P
    gg = [sbt(ph, nc, "gg%d" % r, [128, D], F32) for r in range(2)]
    g3 = sbt(ph, nc, "ggn", [128, D], F32)
    bcast_row(k, 'sp', g3[:], k.norm_g.ap()[l, gi:gi + 1, :], key=g3, w=[g3])
    for r in range(2):
        bcast_row(k, 'sp', gg[r][:], k.mod_d.ap()[l, r:r + 1, gatei * D:(gatei + 1) * D], key=gg[r], r=[("mod", l)],
                  w=[gg[r]])
        P.op('dve', lambda r=r: nc.vector.tensor_tensor(out=gg[r][:], in0=gg[r][:], in1=g3[:], op=ALU.mult),
             r=[gg[r], g3], w=[gg[r]])
    return gg


def proj_tm(k, ph, uT, w_ap, ngroups, tiles, epilogue):
    nc, P = k.nc, k.P
    Wb = Rot([sbt(ph, nc, "pj_w%d" % i, [128, 8, 512], BF16) for i in range(2)])

    def load(g):
        w = Wb.next()
        P.dma('pool', w[:], w_ap[:, g * 512:(g + 1) * 512].rearrange("(k p) n -> p k n", p=128), key=w, w=[w])
        return w

    nxt = load(0)
    for g in range(ngroups):
        w = nxt
        if g + 1 < ngroups:
            nxt = load(g + 1)
        for t in tiles:
            ps = k.psf_rot.next()
            for kk in range(8):
                P.op('pe', lambda kk=kk: nc.tensor.matmul(ps[:], lhsT=uT[:, kk, t * 128:(t + 1) * 128], rhs=w[:, kk, :],
                                                        start=(kk == 0), stop=(kk == 7)),
                     r=[w, ("uT", t)], w=[ps], sig=(kk == 7))
            epilogue(t, g, ps)


def rope_apply(k, ps, outb, cos, sin, nh, nf, scale, tmp):
    nc, P = k.nc, k.P
    n = nh * 4 * nf
    pv = ps[:, 0:n].rearrange("p (h r x f) -> p h r x f", h=nh, r=2, x=2)
    ov = outb[:, 0:n].rearrange("p (h r x f) -> p h r x f", h=nh, r=2, x=2)
    x1, x2 = pv[:, :, :, 0, :], pv[:, :, :, 1, :]
    cb = cos.unsqueeze(1).to_broadcast([128, nh, 2, nf])
    sb_ = sin.unsqueeze(1).to_broadcast([128, nh, 2, nf])
    tv = [t_[:, 0:nh * 2 * nf].rearrange("p (h r f) -> p h r f", h=nh, r=2) for t_ in tmp]
    for (dst, xin, tab) in ((0, x1, cb), (1, x2, sb_), (2, x1, sb_), (3, x2, cb)):
        P.op('dve', lambda dst=dst, xin=xin, tab=tab: nc.vector.scalar_tensor_tensor(
            out=tv[dst], in0=xin, scalar=float(scale), in1=tab, op0=ALU.mult, op1=ALU.mult),
            r=[ps, "rope_tab"], w=[tmp[dst]])
    P.op('dve', lambda: nc.vector.tensor_tensor(out=ov[:, :, :, 0, :], in0=tv[0], in1=tv[1], op=ALU.subtract),
         r=[tmp[0], tmp[1]], w=[outb])
    P.op('dve', lambda: nc.vector.tensor_tensor(out=ov[:, :, :, 1, :], in0=tv[2], in1=tv[3], op=ALU.add),
         r=[tmp[2], tmp[3]], w=[outb])


def rope_tables(k):
    nc, P = k.nc, k.P
    PI = float(np.pi)
    k.ropeR_d = nc.dram_tensor("ropeR_d", [2, 128, TL * 2 * 64], F32)
    k.ropeD_d = nc.dram_tensor("ropeD_d", [2, 128, TL * 2 * 16], F32)
    with ExitStack() as ph:
        pos = sbt(ph, nc, "pos", [128, TL, 2], F32)
        P.dma('sp', pos[:], k.pos[:, :, :], key=pos, w=[pos])
        for (nf, d2, dst) in ((64, 128, k.ropeR_d), (16, 32, k.ropeD_d)):
            n = TL * 2 * nf
            inv = sbt(ph, nc, "inv", [128, nf], F32)
            ang = sbt(ph, nc, "ang", [128, n], F32)
            xx = sbt(ph, nc, "xx", [128, n], F32)
            ki = sbt(ph, nc, "ki", [128, n], I32)
            kf = sbt(ph, nc, "kf", [128, n], F32)
            mm = sbt(ph, nc, "mm", [128, n], F32)
            res = sbt(ph, nc, "res", [128, n], F32)
            P.op('act', lambda: nc.scalar.activation(out=inv[:], in_=k.iotaf[:, 0:nf], func=AF.Exp,
                                                     scale=-float(np.log(10000.0)) * 2.0 / d2), r=[k.iotaf], w=[inv])
            for t in range(TL):
                for rc in range(2):
                    o = (t * 2 + rc) * nf
                    P.op('dve', lambda t=t, rc=rc, o=o: nc.vector.tensor_scalar(
                        out=ang[:, o:o + nf], in0=inv[:], scalar1=pos[:, t, rc:rc + 1], scalar2=None, op0=ALU.mult),
                        r=[inv, pos], w=[ang])
            for ci, off in ((0, PI / 2), (1, 0.0)):
                P.op('dve', lambda: nc.vector.tensor_scalar(out=xx[:], in0=ang[:], scalar1=float(off), scalar2=None,
                                                            op0=ALU.add), r=[ang], w=[xx])
                P.op('dve', lambda: nc.vector.tensor_scalar(out=ki[:], in0=xx[:], scalar1=float(1.0 / (2 * PI)),
                                                            scalar2=None, op0=ALU.mult), r=[xx], w=[ki])
                P.op('dve', lambda: nc.vector.tensor_copy(out=kf[:], in_=ki[:]), r=[ki], w=[kf])
                P.op('dve', lambda: nc.vector.scalar_tensor_tensor(out=xx[:], in0=kf[:], scalar=-2 * PI, in1=xx[:],
                                                                   op0=ALU.mult, op1=ALU.add), r=[kf, xx], w=[xx])
                P.op('dve', lambda: nc.vector.tensor_scalar(out=mm[:], in0=xx[:], scalar1=PI, scalar2=None,
                                                            op0=ALU.is_gt), r=[xx], w=[mm])
                P.op('dve', lambda: nc.vector.scalar_tensor_tensor(out=xx[:], in0=mm[:], scalar=-2 * PI, in1=xx[:],
                                                                   op0=ALU.mult, op1=ALU.add), r=[mm, xx], w=[xx])
                P.op('dve', lambda: nc.vector.tensor_scalar(out=mm[:], in0=xx[:], scalar1=-PI, scalar2=None,
                                                            op0=ALU.is_lt), r=[xx], w=[mm])
                P.op('dve', lambda: nc.vector.scalar_tensor_tensor(out=xx[:], in0=mm[:], scalar=2 * PI, in1=xx[:],
                                                                   op0=ALU.mult, op1=ALU.add), r=[mm, xx], w=[xx])
                P.op('act', lambda: nc.scalar.activation(out=res[:], in_=xx[:], func=AF.Sin), r=[xx], w=[res])
                P.dma('sp', dst.ap()[ci], res[:], key=res, r=[res], w=[("ropetab", nf, ci)])
        P.barrier()


def ret_layer(k, l):
    nc, P = k.nc, k.P
    i = l // 2
    H, DK, DV = 4, 256, 512
    if not hasattr(k, "rqT_d"):
        k.rqT_d = nc.dram_tensor("rqT_d", [TT, 128, 8, 128], BF16)
        k.rkT_d = nc.dram_tensor("rkT_d", [TT, 128, 8, 128], BF16)
        k.k_d = nc.dram_tensor("k_d", [NT, 1024], BF16)
        k.v_d = nc.dram_tensor("v_d", [NT, 2048], BF16)
        k.g_d = [nc.dram_tensor("g%d_d" % p, [NT, 2048], BF16) for p in range(2)]
        k.y_d = nc.dram_tensor("y_d", [NT, 2048], F32)
        k.F_d = nc.dram_tensor("F_d", [1024, 512], F32)
        k.Fg_d = nc.dram_tensor("Fg_d", [2048, 512], F32)
    tiles_all = list(range(TT))
    with ExitStack() as ph:
        uT = sbt(ph, nc, "r_uT", [128, 8, NT], BF16)
        with ExitStack() as ph1:
            prenorm(k, ph1, l, 0, 0, 1, uT, tiles_all)
            P.barrier()
        cos = sbt(ph, nc, "r_cos", [128, TL, 2, 64], F32)
        sin = sbt(ph, nc, "r_sin", [128, TL, 2, 64], F32)
        P.dma('sp', cos[:].rearrange("p t r f -> p (t r f)"), k.ropeR_d.ap()[0], key=cos, w=["rope_tab"])
        P.dma('sp', sin[:].rearrange("p t r f -> p (t r f)"), k.ropeR_d.ap()[1], key=sin, w=["rope_tab"])
        tmp = [sbt(ph, nc, "r_tmp%d" % j, [128, 256], F32) for j in range(4)]
        ob = Rot([sbt(ph, nc, "r_ob%d" % j, [128, 512], BF16) for j in range(3)])
        tb = Rot([sbt(ph, nc, "r_tb%d" % j, [128, 4, 128], BF16) for j in range(3)])

        def epi(t, g, ps):
            o = ob.next()
            if g < 4:
                scale = 1.0 if g < 2 else DK ** -0.5
                if t < TL:
                    rope_apply(k, ps, o, cos[:, t], sin[:, t], 2, 64, scale, tmp)
                else:
                    P.op('act', lambda: nc.scalar.activation(out=o[:], in_=ps[:], func=AF.Copy, scale=float(scale)),
                         r=[ps], w=[o])
                if g >= 2:
                    P.dma('sp', k.k_d[t * 128:(t + 1) * 128, (g - 2) * 512:(g - 1) * 512], o[:], key=o, r=[o],
                          w=[("k_d", t)])
                pb = k.psb_rot.next()
                for j in range(4):
                    P.op('pe', lambda j=j: nc.tensor.transpose(pb[:, j * 128:(j + 1) * 128], o[:, j * 128:(j + 1) * 128],
                                                              k.ident[:]), r=[o, k.ident], w=[pb], sig=(j == 3))
                tt = tb.next()
                P.op('act', lambda: nc.scalar.copy(out=tt[:], in_=pb[:, 0:512].rearrange("p (j n) -> p j n", j=4)),
                     r=[pb], w=[tt])
                dst = k.rqT_d if g < 2 else k.rkT_d
                gg_ = g % 2
                P.dma('sp', dst.ap()[t][:, gg_ * 4:(gg_ + 1) * 4, :], tt[:], key=tt, r=[tt], w=[("qkT_d", g, t)])
            elif g < 8:
                P.op('act', lambda: nc.scalar.copy(out=o[:], in_=ps[:]), r=[ps], w=[o])
                P.dma('sp', k.v_d[t * 128:(t + 1) * 128, (g - 4) * 512:(g - 3) * 512], o[:], key=o, r=[o], w=[("v_d", t)])
            else:
                p_ = (g - 8) // 4
                gc = (g - 8) % 4
                P.op('act', lambda: nc.scalar.activation(out=o[:], in_=ps[:], func=AF.Silu), r=[ps], w=[o])
                P.dma('sp', k.g_d[p_][t * 128:(t + 1) * 128, gc * 512:(gc + 1) * 512], o[:], key=o, r=[o],
                      w=[("g_d", p_, t)])

        proj_tm(k, ph, uT, k.ret_w_in.ap()[i], 16, tiles_all, epi)
        P.barrier()
    with ExitStack() as ph:
        def t_(name, shape, dt):
            return sbt(ph, nc, name, shape, dt)

        lg = t_("lg", [128, 8], F32)
        dm = t_("dm", [128, 128], F32)
        msk = [t_("msk%d" % j, [128, 128], F32) for j in range(8)]
        xi = [t_("xi%d" % j, [128, 128], BF16) for j in range(8)]
        zeta = t_("zeta", [128, 8], F32)
        cd = t_("cd", [128, 8], F32)
        c128 = t_("c128", [128, 1], F32)
        tA = t_("tA", [128, 128], F32)
        tB = t_("tB", [128, 128], F32)
        P.dma('sp', lg[:], k.ret_decay.ap()[i:i + 1].rearrange("o p h -> o (p h)").to_broadcast([128, 8]), key=lg, w=[lg])
        P.op('act', lambda: nc.scalar.activation(out=lg[:], in_=lg[:], func=AF.Exp, scale=-1.0), r=[lg], w=[lg])
        P.op('dve', lambda: nc.vector.tensor_scalar(out=lg[:], in0=lg[:], scalar1=1.0, scalar2=None, op0=ALU.add),
             r=[lg], w=[lg])
        P.op('act', lambda: nc.scalar.activation(out=lg[:], in_=lg[:], func=AF.Ln), r=[lg], w=[lg])
        P.op('dve', lambda: nc.vector.tensor_scalar(out=lg[:], in0=lg[:], scalar1=-1.0, scalar2=None, op0=ALU.mult),
             r=[lg], w=[lg])
        P.op('dve', lambda: nc.vector.memset(c128[:], 128.0), w=[c128])
        P.op('dve', lambda: nc.vector.tensor_scalar(out=dm[:], in0=k.iotaf[:], scalar1=k.iotap[:, 0:1], scalar2=None,
                                                    op0=ALU.subtract), r=[k.iotaf, k.iotap], w=[dm])
        for p_ in range(2):
            sgn = 1.0 if p_ == 0 else -1.0
            for h in range(H):
                j = p_ * 4 + h
                lgc = lg[:, j:j + 1]
                P.op('dve', lambda: nc.vector.tensor_scalar(out=tA[:], in0=dm[:], scalar1=sgn, scalar2=0.0, op0=ALU.mult,
                                                            op1=ALU.max), r=[dm], w=[tA])
                P.op('act', lambda lgc=lgc: nc.scalar.activation(out=tA[:], in_=tA[:], func=AF.Exp, scale=lgc),
                     r=[tA, lg], w=[tA])
                P.op('dve', lambda: nc.vector.tensor_scalar(out=tB[:], in0=dm[:], scalar1=sgn, scalar2=0.0, op0=ALU.mult,
                                                            op1=ALU.is_ge), r=[dm], w=[tB])
                P.op('dve', lambda j=j: nc.vector.tensor_tensor(out=msk[j][:], in0=tA[:], in1=tB[:], op=ALU.mult),
                     r=[tA, tB], w=[msk[j]])
                if p_ == 0:
                    P.op('dve', lambda: nc.vector.tensor_scalar(out=tA[:], in0=k.iotaf[:], scalar1=1.0, scalar2=None,
                                                                op0=ALU.add), r=[k.iotaf], w=[tA])
                else:
                    P.op('dve', lambda: nc.vector.tensor_scalar(out=tA[:], in0=k.iotaf[:], scalar1=-1.0, scalar2=128.0,
                                                                op0=ALU.mult, op1=ALU.add), r=[k.iotaf], w=[tA])
                P.op('act', lambda j=j, lgc=lgc: nc.scalar.activation(out=xi[j][:], in_=tA[:], func=AF.Exp, scale=lgc),
                     r=[tA, lg], w=[xi[j]])
                if p_ == 0:
                    P.op('dve', lambda: nc.vector.tensor_scalar(out=tB[:, 0:1], in0=k.iotap[:], scalar1=-1.0, scalar2=127.0,
                                                                op0=ALU.mult, op1=ALU.add), r=[k.iotap], w=[tB])
                else:
                    P.op('dve', lambda: nc.vector.tensor_copy(out=tB[:, 0:1], in_=k.iotap[:]), r=[k.iotap], w=[tB])
                P.op('act', lambda j=j, lgc=lgc: nc.scalar.activation(out=zeta[:, j:j + 1], in_=tB[:, 0:1], func=AF.Exp,
                                                                     scale=lgc), r=[tB, lg], w=[zeta])
                P.op('act', lambda j=j, lgc=lgc: nc.scalar.activation(out=cd[:, j:j + 1], in_=c128[:], func=AF.Exp,
                                                                     scale=lgc), r=[c128, lg], w=[cd])
        Wo = t_("r_Wo", [128, 16, D], BF16)
        for half in range(2):
            P.dma('pool', Wo[:, half * 8:(half + 1) * 8, :],
                  k.ret_w_out.ap()[i][half * 1024:(half + 1) * 1024, :].rearrange("(f p) n -> p f n", p=128),
                  key=("Wo", half), w=[("Wo", half)])
        gg = gate_tiles(k, ph, l, 1, 2)
        R = [t_("R%d" % h, [128, 2, DV], F32) for h in range(H)]
        Rb = [t_("Rb%d" % h, [128, 2, DV], BF16) for h in range(H)]
        Fg = t_("Fg", [128, 2, DV], F32)
        qTb = Rot([t_("qTc%d" % j, [128, 8, 128], BF16) for j in range(3)])
        kTb = Rot([t_("kTc%d" % j, [128, 8, 128], BF16) for j in range(3)])
        kb = Rot([t_("kc%d" % j, [128, 1024], BF16) for j in range(3)])
        vb = Rot([t_("vc%d" % j, [128, 2048], BF16) for j in range(3)])
        gb = Rot([t_("gc%d" % j, [128, 2048], BF16) for j in range(3)])
        psr = Rot(k.psf[0:6])
        yb = Rot([t_("y%d" % j, [128, 2048], F32) for j in range(2)])
        y1b = Rot([t_("y1_%d" % j, [128, 2048], F32) for j in range(3)])
        ybf = Rot([t_("ybf%d" % j, [128, 2048], BF16) for j in range(2)])
        yTb = Rot([t_("yT%d" % j, [128, 16, 128], BF16) for j in range(2)])
        sTb = Rot([t_("sT%d" % j, [128, 128], BF16) for j in range(8)])
        qxb = Rot([t_("qx%d" % j, [128, 2, 128], BF16) for j in range(8)])
        kzb = Rot([t_("kz%d" % j, [128, 256], BF16) for j in range(8)])
        hb = Rot([t_("r_h%d" % j, [128, D], F32) for j in range(2)])
        ob2 = Rot([t_("r_o%d" % j, [128, D], F32) for j in range(2)])
        st4 = Rot([[t_("r_s%d_%d" % (a, b), [128, 1], F32) for b in range(4)] for a in range(2)])
        st3 = Rot([[t_("r_n%d_%d" % (a, b), [128, 1], F32) for b in range(3)] for a in range(8)])

        def zero_state():
            for h in range(H):
                P.op('pool', lambda h=h: nc.gpsimd.memset(R[h][:], 0.0), w=[R[h]])
                P.op('pool', lambda h=h: nc.gpsimd.memset(Rb[h][:], 0.0), w=[Rb[h]])

        def load_chunk(p_, t):
            qTc, kTc, kc, vc, gc = qTb.next(), kTb.next(), kb.next(), vb.next(), gb.next()
            cs = slice(t * 128, (t + 1) * 128)
            P.dma('sp', qTc[:], k.rqT_d.ap()[t], key=qTc, r=[("qkT_d", g, t) for g in (0, 1)], w=[qTc])
            P.dma('sp', kTc[:], k.rkT_d.ap()[t], key=kTc, r=[("qkT_d", g, t) for g in (2, 3)], w=[kTc])
            P.dma('sp', kc[:], k.k_d[cs, :], key=kc, r=[("k_d", t)], w=[kc])
            P.dma('sp', vc[:], k.v_d[cs, :], key=vc, r=[("v_d", t)], w=[vc])
            P.dma('sp', gc[:], k.g_d[p_][cs, :], key=gc, r=[("g_d", p_, t)], w=[gc])
            y1 = None
            if p_ == 1:
                y1 = y1b.next()
                P.dma('sp', y1[:], k.y_d[cs, :], key=y1, r=[("y_d", t)], w=[y1])
            return qTc, kTc, kc, vc, gc, y1

        def stage_ab(p_, t, bufs):
            qTc, kTc, kc, vc, gc, y1 = bufs
            loc = dict(sT=[], qx=[], kz=[])
            for h in range(H):
                j = p_ * 4 + h
                kz, qx = kzb.next(), qxb.next()
                P.op('act', lambda: nc.scalar.activation(out=kz[:], in_=kc[:, h * DK:(h + 1) * DK], func=AF.Copy,
                                                         scale=zeta[:, j:j + 1]), r=[kc, zeta], w=[kz])
                P.op('pool', lambda: nc.gpsimd.tensor_tensor(
                    out=qx[:], in0=qTc[:, h * 2:h * 2 + 2, :], in1=xi[j][:].unsqueeze(1).to_broadcast([128, 2, 128]),
                    op=ALU.mult), r=[qTc, xi[j]], w=[qx])
                loc["kz"].append(kz)
                loc["qx"].append(qx)
            for h in range(H):
                j = p_ * 4 + h
                ps_s = psr.next()
                for c2 in range(2):
                    P.op('pe', lambda: nc.tensor.matmul(ps_s[:, 0:128], lhsT=kTc[:, h * 2 + c2, :],
                                                        rhs=qTc[:, h * 2 + c2, :], start=(c2 == 0), stop=(c2 == 1)),
                         r=[kTc, qTc], w=[ps_s], sig=(c2 == 1))
                sT = sTb.next()
                P.op('dve', lambda: nc.vector.tensor_tensor(out=sT[:], in0=ps_s[:, 0:128], in1=msk[j][:],
                                                            op=ALU.mult), r=[ps_s, msk[j]], w=[sT])
                loc["sT"].append(sT)
            return loc

        def stage_cd(p_, t, bufs, loc):
            qTc, kTc, kc, vc, gc, y1 = bufs
            cs = slice(t * 128, (t + 1) * 128)
            y = yb.next()
            if p_ == 1:
                yf = ybf.next()
            for h in range(H):
                j = p_ * 4 + h
                hs = slice(h * DV, (h + 1) * DV)
                kz = loc["kz"][h]
                for c2 in range(2):
                    ps_k = psr.next()
                    P.op('pe', lambda: nc.tensor.matmul(ps_k[:], lhsT=kz[:, c2 * 128:(c2 + 1) * 128], rhs=vc[:, hs],
                                                        start=True, stop=True), r=[kz, vc], w=[ps_k])
                    P.op('dve', lambda: nc.vector.scalar_tensor_tensor(
                        out=R[h][:, c2, :], in0=R[h][:, c2, :], scalar=cd[:, j:j + 1], in1=ps_k[:], op0=ALU.mult,
                        op1=ALU.add), r=[R[h], cd, ps_k], w=[R[h]])
            for h in range(H):
                hs = slice(h * DV, (h + 1) * DV)
                sT, qx = loc["sT"][h], loc["qx"][h]
                ss, ms, rstd = st3.next()
                ps_o = psr.next()
                P.op('pe', lambda: nc.tensor.matmul(ps_o[:], lhsT=sT[:], rhs=vc[:, hs], start=True, stop=False),
                     r=[sT, vc], w=[ps_o], sig=False)
                for c2 in range(2):
                    P.op('pe', lambda: nc.tensor.matmul(ps_o[:], lhsT=qx[:, c2, :], rhs=Rb[h][:, c2, :], start=False,
                                                        stop=(c2 == 1)), r=[qx, Rb[h]], w=[ps_o], sig=(c2 == 1))
                P.op('act', lambda: nc.scalar.activation(out=k.junk[:, 0:512], in_=ps_o[:], func=AF.Square, accum_out=ss[:]),
                     r=[ps_o], w=[ss])
                rstd_from_ss(k, ss, ms, rstd, DV)
                P.op('dve', lambda: nc.vector.scalar_tensor_tensor(out=y[:, hs], in0=ps_o[:], scalar=rstd[:, 0:1],
                                                                   in1=gc[:, hs], op0=ALU.mult, op1=ALU.mult),
                     r=[ps_o, rstd, gc], w=[("y", y.name, h)])
                if p_ == 1:
                    P.op('pool', lambda: nc.gpsimd.tensor_tensor(out=yf[:, hs], in0=y[:, hs], in1=y1[:, hs], op=ALU.add),
                         r=[("y", y.name, h), y1], w=[("yf", yf.name, h)])
            for h in range(H):
                P.op('act', lambda: nc.scalar.copy(out=Rb[h][:], in_=R[h][:]), r=[R[h]], w=[Rb[h]])
            if p_ == 0:
                P.dma('pool', k.y_d[cs, :], y[:], key=y, r=[("y", y.name, h) for h in range(H)], w=[("y_d", t)])
            else:
                yT = yTb.next()
                for half in range(2):
                    pb = k.psb_rot.next()
                    for jj in range(8):
                        kk = half * 8 + jj
                        P.op('pe', lambda: nc.tensor.transpose(pb[:, jj * 128:(jj + 1) * 128],
                                                               yf[:, kk * 128:(kk + 1) * 128], k.ident[:]),
                             r=[("yf", yf.name, kk // 4), k.ident], w=[pb], sig=(jj == 7))
                    P.op('act', lambda: nc.scalar.copy(out=yT[:, half * 8:(half + 1) * 8, :],
                                                       in_=pb[:].rearrange("p (j n) -> p j n", j=8)),
                         r=[pb], w=[("yTh", yT.name, half)])
                pss = [psr.next(), psr.next()]
                for cg in range(2):
                    for kk in range(16):
                        P.op('pe', lambda: nc.tensor.matmul(pss[cg][:], lhsT=yT[:, kk, :],
                                                            rhs=Wo[:, kk, cg * 512:(cg + 1) * 512],
                                                            start=(kk == 0), stop=(kk == 15)),
                             r=[("yTh", yT.name, kk // 8), ("Wo", kk // 8)], w=[pss[cg]], sig=(kk == 15))
                resid_epilogue(k, pss, t, gg[0 if t < TL else 1], hb.next(), ob2.next(), st4.next())

        def run_chunks(p_, ts):
            bufs = {0: load_chunk(p_, ts[0])}
            if len(ts) > 1:
                bufs[1] = load_chunk(p_, ts[1])
            locs = {0: stage_ab(p_, ts[0], bufs[0])}
            for n_, t in enumerate(ts):
                if n_ + 2 < len(ts):
                    bufs[n_ + 2] = load_chunk(p_, ts[n_ + 2])
                if n_ + 1 < len(ts):
                    locs[n_ + 1] = stage_ab(p_, ts[n_ + 1], bufs[n_ + 1])
                stage_cd(p_, t, bufs.pop(n_), locs.pop(n_))

        zero_state()
        run_chunks(0, [TL, TL + 1] + list(range(TL)))
        for h in range(H):
            P.dma('sp', k.F_d.ap()[h * 256:(h + 1) * 256, :].rearrange("(c p) n -> p c n", p=128), R[h][:], key=R[h],
                  r=[R[h]], w=["F_d"])
        P.op('pool', lambda: nc.gpsimd.collective_compute("AllGather", ALU.bypass, replica_groups=GROUPS,
                                                          ins=[k.F_d.ap().opt()], outs=[k.Fg_d.ap().opt()]),
             r=["F_d"], w=["Fg_d"], dma="cc", inc=1)
        zero_state()
        run_chunks(1, [TL + 1, TL])
        for h in range(H):
            for rk_ in range(2):
                P.dma('sp', Fg[:], k.Fg_d.ap()[rk_ * 1024 + h * 256:rk_ * 1024 + (h + 1) * 256, :].rearrange(
                    "(c p) n -> p c n", p=128), key=Fg, r=["Fg_d"], w=[Fg])
                if rk_ == 0:
                    P.op('dve', lambda h=h: nc.vector.tensor_scalar(out=R[h][:], in0=Fg[:], scalar1=k.selt[:, 0:1],
                                                                  scalar2=None, op0=ALU.mult), r=[Fg, k.selt], w=[R[h]])
                else:
                    P.op('dve', lambda h=h: nc.vector.scalar_tensor_tensor(out=R[h][:], in0=Fg[:], scalar=k.selt[:, 1:2],
                                                                         in1=R[h][:], op0=ALU.mult, op1=ALU.add),
                         r=[Fg, k.selt, R[h]], w=[R[h]])
            P.op('act', lambda h=h: nc.scalar.copy(out=Rb[h][:], in_=R[h][:]), r=[R[h]], w=[Rb[h]])
        run_chunks(1, list(range(TL - 1, -1, -1)))
        P.barrier()


def diff_layer(k, l, last):
    import math
    nc, P = k.nc, k.P
    i = l // 2
    lam_init = 0.8 - 0.6 * math.exp(-0.3 * l)
    NH = 8
    if not hasattr(k, "qT_d"):
        k.qT_d = nc.dram_tensor("qT_d", [8, 128, NT], BF16)
    if not hasattr(k, "kTl_d"):
        k.kTl_d = [nc.dram_tensor("kTl%d_d" % g, [512, NL], BF16) for g in range(2)]
        k.kTc_d = nc.dram_tensor("kTc_d", [1024, NCX], BF16)
        k.kTg_d = [nc.dram_tensor("kTg%d_d" % g, [1024, NL], BF16) for g in range(2)]
        k.vl_d = [nc.dram_tensor("vl%d_d" % g, [NL, 512], BF16) for g in range(2)]
        k.vc_d = nc.dram_tensor("vc_d", [NCX, 1024], BF16)
        k.vg_d = [nc.dram_tensor("vg%d_d" % g, [2 * NL, 512], BF16) for g in range(2)]
    tiles_all = list(range(TT))
    q_tiles = list(range(TL)) + ([] if last else [TL, TL + 1])
    with ExitStack() as ph:
        uT = sbt(ph, nc, "d_uT", [128, 8, NT], BF16)
        with ExitStack() as ph1:
            prenorm(k, ph1, l, 0, 0, 1, uT, tiles_all)
            P.barrier()
        cos = sbt(ph, nc, "d_cos", [128, TL, 2, 16], F32)
        sin = sbt(ph, nc, "d_sin", [128, TL, 2, 16], F32)
        P.dma('sp', cos[:].rearrange("p t r f -> p (t r f)"), k.ropeD_d.ap()[0], key=cos, w=["rope_tab"])
        P.dma('sp', sin[:].rearrange("p t r f -> p (t r f)"), k.ropeD_d.ap()[1], key=sin, w=["rope_tab"])
        tmp = [sbt(ph, nc, "d_tmp%d" % j, [128, 256], F32) for j in range(4)]
        ob = Rot([sbt(ph, nc, "d_ob%d" % j, [128, 512], BF16) for j in range(3)])
        tb = Rot([sbt(ph, nc, "d_tb%d" % j, [128, 4, 128], BF16) for j in range(3)])

        def epi_qk(isq):
            def epi(t, g, ps):
                if isq and t not in q_tiles:
                    return
                o = ob.next()
                scale = 0.125 if isq else 1.0
                if t < TL:
                    rope_apply(k, ps, o, cos[:, t], sin[:, t], 8, 16, scale, tmp)
                else:
                    P.op('act', lambda: nc.scalar.activation(out=o[:], in_=ps[:], func=AF.Copy, scale=float(scale)),
                         r=[ps], w=[o])
                pb = k.psb_rot.next()
                for j in range(4):
                    P.op('pe', lambda j=j: nc.tensor.transpose(pb[:, j * 128:(j + 1) * 128], o[:, j * 128:(j + 1) * 128],
                                                              k.ident[:]), r=[o, k.ident], w=[pb], sig=(j == 3))
                tt = tb.next()
                P.op('act', lambda: nc.scalar.copy(out=tt[:], in_=pb[:, 0:512].rearrange("p (j n) -> p j n", j=4)),
                     r=[pb], w=[tt])
                if isq:
                    P.dma('sp', k.qT_d.ap()[g * 4:(g + 1) * 4, :, t * 128:(t + 1) * 128].rearrange("j p n -> p j n"), tt[:],
                          key=tt, r=[tt], w=[("qT_d", g, t)])
                elif t < TL:
                    P.dma('sp', k.kTl_d[g].ap()[:, t * 128:(t + 1) * 128].rearrange("(j p) n -> p j n", p=128),
                          tt[:], key=tt, r=[tt], w=[("kTl_d", g, t)])
                    if t == TL - 1:
                        P.op('pool', lambda: nc.gpsimd.collective_compute(
                            "AllGather", ALU.bypass, replica_groups=GROUPS, ins=[k.kTl_d[g].ap().opt()],
                            outs=[k.kTg_d[g].ap().opt()]), r=[("kTl_d", g, t_) for t_ in range(TL)], w=[("kTg_d", g)],
                            dma="cc", inc=1)
                else:
                    P.dma('sp', k.kTc_d.ap()[g * 512:(g + 1) * 512, (t - TL) * 128:(t - TL + 1) * 128].rearrange(
                        "(j p) n -> p j n", p=128), tt[:], key=tt, r=[tt], w=[("kTc_d", g, t)])
            return epi

        def epi_v(t, g, ps):
            o = ob.next()
            P.op('act', lambda: nc.scalar.copy(out=o[:], in_=ps[:]), r=[ps], w=[o])
            if t < TL:
                P.dma('sp', k.vl_d[g][t * 128:(t + 1) * 128, :], o[:], key=o, r=[o], w=[("vl_d", g, t)])
                if t == TL - 1:
                    P.op('pool', lambda: nc.gpsimd.collective_compute(
                        "AllGather", ALU.bypass, replica_groups=GROUPS, ins=[k.vl_d[g].ap().opt()],
                        outs=[k.vg_d[g].ap().opt()]), r=[("vl_d", g, t_) for t_ in range(TL)], w=[("vg_d", g)],
                        dma="cc", inc=1)
            else:
                P.dma('sp', k.vc_d[(t - TL) * 128:(t - TL + 1) * 128, g * 512:(g + 1) * 512], o[:], key=o, r=[o],
                      w=[("vc_d", g, t)])

        wi = k.diff_w_in.ap()[i]
        proj_tm(k, ph, uT, wi[:, 1024:2048], 2, tiles_all, epi_qk(False))
        proj_tm(k, ph, uT, wi[:, 2048:3072], 2, tiles_all, epi_v)
        proj_tm(k, ph, uT, wi[:, 0:1024], 2, tiles_all, epi_qk(True))
        P.barrier()
    with ExitStack() as ph:
        def t_(name, shape, dt):
            return sbt(ph, nc, name, shape, dt)

        NK = 2 * NL + NCX
        NKT = NK // 128
        yT = t_("d_yT", [128, NH, NT], BF16)
        Wo = t_("d_Wo", [128, 8, D], BF16)
        P.dma('pool', Wo[:], k.diff_w_out.ap()[i].rearrange("(f p) n -> p f n", p=128), key=Wo, w=[Wo])
        gg = gate_tiles(k, ph, l, 1, 2)
        ones = t_("d_ones", [128, 128], BF16)
        P.op('dve', lambda: nc.vector.memset(ones[:], 1.0), w=[ones])
        lv = t_("d_lv", [128, 4, 64], F32)
        lp = t_("d_lp", [128, 2, 64], F32)
        ls = t_("d_ls", [128, 2], F32)
        nlam = t_("d_nlam", [128, 1], F32)
        gsub = t_("d_gsub", [128, 1], F32)
        P.dma('sp', lv[:].rearrange("p a b -> p (a b)"),
              k.diff_lam.ap()[i:i + 1].rearrange("o a b -> o (a b)").to_broadcast([128, 256]), key=lv, w=[lv])
        for a in range(2):
            P.op('dve', lambda a=a: nc.vector.tensor_tensor(out=lp[:, a, :], in0=lv[:, 2 * a, :], in1=lv[:, 2 * a + 1, :],
                                                          op=ALU.mult), r=[lv], w=[lp])
            P.op('dve', lambda a=a: nc.vector.reduce_sum(out=ls[:, a:a + 1], in_=lp[:, a, :], axis=AX.X), r=[lp], w=[ls])
        P.op('act', lambda: nc.scalar.activation(out=ls[:], in_=ls[:], func=AF.Exp), r=[ls], w=[ls])
        P.op('dve', lambda: nc.vector.scalar_tensor_tensor(out=nlam[:], in0=ls[:, 1:2], scalar=-float(lam_init),
                                                           in1=ls[:, 0:1], op0=ALU.add, op1=ALU.subtract),
             r=[ls], w=[nlam])
        P.dma('sp', gsub[:], k.diff_subln.ap()[i].rearrange("(p o) -> p o", o=1), key=gsub, w=[gsub],
              allow_slow_non_contiguous=True)
        P.op('dve', lambda: nc.vector.tensor_scalar(out=gsub[:], in0=gsub[:], scalar1=float(1.0 - lam_init), scalar2=None,
                                                    op0=ALU.mult), r=[gsub], w=[gsub])
        kTb = Rot([t_("d_kT%d" % j, [128, NK], BF16) for j in range(2)])
        Vb = Rot([t_("d_V%d" % j, [128, NKT, 128], BF16) for j in range(2)])
        qAb = Rot([t_("d_qA%d" % j, [128, NT], BF16) for j in range(2)])
        qBb = Rot([t_("d_qB%d" % j, [128, NT], BF16) for j in range(2)])
        for qa in qAb.items:
            P.op('pool', lambda qa=qa: nc.gpsimd.memset(qa[64:128, :], 0.0), w=[("qz", qa.name)])
        for qb_ in qBb.items:
            P.op('pool', lambda qb_=qb_: nc.gpsimd.memset(qb_[0:64, :], 0.0), w=[("qz", qb_.name)])
        eb = Rot([t_("d_e%d" % j, [128, 512], BF16) for j in range(8)])
        accD = Rot([t_("d_aD%d" % j, [128, 512], F32) for j in range(2)])
        accP = Rot([t_("d_aP%d" % j, [128, 512], F32) for j in range(2)])
        zhi = Rot([t_("d_zh%d" % j, [128, 512], BF16) for j in range(2)])
        zlo = Rot([t_("d_zl%d" % j, [128, 512], BF16) for j in range(2)])
        rzb = Rot([t_("d_rz%d" % j, [128, 512], F32) for j in range(2)])
        Onb = [Rot([t_("d_On%d_%d" % (w_, j), [128, 512], F32) for j in range(2)]) for w_ in range(2)]
        OTb = Rot([t_("d_OT%d" % j, [128, 512], F32) for j in range(2)])
        sqb = Rot([t_("d_sq%d" % j, [128, 512], BF16) for j in range(2)])
        rsb = Rot([t_("d_rs%d" % j, [128, 512], F32) for j in range(2)])
        lnb = Rot([t_("d_ln%d" % j, [128, 512], F32) for j in range(2)])
        hb = Rot([t_("d_h%d" % j, [128, D], F32) for j in range(2)])
        ob2 = Rot([t_("d_o%d" % j, [128, D], F32) for j in range(2)])
        st4 = Rot([[t_("d_s%d_%d" % (a, b), [128, 1], F32) for b in range(4)] for a in range(2)])
        ps_s = Rot(k.psf[0:3])
        ps_o = Rot(k.psf[3:5])
        ps_z = Rot(k.psf[5:6])

        def load_head(h):
            kT, V, qA, qB = kTb.next(), Vb.next(), qAb.next(), qBb.next()
            hg_, hh = h // 4, h % 4
            for rk_ in range(2):
                P.dma('sp', kT[:, rk_ * NL:(rk_ + 1) * NL],
                      k.kTg_d[hg_][rk_ * 512 + hh * 128:rk_ * 512 + (hh + 1) * 128, :],
                      key=("kT", kT.name, rk_), r=[("kTg_d", hg_)], w=[("kTp", kT.name, rk_)])
                P.dma('sp', V[:, rk_ * TL:(rk_ + 1) * TL, :],
                      k.vg_d[hg_].ap()[rk_ * NL:(rk_ + 1) * NL, hh * 128:(hh + 1) * 128].rearrange("(j p) d -> p j d", p=128),
                      key=("V", V.name, rk_), r=[("vg_d", hg_)], w=[("Vp", V.name, rk_)])
            P.dma('sp', kT[:, 2 * NL:NK], k.kTc_d[h * 128:(h + 1) * 128, :], key=("kT", kT.name, 2),
                  r=[("kTc_d", h // 4, t_) for t_ in (TL, TL + 1)], w=[("kTp", kT.name, 2)])
            P.dma('sp', V[:, 2 * TL:NKT, :], k.vc_d.ap()[:, h * 128:(h + 1) * 128].rearrange("(j p) d -> p j d", p=128),
                  key=("V", V.name, 2), r=[("vc_d", h // 4, t_) for t_ in (TL, TL + 1)], w=[("Vp", V.name, 2)])
            qr = [("qT_d", h // 4, t) for t in q_tiles]
            P.dma('sp', qA[0:64, :], k.qT_d.ap()[h, 0:64, :], key=qA, r=qr, w=[qA])
            P.dma('sp', qB[64:128, :], k.qT_d.ap()[h, 64:128, :], key=qB, r=qr, w=[qB])
            return kT, V, qA, qB

        blocks = [(qb * 512, 512, list(range(NKT))) for qb in range(4)]
        if not last:
            blocks.append((NL, NCX, [NKT - 2, NKT - 1]))
        items = []
        for h in range(NH):
            for (q0, nq, ktiles) in blocks:
                for w_ in range(2):
                    for jn, jt in enumerate(ktiles):
                        items.append(dict(h=h, q0=q0, nq=nq, w=w_, jn=jn, jt=jt, first=(jn == 0),
                                          last=(jn == len(ktiles) - 1)))
        heads = {}
        state = {}

        def emit_S(it):
            h = it["h"]
            if h not in heads:
                assert h == 0
                heads[h] = load_head(h)
            kT, V, qA, qB = heads[h]
            q = qA if it["w"] == 0 else qB
            nq, q0, jt = it["nq"], it["q0"], it["jt"]
            ps = ps_s.next()
            e = eb.next()
            it["e"] = e
            P.op('pe', lambda: nc.tensor.matmul(ps[:, 0:nq], lhsT=kT[:, jt * 128:(jt + 1) * 128], rhs=q[:, q0:q0 + nq],
                                                start=True, stop=True),
                 r=[("kTp", kT.name, min(jt // TL, 2)), q, ("qz", q.name)], w=[ps])
            P.op('act', lambda: nc.scalar.activation(out=e[:, 0:nq], in_=ps[:, 0:nq], func=AF.Exp), r=[ps], w=[e])

        def emit_PV(it):
            h, w_, nq, q0, jt, jn = it["h"], it["w"], it["nq"], it["q0"], it["jt"], it["jn"]
            kT, V, qA, qB = heads[h]
            e = it["e"]
            if it["first"] and w_ == 0 and q0 == 0 and h + 1 < NH:
                heads[h + 1] = load_head(h + 1)
            if it["first"]:
                state["pO"] = ps_o.next()
                state["aD"], state["aP"] = accD.next(), accP.next()
                state["usedP"] = False
            pO, aD, aP = state["pO"], state["aD"], state["aP"]
            P.op('pe', lambda: nc.tensor.matmul(pO[:, 0:nq], lhsT=V[:, jt, :], rhs=e[:, 0:nq], start=it["first"],
                                                stop=it["last"]),
                 r=[("Vp", V.name, min(jt // TL, 2)), e], w=[pO], sig=it["last"])
            if it["first"]:
                state["pZ"] = ps_z.next()
                state["zstart"] = True
                state["usedD"] = False
            pZ = state["pZ"]
            who = ("dve", "pool", "dve", "pool", "dve")[jn % 5]
            if who == "pe":
                P.op('pe', lambda: nc.tensor.matmul(pZ[:, 0:nq], lhsT=ones[:], rhs=e[:, 0:nq], start=state["zstart"],
                                                    stop=False), r=[ones, e], w=[pZ], sig=False)
                state["zstart"] = False
            elif who == "pool":
                if not state["usedP"]:
                    P.op('pool', lambda: nc.gpsimd.tensor_copy(out=aP[:, 0:nq], in_=e[:, 0:nq]), r=[e], w=[aP])
                    state["usedP"] = True
                else:
                    P.op('pool', lambda: nc.gpsimd.tensor_tensor(out=aP[:, 0:nq], in0=aP[:, 0:nq], in1=e[:, 0:nq],
                                                                 op=ALU.add), r=[e, aP], w=[aP])
            elif not state["usedD"]:
                P.op('dve', lambda: nc.vector.tensor_copy(out=aD[:, 0:nq], in_=e[:, 0:nq]), r=[e], w=[aD])
                state["usedD"] = True
            else:
                P.op('dve', lambda: nc.vector.tensor_tensor(out=aD[:, 0:nq], in0=aD[:, 0:nq], in1=e[:, 0:nq], op=ALU.add),
                     r=[e, aD], w=[aD])
            if not it["last"]:
                return
            if state["usedP"]:
                P.op('dve', lambda: nc.vector.tensor_tensor(out=aD[:, 0:nq], in0=aD[:, 0:nq], in1=aP[:, 0:nq], op=ALU.add),
                     r=[aD, aP], w=[aD])
            zh, zl, rz = zhi.next(), zlo.next(), rzb.next()
            P.op('dve', lambda: nc.vector.tensor_copy(out=zh[:, 0:nq], in_=aD[:, 0:nq]), r=[aD], w=[zh])
            P.op('dve', lambda: nc.vector.tensor_tensor(out=zl[:, 0:nq], in0=aD[:, 0:nq], in1=zh[:, 0:nq], op=ALU.subtract),
                 r=[aD, zh], w=[zl])
            P.op('pe', lambda: nc.tensor.matmul(pZ[:, 0:nq], lhsT=ones[:], rhs=zh[:, 0:nq], start=state["zstart"],
                                                stop=False), r=[ones, zh], w=[pZ], sig=False)
            P.op('pe', lambda: nc.tensor.matmul(pZ[:, 0:nq], lhsT=ones[:], rhs=zl[:, 0:nq], start=False, stop=True),
                 r=[ones, zl], w=[pZ])
            On = Onb[w_].next()
            P.op('act', lambda: nc.scalar.activation(out=rz[:, 0:nq], in_=pZ[:, 0:nq], func=AF.Ln), r=[pZ], w=[rz])
            P.op('act', lambda: nc.scalar.activation(out=rz[:, 0:nq], in_=rz[:, 0:nq], func=AF.Exp, scale=-1.0),
                 r=[rz], w=[rz])
            P.op('dve', lambda: nc.vector.tensor_tensor(out=On[:, 0:nq], in0=pO[:, 0:nq], in1=rz[:, 0:nq], op=ALU.mult),
                 r=[pO, rz], w=[On])
            state["On%d" % w_] = On
            if w_ == 0:
                return
            On0, On1 = state["On0"], state["On1"]
            OT, sq, rs, ln_ = OTb.next(), sqb.next(), rsb.next(), lnb.next()
            P.op('dve', lambda: nc.vector.scalar_tensor_tensor(out=OT[:, 0:nq], in0=On1[:, 0:nq], scalar=nlam[:, 0:1],
                                                               in1=On0[:, 0:nq], op0=ALU.mult, op1=ALU.add),
                 r=[On0, On1, nlam], w=[OT])
            P.op('act', lambda: nc.scalar.activation(out=sq[:, 0:nq], in_=OT[:, 0:nq], func=AF.Square), r=[OT], w=[sq])
            pS = ps_z.next()
            P.op('pe', lambda: nc.tensor.matmul(pS[:, 0:nq], lhsT=ones[:], rhs=sq[:, 0:nq], start=True, stop=True),
                 r=[ones, sq], w=[pS])
            P.op('act', lambda: nc.scalar.activation(out=ln_[:, 0:nq], in_=pS[:, 0:nq], func=AF.Ln, scale=1.0 / 128,
                                                     bias=k.epsb[:, 0:1]), r=[pS, k.epsb], w=[ln_])
            P.op('act', lambda: nc.scalar.activation(out=rs[:, 0:nq], in_=ln_[:, 0:nq], func=AF.Exp, scale=-0.5),
                 r=[ln_], w=[rs])
            P.op('dve', lambda: nc.vector.scalar_tensor_tensor(out=yT[:, h, q0:q0 + nq], in0=OT[:, 0:nq],
                                                               scalar=gsub[:, 0:1], in1=rs[:, 0:nq], op0=ALU.mult,
                                                               op1=ALU.mult), r=[OT, gsub, rs], w=[("yT", h, q0)])

        pend = []
        for it in items:
            emit_S(it)
            pend.append(it)
            if len(pend) > 2:
                emit_PV(pend.pop(0))
        while pend:
            emit_PV(pend.pop(0))
        for t in q_tiles:
            pss = [k.psf_rot.next(), k.psf_rot.next()]
            for cg in range(2):
                for kk in range(8):
                    P.op('pe', lambda cg=cg, kk=kk: nc.tensor.matmul(pss[cg][:], lhsT=yT[:, kk, t * 128:(t + 1) * 128],
                                                                   rhs=Wo[:, kk, cg * 512:(cg + 1) * 512],
                                                                   start=(kk == 0), stop=(kk == 7)),
                         r=[("yT", kk, (t * 128 // 512) * 512 if t < TL else NL), Wo], w=[pss[cg]], sig=(kk == 7))
            resid_epilogue(k, pss, t, gg[0 if t < TL else 1], hb.next(), ob2.next(), st4.next())
        P.barrier()

def prep_inputs(inp):
    f = lambda a: np.ascontiguousarray(np.asarray(a, dtype=np.float32))
    x, c, ctx, c_ctx = f(inp["x"]), f(inp["c"]), f(inp["ctx"]), f(inp["c_ctx"])
    shared = {n: f(inp[n]) for n in ("ada_w", "ada_b", "norm_g", "ret_w_out", "diff_w_in", "diff_w_out",
                                     "diff_subln", "ffn_w_up", "ffn_conv_b", "ffn_w_down")}
    shared["diff_lam"] = np.ascontiguousarray(np.stack([f(inp["diff_lq1"]), f(inp["diff_lk1"]), f(inp["diff_lq2"]),
                                                        f(inp["diff_lk2"])], axis=1))
    rw = f(inp["ret_w_in"])
    rw_sw = np.ascontiguousarray(np.concatenate([rw[:, :, :4096], rw[:, :, 6144:8192], rw[:, :, 4096:6144]], axis=2))
    dec = np.stack([f(inp["ret_decay_f"]), f(inp["ret_decay_b"])], axis=1)
    dec_sw = np.ascontiguousarray(dec[:, ::-1, :])
    cwv = f(inp["ffn_conv_w"])
    cw_sw = np.ascontiguousarray(cwv[:, ::-1, :])
    maps = []
    for core in range(8):
        b, half = core // 2, core % 2
        xs = x[b, half * NL:(half + 1) * NL]
        cs = ctx[b]
        pos = np.arange(half * NL, (half + 1) * NL)
        if half:
            xs, cs, pos = xs[::-1], cs[::-1], pos[::-1]
        posrc = np.stack([pos // 64, pos % 64], axis=-1).astype(np.float32)
        m = dict(shared)
        m["x"] = np.ascontiguousarray(xs)
        m["ctx"] = np.ascontiguousarray(cs)
        m["c2"] = np.ascontiguousarray(np.stack([c[b], c_ctx]))
        sel = np.zeros((128, 2), np.float32)
        sel[:, 1 - half] = 1.0
        m["sel"] = sel
        m["pos"] = np.ascontiguousarray(posrc.reshape(TL, 128, 2).transpose(1, 0, 2))
        m["ret_w_in"] = rw_sw if half else rw
        m["ret_decay"] = dec_sw if half else np.ascontiguousarray(dec)
        m["ffn_conv_w"] = cw_sw if half else cwv
        maps.append(m)
    return maps


def gather_out(results):
    out = np.empty((4, 2 * NL, D), np.float32)
    for core in range(8):
        b, half = core // 2, core % 2
        o = np.asarray(results[core]["out"])[:NL]
        out[b, half * NL:(half + 1) * NL] = o[::-1] if half else o
    return out


def kernel(**inputs):
    nc = build("full")
    maps = prep_inputs(inputs)
    res = run_bass_kernel_spmd(nc, maps, core_ids=list(range(8)))
    return gather_out(res.results)
```

```python
import numpy as np
from contextlib import ExitStack
import concourse.bass as bass
import concourse.mybir as mybir
from concourse.bass_utils import run_bass_kernel_spmd

F32 = mybir.dt.float32
BF16 = mybir.dt.bfloat16
I32 = mybir.dt.int32
AF = mybir.ActivationFunctionType
ALU = mybir.AluOpType
AX = mybir.AxisListType

D = 1024
NL = 2048
NCX = 256
NT = NL + NCX
TL, TCX, TT = 16, 2, 18
DEPTH = 4
FH = 2816
NFC = 22
EPS = 1e-6
GROUPS = [[0, 1], [2, 3], [4, 5], [6, 7]]


class Prog:
    def __init__(self, nc, stack):
        self.nc = nc
        self.stack = stack
        self.E = dict(pe=nc.tensor, act=nc.scalar, dve=nc.vector, pool=nc.gpsimd, sp=nc.sync)
        self.phys = []
        self.pcnt = []
        self.key2p = {}
        self.free = {}
        self.pclass = {}
        self.lastw = {}
        self.rd = {}
        self.dmalast = {}
        self.waited = {e: {} for e in self.E}
        self.nins = 0

    def _sem(self, k, cls="c"):
        if k not in self.key2p:
            fl = self.free.setdefault(cls, [])
            if fl:
                p = fl.pop()
            else:
                p = len(self.phys)
                self.phys.append(self.stack.enter_context(self.nc.semaphore("s%d" % p)))
                self.pcnt.append(0)
                self.pclass[p] = cls
            self.key2p[k] = p
        p = self.key2p[k]
        assert self.pclass[p] == cls, (k, cls, self.pclass[p])
        return p

    @staticmethod
    def _rk(x):
        return x if isinstance(x, (str, tuple)) else ("T", x.name)

    def op(self, eng, fn, r=(), w=(), sig=True, dma=None, inc=None):
        r = [self._rk(x) for x in r]
        w = [self._rk(x) for x in w]
        if dma is not None:
            dma = self._rk(dma)
        kind = 'd' if dma is not None else 'c'
        if dma is None:
            p = self._sem(eng)
        else:
            p = self._sem(('d', dma), "cc" if inc == 1 else ("sw" if eng == "pool" else "hw"))
        deps = {}

        def add(tok, raw):
            k, v, e, kd = tok
            if kd == 'c' and kind == 'c' and e == eng:
                if eng == 'pe' or not raw:
                    return
            if deps.get(k, 0) < v:
                deps[k] = v

        for x in r:
            if x in self.lastw:
                add(self.lastw[x], True)
        for x in w:
            if x in self.lastw:
                add(self.lastw[x], False)
            for tok in self.rd.get(x, {}).values():
                add(tok, False)
        if dma is not None and dma in self.dmalast:
            add(self.dmalast[dma], False)
        wd = self.waited[eng]
        for k, v in deps.items():
            if wd.get(k, 0) >= v:
                continue
            assert self.pcnt[k] >= v, ("wait on unsignaled op", k, v, self.pcnt[k])
            self.E[eng].wait_ge(self.phys[k], v)
            wd[k] = v
            self.nins += 1
        ins = fn()
        self.nins += 1
        if sig:
            if inc is None:
                inc = 16 if kind == 'd' else 1
            self.pcnt[p] += inc
            ins.then_inc(self.phys[p], inc)
            val = self.pcnt[p]
        else:
            val = self.pcnt[p] + 1
        tok = (p, val, eng, kind)
        for x in r:
            self.rd.setdefault(x, {})[p] = tok
        for x in w:
            self.lastw[x] = tok
            self.rd[x] = {}
        if dma is not None:
            self.dmalast[dma] = tok

    def dma(self, q, out, in_, key, r=(), w=(), **kw):
        self.op(q, lambda: self.E[q].dma_start(out=out, in_=in_, **kw), r=r, w=w, dma=key)

    def barrier(self):
        for e in self.E:
            wd = self.waited[e]
            for p, h in enumerate(self.phys):
                v = self.pcnt[p]
                if v > wd.get(p, 0):
                    self.E[e].wait_ge(h, v)
                    wd[p] = v
                    self.nins += 1
        self.lastw.clear()
        self.rd.clear()
        self.dmalast.clear()
        for k in [k for k in self.key2p if isinstance(k, tuple) and k[0] == 'd']:
            p = self.key2p.pop(k)
            self.free[self.pclass[p]].append(p)


_UNIQ = [0]


def sbt(ph, nc, name, shape, dt):
    _UNIQ[0] += 1
    return ph.enter_context(nc.sbuf_tensor("%s_%d" % (name, _UNIQ[0]), list(shape), dt))


class Rot:
    def __init__(self, items):
        self.items = list(items)
        self.i = 0

    def next(self):
        x = self.items[self.i % len(self.items)]
        self.i += 1
        return x


class K:
    pass


def build(mode="full"):
    nc = bass.Bass("TRN2", target_bir_lowering=False)
    stack = ExitStack()
    with stack:
        _build(nc, stack, mode)
    return nc


def _dram_in(nc, name, shape, dt=F32):
    return nc.dram_tensor(name, list(shape), dt, kind="ExternalInput")


def _build(nc, stack, mode):
    P = Prog(nc, stack)
    k = K()
    k.nc, k.P, k.mode = nc, P, mode
    k.x = _dram_in(nc, "x", [NL, D])
    k.ctx = _dram_in(nc, "ctx", [NCX, D])
    k.c2 = _dram_in(nc, "c2", [2, D])
    k.sel = _dram_in(nc, "sel", [128, 2])
    k.pos = _dram_in(nc, "pos", [128, TL, 2])
    k.ada_w = _dram_in(nc, "ada_w", [DEPTH, D, 6 * D])
    k.ada_b = _dram_in(nc, "ada_b", [DEPTH, 6 * D])
    k.norm_g = _dram_in(nc, "norm_g", [DEPTH, 4, D])
    k.ret_w_in = _dram_in(nc, "ret_w_in", [2, D, 8192])
    k.ret_w_out = _dram_in(nc, "ret_w_out", [2, 2048, D])
    k.ret_decay = _dram_in(nc, "ret_decay", [2, 2, 4])
    k.diff_w_in = _dram_in(nc, "diff_w_in", [2, D, 3 * D])
    k.diff_w_out = _dram_in(nc, "diff_w_out", [2, D, D])
    k.diff_lam = _dram_in(nc, "diff_lam", [2, 4, 64])
    k.diff_subln = _dram_in(nc, "diff_subln", [2, 128])
    k.ffn_w_up = _dram_in(nc, "ffn_w_up", [DEPTH, D, 2 * FH])
    k.ffn_conv_w = _dram_in(nc, "ffn_conv_w", [DEPTH, 3, FH])
    k.ffn_conv_b = _dram_in(nc, "ffn_conv_b", [DEPTH, FH])
    k.ffn_w_down = _dram_in(nc, "ffn_w_down", [DEPTH, FH, D])
    k.out = nc.dram_tensor("out", [NT, D], F32, kind="ExternalOutput")
    k.h_d = nc.dram_tensor("h_d", [NT, D], F32)
    k.mod_d = nc.dram_tensor("mod_d", [DEPTH, 2, 6 * D], F32)
    k.hx_d = nc.dram_tensor("hx_d", [128, 8], BF16)
    k.hg_d = nc.dram_tensor("hg_d", [256, 8], BF16)

    def sb(name, shape, dt):
        return stack.enter_context(nc.sbuf_tensor(name, list(shape), dt))

    k.ident = sb("ident", [128, 128], BF16)
    k.iotaf = sb("iotaf", [128, 128], F32)
    k.iotap = sb("iotap", [128, 1], F32)
    k.neghalf = sb("neghalf", [128, 1], F32)
    k.epsb = sb("epsb", [128, 1], F32)
    k.selt = sb("selt", [128, 2], F32)
    k.junk = sb("junk", [128, 1024], BF16)
    k.psf = [stack.enter_context(nc.psum_tensor("psf%d" % i, [128, 512], F32)) for i in range(6)]
    k.psb = [stack.enter_context(nc.psum_tensor("psb%d" % i, [128, 1024], BF16)) for i in range(2)]
    k.psf_rot = Rot(k.psf)
    k.psb_rot = Rot(k.psb)

    consts(k)
    ada_all(k)
    init_h(k)
    if mode == "ffn0":
        ffn_layer(k, 0, False)
    if mode in ("ret0", "L0"):
        rope_tables(k)
        ret_layer(k, 0)
        if mode == "L0":
            ffn_layer(k, 0, False)
    if mode == "diff1":
        rope_tables(k)
        diff_layer(k, 1, False)
    if mode == "full":
        rope_tables(k)
        for l in range(DEPTH):
            last = l == DEPTH - 1
            if l % 2 == 0:
                ret_layer(k, l)
            else:
                diff_layer(k, l, last)
            ffn_layer(k, l, last)
    P.barrier()
    with ExitStack() as ph:
        ob = [sbt(ph, nc, "ob%d" % i, [128, D], F32) for i in range(3)]
        rot = Rot(ob)
        for t in range(TT):
            b = rot.next()
            P.dma('sp', b[:], k.h_d[t * 128:(t + 1) * 128, :], key=b, r=[("h", t)], w=[b])
            P.dma('sp', k.out[t * 128:(t + 1) * 128, :], b[:], key=b, r=[b], w=[("out", t)])
        P.barrier()
    print("instructions:", P.nins, "sems:", len(P.phys))


def consts(k):
    nc, P = k.nc, k.P
    with ExitStack() as ph:
        ii = sbt(ph, nc, "ii", [128, 128], I32)
        pp = sbt(ph, nc, "pp", [128, 1], I32)
        P.op('pool', lambda: nc.gpsimd.iota(ii[:], pattern=[[1, 128]], base=0, channel_multiplier=0), w=[ii])
        P.op('pool', lambda: nc.gpsimd.iota(pp[:], pattern=[[0, 1]], base=0, channel_multiplier=1), w=[pp])
        P.op('dve', lambda: nc.vector.tensor_copy(out=k.iotaf[:], in_=ii[:]), r=[ii], w=[k.iotaf])
        P.op('dve', lambda: nc.vector.tensor_copy(out=k.iotap[:], in_=pp[:]), r=[pp], w=[k.iotap])
        P.op('dve', lambda: nc.vector.tensor_scalar(out=k.ident[:], in0=k.iotaf[:], scalar1=k.iotap[:, 0:1],
                                                    scalar2=None, op0=ALU.is_equal),
             r=[k.iotaf, k.iotap], w=[k.ident])
        P.op('dve', lambda: nc.vector.memset(k.neghalf[:], -0.5), w=[k.neghalf])
        P.op('dve', lambda: nc.vector.memset(k.epsb[:], EPS), w=[k.epsb])
        P.dma('sp', k.selt[:], k.sel[:, :], key=k.selt, w=[k.selt])
        P.barrier()


def init_h(k):
    P = k.P
    P.dma('sp', k.h_d[0:NL, :], k.x[:, :], key="inith0", w=[("h", t) for t in range(TL)])
    P.dma('sp', k.h_d[NL:NT, :], k.ctx[:, :], key="inith1", w=[("h", t) for t in range(TL, TT)])


def ada_all(k):
    nc, P = k.nc, k.P
    with ExitStack() as ph:
        cT = sbt(ph, nc, "cT", [128, 8, 2], F32)
        sT = sbt(ph, nc, "sT", [128, 8, 2], F32)
        wb = [sbt(ph, nc, "adaw%d" % i, [128, 8, 512], F32) for i in range(2)]
        bias = sbt(ph, nc, "adab", [2, 6 * D], F32)
        row = sbt(ph, nc, "adarow", [2, 6 * D], F32)
        wrot = Rot(wb)
        for r in range(2):
            P.dma('sp', cT[:, :, r], k.c2.ap()[r].rearrange("(k p) -> p k", p=128), key=cT, w=[cT],
                  allow_slow_non_contiguous=True)
        P.op('act', lambda: nc.scalar.activation(out=sT[:], in_=cT[:], func=AF.Silu), r=[cT], w=[sT])
        for l in range(DEPTH):
            P.dma('sp', bias[:], k.ada_b.ap()[l:l + 1, :].to_broadcast([2, 6 * D]), key=bias, w=[bias])
            for g in range(12):
                w = wrot.next()
                P.dma('sp', w[:], k.ada_w.ap()[l][:, g * 512:(g + 1) * 512].rearrange("(k p) n -> p k n", p=128),
                      key=w, w=[w])
                ps = k.psf_rot.next()
                for kk in range(8):
                    P.op('pe', lambda kk=kk, w=w, ps=ps: nc.tensor.matmul(ps[0:2, :], lhsT=sT[:, kk, :], rhs=w[:, kk, :],
                                                                     start=(kk == 0), stop=(kk == 7)),
                         r=[sT, w], w=[ps], sig=(kk == 7))
                P.op('dve', lambda g=g, ps=ps: nc.vector.tensor_tensor(out=row[:, g * 512:(g + 1) * 512], in0=ps[0:2, :],
                                                                    in1=bias[:, g * 512:(g + 1) * 512], op=ALU.add),
                     r=[ps, bias], w=[row])
            P.dma('sp', k.mod_d.ap()[l], row[:], key=row, r=[row], w=[("mod", l)])
        P.barrier()


def bcast_row(k, q, dst, src_row, key, r=(), w=()):
    n = dst.shape[-1]
    k.P.dma(q, dst, src_row.to_broadcast([128, n]), key=key, r=r, w=w)


def rstd_from_ss(k, ss, ms, rstd, n):
    nc, P = k.nc, k.P
    P.op('act', lambda: nc.scalar.activation(out=ms[:], in_=ss[:], func=AF.Ln, scale=1.0 / n, bias=k.epsb[:, 0:1]),
         r=[ss, k.epsb], w=[ms])
    P.op('act', lambda: nc.scalar.activation(out=rstd[:], in_=ms[:], func=AF.Exp, scale=-0.5), r=[ms], w=[rstd])


def prenorm(k, ph, l, gi, shi, sci, uT, tiles, hook=None):
    nc, P = k.nc, k.P

    def t_(name, shape, dt):
        return sbt(ph, nc, name, shape, dt)

    gb = t_("pn_g", [128, D], F32)
    gs = [t_("pn_gs%d" % r, [128, D], F32) for r in range(2)]
    sh = [t_("pn_sh%d" % r, [128, D], F32) for r in range(2)]
    hb = Rot([t_("pn_h%d" % i, [128, D], F32) for i in range(3)])
    tb = Rot([t_("pn_t%d" % i, [128, D], F32) for i in range(2)])
    ub = Rot([t_("pn_u%d" % i, [128, D], BF16) for i in range(2)])
    st = Rot([[t_("pn_s%d_%d" % (i, j), [128, 1], F32) for j in range(3)] for i in range(3)])
    bcast_row(k, 'sp', gb[:], k.norm_g.ap()[l, gi:gi + 1, :], key=gb, w=[gb])
    for r in range(2):
        bcast_row(k, 'sp', gs[r][:], k.mod_d.ap()[l, r:r + 1, sci * D:(sci + 1) * D], key=gs[r], r=[("mod", l)], w=[gs[r]])
        bcast_row(k, 'sp', sh[r][:], k.mod_d.ap()[l, r:r + 1, shi * D:(shi + 1) * D], key=sh[r], r=[("mod", l)], w=[sh[r]])
        P.op('dve', lambda r=r: nc.vector.scalar_tensor_tensor(out=gs[r][:], in0=gs[r][:], scalar=1.0, in1=gb[:],
                                                           op0=ALU.add, op1=ALU.mult), r=[gs[r], gb], w=[gs[r]])
    def stage_a(t):
        h = hb.next()
        ss, ms, rstd = st.next()
        P.dma('sp', h[:], k.h_d[t * 128:(t + 1) * 128, :], key=h, r=[("h", t)], w=[h])
        P.op('act', lambda: nc.scalar.activation(out=k.junk[:], in_=h[:], func=AF.Square, accum_out=ss[:]),
             r=[h], w=[ss])
        rstd_from_ss(k, ss, ms, rstd, D)
        return h, rstd

    def stage_b(t, h, rstd):
        r = 0 if t < TL else 1
        tt = tb.next()
        u = ub.next()
        P.op('dve', lambda: nc.vector.scalar_tensor_tensor(
            out=tt[:], in0=h[:], scalar=rstd[:, 0:1], in1=gs[r][:], op0=ALU.mult, op1=ALU.mult),
            r=[h, rstd, gs[r]], w=[tt])
        P.op('dve', lambda: nc.vector.tensor_tensor(out=u[:], in0=tt[:], in1=sh[r][:], op=ALU.add),
             r=[tt, sh[r]], w=[u])
        pb = k.psb_rot.next()
        for kk in range(8):
            P.op('pe', lambda kk=kk: nc.tensor.transpose(pb[:, kk * 128:(kk + 1) * 128], u[:, kk * 128:(kk + 1) * 128],
                                                        k.ident[:]),
                 r=[u, k.ident], w=[pb], sig=(kk == 7))
        P.op('act', lambda: nc.scalar.copy(out=uT[:, :, t * 128:(t + 1) * 128],
                                           in_=pb[:].rearrange("p (k n) -> p k n", k=8)),
             r=[pb], w=[("uT", t)])
        if hook is not None:
            hook(t)

    prev = None
    for t in tiles:
        a = stage_a(t)
        if prev is not None:
            stage_b(*prev)
        prev = (t,) + a
    stage_b(*prev)


def uT_res(c, n):
    return [("uT", t) for t in range(c // 128, (c + n - 1) // 128 + 1)]


def ffn_layer(k, l, last):
    nc, P = k.nc, k.P
    tiles_all = list(range(TL)) + ([] if last else list(range(TL, TT)))
    with ExitStack() as ph:
        def t_(name, shape, dt):
            return sbt(ph, nc, name, shape, dt)

        uT = t_("f_uT", [128, 8, NT], BF16)
        Wd = t_("f_Wd", [128, NFC, D], BF16)
        cw = t_("f_cw", [128, 4, NFC], F32)
        uH = [t_("f_uH%d" % i, [128, 8, 2], BF16) for i in range(2)]
        hx = t_("f_hx", [128, 8], BF16)
        hg = t_("f_hg", [128, 2, 8], BF16)
        hgf = t_("f_hgf", [128, 8], F32)
        for half in range(2):
            f0, f1 = half * 11, (half + 1) * 11
            P.dma('pool', Wd[:, f0:f1, :], k.ffn_w_down.ap()[l][f0 * 128:f1 * 128, :].rearrange("(f p) n -> p f n", p=128),
                  key=("Wd", half), w=[("Wd", half)])
        for tap in range(3):
            P.dma('sp', cw[:, tap, :], k.ffn_conv_w.ap()[l, tap].rearrange("(f p) -> p f", p=128), key=cw, w=[cw],
                  allow_slow_non_contiguous=True)
        P.dma('sp', cw[:, 3, :], k.ffn_conv_b.ap()[l].rearrange("(f p) -> p f", p=128), key=cw, w=[cw],
              allow_slow_non_contiguous=True)
        with ExitStack() as ph1:
            order = [TL - 1] + [t for t in tiles_all if t != TL - 1]

            def hook(t):
                if t != TL - 1:
                    return
                P.op('dve', lambda: nc.vector.tensor_copy(out=hx[:], in_=uT[:, :, NL - 1]), r=[("uT", TL - 1)], w=[hx])
                P.dma('pool', k.hx_d[:, :], hx[:], key=hx, r=[hx], w=["hx_d"])
                P.op('pool', lambda: nc.gpsimd.collective_compute("AllGather", ALU.bypass, replica_groups=GROUPS,
                                                                  ins=[k.hx_d.ap().opt()], outs=[k.hg_d.ap().opt()]),
                     r=["hx_d"], w=["hg_d"], dma="cc", inc=1)

            prenorm(k, ph1, l, 2, 3, 4, uT, order, hook)
            P.dma('sp', hg[:], k.hg_d.ap().rearrange("(r p) k -> p r k", p=128), key=hg, r=["hg_d"], w=[hg])
            P.op('dve', lambda: nc.vector.tensor_scalar(out=hgf[:], in0=hg[:, 0, :], scalar1=k.selt[:, 0:1], scalar2=None,
                                                        op0=ALU.mult), r=[hg, k.selt], w=[hgf])
            P.op('dve', lambda: nc.vector.scalar_tensor_tensor(out=uH[1][:, :, 1], in0=hg[:, 1, :], scalar=k.selt[:, 1:2],
                                                               in1=hgf[:], op0=ALU.mult, op1=ALU.add),
                 r=[hg, hgf, k.selt], w=[uH[1]])
            P.op('dve', lambda: nc.vector.tensor_copy(out=uH[1][:, :, 0], in_=uT[:, :, NL // 2 - 1]), r=[("uT", 7)], w=[uH[1]])
            P.op('dve', lambda: nc.vector.tensor_copy(out=uH[0][:, :, 0], in_=uT[:, :, NL // 2]), r=[("uT", 8)], w=[uH[0]])
            P.op('dve', lambda: nc.vector.tensor_copy(out=uH[0][:, :, 1], in_=uT[:, :, NL // 2]), r=[("uT", 8)], w=[uH[0]])
            P.barrier()
        segs = [dict(tiles=list(range(0, 8)), subs=[(0, 1024, False, True)], uH=uH[0]),
                dict(tiles=list(range(8, 16)) + ([] if last else [16, 17]),
                     subs=[(1024, 1024, True, True)] + ([] if last else [(2048, 256, False, False)]), uH=uH[1])]
        with ExitStack() as ph2:
            def t2(name, shape, dt):
                return sbt(ph2, nc, name, shape, dt)

            NCOL = 1280
            aT = t2("f_aT", [128, NFC, NCOL], BF16)
            Wv = Rot([t2("f_Wv%d" % i, [128, 8, 256], BF16) for i in range(2)])
            Wg = Rot([t2("f_Wg%d" % i, [128, 8, 256], BF16) for i in range(2)])
            GW = NCOL + 4
            Gb = Rot([t2("f_G%d" % i, [128, GW], F32) for i in range(2)])
            Vb = Rot([t2("f_V%d" % i, [128, NCOL], F32) for i in range(2)])
            Tb = Rot([t2("f_T%d" % i, [128, NCOL], F32) for i in range(1)])
            gg = [t2("f_gg%d" % r, [128, D], F32) for r in range(2)]
            g3 = t2("f_g3", [128, D], F32)
            hb = Rot([t2("f_h%d" % i, [128, D], F32) for i in range(2)])
            ob = Rot([t2("f_o%d" % i, [128, D], F32) for i in range(2)])
            st = Rot([[t2("f_s%d_%d" % (i, j), [128, 1], F32) for j in range(4)] for i in range(2)])
            for G in Gb.items:
                P.op('pool', lambda G=G: nc.gpsimd.memset(G[:], 0.0), w=[G])
            bcast_row(k, 'sp', g3[:], k.norm_g.ap()[l, 3:4, :], key=g3, w=[g3])
            for r in range(2):
                bcast_row(k, 'sp', gg[r][:], k.mod_d.ap()[l, r:r + 1, 5 * D:6 * D], key=gg[r], r=[("mod", l)], w=[gg[r]])
                P.op('dve', lambda r=r: nc.vector.tensor_tensor(out=gg[r][:], in0=gg[r][:], in1=g3[:], op=ALU.mult),
                     r=[gg[r], g3], w=[gg[r]])
            wup = k.ffn_w_up.ap()[l]

            def load_w(fcg):
                wv, wg = Wv.next(), Wg.next()
                P.dma('pool', wv[:], wup[:, fcg * 256:(fcg + 1) * 256].rearrange("(k p) n -> p k n", p=128), key=wv, w=[wv])
                P.dma('pool', wg[:], wup[:, FH + fcg * 256:FH + (fcg + 1) * 256].rearrange("(k p) n -> p k n", p=128),
                      key=wg, w=[wg])
                return wv, wg

            for seg in segs:
                subs = seg["subs"]
                uHs = seg["uH"]
                offs, goffs = [], []
                o, go = 0, 0
                for (c0, n, lh, rh) in subs:
                    offs.append(o)
                    goffs.append(go)
                    o += n
                    go += n + 2
                nxt = load_w(0)
                for fcg in range(11):
                    wv, wg = nxt
                    if fcg + 1 < 11:
                        nxt = load_w(fcg + 1)
                    for j in range(2):
                        fc = fcg * 2 + j
                        G, V, T = Gb.next(), Vb.next(), Tb.next()
                        for si, (c0, n, lh, rh) in enumerate(subs):
                            for b0 in range(0, n, 512):
                                nb = min(512, n - b0)
                                for (wt, dst, doff) in ((wv, V, offs[si] + b0), (wg, G, goffs[si] + 1 + b0)):
                                    ps = k.psf_rot.next()
                                    for kk in range(8):
                                        P.op('pe', lambda kk=kk, wt=wt, ps=ps, c=c0 + b0, nb=nb, j=j: nc.tensor.matmul(
                                            ps[:, 0:nb], lhsT=wt[:, kk, j * 128:(j + 1) * 128], rhs=uT[:, kk, c:c + nb],
                                            start=(kk == 0), stop=(kk == 7)),
                                            r=[wt] + uT_res(c0 + b0, nb), w=[ps], sig=(kk == 7))
                                    P.op('act', lambda ps=ps, dst=dst, doff=doff, nb=nb: nc.scalar.copy(
                                        out=dst[:, doff:doff + nb], in_=ps[:, 0:nb]), r=[ps], w=[dst])
                        ps = k.psf_rot.next()
                        for kk in range(8):
                            P.op('pe', lambda kk=kk, ps=ps, j=j, wg=wg: nc.tensor.matmul(
                                ps[:, 0:2], lhsT=wg[:, kk, j * 128:(j + 1) * 128], rhs=uHs[:, kk, :],
                                start=(kk == 0), stop=(kk == 7)), r=[wg, uHs], w=[ps], sig=(kk == 7))
                        for si, (c0, n, lh, rh) in enumerate(subs):
                            go = goffs[si]
                            if lh:
                                P.op('act', lambda ps=ps, G=G, go=go: nc.scalar.copy(out=G[:, go:go + 1], in_=ps[:, 0:1]),
                                     r=[ps], w=[G])
                            if rh:
                                P.op('act', lambda ps=ps, G=G, go=go, n=n: nc.scalar.copy(
                                    out=G[:, go + n + 1:go + n + 2], in_=ps[:, 1:2]), r=[ps], w=[G])
                        for si, (c0, n, lh, rh) in enumerate(subs):
                            go, o = goffs[si], offs[si]
                            P.op('dve', lambda G=G, T=T, go=go, o=o, n=n, fc=fc: nc.vector.tensor_scalar(
                                out=T[:, o:o + n], in0=G[:, go + 1:go + 1 + n], scalar1=cw[:, 1, fc:fc + 1],
                                scalar2=cw[:, 3, fc:fc + 1], op0=ALU.mult, op1=ALU.add), r=[G, cw], w=[T])
                            P.op('dve', lambda G=G, T=T, go=go, o=o, n=n, fc=fc: nc.vector.scalar_tensor_tensor(
                                out=T[:, o:o + n], in0=G[:, go:go + n], scalar=cw[:, 0, fc:fc + 1], in1=T[:, o:o + n],
                                op0=ALU.mult, op1=ALU.add), r=[G, cw, T], w=[T])
                            P.op('dve', lambda G=G, T=T, go=go, o=o, n=n, fc=fc: nc.vector.scalar_tensor_tensor(
                                out=T[:, o:o + n], in0=G[:, go + 2:go + 2 + n], scalar=cw[:, 2, fc:fc + 1], in1=T[:, o:o + n],
                                op0=ALU.mult, op1=ALU.add), r=[G, cw, T], w=[T])
                            P.op('act', lambda T=T, o=o, n=n: nc.scalar.activation(out=T[:, o:o + n], in_=T[:, o:o + n],
                                                                                func=AF.Silu), r=[T], w=[T])
                            P.op('dve', lambda T=T, V=V, o=o, n=n, fc=fc: nc.vector.tensor_tensor(
                                out=aT[:, fc, o:o + n], in0=T[:, o:o + n], in1=V[:, o:o + n], op=ALU.mult),
                                r=[T, V], w=[("aT", fc)])
                col = 0
                for t in seg["tiles"]:
                    r = 0 if t < TL else 1
                    pss = [k.psf_rot.next(), k.psf_rot.next()]
                    for cg in range(2):
                        for fc in range(NFC):
                            P.op('pe', lambda fc=fc, cg=cg, col=col: nc.tensor.matmul(
                                pss[cg][:], lhsT=aT[:, fc, col:col + 128], rhs=Wd[:, fc, cg * 512:(cg + 1) * 512],
                                start=(fc == 0), stop=(fc == NFC - 1)),
                                r=[("aT", fc), ("Wd", fc // 11)], w=[pss[cg]], sig=(fc == NFC - 1))
                    h, o_ = hb.next(), ob.next()
                    s0, s1, ms, rstd = st.next()
                    P.dma('sp', h[:], k.h_d[t * 128:(t + 1) * 128, :], key=h, r=[("h", t)], w=[h])
                    for cg, s in ((0, s0), (1, s1)):
                        P.op('act', lambda cg=cg, s=s: nc.scalar.activation(out=k.junk[:, 0:512], in_=pss[cg][:], func=AF.Square,
                                                                        accum_out=s[:]), r=[pss[cg]], w=[s])
                    P.op('dve', lambda s0=s0, s1=s1: nc.vector.tensor_tensor(out=s0[:], in0=s0[:], in1=s1[:], op=ALU.add),
                         r=[s0, s1], w=[s0])
                    rstd_from_ss(k, s0, ms, rstd, D)
                    for cg in range(2):
                        P.op('dve', lambda cg=cg, o_=o_, rstd=rstd, r=r: nc.vector.scalar_tensor_tensor(
                            out=o_[:, cg * 512:(cg + 1) * 512], in0=pss[cg][:], scalar=rstd[:, 0:1],
                            in1=gg[r][:, cg * 512:(cg + 1) * 512], op0=ALU.mult, op1=ALU.mult),
                            r=[pss[cg], rstd, gg[r]], w=[o_])
                    P.op('dve', lambda o_=o_, h=h: nc.vector.tensor_tensor(out=o_[:], in0=o_[:], in1=h[:], op=ALU.add),
                         r=[o_, h], w=[o_])
                    P.dma('sp', k.h_d[t * 128:(t + 1) * 128, :], o_[:], key=o_, r=[o_], w=[("h", t)])
                    col += 128
            P.barrier()


def resid_epilogue(k, pss, t, ggr, h, o_, stt4):
    nc, P = k.nc, k.P
    s0, s1, ms, rstd = stt4
    P.dma('sp', h[:], k.h_d[t * 128:(t + 1) * 128, :], key=h, r=[("h", t)], w=[h])
    for cg, s_ in ((0, s0), (1, s1)):
        P.op('act', lambda cg=cg, s_=s_: nc.scalar.activation(out=k.junk[:, 0:512], in_=pss[cg][:], func=AF.Square,
                                                            accum_out=s_[:]), r=[pss[cg]], w=[s_])
    P.op('dve', lambda: nc.vector.tensor_tensor(out=s0[:], in0=s0[:], in1=s1[:], op=ALU.add), r=[s0, s1], w=[s0])
    rstd_from_ss(k, s0, ms, rstd, D)
    for cg in range(2):
        P.op('dve', lambda cg=cg: nc.vector.scalar_tensor_tensor(
            out=o_[:, cg * 512:(cg + 1) * 512], in0=pss[cg][:], scalar=rstd[:, 0:1],
            in1=ggr[:, cg * 512:(cg + 1) * 512], op0=ALU.mult, op1=ALU.mult), r=[pss[cg], rstd, ggr], w=[o_])
    P.op('dve', lambda: nc.vector.tensor_tensor(out=o_[:], in0=o_[:], in1=h[:], op=ALU.add), r=[o_, h], w=[o_])
    P.dma('pool', k.h_d[t * 128:(t + 1) * 128, :], o_[:], key=o_, r=[o_], w=[("h", t)])


def gate_tiles(k, ph, l, gi, gatei):
    nc, P = k.nc, k.P
    gg = [sbt(ph, nc, "gg%d" % r, [128, D], F32) for r in range(2)]
    g3 = sbt(ph, nc, "ggn", [128, D], F32)
    bcast_row(k, 'sp', g3[:], k.norm_g.ap()[l, gi:gi + 1, :], key=g3, w=[g3])
    for r in range(2):
        bcast_row(k, 'sp', gg[r][:], k.mod_d.ap()[l, r:r + 1, gatei * D:(gatei + 1) * D], key=gg[r], r=[("mod", l)],
                  w=[gg[r]])
        P.op('dve', lambda r=r: nc.vector.tensor_tensor(out=gg[r][:], in0=gg[r][:], in1=g3[:], op=ALU.mult),
             r=[gg[r], g3], w=[gg[r]])
    return gg


def proj_tm(k, ph, uT, w_ap, ngroups, tiles, epilogue):
    nc, P = k.nc, k.P
    Wb = Rot([sbt(ph, nc, "pj_w%d" % i, [128, 8, 512], BF16) for i in range(2)])

    def load(g):
        w = Wb.next()
        P.dma('pool', w[:], w_ap[:, g * 512:(g + 1) * 512].rearrange("(k p) n -> p k n", p=128), key=w, w=[w])
        return w

    nxt = load(0)
    for g in range(ngroups):
        w = nxt
        if g + 1 < ngroups:
            nxt = load(g + 1)
        for t in tiles:
            ps = k.psf_rot.next()
            for kk in range(8):
                P.op('pe', lambda kk=kk: nc.tensor.matmul(ps[:], lhsT=uT[:, kk, t * 128:(t + 1) * 128], rhs=w[:, kk, :],
                                                        start=(kk == 0), stop=(kk == 7)),
                     r=[w, ("uT", t)], w=[ps], sig=(kk == 7))
            epilogue(t, g, ps)


def rope_apply(k, ps, outb, cos, sin, nh, nf, scale, tmp):
    nc, P = k.nc, k.P
    n = nh * 4 * nf
    pv = ps[:, 0:n].rearrange("p (h r x f) -> p h r x f", h=nh, r=2, x=2)
    ov = outb[:, 0:n].rearrange("p (h r x f) -> p h r x f", h=nh, r=2, x=2)
    x1, x2 = pv[:, :, :, 0, :], pv[:, :, :, 1, :]
    cb = cos.unsqueeze(1).to_broadcast([128, nh, 2, nf])
    sb_ = sin.unsqueeze(1).to_broadcast([128, nh, 2, nf])
    tv = [t_[:, 0:nh * 2 * nf].rearrange("p (h r f) -> p h r f", h=nh, r=2) for t_ in tmp]
    for (dst, xin, tab) in ((0, x1, cb), (1, x2, sb_), (2, x1, sb_), (3, x2, cb)):
        P.op('dve', lambda dst=dst, xin=xin, tab=tab: nc.vector.scalar_tensor_tensor(
            out=tv[dst], in0=xin, scalar=float(scale), in1=tab, op0=ALU.mult, op1=ALU.mult),
            r=[ps, "rope_tab"], w=[tmp[dst]])
    P.op('dve', lambda: nc.vector.tensor_tensor(out=ov[:, :, :, 0, :], in0=tv[0], in1=tv[1], op=ALU.subtract),
         r=[tmp[0], tmp[1]], w=[outb])
    P.op('dve', lambda: nc.vector.tensor_tensor(out=ov[:, :, :, 1, :], in0=tv[2], in1=tv[3], op=ALU.add),
         r=[tmp[2], tmp[3]], w=[outb])


def rope_tables(k):
    nc, P = k.nc, k.P
    PI = float(np.pi)
    k.ropeR_d = nc.dram_tensor("ropeR_d", [2, 128, TL * 2 * 64], F32)
    k.ropeD_d = nc.dram_tensor("ropeD_d", [2, 128, TL * 2 * 16], F32)
    with ExitStack() as ph:
        pos = sbt(ph, nc, "pos", [128, TL, 2], F32)
        P.dma('sp', pos[:], k.pos[:, :, :], key=pos, w=[pos])
        for (nf, d2, dst) in ((64, 128, k.ropeR_d), (16, 32, k.ropeD_d)):
            n = TL * 2 * nf
            inv = sbt(ph, nc, "inv", [128, nf], F32)
            ang = sbt(ph, nc, "ang", [128, n], F32)
            xx = sbt(ph, nc, "xx", [128, n], F32)
            ki = sbt(ph, nc, "ki", [128, n], I32)
            kf = sbt(ph, nc, "kf", [128, n], F32)
            mm = sbt(ph, nc, "mm", [128, n], F32)
            res = sbt(ph, nc, "res", [128, n], F32)
            P.op('act', lambda: nc.scalar.activation(out=inv[:], in_=k.iotaf[:, 0:nf], func=AF.Exp,
                                                     scale=-float(np.log(10000.0)) * 2.0 / d2), r=[k.iotaf], w=[inv])
            for t in range(TL):
                for rc in range(2):
                    o = (t * 2 + rc) * nf
                    P.op('dve', lambda t=t, rc=rc, o=o: nc.vector.tensor_scalar(
                        out=ang[:, o:o + nf], in0=inv[:], scalar1=pos[:, t, rc:rc + 1], scalar2=None, op0=ALU.mult),
                        r=[inv, pos], w=[ang])
            for ci, off in ((0, PI / 2), (1, 0.0)):
                P.op('dve', lambda: nc.vector.tensor_scalar(out=xx[:], in0=ang[:], scalar1=float(off), scalar2=None,
                                                            op0=ALU.add), r=[ang], w=[xx])
                P.op('dve', lambda: nc.vector.tensor_scalar(out=ki[:], in0=xx[:], scalar1=float(1.0 / (2 * PI)),
                                                            scalar2=None, op0=ALU.mult), r=[xx], w=[ki])
                P.op('dve', lambda: nc.vector.tensor_copy(out=kf[:], in_=ki[:]), r=[ki], w=[kf])
                P.op('dve', lambda: nc.vector.scalar_tensor_tensor(out=xx[:], in0=kf[:], scalar=-2 * PI, in1=xx[:],
                                                                   op0=ALU.mult, op1=ALU.add), r=[kf, xx], w=[xx])
                P.op('dve', lambda: nc.vector.tensor_scalar(out=mm[:], in0=xx[:], scalar1=PI, scalar2=None,
                                                            op0=ALU.is_gt), r=[xx], w=[mm])
                P.op('dve', lambda: nc.vector.scalar_tensor_tensor(out=xx[:], in0=mm[:], scalar=-2 * PI, in1=xx[:],
                                                                   op0=ALU.mult, op1=ALU.add), r=[mm, xx], w=[xx])
                P.op('dve', lambda: nc.vector.tensor_scalar(out=mm[:], in0=xx[:], scalar1=-PI, scalar2=None,
                                                            op0=ALU.is_lt), r=[xx], w=[mm])
                P.op('dve', lambda: nc.vector.scalar_tensor_tensor(out=xx[:], in0=mm[:], scalar=2 * PI, in1=xx[:],
                                                                   op0=ALU.mult, op1=ALU.add), r=[mm, xx], w=[xx])
                P.op('act', lambda: nc.scalar.activation(out=res[:], in_=xx[:], func=AF.Sin), r=[xx], w=[res])
                P.dma('sp', dst.ap()[ci], res[:], key=res, r=[res], w=[("ropetab", nf, ci)])
        P.barrier()


def ret_layer(k, l):
    nc, P = k.nc, k.P
    i = l // 2
    H, DK, DV = 4, 256, 512
    if not hasattr(k, "rqT_d"):
        k.rqT_d = nc.dram_tensor("rqT_d", [TT, 128, 8, 128], BF16)
        k.rkT_d = nc.dram_tensor("rkT_d", [TT, 128, 8, 128], BF16)
        k.k_d = nc.dram_tensor("k_d", [NT, 1024], BF16)
        k.v_d = nc.dram_tensor("v_d", [NT, 2048], BF16)
        k.g_d = [nc.dram_tensor("g%d_d" % p, [NT, 2048], BF16) for p in range(2)]
        k.y_d = nc.dram_tensor("y_d", [NT, 2048], F32)
        k.F_d = nc.dram_tensor("F_d", [1024, 512], F32)
        k.Fg_d = nc.dram_tensor("Fg_d", [2048, 512], F32)
    tiles_all = list(range(TT))
    with ExitStack() as ph:
        uT = sbt(ph, nc, "r_uT", [128, 8, NT], BF16)
        with ExitStack() as ph1:
            prenorm(k, ph1, l, 0, 0, 1, uT, tiles_all)
            P.barrier()
        cos = sbt(ph, nc, "r_cos", [128, TL, 2, 64], F32)
        sin = sbt(ph, nc, "r_sin", [128, TL, 2, 64], F32)
        P.dma('sp', cos[:].rearrange("p t r f -> p (t r f)"), k.ropeR_d.ap()[0], key=cos, w=["rope_tab"])
        P.dma('sp', sin[:].rearrange("p t r f -> p (t r f)"), k.ropeR_d.ap()[1], key=sin, w=["rope_tab"])
        tmp = [sbt(ph, nc, "r_tmp%d" % j, [128, 256], F32) for j in range(4)]
        ob = Rot([sbt(ph, nc, "r_ob%d" % j, [128, 512], BF16) for j in range(3)])
        tb = Rot([sbt(ph, nc, "r_tb%d" % j, [128, 4, 128], BF16) for j in range(3)])

        def epi(t, g, ps):
            o = ob.next()
            if g < 4:
                scale = 1.0 if g < 2 else DK ** -0.5
                if t < TL:
                    rope_apply(k, ps, o, cos[:, t], sin[:, t], 2, 64, scale, tmp)
                else:
                    P.op('act', lambda: nc.scalar.activation(out=o[:], in_=ps[:], func=AF.Copy, scale=float(scale)),
                         r=[ps], w=[o])
                if g >= 2:
                    P.dma('sp', k.k_d[t * 128:(t + 1) * 128, (g - 2) * 512:(g - 1) * 512], o[:], key=o, r=[o],
                          w=[("k_d", t)])
                pb = k.psb_rot.next()
                for j in range(4):
                    P.op('pe', lambda j=j: nc.tensor.transpose(pb[:, j * 128:(j + 1) * 128], o[:, j * 128:(j + 1) * 128],
                                                              k.ident[:]), r=[o, k.ident], w=[pb], sig=(j == 3))
                tt = tb.next()
                P.op('act', lambda: nc.scalar.copy(out=tt[:], in_=pb[:, 0:512].rearrange("p (j n) -> p j n", j=4)),
                     r=[pb], w=[tt])
                dst = k.rqT_d if g < 2 else k.rkT_d
                gg_ = g % 2
                P.dma('sp', dst.ap()[t][:, gg_ * 4:(gg_ + 1) * 4, :], tt[:], key=tt, r=[tt], w=[("qkT_d", g, t)])
            elif g < 8:
                P.op('act', lambda: nc.scalar.copy(out=o[:], in_=ps[:]), r=[ps], w=[o])
                P.dma('sp', k.v_d[t * 128:(t + 1) * 128, (g - 4) * 512:(g - 3) * 512], o[:], key=o, r=[o], w=[("v_d", t)])
            else:
                p_ = (g - 8) // 4
                gc = (g - 8) % 4
                P.op('act', lambda: nc.scalar.activation(out=o[:], in_=ps[:], func=AF.Silu), r=[ps], w=[o])
                P.dma('sp', k.g_d[p_][t * 128:(t + 1) * 128, gc * 512:(gc + 1) * 512], o[:], key=o, r=[o],
                      w=[("g_d", p_, t)])

        proj_tm(k, ph, uT, k.ret_w_in.ap()[i], 16, tiles_all, epi)
        P.barrier()
    with ExitStack() as ph:
        def t_(name, shape, dt):
            return sbt(ph, nc, name, shape, dt)

        lg = t_("lg", [128, 8], F32)
        dm = t_("dm", [128, 128], F32)
        msk = [t_("msk%d" % j, [128, 128], F32) for j in range(8)]
        xi = [t_("xi%d" % j, [128, 128], BF16) for j in range(8)]
        zeta = t_("zeta", [128, 8], F32)
        cd = t_("cd", [128, 8], F32)
        c128 = t_("c128", [128, 1], F32)
        tA = t_("tA", [128, 128], F32)
        tB = t_("tB", [128, 128], F32)
        P.dma('sp', lg[:], k.ret_decay.ap()[i:i + 1].rearrange("o p h -> o (p h)").to_broadcast([128, 8]), key=lg, w=[lg])
        P.op('act', lambda: nc.scalar.activation(out=lg[:], in_=lg[:], func=AF.Exp, scale=-1.0), r=[lg], w=[lg])
        P.op('dve', lambda: nc.vector.tensor_scalar(out=lg[:], in0=lg[:], scalar1=1.0, scalar2=None, op0=ALU.add),
             r=[lg], w=[lg])
        P.op('act', lambda: nc.scalar.activation(out=lg[:], in_=lg[:], func=AF.Ln), r=[lg], w=[lg])
        P.op('dve', lambda: nc.vector.tensor_scalar(out=lg[:], in0=lg[:], scalar1=-1.0, scalar2=None, op0=ALU.mult),
             r=[lg], w=[lg])
        P.op('dve', lambda: nc.vector.memset(c128[:], 128.0), w=[c128])
        P.op('dve', lambda: nc.vector.tensor_scalar(out=dm[:], in0=k.iotaf[:], scalar1=k.iotap[:, 0:1], scalar2=None,
                                                    op0=ALU.subtract), r=[k.iotaf, k.iotap], w=[dm])
        for p_ in range(2):
            sgn = 1.0 if p_ == 0 else -1.0
            for h in range(H):
                j = p_ * 4 + h
                lgc = lg[:, j:j + 1]
                P.op('dve', lambda: nc.vector.tensor_scalar(out=tA[:], in0=dm[:], scalar1=sgn, scalar2=0.0, op0=ALU.mult,
                                                            op1=ALU.max), r=[dm], w=[tA])
                P.op('act', lambda lgc=lgc: nc.scalar.activation(out=tA[:], in_=tA[:], func=AF.Exp, scale=lgc),
                     r=[tA, lg], w=[tA])
                P.op('dve', lambda: nc.vector.tensor_scalar(out=tB[:], in0=dm[:], scalar1=sgn, scalar2=0.0, op0=ALU.mult,
                                                            op1=ALU.is_ge), r=[dm], w=[tB])
                P.op('dve', lambda j=j: nc.vector.tensor_tensor(out=msk[j][:], in0=tA[:], in1=tB[:], op=ALU.mult),
                     r=[tA, tB], w=[msk[j]])
                if p_ == 0:
                    P.op('dve', lambda: nc.vector.tensor_scalar(out=tA[:], in0=k.iotaf[:], scalar1=1.0, scalar2=None,
                                                                op0=ALU.add), r=[k.iotaf], w=[tA])
                else:
                    P.op('dve', lambda: nc.vector.tensor_scalar(out=tA[:], in0=k.iotaf[:], scalar1=-1.0, scalar2=128.0,
                                                                op0=ALU.mult, op1=ALU.add), r=[k.iotaf], w=[tA])
                P.op('act', lambda j=j, lgc=lgc: nc.scalar.activation(out=xi[j][:], in_=tA[:], func=AF.Exp, scale=lgc),
                     r=[tA, lg], w=[xi[j]])
                if p_ == 0:
                    P.op('dve', lambda: nc.vector.tensor_scalar(out=tB[:, 0:1], in0=k.iotap[:], scalar1=-1.0, scalar2=127.0,
                                                                op0=ALU.mult, op1=ALU.add), r=[k.iotap], w=[tB])
                else:
                    P.op('dve', lambda: nc.vector.tensor_copy(out=tB[:, 0:1], in_=k.iotap[:]), r=[k.iotap], w=[tB])
                P.op('act', lambda j=j, lgc=lgc: nc.scalar.activation(out=zeta[:, j:j + 1], in_=tB[:, 0:1], func=AF.Exp,
                                                                     scale=lgc), r=[tB, lg], w=[zeta])
                P.op('act', lambda j=j, lgc=lgc: nc.scalar.activation(out=cd[:, j:j + 1], in_=c128[:], func=AF.Exp,
                                                                     scale=lgc), r=[c128, lg], w=[cd])
        Wo = t_("r_Wo", [128, 16, D], BF16)
        for half in range(2):
            P.dma('pool', Wo[:, half * 8:(half + 1) * 8, :],
                  k.ret_w_out.ap()[i][half * 1024:(half + 1) * 1024, :].rearrange("(f p) n -> p f n", p=128),
                  key=("Wo", half), w=[("Wo", half)])
        gg = gate_tiles(k, ph, l, 1, 2)
        R = [t_("R%d" % h, [128, 2, DV], F32) for h in range(H)]
        Rb = [t_("Rb%d" % h, [128, 2, DV], BF16) for h in range(H)]
        Fg = t_("Fg", [128, 2, DV], F32)
        qTb = Rot([t_("qTc%d" % j, [128, 8, 128], BF16) for j in range(3)])
        kTb = Rot([t_("kTc%d" % j, [128, 8, 128], BF16) for j in range(3)])
        kb = Rot([t_("kc%d" % j, [128, 1024], BF16) for j in range(3)])
        vb = Rot([t_("vc%d" % j, [128, 2048], BF16) for j in range(3)])
        gb = Rot([t_("gc%d" % j, [128, 2048], BF16) for j in range(3)])
        psr = Rot(k.psf[0:6])
        yb = Rot([t_("y%d" % j, [128, 2048], F32) for j in range(2)])
        y1b = Rot([t_("y1_%d" % j, [128, 2048], F32) for j in range(3)])
        ybf = Rot([t_("ybf%d" % j, [128, 2048], BF16) for j in range(2)])
        yTb = Rot([t_("yT%d" % j, [128, 16, 128], BF16) for j in range(2)])
        sTb = Rot([t_("sT%d" % j, [128, 128], BF16) for j in range(8)])
        qxb = Rot([t_("qx%d" % j, [128, 2, 128], BF16) for j in range(8)])
        kzb = Rot([t_("kz%d" % j, [128, 256], BF16) for j in range(8)])
        hb = Rot([t_("r_h%d" % j, [128, D], F32) for j in range(2)])
        ob2 = Rot([t_("r_o%d" % j, [128, D], F32) for j in range(2)])
        st4 = Rot([[t_("r_s%d_%d" % (a, b), [128, 1], F32) for b in range(4)] for a in range(2)])
        st3 = Rot([[t_("r_n%d_%d" % (a, b), [128, 1], F32) for b in range(3)] for a in range(8)])

        def zero_state():
            for h in range(H):
                P.op('pool', lambda h=h: nc.gpsimd.memset(R[h][:], 0.0), w=[R[h]])
                P.op('pool', lambda h=h: nc.gpsimd.memset(Rb[h][:], 0.0), w=[Rb[h]])

        def load_chunk(p_, t):
            qTc, kTc, kc, vc, gc = qTb.next(), kTb.next(), kb.next(), vb.next(), gb.next()
            cs = slice(t * 128, (t + 1) * 128)
            P.dma('sp', qTc[:], k.rqT_d.ap()[t], key=qTc, r=[("qkT_d", g, t) for g in (0, 1)], w=[qTc])
            P.dma('sp', kTc[:], k.rkT_d.ap()[t], key=kTc, r=[("qkT_d", g, t) for g in (2, 3)], w=[kTc])
            P.dma('sp', kc[:], k.k_d[cs, :], key=kc, r=[("k_d", t)], w=[kc])
            P.dma('sp', vc[:], k.v_d[cs, :], key=vc, r=[("v_d", t)], w=[vc])
            P.dma('sp', gc[:], k.g_d[p_][cs, :], key=gc, r=[("g_d", p_, t)], w=[gc])
            y1 = None
            if p_ == 1:
                y1 = y1b.next()
                P.dma('sp', y1[:], k.y_d[cs, :], key=y1, r=[("y_d", t)], w=[y1])
            return qTc, kTc, kc, vc, gc, y1

        def stage_ab(p_, t, bufs):
            qTc, kTc, kc, vc, gc, y1 = bufs
            loc = dict(sT=[], qx=[], kz=[])
            for h in range(H):
                j = p_ * 4 + h
                kz, qx = kzb.next(), qxb.next()
                P.op('act', lambda: nc.scalar.activation(out=kz[:], in_=kc[:, h * DK:(h + 1) * DK], func=AF.Copy,
                                                         scale=zeta[:, j:j + 1]), r=[kc, zeta], w=[kz])
                P.op('pool', lambda: nc.gpsimd.tensor_tensor(
                    out=qx[:], in0=qTc[:, h * 2:h * 2 + 2, :], in1=xi[j][:].unsqueeze(1).to_broadcast([128, 2, 128]),
                    op=ALU.mult), r=[qTc, xi[j]], w=[qx])
                loc["kz"].append(kz)
                loc["qx"].append(qx)
            for h in range(H):
                j = p_ * 4 + h
                ps_s = psr.next()
                for c2 in range(2):
                    P.op('pe', lambda: nc.tensor.matmul(ps_s[:, 0:128], lhsT=kTc[:, h * 2 + c2, :],
                                                        rhs=qTc[:, h * 2 + c2, :], start=(c2 == 0), stop=(c2 == 1)),
                         r=[kTc, qTc], w=[ps_s], sig=(c2 == 1))
                sT = sTb.next()
                P.op('dve', lambda: nc.vector.tensor_tensor(out=sT[:], in0=ps_s[:, 0:128], in1=msk[j][:],
                                                            op=ALU.mult), r=[ps_s, msk[j]], w=[sT])
                loc["sT"].append(sT)
            return loc

        def stage_cd(p_, t, bufs, loc):
            qTc, kTc, kc, vc, gc, y1 = bufs
            cs = slice(t * 128, (t + 1) * 128)
            y = yb.next()
            if p_ == 1:
                yf = ybf.next()
            for h in range(H):
                j = p_ * 4 + h
                hs = slice(h * DV, (h + 1) * DV)
                kz = loc["kz"][h]
                for c2 in range(2):
                    ps_k = psr.next()
                    P.op('pe', lambda: nc.tensor.matmul(ps_k[:], lhsT=kz[:, c2 * 128:(c2 + 1) * 128], rhs=vc[:, hs],
                                                        start=True, stop=True), r=[kz, vc], w=[ps_k])
                    P.op('dve', lambda: nc.vector.scalar_tensor_tensor(
                        out=R[h][:, c2, :], in0=R[h][:, c2, :], scalar=cd[:, j:j + 1], in1=ps_k[:], op0=ALU.mult,
                        op1=ALU.add), r=[R[h], cd, ps_k], w=[R[h]])
            for h in range(H):
                hs = slice(h * DV, (h + 1) * DV)
                sT, qx = loc["sT"][h], loc["qx"][h]
                ss, ms, rstd = st3.next()
                ps_o = psr.next()
                P.op('pe', lambda: nc.tensor.matmul(ps_o[:], lhsT=sT[:], rhs=vc[:, hs], start=True, stop=False),
                     r=[sT, vc], w=[ps_o], sig=False)
                for c2 in range(2):
                    P.op('pe', lambda: nc.tensor.matmul(ps_o[:], lhsT=qx[:, c2, :], rhs=Rb[h][:, c2, :], start=False,
                                                        stop=(c2 == 1)), r=[qx, Rb[h]], w=[ps_o], sig=(c2 == 1))
                P.op('act', lambda: nc.scalar.activation(out=k.junk[:, 0:512], in_=ps_o[:], func=AF.Square, accum_out=ss[:]),
                     r=[ps_o], w=[ss])
                rstd_from_ss(k, ss, ms, rstd, DV)
                P.op('dve', lambda: nc.vector.scalar_tensor_tensor(out=y[:, hs], in0=ps_o[:], scalar=rstd[:, 0:1],
                                                                   in1=gc[:, hs], op0=ALU.mult, op1=ALU.mult),
                     r=[ps_o, rstd, gc], w=[("y", y.name, h)])
                if p_ == 1:
                    P.op('pool', lambda: nc.gpsimd.tensor_tensor(out=yf[:, hs], in0=y[:, hs], in1=y1[:, hs], op=ALU.add),
                         r=[("y", y.name, h), y1], w=[("yf", yf.name, h)])
            for h in range(H):
                P.op('act', lambda: nc.scalar.copy(out=Rb[h][:], in_=R[h][:]), r=[R[h]], w=[Rb[h]])
            if p_ == 0:
                P.dma('pool', k.y_d[cs, :], y[:], key=y, r=[("y", y.name, h) for h in range(H)], w=[("y_d", t)])
            else:
                yT = yTb.next()
                for half in range(2):
                    pb = k.psb_rot.next()
                    for jj in range(8):
                        kk = half * 8 + jj
                        P.op('pe', lambda: nc.tensor.transpose(pb[:, jj * 128:(jj + 1) * 128],
                                                               yf[:, kk * 128:(kk + 1) * 128], k.ident[:]),
                             r=[("yf", yf.name, kk // 4), k.ident], w=[pb], sig=(jj == 7))
                    P.op('act', lambda: nc.scalar.copy(out=yT[:, half * 8:(half + 1) * 8, :],
                                                       in_=pb[:].rearrange("p (j n) -> p j n", j=8)),
                         r=[pb], w=[("yTh", yT.name, half)])
                pss = [psr.next(), psr.next()]
                for cg in range(2):
                    for kk in range(16):
                        P.op('pe', lambda: nc.tensor.matmul(pss[cg][:], lhsT=yT[:, kk, :],
                                                            rhs=Wo[:, kk, cg * 512:(cg + 1) * 512],
                                                            start=(kk == 0), stop=(kk == 15)),
                             r=[("yTh", yT.name, kk // 8), ("Wo", kk // 8)], w=[pss[cg]], sig=(kk == 15))
                resid_epilogue(k, pss, t, gg[0 if t < TL else 1], hb.next(), ob2.next(), st4.next())

        def run_chunks(p_, ts):
            bufs = {0: load_chunk(p_, ts[0])}
            if len(ts) > 1:
                bufs[1] = load_chunk(p_, ts[1])
            locs = {0: stage_ab(p_, ts[0], bufs[0])}
            for n_, t in enumerate(ts):
                if n_ + 2 < len(ts):
                    bufs[n_ + 2] = load_chunk(p_, ts[n_ + 2])
                if n_ + 1 < len(ts):
                    locs[n_ + 1] = stage_ab(p_, ts[n_ + 1], bufs[n_ + 1])
                stage_cd(p_, t, bufs.pop(n_), locs.pop(n_))

        zero_state()
        run_chunks(0, [TL, TL + 1] + list(range(TL)))
        for h in range(H):
            P.dma('sp', k.F_d.ap()[h * 256:(h + 1) * 256, :].rearrange("(c p) n -> p c n", p=128), R[h][:], key=R[h],
                  r=[R[h]], w=["F_d"])
        P.op('pool', lambda: nc.gpsimd.collective_compute("AllGather", ALU.bypass, replica_groups=GROUPS,
                                                          ins=[k.F_d.ap().opt()], outs=[k.Fg_d.ap().opt()]),
             r=["F_d"], w=["Fg_d"], dma="cc", inc=1)
        zero_state()
        run_chunks(1, [TL + 1, TL])
        for h in range(H):
            for rk_ in range(2):
                P.dma('sp', Fg[:], k.Fg_d.ap()[rk_ * 1024 + h * 256:rk_ * 1024 + (h + 1) * 256, :].rearrange(
                    "(c p) n -> p c n", p=128), key=Fg, r=["Fg_d"], w=[Fg])
                if rk_ == 0:
                    P.op('dve', lambda h=h: nc.vector.tensor_scalar(out=R[h][:], in0=Fg[:], scalar1=k.selt[:, 0:1],
                                                                  scalar2=None, op0=ALU.mult), r=[Fg, k.selt], w=[R[h]])
                else:
                    P.op('dve', lambda h=h: nc.vector.scalar_tensor_tensor(out=R[h][:], in0=Fg[:], scalar=k.selt[:, 1:2],
                                                                         in1=R[h][:], op0=ALU.mult, op1=ALU.add),
                         r=[Fg, k.selt, R[h]], w=[R[h]])
            P.op('act', lambda h=h: nc.scalar.copy(out=Rb[h][:], in_=R[h][:]), r=[R[h]], w=[Rb[h]])
        run_chunks(1, list(range(TL - 1, -1, -1)))
        P.barrier()


def diff_layer(k, l, last):
    import math
    nc, P = k.nc, k.P
    i = l // 2
    lam_init = 0.8 - 0.6 * math.exp(-0.3 * l)
    NH = 8
    if not hasattr(k, "qT_d"):
        k.qT_d = nc.dram_tensor("qT_d", [8, 128, NT], BF16)
    if not hasattr(k, "kTl_d"):
        k.kTl_d = [nc.dram_tensor("kTl%d_d" % g, [512, NL], BF16) for g in range(2)]
        k.kTc_d = nc.dram_tensor("kTc_d", [1024, NCX], BF16)
        k.kTg_d = [nc.dram_tensor("kTg%d_d" % g, [1024, NL], BF16) for g in range(2)]
        k.vl_d = [nc.dram_tensor("vl%d_d" % g, [NL, 512], BF16) for g in range(2)]
        k.vc_d = nc.dram_tensor("vc_d", [NCX, 1024], BF16)
        k.vg_d = [nc.dram_tensor("vg%d_d" % g, [2 * NL, 512], BF16) for g in range(2)]
    tiles_all = list(range(TT))
    q_tiles = list(range(TL)) + ([] if last else [TL, TL + 1])
    with ExitStack() as ph:
        uT = sbt(ph, nc, "d_uT", [128, 8, NT], BF16)
        with ExitStack() as ph1:
            prenorm(k, ph1, l, 0, 0, 1, uT, tiles_all)
            P.barrier()
        cos = sbt(ph, nc, "d_cos", [128, TL, 2, 16], F32)
        sin = sbt(ph, nc, "d_sin", [128, TL, 2, 16], F32)
        P.dma('sp', cos[:].rearrange("p t r f -> p (t r f)"), k.ropeD_d.ap()[0], key=cos, w=["rope_tab"])
        P.dma('sp', sin[:].rearrange("p t r f -> p (t r f)"), k.ropeD_d.ap()[1], key=sin, w=["rope_tab"])
        tmp = [sbt(ph, nc, "d_tmp%d" % j, [128, 256], F32) for j in range(4)]
        ob = Rot([sbt(ph, nc, "d_ob%d" % j, [128, 512], BF16) for j in range(3)])
        tb = Rot([sbt(ph, nc, "d_tb%d" % j, [128, 4, 128], BF16) for j in range(3)])

        def epi_qk(isq):
            def epi(t, g, ps):
                if isq and t not in q_tiles:
                    return
                o = ob.next()
                scale = 0.125 if isq else 1.0
                if t < TL:
                    rope_apply(k, ps, o, cos[:, t], sin[:, t], 8, 16, scale, tmp)
                else:
                    P.op('act', lambda: nc.scalar.activation(out=o[:], in_=ps[:], func=AF.Copy, scale=float(scale)),
                         r=[ps], w=[o])
                pb = k.psb_rot.next()
                for j in range(4):
                    P.op('pe', lambda j=j: nc.tensor.transpose(pb[:, j * 128:(j + 1) * 128], o[:, j * 128:(j + 1) * 128],
                                                              k.ident[:]), r=[o, k.ident], w=[pb], sig=(j == 3))
                tt = tb.next()
                P.op('act', lambda: nc.scalar.copy(out=tt[:], in_=pb[:, 0:512].rearrange("p (j n) -> p j n", j=4)),
                     r=[pb], w=[tt])
                if isq:
                    P.dma('sp', k.qT_d.ap()[g * 4:(g + 1) * 4, :, t * 128:(t + 1) * 128].rearrange("j p n -> p j n"), tt[:],
                          key=tt, r=[tt], w=[("qT_d", g, t)])
                elif t < TL:
                    P.dma('sp', k.kTl_d[g].ap()[:, t * 128:(t + 1) * 128].rearrange("(j p) n -> p j n", p=128),
                          tt[:], key=tt, r=[tt], w=[("kTl_d", g, t)])
                    if t == TL - 1:
                        P.op('pool', lambda: nc.gpsimd.collective_compute(
                            "AllGather", ALU.bypass, replica_groups=GROUPS, ins=[k.kTl_d[g].ap().opt()],
                            outs=[k.kTg_d[g].ap().opt()]), r=[("kTl_d", g, t_) for t_ in range(TL)], w=[("kTg_d", g)],
                            dma="cc", inc=1)
                else:
                    P.dma('sp', k.kTc_d.ap()[g * 512:(g + 1) * 512, (t - TL) * 128:(t - TL + 1) * 128].rearrange(
                        "(j p) n -> p j n", p=128), tt[:], key=tt, r=[tt], w=[("kTc_d", g, t)])
            return epi

        def epi_v(t, g, ps):
            o = ob.next()
            P.op('act', lambda: nc.scalar.copy(out=o[:], in_=ps[:]), r=[ps], w=[o])
            if t < TL:
                P.dma('sp', k.vl_d[g][t * 128:(t + 1) * 128, :], o[:], key=o, r=[o], w=[("vl_d", g, t)])
                if t == TL - 1:
                    P.op('pool', lambda: nc.gpsimd.collective_compute(
                        "AllGather", ALU.bypass, replica_groups=GROUPS, ins=[k.vl_d[g].ap().opt()],
                        outs=[k.vg_d[g].ap().opt()]), r=[("vl_d", g, t_) for t_ in range(TL)], w=[("vg_d", g)],
                        dma="cc", inc=1)
            else:
                P.dma('sp', k.vc_d[(t - TL) * 128:(t - TL + 1) * 128, g * 512:(g + 1) * 512], o[:], key=o, r=[o],
                      w=[("vc_d", g, t)])

        wi = k.diff_w_in.ap()[i]
        proj_tm(k, ph, uT, wi[:, 1024:2048], 2, tiles_all, epi_qk(False))
        proj_tm(k, ph, uT, wi[:, 2048:3072], 2, tiles_all, epi_v)
        proj_tm(k, ph, uT, wi[:, 0:1024], 2, tiles_all, epi_qk(True))
        P.barrier()
    with ExitStack() as ph:
        def t_(name, shape, dt):
            return sbt(ph, nc, name, shape, dt)

        NK = 2 * NL + NCX
        NKT = NK // 128
        yT = t_("d_yT", [128, NH, NT], BF16)
        Wo = t_("d_Wo", [128, 8, D], BF16)
        P.dma('pool', Wo[:], k.diff_w_out.ap()[i].rearrange("(f p) n -> p f n", p=128), key=Wo, w=[Wo])
        gg = gate_tiles(k, ph, l, 1, 2)
        ones = t_("d_ones", [128, 128], BF16)
        P.op('dve', lambda: nc.vector.memset(ones[:], 1.0), w=[ones])
        lv = t_("d_lv", [128, 4, 64], F32)
        lp = t_("d_lp", [128, 2, 64], F32)
        ls = t_("d_ls", [128, 2], F32)
        nlam = t_("d_nlam", [128, 1], F32)
        gsub = t_("d_gsub", [128, 1], F32)
        P.dma('sp', lv[:].rearrange("p a b -> p (a b)"),
              k.diff_lam.ap()[i:i + 1].rearrange("o a b -> o (a b)").to_broadcast([128, 256]), key=lv, w=[lv])
        for a in range(2):
            P.op('dve', lambda a=a: nc.vector.tensor_tensor(out=lp[:, a, :], in0=lv[:, 2 * a, :], in1=lv[:, 2 * a + 1, :],
                                                          op=ALU.mult), r=[lv], w=[lp])
            P.op('dve', lambda a=a: nc.vector.reduce_sum(out=ls[:, a:a + 1], in_=lp[:, a, :], axis=AX.X), r=[lp], w=[ls])
        P.op('act', lambda: nc.scalar.activation(out=ls[:], in_=ls[:], func=AF.Exp), r=[ls], w=[ls])
        P.op('dve', lambda: nc.vector.scalar_tensor_tensor(out=nlam[:], in0=ls[:, 1:2], scalar=-float(lam_init),
                                                           in1=ls[:, 0:1], op0=ALU.add, op1=ALU.subtract),
             r=[ls], w=[nlam])
        P.dma('sp', gsub[:], k.diff_subln.ap()[i].rearrange("(p o) -> p o", o=1), key=gsub, w=[gsub],
              allow_slow_non_contiguous=True)
        P.op('dve', lambda: nc.vector.tensor_scalar(out=gsub[:], in0=gsub[:], scalar1=float(1.0 - lam_init), scalar2=None,
                                                    op0=ALU.mult), r=[gsub], w=[gsub])
        kTb = Rot([t_("d_kT%d" % j, [128, NK], BF16) for j in range(2)])
        Vb = Rot([t_("d_V%d" % j, [128, NKT, 128], BF16) for j in range(2)])
        qAb = Rot([t_("d_qA%d" % j, [128, NT], BF16) for j in range(2)])
        qBb = Rot([t_("d_qB%d" % j, [128, NT], BF16) for j in range(2)])
        for qa in qAb.items:
            P.op('pool', lambda qa=qa: nc.gpsimd.memset(qa[64:128, :], 0.0), w=[("qz", qa.name)])
        for qb_ in qBb.items:
            P.op('pool', lambda qb_=qb_: nc.gpsimd.memset(qb_[0:64, :], 0.0), w=[("qz", qb_.name)])
        eb = Rot([t_("d_e%d" % j, [128, 512], BF16) for j in range(8)])
        accD = Rot([t_("d_aD%d" % j, [128, 512], F32) for j in range(2)])
        accP = Rot([t_("d_aP%d" % j, [128, 512], F32) for j in range(2)])
        zhi = Rot([t_("d_zh%d" % j, [128, 512], BF16) for j in range(2)])
        zlo = Rot([t_("d_zl%d" % j, [128, 512], BF16) for j in range(2)])
        rzb = Rot([t_("d_rz%d" % j, [128, 512], F32) for j in range(2)])
        Onb = [Rot([t_("d_On%d_%d" % (w_, j), [128, 512], F32) for j in range(2)]) for w_ in range(2)]
        OTb = Rot([t_("d_OT%d" % j, [128, 512], F32) for j in range(2)])
        sqb = Rot([t_("d_sq%d" % j, [128, 512], BF16) for j in range(2)])
        rsb = Rot([t_("d_rs%d" % j, [128, 512], F32) for j in range(2)])
        lnb = Rot([t_("d_ln%d" % j, [128, 512], F32) for j in range(2)])
        hb = Rot([t_("d_h%d" % j, [128, D], F32) for j in range(2)])
        ob2 = Rot([t_("d_o%d" % j, [128, D], F32) for j in range(2)])
        st4 = Rot([[t_("d_s%d_%d" % (a, b), [128, 1], F32) for b in range(4)] for a in range(2)])
        class View:
            def __init__(self, t):
                self.t, self.name = t, t.name + "_f32"

            def __getitem__(self, idx):
                return self.t[:].bitcast(F32)[idx]

        ps_s = Rot(k.psf[0:3])
        ps_o = Rot(k.psf[3:5])
        ps_z = Rot([k.psf[5], View(k.psb[0])])
        ps_ln = View(k.psb[1])
        deferred = []

        def load_head(h):
            kT, V, qA, qB = kTb.next(), Vb.next(), qAb.next(), qBb.next()
            hg_, hh = h // 4, h % 4
            for rk_ in range(2):
                P.dma('sp', kT[:, rk_ * NL:(rk_ + 1) * NL],
                      k.kTg_d[hg_][rk_ * 512 + hh * 128:rk_ * 512 + (hh + 1) * 128, :],
                      key=("kT", kT.name, rk_), r=[("kTg_d", hg_)], w=[("kTp", kT.name, rk_)])
                P.dma('sp', V[:, rk_ * TL:(rk_ + 1) * TL, :],
                      k.vg_d[hg_].ap()[rk_ * NL:(rk_ + 1) * NL, hh * 128:(hh + 1) * 128].rearrange("(j p) d -> p j d", p=128),
                      key=("V", V.name, rk_), r=[("vg_d", hg_)], w=[("Vp", V.name, rk_)])
            P.dma('sp', kT[:, 2 * NL:NK], k.kTc_d[h * 128:(h + 1) * 128, :], key=("kT", kT.name, 2),
                  r=[("kTc_d", h // 4, t_) for t_ in (TL, TL + 1)], w=[("kTp", kT.name, 2)])
            P.dma('sp', V[:, 2 * TL:NKT, :], k.vc_d.ap()[:, h * 128:(h + 1) * 128].rearrange("(j p) d -> p j d", p=128),
                  key=("V", V.name, 2), r=[("vc_d", h // 4, t_) for t_ in (TL, TL + 1)], w=[("Vp", V.name, 2)])
            qr = [("qT_d", h // 4, t) for t in q_tiles]
            P.dma('sp', qA[0:64, :], k.qT_d.ap()[h, 0:64, :], key=qA, r=qr, w=[qA])
            P.dma('sp', qB[64:128, :], k.qT_d.ap()[h, 64:128, :], key=qB, r=qr, w=[qB])
            return kT, V, qA, qB

        blocks = [(qb * 512, 512, list(range(NKT))) for qb in range(4)]
        if not last:
            blocks.append((NL, NCX, [NKT - 2, NKT - 1]))
        items = []
        for h in range(NH):
            for (q0, nq, ktiles) in blocks:
                for w_ in range(2):
                    for jn, jt in enumerate(ktiles):
                        items.append(dict(h=h, q0=q0, nq=nq, w=w_, jn=jn, jt=jt, first=(jn == 0),
                                          last=(jn == len(ktiles) - 1)))
        heads = {}
        state = {}

        def emit_S(it):
            h = it["h"]
            if h not in heads:
                assert h == 0
                heads[h] = load_head(h)
            kT, V, qA, qB = heads[h]
            q = qA if it["w"] == 0 else qB
            nq, q0, jt = it["nq"], it["q0"], it["jt"]
            ps = ps_s.next()
            e = eb.next()
            it["e"] = e
            P.op('pe', lambda: nc.tensor.matmul(ps[:, 0:nq], lhsT=kT[:, jt * 128:(jt + 1) * 128], rhs=q[:, q0:q0 + nq],
                                                start=True, stop=True),
                 r=[("kTp", kT.name, min(jt // TL, 2)), q, ("qz", q.name)], w=[ps])
            P.op('act', lambda: nc.scalar.activation(out=e[:, 0:nq], in_=ps[:, 0:nq], func=AF.Exp), r=[ps], w=[e])

        def emit_PV(it):
            h, w_, nq, q0, jt, jn = it["h"], it["w"], it["nq"], it["q0"], it["jt"], it["jn"]
            kT, V, qA, qB = heads[h]
            e = it["e"]
            if it["first"] and w_ == 0 and q0 == 0 and h + 1 < NH:
                heads[h + 1] = load_head(h + 1)
            if it["first"]:
                state["pO"] = ps_o.next()
                state["aD"], state["aP"] = accD.next(), accP.next()
                state["usedP"] = False
            pO, aD, aP = state["pO"], state["aD"], state["aP"]
            P.op('pe', lambda: nc.tensor.matmul(pO[:, 0:nq], lhsT=V[:, jt, :], rhs=e[:, 0:nq], start=it["first"],
                                                stop=it["last"]),
                 r=[("Vp", V.name, min(jt // TL, 2)), e], w=[pO], sig=it["last"])
            if it["first"]:
                state["pZ"] = ps_z.next()
            pZ = state["pZ"]
            P.op('pe', lambda: nc.tensor.matmul(pZ[:, 0:nq], lhsT=ones[:], rhs=e[:, 0:nq], start=it["first"],
                                                stop=it["last"]), r=[ones, e], w=[pZ], sig=True)
            if not it["last"]:
                return
            rz = rzb.next()
            On = Onb[w_].next()
            P.op('act', lambda: nc.scalar.activation(out=rz[:, 0:nq], in_=pZ[:, 0:nq], func=AF.Ln), r=[pZ], w=[rz])
            P.op('act', lambda: nc.scalar.activation(out=rz[:, 0:nq], in_=rz[:, 0:nq], func=AF.Exp, scale=-1.0),
                 r=[rz], w=[rz])
            P.op('dve', lambda: nc.vector.tensor_tensor(out=On[:, 0:nq], in0=pO[:, 0:nq], in1=rz[:, 0:nq], op=ALU.mult),
                 r=[pO, rz], w=[On])
            state["On%d" % w_] = On
            if w_ == 0:
                return
            On0, On1 = state["On0"], state["On1"]
            OT, sq, rs, ln_ = OTb.next(), sqb.next(), rsb.next(), lnb.next()
            P.op('dve', lambda: nc.vector.scalar_tensor_tensor(out=OT[:, 0:nq], in0=On1[:, 0:nq], scalar=nlam[:, 0:1],
                                                               in1=On0[:, 0:nq], op0=ALU.mult, op1=ALU.add),
                 r=[On0, On1, nlam], w=[OT])
            P.op('act', lambda: nc.scalar.activation(out=sq[:, 0:nq], in_=OT[:, 0:nq], func=AF.Square), r=[OT], w=[sq])

            def part2():
                pS = ps_ln
                P.op('pe', lambda: nc.tensor.matmul(pS[:, 0:nq], lhsT=ones[:], rhs=sq[:, 0:nq], start=True, stop=True),
                     r=[ones, sq], w=[pS])
                P.op('act', lambda: nc.scalar.activation(out=ln_[:, 0:nq], in_=pS[:, 0:nq], func=AF.Ln, scale=1.0 / 128,
                                                         bias=k.epsb[:, 0:1]), r=[pS, k.epsb], w=[ln_])
                P.op('act', lambda: nc.scalar.activation(out=rs[:, 0:nq], in_=ln_[:, 0:nq], func=AF.Exp, scale=-0.5),
                     r=[ln_], w=[rs])
                P.op('dve', lambda: nc.vector.scalar_tensor_tensor(out=yT[:, h, q0:q0 + nq], in0=OT[:, 0:nq],
                                                                   scalar=gsub[:, 0:1], in1=rs[:, 0:nq], op0=ALU.mult,
                                                                   op1=ALU.mult), r=[OT, gsub, rs], w=[("yT", h, q0)])

            deferred.append([4, part2])

        def tick():
            for d in list(deferred):
                d[0] -= 1
                if d[0] <= 0:
                    deferred.remove(d)
                    d[1]()

        pend = []
        for it in items:
            emit_S(it)
            pend.append(it)
            if len(pend) > 2:
                emit_PV(pend.pop(0))
                tick()
        while pend:
            emit_PV(pend.pop(0))
            tick()
        while deferred:
            tick()
        for t in q_tiles:
            pss = [k.psf_rot.next(), k.psf_rot.next()]
            for cg in range(2):
                for kk in range(8):
                    P.op('pe', lambda cg=cg, kk=kk: nc.tensor.matmul(pss[cg][:], lhsT=yT[:, kk, t * 128:(t + 1) * 128],
                                                                   rhs=Wo[:, kk, cg * 512:(cg + 1) * 512],
                                                                   start=(kk == 0), stop=(kk == 7)),
                         r=[("yT", kk, (t * 128 // 512) * 512 if t < TL else NL), Wo], w=[pss[cg]], sig=(kk == 7))
            resid_epilogue(k, pss, t, gg[0 if t < TL else 1], hb.next(), ob2.next(), st4.next())
        P.barrier()

def prep_inputs(inp):
    f = lambda a: np.ascontiguousarray(np.asarray(a, dtype=np.float32))
    x, c, ctx, c_ctx = f(inp["x"]), f(inp["c"]), f(inp["ctx"]), f(inp["c_ctx"])
    shared = {n: f(inp[n]) for n in ("ada_w", "ada_b", "norm_g", "ret_w_out", "diff_w_in", "diff_w_out",
                                     "diff_subln", "ffn_w_up", "ffn_conv_b", "ffn_w_down")}
    shared["diff_lam"] = np.ascontiguousarray(np.stack([f(inp["diff_lq1"]), f(inp["diff_lk1"]), f(inp["diff_lq2"]),
                                                        f(inp["diff_lk2"])], axis=1))
    rw = f(inp["ret_w_in"])
    rw_sw = np.ascontiguousarray(np.concatenate([rw[:, :, :4096], rw[:, :, 6144:8192], rw[:, :, 4096:6144]], axis=2))
    dec = np.stack([f(inp["ret_decay_f"]), f(inp["ret_decay_b"])], axis=1)
    dec_sw = np.ascontiguousarray(dec[:, ::-1, :])
    cwv = f(inp["ffn_conv_w"])
    cw_sw = np.ascontiguousarray(cwv[:, ::-1, :])
    maps = []
    for core in range(8):
        b, half = core // 2, core % 2
        xs = x[b, half * NL:(half + 1) * NL]
        cs = ctx[b]
        pos = np.arange(half * NL, (half + 1) * NL)
        if half:
            xs, cs, pos = xs[::-1], cs[::-1], pos[::-1]
        posrc = np.stack([pos // 64, pos % 64], axis=-1).astype(np.float32)
        m = dict(shared)
        m["x"] = np.ascontiguousarray(xs)
        m["ctx"] = np.ascontiguousarray(cs)
        m["c2"] = np.ascontiguousarray(np.stack([c[b], c_ctx]))
        sel = np.zeros((128, 2), np.float32)
        sel[:, 1 - half] = 1.0
        m["sel"] = sel
        m["pos"] = np.ascontiguousarray(posrc.reshape(TL, 128, 2).transpose(1, 0, 2))
        m["ret_w_in"] = rw_sw if half else rw
        m["ret_decay"] = dec_sw if half else np.ascontiguousarray(dec)
        m["ffn_conv_w"] = cw_sw if half else cwv
        maps.append(m)
    return maps


def gather_out(results):
    out = np.empty((4, 2 * NL, D), np.float32)
    for core in range(8):
        b, half = core // 2, core % 2
        o = np.asarray(results[core]["out"])[:NL]
        out[b, half * NL:(half + 1) * NL] = o[::-1] if half else o
    return out


def kernel(**inputs):
    nc = build("full")
    maps = prep_inputs(inputs)
    res = run_bass_kernel_spmd(nc, maps, core_ids=list(range(8)))
    return gather_out(res.results)
```

```python
import numpy as np
from contextlib import ExitStack
import concourse.bass as bass
import concourse.mybir as mybir
from concourse.bass_utils import run_bass_kernel_spmd

F32 = mybir.dt.float32
BF16 = mybir.dt.bfloat16
I32 = mybir.dt.int32
AF = mybir.ActivationFunctionType
ALU = mybir.AluOpType
AX = mybir.AxisListType

D = 1024
NL = 2048
NCX = 256
NT = NL + NCX
TL, TCX, TT = 16, 2, 18
DEPTH = 4
FH = 2816
NFC = 22
EPS = 1e-6
GROUPS = [[0, 1], [2, 3], [4, 5], [6, 7]]


class Prog:
    def __init__(self, nc, stack):
        self.nc = nc
        self.stack = stack
        self.E = dict(pe=nc.tensor, act=nc.scalar, dve=nc.vector, pool=nc.gpsimd, sp=nc.sync)
        self.phys = []
        self.pcnt = []
        self.key2p = {}
        self.free = {}
        self.pclass = {}
        self.lastw = {}
        self.rd = {}
        self.dmalast = {}
        self.waited = {e: {} for e in self.E}
        self.nins = 0

    def _sem(self, k, cls="c"):
        if k not in self.key2p:
            fl = self.free.setdefault(cls, [])
            if fl:
                p = fl.pop()
            else:
                p = len(self.phys)
                self.phys.append(self.stack.enter_context(self.nc.semaphore("s%d" % p)))
                self.pcnt.append(0)
                self.pclass[p] = cls
            self.key2p[k] = p
        p = self.key2p[k]
        assert self.pclass[p] == cls, (k, cls, self.pclass[p])
        return p

    @staticmethod
    def _rk(x):
        return x if isinstance(x, (str, tuple)) else ("T", x.name)

    def op(self, eng, fn, r=(), w=(), sig=True, dma=None, inc=None):
        r = [self._rk(x) for x in r]
        w = [self._rk(x) for x in w]
        if dma is not None:
            dma = self._rk(dma)
        kind = 'd' if dma is not None else 'c'
        if dma is None:
            p = self._sem(eng)
        else:
            p = self._sem(('d', dma), "cc" if inc == 1 else ("sw" if eng == "pool" else "hw"))
        deps = {}

        def add(tok, raw):
            k, v, e, kd = tok
            if kd == 'c' and kind == 'c' and e == eng:
                if eng == 'pe' or not raw:
                    return
            if deps.get(k, 0) < v:
                deps[k] = v

        for x in r:
            if x in self.lastw:
                add(self.lastw[x], True)
        for x in w:
            if x in self.lastw:
                add(self.lastw[x], False)
            for tok in self.rd.get(x, {}).values():
                add(tok, False)
        if dma is not None and dma in self.dmalast:
            add(self.dmalast[dma], False)
        wd = self.waited[eng]
        for k, v in deps.items():
            if wd.get(k, 0) >= v:
                continue
            assert self.pcnt[k] >= v, ("wait on unsignaled op", k, v, self.pcnt[k])
            self.E[eng].wait_ge(self.phys[k], v)
            wd[k] = v
            self.nins += 1
        ins = fn()
        self.nins += 1
        if sig:
            if inc is None:
                inc = 16 if kind == 'd' else 1
            self.pcnt[p] += inc
            ins.then_inc(self.phys[p], inc)
            val = self.pcnt[p]
        else:
            val = self.pcnt[p] + 1
        tok = (p, val, eng, kind)
        for x in r:
            self.rd.setdefault(x, {})[p] = tok
        for x in w:
            self.lastw[x] = tok
            self.rd[x] = {}
        if dma is not None:
            self.dmalast[dma] = tok

    def dma(self, q, out, in_, key, r=(), w=(), **kw):
        self.op(q, lambda: self.E[q].dma_start(out=out, in_=in_, **kw), r=r, w=w, dma=key)

    def barrier(self):
        for e in self.E:
            wd = self.waited[e]
            for p, h in enumerate(self.phys):
                v = self.pcnt[p]
                if v > wd.get(p, 0):
                    self.E[e].wait_ge(h, v)
                    wd[p] = v
                    self.nins += 1
        self.lastw.clear()
        self.rd.clear()
        self.dmalast.clear()
        for k in [k for k in self.key2p if isinstance(k, tuple) and k[0] == 'd']:
            p = self.key2p.pop(k)
            self.free[self.pclass[p]].append(p)


_UNIQ = [0]


def sbt(ph, nc, name, shape, dt):
    _UNIQ[0] += 1
    return ph.enter_context(nc.sbuf_tensor("%s_%d" % (name, _UNIQ[0]), list(shape), dt))


class Rot:
    def __init__(self, items):
        self.items = list(items)
        self.i = 0

    def next(self):
        x = self.items[self.i % len(self.items)]
        self.i += 1
        return x


class K:
    pass


def build(mode="full"):
    nc = bass.Bass("TRN2", target_bir_lowering=False)
    stack = ExitStack()
    with stack:
        _build(nc, stack, mode)
    return nc


def _dram_in(nc, name, shape, dt=F32):
    return nc.dram_tensor(name, list(shape), dt, kind="ExternalInput")


def _build(nc, stack, mode):
    P = Prog(nc, stack)
    k = K()
    k.nc, k.P, k.mode = nc, P, mode
    k.x = _dram_in(nc, "x", [NL, D])
    k.ctx = _dram_in(nc, "ctx", [NCX, D])
    k.c2 = _dram_in(nc, "c2", [2, D])
    k.sel = _dram_in(nc, "sel", [128, 2])
    k.pos = _dram_in(nc, "pos", [128, TL, 2])
    k.ada_w = _dram_in(nc, "ada_w", [DEPTH, D, 6 * D])
    k.ada_b = _dram_in(nc, "ada_b", [DEPTH, 6 * D])
    k.norm_g = _dram_in(nc, "norm_g", [DEPTH, 4, D])
    k.ret_w_in = _dram_in(nc, "ret_w_in", [2, D, 8192])
    k.ret_w_out = _dram_in(nc, "ret_w_out", [2, 2048, D])
    k.ret_decay = _dram_in(nc, "ret_decay", [2, 2, 4])
    k.diff_w_in = _dram_in(nc, "diff_w_in", [2, D, 3 * D])
    k.diff_w_out = _dram_in(nc, "diff_w_out", [2, D, D])
    k.diff_lam = _dram_in(nc, "diff_lam", [2, 4, 64])
    k.diff_subln = _dram_in(nc, "diff_subln", [2, 128])
    k.ffn_w_up = _dram_in(nc, "ffn_w_up", [DEPTH, D, 2 * FH])
    k.ffn_conv_w = _dram_in(nc, "ffn_conv_w", [DEPTH, 3, FH])
    k.ffn_conv_b = _dram_in(nc, "ffn_conv_b", [DEPTH, FH])
    k.ffn_w_down = _dram_in(nc, "ffn_w_down", [DEPTH, FH, D])
    k.out = nc.dram_tensor("out", [NT, D], F32, kind="ExternalOutput")
    k.h_d = nc.dram_tensor("h_d", [NT, D], F32)
    k.mod_d = nc.dram_tensor("mod_d", [DEPTH, 2, 6 * D], F32)
    k.hx_d = nc.dram_tensor("hx_d", [128, 8], BF16)
    k.hg_d = nc.dram_tensor("hg_d", [256, 8], BF16)

    def sb(name, shape, dt):
        return stack.enter_context(nc.sbuf_tensor(name, list(shape), dt))

    k.ident = sb("ident", [128, 128], BF16)
    k.iotaf = sb("iotaf", [128, 128], F32)
    k.iotap = sb("iotap", [128, 1], F32)
    k.neghalf = sb("neghalf", [128, 1], F32)
    k.epsb = sb("epsb", [128, 1], F32)
    k.selt = sb("selt", [128, 2], F32)
    k.junk = sb("junk", [128, 1024], BF16)
    k.psf = [stack.enter_context(nc.psum_tensor("psf%d" % i, [128, 512], F32)) for i in range(6)]
    k.psb = [stack.enter_context(nc.psum_tensor("psb%d" % i, [128, 1024], BF16)) for i in range(2)]
    k.psf_rot = Rot(k.psf)
    k.psb_rot = Rot(k.psb)

    consts(k)
    ada_all(k)
    init_h(k)
    if mode == "ffn0":
        ffn_layer(k, 0, False)
    if mode in ("ret0", "L0"):
        rope_tables(k)
        ret_layer(k, 0)
        if mode == "L0":
            ffn_layer(k, 0, False)
    if mode == "diff1":
        rope_tables(k)
        diff_layer(k, 1, False)
    if mode == "full":
        rope_tables(k)
        for l in range(DEPTH):
            last = l == DEPTH - 1
            if l % 2 == 0:
                ret_layer(k, l)
            else:
                diff_layer(k, l, last)
            ffn_layer(k, l, last)
    P.barrier()
    with ExitStack() as ph:
        ob = [sbt(ph, nc, "ob%d" % i, [128, D], F32) for i in range(3)]
        rot = Rot(ob)
        for t in range(TT):
            b = rot.next()
            P.dma('sp', b[:], k.h_d[t * 128:(t + 1) * 128, :], key=b, r=[("h", t)], w=[b])
            P.dma('sp', k.out[t * 128:(t + 1) * 128, :], b[:], key=b, r=[b], w=[("out", t)])
        P.barrier()
    print("instructions:", P.nins, "sems:", len(P.phys))


def consts(k):
    nc, P = k.nc, k.P
    with ExitStack() as ph:
        ii = sbt(ph, nc, "ii", [128, 128], I32)
        pp = sbt(ph, nc, "pp", [128, 1], I32)
        P.op('pool', lambda: nc.gpsimd.iota(ii[:], pattern=[[1, 128]], base=0, channel_multiplier=0), w=[ii])
        P.op('pool', lambda: nc.gpsimd.iota(pp[:], pattern=[[0, 1]], base=0, channel_multiplier=1), w=[pp])
        P.op('dve', lambda: nc.vector.tensor_copy(out=k.iotaf[:], in_=ii[:]), r=[ii], w=[k.iotaf])
        P.op('dve', lambda: nc.vector.tensor_copy(out=k.iotap[:], in_=pp[:]), r=[pp], w=[k.iotap])
        P.op('dve', lambda: nc.vector.tensor_scalar(out=k.ident[:], in0=k.iotaf[:], scalar1=k.iotap[:, 0:1],
                                                    scalar2=None, op0=ALU.is_equal),
             r=[k.iotaf, k.iotap], w=[k.ident])
        P.op('dve', lambda: nc.vector.memset(k.neghalf[:], -0.5), w=[k.neghalf])
        P.op('dve', lambda: nc.vector.memset(k.epsb[:], EPS), w=[k.epsb])
        P.dma('sp', k.selt[:], k.sel[:, :], key=k.selt, w=[k.selt])
        P.barrier()


def init_h(k):
    P = k.P
    P.dma('sp', k.h_d[0:NL, :], k.x[:, :], key="inith0", w=[("h", t) for t in range(TL)])
    P.dma('sp', k.h_d[NL:NT, :], k.ctx[:, :], key="inith1", w=[("h", t) for t in range(TL, TT)])


def ada_all(k):
    nc, P = k.nc, k.P
    with ExitStack() as ph:
        cT = sbt(ph, nc, "cT", [128, 8, 2], F32)
        sT = sbt(ph, nc, "sT", [128, 8, 2], F32)
        wb = [sbt(ph, nc, "adaw%d" % i, [128, 8, 512], F32) for i in range(2)]
        bias = sbt(ph, nc, "adab", [2, 6 * D], F32)
        row = sbt(ph, nc, "adarow", [2, 6 * D], F32)
        wrot = Rot(wb)
        for r in range(2):
            P.dma('sp', cT[:, :, r], k.c2.ap()[r].rearrange("(k p) -> p k", p=128), key=cT, w=[cT],
                  allow_slow_non_contiguous=True)
        P.op('act', lambda: nc.scalar.activation(out=sT[:], in_=cT[:], func=AF.Silu), r=[cT], w=[sT])
        for l in range(DEPTH):
            P.dma('sp', bias[:], k.ada_b.ap()[l:l + 1, :].to_broadcast([2, 6 * D]), key=bias, w=[bias])
            for g in range(12):
                w = wrot.next()
                P.dma('sp', w[:], k.ada_w.ap()[l][:, g * 512:(g + 1) * 512].rearrange("(k p) n -> p k n", p=128),
                      key=w, w=[w])
                ps = k.psf_rot.next()
                for kk in range(8):
                    P.op('pe', lambda kk=kk, w=w, ps=ps: nc.tensor.matmul(ps[0:2, :], lhsT=sT[:, kk, :], rhs=w[:, kk, :],
                                                                     start=(kk == 0), stop=(kk == 7)),
                         r=[sT, w], w=[ps], sig=(kk == 7))
                P.op('dve', lambda g=g, ps=ps: nc.vector.tensor_tensor(out=row[:, g * 512:(g + 1) * 512], in0=ps[0:2, :],
                                                                    in1=bias[:, g * 512:(g + 1) * 512], op=ALU.add),
                     r=[ps, bias], w=[row])
            P.dma('sp', k.mod_d.ap()[l], row[:], key=row, r=[row], w=[("mod", l)])
        P.barrier()


def bcast_row(k, q, dst, src_row, key, r=(), w=()):
    n = dst.shape[-1]
    k.P.dma(q, dst, src_row.to_broadcast([128, n]), key=key, r=r, w=w)


def rstd_from_ss(k, ss, ms, rstd, n):
    nc, P = k.nc, k.P
    P.op('act', lambda: nc.scalar.activation(out=ms[:], in_=ss[:], func=AF.Ln, scale=1.0 / n, bias=k.epsb[:, 0:1]),
         r=[ss, k.epsb], w=[ms])
    P.op('act', lambda: nc.scalar.activation(out=rstd[:], in_=ms[:], func=AF.Exp, scale=-0.5), r=[ms], w=[rstd])


def prenorm(k, ph, l, gi, shi, sci, uT, tiles, hook=None):
    nc, P = k.nc, k.P

    def t_(name, shape, dt):
        return sbt(ph, nc, name, shape, dt)

    gb = t_("pn_g", [128, D], F32)
    gs = [t_("pn_gs%d" % r, [128, D], F32) for r in range(2)]
    sh = [t_("pn_sh%d" % r, [128, D], F32) for r in range(2)]
    hb = Rot([t_("pn_h%d" % i, [128, D], F32) for i in range(3)])
    tb = Rot([t_("pn_t%d" % i, [128, D], F32) for i in range(2)])
    ub = Rot([t_("pn_u%d" % i, [128, D], BF16) for i in range(2)])
    st = Rot([[t_("pn_s%d_%d" % (i, j), [128, 1], F32) for j in range(3)] for i in range(3)])
    bcast_row(k, 'sp', gb[:], k.norm_g.ap()[l, gi:gi + 1, :], key=gb, w=[gb])
    for r in range(2):
        bcast_row(k, 'sp', gs[r][:], k.mod_d.ap()[l, r:r + 1, sci * D:(sci + 1) * D], key=gs[r], r=[("mod", l)], w=[gs[r]])
        bcast_row(k, 'sp', sh[r][:], k.mod_d.ap()[l, r:r + 1, shi * D:(shi + 1) * D], key=sh[r], r=[("mod", l)], w=[sh[r]])
        P.op('dve', lambda r=r: nc.vector.scalar_tensor_tensor(out=gs[r][:], in0=gs[r][:], scalar=1.0, in1=gb[:],
                                                           op0=ALU.add, op1=ALU.mult), r=[gs[r], gb], w=[gs[r]])
    def stage_a(t):
        h = hb.next()
        ss, ms, rstd = st.next()
        P.dma('sp', h[:], k.h_d[t * 128:(t + 1) * 128, :], key=h, r=[("h", t)], w=[h])
        P.op('act', lambda: nc.scalar.activation(out=k.junk[:], in_=h[:], func=AF.Square, accum_out=ss[:]),
             r=[h], w=[ss])
        rstd_from_ss(k, ss, ms, rstd, D)
        return h, rstd

    def stage_b(t, h, rstd):
        r = 0 if t < TL else 1
        tt = tb.next()
        u = ub.next()
        P.op('dve', lambda: nc.vector.scalar_tensor_tensor(
            out=tt[:], in0=h[:], scalar=rstd[:, 0:1], in1=gs[r][:], op0=ALU.mult, op1=ALU.mult),
            r=[h, rstd, gs[r]], w=[tt])
        P.op('dve', lambda: nc.vector.tensor_tensor(out=u[:], in0=tt[:], in1=sh[r][:], op=ALU.add),
             r=[tt, sh[r]], w=[u])
        pb = k.psb_rot.next()
        for kk in range(8):
            P.op('pe', lambda kk=kk: nc.tensor.transpose(pb[:, kk * 128:(kk + 1) * 128], u[:, kk * 128:(kk + 1) * 128],
                                                        k.ident[:]),
                 r=[u, k.ident], w=[pb], sig=(kk == 7))
        P.op('act', lambda: nc.scalar.copy(out=uT[:, :, t * 128:(t + 1) * 128],
                                           in_=pb[:].rearrange("p (k n) -> p k n", k=8)),
             r=[pb], w=[("uT", t)])
        if hook is not None:
            hook(t)

    prev = None
    for t in tiles:
        a = stage_a(t)
        if prev is not None:
            stage_b(*prev)
        prev = (t,) + a
    stage_b(*prev)


def uT_res(c, n):
    return [("uT", t) for t in range(c // 128, (c + n - 1) // 128 + 1)]


def ffn_layer(k, l, last):
    nc, P = k.nc, k.P
    tiles_all = list(range(TL)) + ([] if last else list(range(TL, TT)))
    with ExitStack() as ph:
        def t_(name, shape, dt):
            return sbt(ph, nc, name, shape, dt)

        uT = t_("f_uT", [128, 8, NT], BF16)
        Wd = t_("f_Wd", [128, NFC, D], BF16)
        cw = t_("f_cw", [128, 4, NFC], F32)
        uH = [t_("f_uH%d" % i, [128, 8, 2], BF16) for i in range(2)]
        hx = t_("f_hx", [128, 8], BF16)
        hg = t_("f_hg", [128, 2, 8], BF16)
        hgf = t_("f_hgf", [128, 8], F32)
        for half in range(2):
            f0, f1 = half * 11, (half + 1) * 11
            P.dma('pool', Wd[:, f0:f1, :], k.ffn_w_down.ap()[l][f0 * 128:f1 * 128, :].rearrange("(f p) n -> p f n", p=128),
                  key=("Wd", half), w=[("Wd", half)])
        for tap in range(3):
            P.dma('sp', cw[:, tap, :], k.ffn_conv_w.ap()[l, tap].rearrange("(f p) -> p f", p=128), key=cw, w=[cw],
                  allow_slow_non_contiguous=True)
        P.dma('sp', cw[:, 3, :], k.ffn_conv_b.ap()[l].rearrange("(f p) -> p f", p=128), key=cw, w=[cw],
              allow_slow_non_contiguous=True)
        with ExitStack() as ph1:
            order = [TL - 1] + [t for t in tiles_all if t != TL - 1]

            def hook(t):
                if t != TL - 1:
                    return
                P.op('dve', lambda: nc.vector.tensor_copy(out=hx[:], in_=uT[:, :, NL - 1]), r=[("uT", TL - 1)], w=[hx])
                P.dma('pool', k.hx_d[:, :], hx[:], key=hx, r=[hx], w=["hx_d"])
                P.op('pool', lambda: nc.gpsimd.collective_compute("AllGather", ALU.bypass, replica_groups=GROUPS,
                                                                  ins=[k.hx_d.ap().opt()], outs=[k.hg_d.ap().opt()]),
                     r=["hx_d"], w=["hg_d"], dma="cc", inc=1)

            prenorm(k, ph1, l, 2, 3, 4, uT, order, hook)
            P.dma('sp', hg[:], k.hg_d.ap().rearrange("(r p) k -> p r k", p=128), key=hg, r=["hg_d"], w=[hg])
            P.op('dve', lambda: nc.vector.tensor_scalar(out=hgf[:], in0=hg[:, 0, :], scalar1=k.selt[:, 0:1], scalar2=None,
                                                        op0=ALU.mult), r=[hg, k.selt], w=[hgf])
            P.op('dve', lambda: nc.vector.scalar_tensor_tensor(out=uH[1][:, :, 1], in0=hg[:, 1, :], scalar=k.selt[:, 1:2],
                                                               in1=hgf[:], op0=ALU.mult, op1=ALU.add),
                 r=[hg, hgf, k.selt], w=[uH[1]])
            P.op('dve', lambda: nc.vector.tensor_copy(out=uH[1][:, :, 0], in_=uT[:, :, NL // 2 - 1]), r=[("uT", 7)], w=[uH[1]])
            P.op('dve', lambda: nc.vector.tensor_copy(out=uH[0][:, :, 0], in_=uT[:, :, NL // 2]), r=[("uT", 8)], w=[uH[0]])
            P.op('dve', lambda: nc.vector.tensor_copy(out=uH[0][:, :, 1], in_=uT[:, :, NL // 2]), r=[("uT", 8)], w=[uH[0]])
            P.barrier()
        segs = [dict(tiles=list(range(0, 8)), subs=[(0, 1024, False, True)], uH=uH[0]),
                dict(tiles=list(range(8, 16)) + ([] if last else [16, 17]),
                     subs=[(1024, 1024, True, True)] + ([] if last else [(2048, 256, False, False)]), uH=uH[1])]
        with ExitStack() as ph2:
            def t2(name, shape, dt):
                return sbt(ph2, nc, name, shape, dt)

            NCOL = 1280
            aT = t2("f_aT", [128, NFC, NCOL], BF16)
            Wv = Rot([t2("f_Wv%d" % i, [128, 8, 256], BF16) for i in range(2)])
            Wg = Rot([t2("f_Wg%d" % i, [128, 8, 256], BF16) for i in range(2)])
            GW = NCOL + 4
            Gb = Rot([t2("f_G%d" % i, [128, GW], F32) for i in range(2)])
            Vb = Rot([t2("f_V%d" % i, [128, NCOL], F32) for i in range(2)])
            Tb = Rot([t2("f_T%d" % i, [128, NCOL], F32) for i in range(1)])
            gg = [t2("f_gg%d" % r, [128, D], F32) for r in range(2)]
            g3 = t2("f_g3", [128, D], F32)
            hb = Rot([t2("f_h%d" % i, [128, D], F32) for i in range(2)])
            ob = Rot([t2("f_o%d" % i, [128, D], F32) for i in range(2)])
            st = Rot([[t2("f_s%d_%d" % (i, j), [128, 1], F32) for j in range(4)] for i in range(2)])
            for G in Gb.items:
                P.op('pool', lambda G=G: nc.gpsimd.memset(G[:], 0.0), w=[G])
            bcast_row(k, 'sp', g3[:], k.norm_g.ap()[l, 3:4, :], key=g3, w=[g3])
            for r in range(2):
                bcast_row(k, 'sp', gg[r][:], k.mod_d.ap()[l, r:r + 1, 5 * D:6 * D], key=gg[r], r=[("mod", l)], w=[gg[r]])
                P.op('dve', lambda r=r: nc.vector.tensor_tensor(out=gg[r][:], in0=gg[r][:], in1=g3[:], op=ALU.mult),
                     r=[gg[r], g3], w=[gg[r]])
            wup = k.ffn_w_up.ap()[l]

            def load_w(fcg):
                wv, wg = Wv.next(), Wg.next()
                P.dma('pool', wv[:], wup[:, fcg * 256:(fcg + 1) * 256].rearrange("(k p) n -> p k n", p=128), key=wv, w=[wv])
                P.dma('pool', wg[:], wup[:, FH + fcg * 256:FH + (fcg + 1) * 256].rearrange("(k p) n -> p k n", p=128),
                      key=wg, w=[wg])
                return wv, wg

            for seg in segs:
                subs = seg["subs"]
                uHs = seg["uH"]
                offs, goffs = [], []
                o, go = 0, 0
                for (c0, n, lh, rh) in subs:
                    offs.append(o)
                    goffs.append(go)
                    o += n
                    go += n + 2
                nxt = load_w(0)
                for fcg in range(11):
                    wv, wg = nxt
                    if fcg + 1 < 11:
                        nxt = load_w(fcg + 1)
                    for j in range(2):
                        fc = fcg * 2 + j
                        G, V, T = Gb.next(), Vb.next(), Tb.next()
                        for si, (c0, n, lh, rh) in enumerate(subs):
                            for b0 in range(0, n, 512):
                                nb = min(512, n - b0)
                                for (wt, dst, doff) in ((wv, V, offs[si] + b0), (wg, G, goffs[si] + 1 + b0)):
                                    ps = k.psf_rot.next()
                                    for kk in range(8):
                                        P.op('pe', lambda kk=kk, wt=wt, ps=ps, c=c0 + b0, nb=nb, j=j: nc.tensor.matmul(
                                            ps[:, 0:nb], lhsT=wt[:, kk, j * 128:(j + 1) * 128], rhs=uT[:, kk, c:c + nb],
                                            start=(kk == 0), stop=(kk == 7)),
                                            r=[wt] + uT_res(c0 + b0, nb), w=[ps], sig=(kk == 7))
                                    P.op('act', lambda ps=ps, dst=dst, doff=doff, nb=nb: nc.scalar.copy(
                                        out=dst[:, doff:doff + nb], in_=ps[:, 0:nb]), r=[ps], w=[dst])
                        ps = k.psf_rot.next()
                        for kk in range(8):
                            P.op('pe', lambda kk=kk, ps=ps, j=j, wg=wg: nc.tensor.matmul(
                                ps[:, 0:2], lhsT=wg[:, kk, j * 128:(j + 1) * 128], rhs=uHs[:, kk, :],
                                start=(kk == 0), stop=(kk == 7)), r=[wg, uHs], w=[ps], sig=(kk == 7))
                        for si, (c0, n, lh, rh) in enumerate(subs):
                            go = goffs[si]
                            if lh:
                                P.op('act', lambda ps=ps, G=G, go=go: nc.scalar.copy(out=G[:, go:go + 1], in_=ps[:, 0:1]),
                                     r=[ps], w=[G])
                            if rh:
                                P.op('act', lambda ps=ps, G=G, go=go, n=n: nc.scalar.copy(
                                    out=G[:, go + n + 1:go + n + 2], in_=ps[:, 1:2]), r=[ps], w=[G])
                        for si, (c0, n, lh, rh) in enumerate(subs):
                            go, o = goffs[si], offs[si]
                            P.op('dve', lambda G=G, T=T, go=go, o=o, n=n, fc=fc: nc.vector.tensor_scalar(
                                out=T[:, o:o + n], in0=G[:, go + 1:go + 1 + n], scalar1=cw[:, 1, fc:fc + 1],
                                scalar2=cw[:, 3, fc:fc + 1], op0=ALU.mult, op1=ALU.add), r=[G, cw], w=[T])
                            P.op('dve', lambda G=G, T=T, go=go, o=o, n=n, fc=fc: nc.vector.scalar_tensor_tensor(
                                out=T[:, o:o + n], in0=G[:, go:go + n], scalar=cw[:, 0, fc:fc + 1], in1=T[:, o:o + n],
                                op0=ALU.mult, op1=ALU.add), r=[G, cw, T], w=[T])
                            P.op('dve', lambda G=G, T=T, go=go, o=o, n=n, fc=fc: nc.vector.scalar_tensor_tensor(
                                out=T[:, o:o + n], in0=G[:, go + 2:go + 2 + n], scalar=cw[:, 2, fc:fc + 1], in1=T[:, o:o + n],
                                op0=ALU.mult, op1=ALU.add), r=[G, cw, T], w=[T])
                            P.op('act', lambda T=T, o=o, n=n: nc.scalar.activation(out=T[:, o:o + n], in_=T[:, o:o + n],
                                                                                func=AF.Silu), r=[T], w=[T])
                            P.op('dve', lambda T=T, V=V, o=o, n=n, fc=fc: nc.vector.tensor_tensor(
                                out=aT[:, fc, o:o + n], in0=T[:, o:o + n], in1=V[:, o:o + n], op=ALU.mult),
                                r=[T, V], w=[("aT", fc)])
                col = 0
                for t in seg["tiles"]:
                    r = 0 if t < TL else 1
                    pss = [k.psf_rot.next(), k.psf_rot.next()]
                    for cg in range(2):
                        for fc in range(NFC):
                            P.op('pe', lambda fc=fc, cg=cg, col=col: nc.tensor.matmul(
                                pss[cg][:], lhsT=aT[:, fc, col:col + 128], rhs=Wd[:, fc, cg * 512:(cg + 1) * 512],
                                start=(fc == 0), stop=(fc == NFC - 1)),
                                r=[("aT", fc), ("Wd", fc // 11)], w=[pss[cg]], sig=(fc == NFC - 1))
                    h, o_ = hb.next(), ob.next()
                    s0, s1, ms, rstd = st.next()
                    P.dma('sp', h[:], k.h_d[t * 128:(t + 1) * 128, :], key=h, r=[("h", t)], w=[h])
                    for cg, s in ((0, s0), (1, s1)):
                        P.op('act', lambda cg=cg, s=s: nc.scalar.activation(out=k.junk[:, 0:512], in_=pss[cg][:], func=AF.Square,
                                                                        accum_out=s[:]), r=[pss[cg]], w=[s])
                    P.op('dve', lambda s0=s0, s1=s1: nc.vector.tensor_tensor(out=s0[:], in0=s0[:], in1=s1[:], op=ALU.add),
                         r=[s0, s1], w=[s0])
                    rstd_from_ss(k, s0, ms, rstd, D)
                    for cg in range(2):
                        P.op('dve', lambda cg=cg, o_=o_, rstd=rstd, r=r: nc.vector.scalar_tensor_tensor(
                            out=o_[:, cg * 512:(cg + 1) * 512], in0=pss[cg][:], scalar=rstd[:, 0:1],
                            in1=gg[r][:, cg * 512:(cg + 1) * 512], op0=ALU.mult, op1=ALU.mult),
                            r=[pss[cg], rstd, gg[r]], w=[o_])
                    P.op('dve', lambda o_=o_, h=h: nc.vector.tensor_tensor(out=o_[:], in0=o_[:], in1=h[:], op=ALU.add),
                         r=[o_, h], w=[o_])
                    P.dma('sp', k.h_d[t * 128:(t + 1) * 128, :], o_[:], key=o_, r=[o_], w=[("h", t)])
                    col += 128
            P.barrier()


def resid_epilogue(k, pss, t, ggr, h, o_, stt4):
    nc, P = k.nc, k.P
    s0, s1, ms, rstd = stt4
    P.dma('sp', h[:], k.h_d[t * 128:(t + 1) * 128, :], key=h, r=[("h", t)], w=[h])
    for cg, s_ in ((0, s0), (1, s1)):
        P.op('act', lambda cg=cg, s_=s_: nc.scalar.activation(out=k.junk[:, 0:512], in_=pss[cg][:], func=AF.Square,
                                                            accum_out=s_[:]), r=[pss[cg]], w=[s_])
    P.op('dve', lambda: nc.vector.tensor_tensor(out=s0[:], in0=s0[:], in1=s1[:], op=ALU.add), r=[s0, s1], w=[s0])
    rstd_from_ss(k, s0, ms, rstd, D)
    for cg in range(2):
        P.op('dve', lambda cg=cg: nc.vector.scalar_tensor_tensor(
            out=o_[:, cg * 512:(cg + 1) * 512], in0=pss[cg][:], scalar=rstd[:, 0:1],
            in1=ggr[:, cg * 512:(cg + 1) * 512], op0=ALU.mult, op1=ALU.mult), r=[pss[cg], rstd, ggr], w=[o_])
    P.op('dve', lambda: nc.vector.tensor_tensor(out=o_[:], in0=o_[:], in1=h[:], op=ALU.add), r=[o_, h], w=[o_])
    P.dma('pool', k.h_d[t * 128:(t + 1) * 128, :], o_[:], key=o_, r=[o_], w=[("h", t)])


def gate_tiles(k, ph, l, gi, gatei):
    nc, P = k.nc, k.P
    gg = [sbt(ph, nc, "gg%d" % r, [128, D], F32) for r in range(2)]
    g3 = sbt(ph, nc, "ggn", [128, D], F32)
    bcast_row(k, 'sp', g3[:], k.norm_g.ap()[l, gi:gi + 1, :], key=g3, w=[g3])
    for r in range(2):
        bcast_row(k, 'sp', gg[r][:], k.mod_d.ap()[l, r:r + 1, gatei * D:(gatei + 1) * D], key=gg[r], r=[("mod", l)],
                  w=[gg[r]])
        P.op('dve', lambda r=r: nc.vector.tensor_tensor(out=gg[r][:], in0=gg[r][:], in1=g3[:], op=ALU.mult),
             r=[gg[r], g3], w=[gg[r]])
    return gg


def proj_tm(k, ph, uT, w_ap, ngroups, tiles, epilogue):
    nc, P = k.nc, k.P
    Wb = Rot([sbt(ph, nc, "pj_w%d" % i, [128, 8, 512], BF16) for i in range(2)])

    def load(g):
        w = Wb.next()
        P.dma('pool', w[:], w_ap[:, g * 512:(g + 1) * 512].rearrange("(k p) n -> p k n", p=128), key=w, w=[w])
        return w

    nxt = load(0)
    for g in range(ngroups):
        w = nxt
        if g + 1 < ngroups:
            nxt = load(g + 1)
        for t in tiles:
            ps = k.psf_rot.next()
            for kk in range(8):
                P.op('pe', lambda kk=kk: nc.tensor.matmul(ps[:], lhsT=uT[:, kk, t * 128:(t + 1) * 128], rhs=w[:, kk, :],
                                                        start=(kk == 0), stop=(kk == 7)),
                     r=[w, ("uT", t)], w=[ps], sig=(kk == 7))
            epilogue(t, g, ps)


def rope_apply(k, ps, outb, cos, sin, nh, nf, scale, tmp):
    nc, P = k.nc, k.P
    n = nh * 4 * nf
    pv = ps[:, 0:n].rearrange("p (h r x f) -> p h r x f", h=nh, r=2, x=2)
    ov = outb[:, 0:n].rearrange("p (h r x f) -> p h r x f", h=nh, r=2, x=2)
    x1, x2 = pv[:, :, :, 0, :], pv[:, :, :, 1, :]
    cb = cos.unsqueeze(1).to_broadcast([128, nh, 2, nf])
    sb_ = sin.unsqueeze(1).to_broadcast([128, nh, 2, nf])
    tv = [t_[:, 0:nh * 2 * nf].rearrange("p (h r f) -> p h r f", h=nh, r=2) for t_ in tmp]
    for (dst, xin, tab) in ((0, x1, cb), (1, x2, sb_), (2, x1, sb_), (3, x2, cb)):
        P.op('dve', lambda dst=dst, xin=xin, tab=tab: nc.vector.scalar_tensor_tensor(
            out=tv[dst], in0=xin, scalar=float(scale), in1=tab, op0=ALU.mult, op1=ALU.mult),
            r=[ps, "rope_tab"], w=[tmp[dst]])
    P.op('dve', lambda: nc.vector.tensor_tensor(out=ov[:, :, :, 0, :], in0=tv[0], in1=tv[1], op=ALU.subtract),
         r=[tmp[0], tmp[1]], w=[outb])
    P.op('dve', lambda: nc.vector.tensor_tensor(out=ov[:, :, :, 1, :], in0=tv[2], in1=tv[3], op=ALU.add),
         r=[tmp[2], tmp[3]], w=[outb])


def rope_tables(k):
    nc, P = k.nc, k.P
    PI = float(np.pi)
    k.ropeR_d = nc.dram_tensor("ropeR_d", [2, 128, TL * 2 * 64], F32)
    k.ropeD_d = nc.dram_tensor("ropeD_d", [2, 128, TL * 2 * 16], F32)
    with ExitStack() as ph:
        pos = sbt(ph, nc, "pos", [128, TL, 2], F32)
        P.dma('sp', pos[:], k.pos[:, :, :], key=pos, w=[pos])
        for (nf, d2, dst) in ((64, 128, k.ropeR_d), (16, 32, k.ropeD_d)):
            n = TL * 2 * nf
            inv = sbt(ph, nc, "inv", [128, nf], F32)
            ang = sbt(ph, nc, "ang", [128, n], F32)
            xx = sbt(ph, nc, "xx", [128, n], F32)
            ki = sbt(ph, nc, "ki", [128, n], I32)
            kf = sbt(ph, nc, "kf", [128, n], F32)
            mm = sbt(ph, nc, "mm", [128, n], F32)
            res = sbt(ph, nc, "res", [128, n], F32)
            P.op('act', lambda: nc.scalar.activation(out=inv[:], in_=k.iotaf[:, 0:nf], func=AF.Exp,
                                                     scale=-float(np.log(10000.0)) * 2.0 / d2), r=[k.iotaf], w=[inv])
            for t in range(TL):
                for rc in range(2):
                    o = (t * 2 + rc) * nf
                    P.op('dve', lambda t=t, rc=rc, o=o: nc.vector.tensor_scalar(
                        out=ang[:, o:o + nf], in0=inv[:], scalar1=pos[:, t, rc:rc + 1], scalar2=None, op0=ALU.mult),
                        r=[inv, pos], w=[ang])
            for ci, off in ((0, PI / 2), (1, 0.0)):
                P.op('dve', lambda: nc.vector.tensor_scalar(out=xx[:], in0=ang[:], scalar1=float(off), scalar2=None,
                                                            op0=ALU.add), r=[ang], w=[xx])
                P.op('dve', lambda: nc.vector.tensor_scalar(out=ki[:], in0=xx[:], scalar1=float(1.0 / (2 * PI)),
                                                            scalar2=None, op0=ALU.mult), r=[xx], w=[ki])
                P.op('dve', lambda: nc.vector.tensor_copy(out=kf[:], in_=ki[:]), r=[ki], w=[kf])
                P.op('dve', lambda: nc.vector.scalar_tensor_tensor(out=xx[:], in0=kf[:], scalar=-2 * PI, in1=xx[:],
                                                                   op0=ALU.mult, op1=ALU.add), r=[kf, xx], w=[xx])
                P.op('dve', lambda: nc.vector.tensor_scalar(out=mm[:], in0=xx[:], scalar1=PI, scalar2=None,
                                                            op0=ALU.is_gt), r=[xx], w=[mm])
                P.op('dve', lambda: nc.vector.scalar_tensor_tensor(out=xx[:], in0=mm[:], scalar=-2 * PI, in1=xx[:],
                                                                   op0=ALU.mult, op1=ALU.add), r=[mm, xx], w=[xx])
                P.op('dve', lambda: nc.vector.tensor_scalar(out=mm[:], in0=xx[:], scalar1=-PI, scalar2=None,
                                                            op0=ALU.is_lt), r=[xx], w=[mm])
                P.op('dve', lambda: nc.vector.scalar_tensor_tensor(out=xx[:], in0=mm[:], scalar=2 * PI, in1=xx[:],
                                                                   op0=ALU.mult, op1=ALU.add), r=[mm, xx], w=[xx])
                P.op('act', lambda: nc.scalar.activation(out=res[:], in_=xx[:], func=AF.Sin), r=[xx], w=[res])
                P.dma('sp', dst.ap()[ci], res[:], key=res, r=[res], w=[("ropetab", nf, ci)])
        P.barrier()


def ret_layer(k, l):
    nc, P = k.nc, k.P
    i = l // 2
    H, DK, DV = 4, 256, 512
    if not hasattr(k, "rqT_d"):
        k.rqT_d = nc.dram_tensor("rqT_d", [TT, 128, 8, 128], BF16)
        k.rkT_d = nc.dram_tensor("rkT_d", [TT, 128, 8, 128], BF16)
        k.k_d = nc.dram_tensor("k_d", [NT, 1024], BF16)
        k.v_d = nc.dram_tensor("v_d", [NT, 2048], BF16)
        k.g_d = [nc.dram_tensor("g%d_d" % p, [NT, 2048], BF16) for p in range(2)]
        k.y_d = nc.dram_tensor("y_d", [NT, 2048], F32)
        k.F_d = nc.dram_tensor("F_d", [1024, 512], F32)
        k.Fg_d = nc.dram_tensor("Fg_d", [2048, 512], F32)
    tiles_all = list(range(TT))
    with ExitStack() as ph:
        uT = sbt(ph, nc, "r_uT", [128, 8, NT], BF16)
        with ExitStack() as ph1:
            prenorm(k, ph1, l, 0, 0, 1, uT, tiles_all)
            P.barrier()
        cos = sbt(ph, nc, "r_cos", [128, TL, 2, 64], F32)
        sin = sbt(ph, nc, "r_sin", [128, TL, 2, 64], F32)
        P.dma('sp', cos[:].rearrange("p t r f -> p (t r f)"), k.ropeR_d.ap()[0], key=cos, w=["rope_tab"])
        P.dma('sp', sin[:].rearrange("p t r f -> p (t r f)"), k.ropeR_d.ap()[1], key=sin, w=["rope_tab"])
        tmp = [sbt(ph, nc, "r_tmp%d" % j, [128, 256], F32) for j in range(4)]
        ob = Rot([sbt(ph, nc, "r_ob%d" % j, [128, 512], BF16) for j in range(3)])
        tb = Rot([sbt(ph, nc, "r_tb%d" % j, [128, 4, 128], BF16) for j in range(3)])

        def epi(t, g, ps):
            o = ob.next()
            if g < 4:
                scale = 1.0 if g < 2 else DK ** -0.5
                if t < TL:
                    rope_apply(k, ps, o, cos[:, t], sin[:, t], 2, 64, scale, tmp)
                else:
                    P.op('act', lambda: nc.scalar.activation(out=o[:], in_=ps[:], func=AF.Copy, scale=float(scale)),
                         r=[ps], w=[o])
                if g >= 2:
                    P.dma('sp', k.k_d[t * 128:(t + 1) * 128, (g - 2) * 512:(g - 1) * 512], o[:], key=o, r=[o],
                          w=[("k_d", t)])
                pb = k.psb_rot.next()
                for j in range(4):
                    P.op('pe', lambda j=j: nc.tensor.transpose(pb[:, j * 128:(j + 1) * 128], o[:, j * 128:(j + 1) * 128],
                                                              k.ident[:]), r=[o, k.ident], w=[pb], sig=(j == 3))
                tt = tb.next()
                P.op('act', lambda: nc.scalar.copy(out=tt[:], in_=pb[:, 0:512].rearrange("p (j n) -> p j n", j=4)),
                     r=[pb], w=[tt])
                dst = k.rqT_d if g < 2 else k.rkT_d
                gg_ = g % 2
                P.dma('sp', dst.ap()[t][:, gg_ * 4:(gg_ + 1) * 4, :], tt[:], key=tt, r=[tt], w=[("qkT_d", g, t)])
            elif g < 8:
                P.op('act', lambda: nc.scalar.copy(out=o[:], in_=ps[:]), r=[ps], w=[o])
                P.dma('sp', k.v_d[t * 128:(t + 1) * 128, (g - 4) * 512:(g - 3) * 512], o[:], key=o, r=[o], w=[("v_d", t)])
            else:
                p_ = (g - 8) // 4
                gc = (g - 8) % 4
                P.op('act', lambda: nc.scalar.activation(out=o[:], in_=ps[:], func=AF.Silu), r=[ps], w=[o])
                P.dma('sp', k.g_d[p_][t * 128:(t + 1) * 128, gc * 512:(gc + 1) * 512], o[:], key=o, r=[o],
                      w=[("g_d", p_, t)])

        proj_tm(k, ph, uT, k.ret_w_in.ap()[i], 16, tiles_all, epi)
        P.barrier()
    with ExitStack() as ph:
        def t_(name, shape, dt):
            return sbt(ph, nc, name, shape, dt)

        lg = t_("lg", [128, 8], F32)
        dm = t_("dm", [128, 128], F32)
        msk = [t_("msk%d" % j, [128, 128], F32) for j in range(8)]
        xi = [t_("xi%d" % j, [128, 128], BF16) for j in range(8)]
        zeta = t_("zeta", [128, 8], F32)
        cd = t_("cd", [128, 8], F32)
        c128 = t_("c128", [128, 1], F32)
        tA = t_("tA", [128, 128], F32)
        tB = t_("tB", [128, 128], F32)
        P.dma('sp', lg[:], k.ret_decay.ap()[i:i + 1].rearrange("o p h -> o (p h)").to_broadcast([128, 8]), key=lg, w=[lg])
        P.op('act', lambda: nc.scalar.activation(out=lg[:], in_=lg[:], func=AF.Exp, scale=-1.0), r=[lg], w=[lg])
        P.op('dve', lambda: nc.vector.tensor_scalar(out=lg[:], in0=lg[:], scalar1=1.0, scalar2=None, op0=ALU.add),
             r=[lg], w=[lg])
        P.op('act', lambda: nc.scalar.activation(out=lg[:], in_=lg[:], func=AF.Ln), r=[lg], w=[lg])
        P.op('dve', lambda: nc.vector.tensor_scalar(out=lg[:], in0=lg[:], scalar1=-1.0, scalar2=None, op0=ALU.mult),
             r=[lg], w=[lg])
        P.op('dve', lambda: nc.vector.memset(c128[:], 128.0), w=[c128])
        P.op('dve', lambda: nc.vector.tensor_scalar(out=dm[:], in0=k.iotaf[:], scalar1=k.iotap[:, 0:1], scalar2=None,
                                                    op0=ALU.subtract), r=[k.iotaf, k.iotap], w=[dm])
        for p_ in range(2):
            sgn = 1.0 if p_ == 0 else -1.0
            for h in range(H):
                j = p_ * 4 + h
                lgc = lg[:, j:j + 1]
                P.op('dve', lambda: nc.vector.tensor_scalar(out=tA[:], in0=dm[:], scalar1=sgn, scalar2=0.0, op0=ALU.mult,
                                                            op1=ALU.max), r=[dm], w=[tA])
                P.op('act', lambda lgc=lgc: nc.scalar.activation(out=tA[:], in_=tA[:], func=AF.Exp, scale=lgc),
                     r=[tA, lg], w=[tA])
                P.op('dve', lambda: nc.vector.tensor_scalar(out=tB[:], in0=dm[:], scalar1=sgn, scalar2=0.0, op0=ALU.mult,
                                                            op1=ALU.is_ge), r=[dm], w=[tB])
                P.op('dve', lambda j=j: nc.vector.tensor_tensor(out=msk[j][:], in0=tA[:], in1=tB[:], op=ALU.mult),
                     r=[tA, tB], w=[msk[j]])
                if p_ == 0:
                    P.op('dve', lambda: nc.vector.tensor_scalar(out=tA[:], in0=k.iotaf[:], scalar1=1.0, scalar2=None,
                                                                op0=ALU.add), r=[k.iotaf], w=[tA])
                else:
                    P.op('dve', lambda: nc.vector.tensor_scalar(out=tA[:], in0=k.iotaf[:], scalar1=-1.0, scalar2=128.0,
                                                                op0=ALU.mult, op1=ALU.add), r=[k.iotaf], w=[tA])
                P.op('act', lambda j=j, lgc=lgc: nc.scalar.activation(out=xi[j][:], in_=tA[:], func=AF.Exp, scale=lgc),
                     r=[tA, lg], w=[xi[j]])
                if p_ == 0:
                    P.op('dve', lambda: nc.vector.tensor_scalar(out=tB[:, 0:1], in0=k.iotap[:], scalar1=-1.0, scalar2=127.0,
                                                                op0=ALU.mult, op1=ALU.add), r=[k.iotap], w=[tB])
                else:
                    P.op('dve', lambda: nc.vector.tensor_copy(out=tB[:, 0:1], in_=k.iotap[:]), r=[k.iotap], w=[tB])
                P.op('act', lambda j=j, lgc=lgc: nc.scalar.activation(out=zeta[:, j:j + 1], in_=tB[:, 0:1], func=AF.Exp,
                                                                     scale=lgc), r=[tB, lg], w=[zeta])
                P.op('act', lambda j=j, lgc=lgc: nc.scalar.activation(out=cd[:, j:j + 1], in_=c128[:], func=AF.Exp,
                                                                     scale=lgc), r=[c128, lg], w=[cd])
        Wo = t_("r_Wo", [128, 16, D], BF16)
        for half in range(2):
            P.dma('pool', Wo[:, half * 8:(half + 1) * 8, :],
                  k.ret_w_out.ap()[i][half * 1024:(half + 1) * 1024, :].rearrange("(f p) n -> p f n", p=128),
                  key=("Wo", half), w=[("Wo", half)])
        gg = gate_tiles(k, ph, l, 1, 2)
        R = [t_("R%d" % h, [128, 2, DV], F32) for h in range(H)]
        Rb = [t_("Rb%d" % h, [128, 2, DV], BF16) for h in range(H)]
        Fg = t_("Fg", [128, 2, DV], F32)
        qTb = Rot([t_("qTc%d" % j, [128, 8, 128], BF16) for j in range(3)])
        kTb = Rot([t_("kTc%d" % j, [128, 8, 128], BF16) for j in range(3)])
        kb = Rot([t_("kc%d" % j, [128, 1024], BF16) for j in range(3)])
        vb = Rot([t_("vc%d" % j, [128, 2048], BF16) for j in range(3)])
        gb = Rot([t_("gc%d" % j, [128, 2048], BF16) for j in range(3)])
        psr = Rot(k.psf[0:6])
        yb = Rot([t_("y%d" % j, [128, 2048], F32) for j in range(2)])
        y1b = Rot([t_("y1_%d" % j, [128, 2048], F32) for j in range(3)])
        ybf = Rot([t_("ybf%d" % j, [128, 2048], BF16) for j in range(2)])
        yTb = Rot([t_("yT%d" % j, [128, 16, 128], BF16) for j in range(2)])
        sTb = Rot([t_("sT%d" % j, [128, 128], BF16) for j in range(8)])
        qxb = Rot([t_("qx%d" % j, [128, 2, 128], BF16) for j in range(8)])
        kzb = Rot([t_("kz%d" % j, [128, 256], BF16) for j in range(8)])
        hb = Rot([t_("r_h%d" % j, [128, D], F32) for j in range(2)])
        ob2 = Rot([t_("r_o%d" % j, [128, D], F32) for j in range(2)])
        st4 = Rot([[t_("r_s%d_%d" % (a, b), [128, 1], F32) for b in range(4)] for a in range(2)])
        st3 = Rot([[t_("r_n%d_%d" % (a, b), [128, 1], F32) for b in range(3)] for a in range(8)])

        def zero_state():
            for h in range(H):
                P.op('pool', lambda h=h: nc.gpsimd.memset(R[h][:], 0.0), w=[R[h]])
                P.op('pool', lambda h=h: nc.gpsimd.memset(Rb[h][:], 0.0), w=[Rb[h]])

        def load_chunk(p_, t):
            qTc, kTc, kc, vc, gc = qTb.next(), kTb.next(), kb.next(), vb.next(), gb.next()
            cs = slice(t * 128, (t + 1) * 128)
            P.dma('sp', qTc[:], k.rqT_d.ap()[t], key=qTc, r=[("qkT_d", g, t) for g in (0, 1)], w=[qTc])
            P.dma('sp', kTc[:], k.rkT_d.ap()[t], key=kTc, r=[("qkT_d", g, t) for g in (2, 3)], w=[kTc])
            P.dma('sp', kc[:], k.k_d[cs, :], key=kc, r=[("k_d", t)], w=[kc])
            P.dma('sp', vc[:], k.v_d[cs, :], key=vc, r=[("v_d", t)], w=[vc])
            P.dma('sp', gc[:], k.g_d[p_][cs, :], key=gc, r=[("g_d", p_, t)], w=[gc])
            y1 = None
            if p_ == 1:
                y1 = y1b.next()
                P.dma('sp', y1[:], k.y_d[cs, :], key=y1, r=[("y_d", t)], w=[y1])
            return qTc, kTc, kc, vc, gc, y1

        def stage_ab(p_, t, bufs):
            qTc, kTc, kc, vc, gc, y1 = bufs
            loc = dict(sT=[], qx=[], kz=[])
            for h in range(H):
                j = p_ * 4 + h
                kz, qx = kzb.next(), qxb.next()
                P.op('act', lambda: nc.scalar.activation(out=kz[:], in_=kc[:, h * DK:(h + 1) * DK], func=AF.Copy,
                                                         scale=zeta[:, j:j + 1]), r=[kc, zeta], w=[kz])
                P.op('pool', lambda: nc.gpsimd.tensor_tensor(
                    out=qx[:], in0=qTc[:, h * 2:h * 2 + 2, :], in1=xi[j][:].unsqueeze(1).to_broadcast([128, 2, 128]),
                    op=ALU.mult), r=[qTc, xi[j]], w=[qx])
                loc["kz"].append(kz)
                loc["qx"].append(qx)
            for h in range(H):
                j = p_ * 4 + h
                ps_s = psr.next()
                for c2 in range(2):
                    P.op('pe', lambda: nc.tensor.matmul(ps_s[:, 0:128], lhsT=kTc[:, h * 2 + c2, :],
                                                        rhs=qTc[:, h * 2 + c2, :], start=(c2 == 0), stop=(c2 == 1)),
                         r=[kTc, qTc], w=[ps_s], sig=(c2 == 1))
                sT = sTb.next()
                P.op('dve', lambda: nc.vector.tensor_tensor(out=sT[:], in0=ps_s[:, 0:128], in1=msk[j][:],
                                                            op=ALU.mult), r=[ps_s, msk[j]], w=[sT])
                loc["sT"].append(sT)
            return loc

        def stage_cd(p_, t, bufs, loc):
            qTc, kTc, kc, vc, gc, y1 = bufs
            cs = slice(t * 128, (t + 1) * 128)
            y = yb.next()
            if p_ == 1:
                yf = ybf.next()
            for h in range(H):
                j = p_ * 4 + h
                hs = slice(h * DV, (h + 1) * DV)
                kz = loc["kz"][h]
                for c2 in range(2):
                    ps_k = psr.next()
                    P.op('pe', lambda: nc.tensor.matmul(ps_k[:], lhsT=kz[:, c2 * 128:(c2 + 1) * 128], rhs=vc[:, hs],
                                                        start=True, stop=True), r=[kz, vc], w=[ps_k])
                    P.op('dve', lambda: nc.vector.scalar_tensor_tensor(
                        out=R[h][:, c2, :], in0=R[h][:, c2, :], scalar=cd[:, j:j + 1], in1=ps_k[:], op0=ALU.mult,
                        op1=ALU.add), r=[R[h], cd, ps_k], w=[R[h]])
            for h in range(H):
                hs = slice(h * DV, (h + 1) * DV)
                sT, qx = loc["sT"][h], loc["qx"][h]
                ss, ms, rstd = st3.next()
                ps_o = psr.next()
                P.op('pe', lambda: nc.tensor.matmul(ps_o[:], lhsT=sT[:], rhs=vc[:, hs], start=True, stop=False),
                     r=[sT, vc], w=[ps_o], sig=False)
                for c2 in range(2):
                    P.op('pe', lambda: nc.tensor.matmul(ps_o[:], lhsT=qx[:, c2, :], rhs=Rb[h][:, c2, :], start=False,
                                                        stop=(c2 == 1)), r=[qx, Rb[h]], w=[ps_o], sig=(c2 == 1))
                P.op('act', lambda: nc.scalar.activation(out=k.junk[:, 0:512], in_=ps_o[:], func=AF.Square, accum_out=ss[:]),
                     r=[ps_o], w=[ss])
                rstd_from_ss(k, ss, ms, rstd, DV)
                P.op('dve', lambda: nc.vector.scalar_tensor_tensor(out=y[:, hs], in0=ps_o[:], scalar=rstd[:, 0:1],
                                                                   in1=gc[:, hs], op0=ALU.mult, op1=ALU.mult),
                     r=[ps_o, rstd, gc], w=[("y", y.name, h)])
                if p_ == 1:
                    P.op('pool', lambda: nc.gpsimd.tensor_tensor(out=yf[:, hs], in0=y[:, hs], in1=y1[:, hs], op=ALU.add),
                         r=[("y", y.name, h), y1], w=[("yf", yf.name, h)])
            for h in range(H):
                P.op('act', lambda: nc.scalar.copy(out=Rb[h][:], in_=R[h][:]), r=[R[h]], w=[Rb[h]])
            if p_ == 0:
                P.dma('pool', k.y_d[cs, :], y[:], key=y, r=[("y", y.name, h) for h in range(H)], w=[("y_d", t)])
            else:
                yT = yTb.next()
                for half in range(2):
                    pb = k.psb_rot.next()
                    for jj in range(8):
                        kk = half * 8 + jj
                        P.op('pe', lambda: nc.tensor.transpose(pb[:, jj * 128:(jj + 1) * 128],
                                                               yf[:, kk * 128:(kk + 1) * 128], k.ident[:]),
                             r=[("yf", yf.name, kk // 4), k.ident], w=[pb], sig=(jj == 7))
                    P.op('act', lambda: nc.scalar.copy(out=yT[:, half * 8:(half + 1) * 8, :],
                                                       in_=pb[:].rearrange("p (j n) -> p j n", j=8)),
                         r=[pb], w=[("yTh", yT.name, half)])
                pss = [psr.next(), psr.next()]
                for cg in range(2):
                    for kk in range(16):
                        P.op('pe', lambda: nc.tensor.matmul(pss[cg][:], lhsT=yT[:, kk, :],
                                                            rhs=Wo[:, kk, cg * 512:(cg + 1) * 512],
                                                            start=(kk == 0), stop=(kk == 15)),
                             r=[("yTh", yT.name, kk // 8), ("Wo", kk // 8)], w=[pss[cg]], sig=(kk == 15))
                resid_epilogue(k, pss, t, gg[0 if t < TL else 1], hb.next(), ob2.next(), st4.next())

        def run_chunks(p_, ts):
            bufs = {0: load_chunk(p_, ts[0])}
            if len(ts) > 1:
                bufs[1] = load_chunk(p_, ts[1])
            locs = {0: stage_ab(p_, ts[0], bufs[0])}
            for n_, t in enumerate(ts):
                if n_ + 2 < len(ts):
                    bufs[n_ + 2] = load_chunk(p_, ts[n_ + 2])
                if n_ + 1 < len(ts):
                    locs[n_ + 1] = stage_ab(p_, ts[n_ + 1], bufs[n_ + 1])
                stage_cd(p_, t, bufs.pop(n_), locs.pop(n_))

        zero_state()
        run_chunks(0, [TL, TL + 1] + list(range(TL)))
        for h in range(H):
            P.dma('sp', k.F_d.ap()[h * 256:(h + 1) * 256, :].rearrange("(c p) n -> p c n", p=128), R[h][:], key=R[h],
                  r=[R[h]], w=["F_d"])
        P.op('pool', lambda: nc.gpsimd.collective_compute("AllGather", ALU.bypass, replica_groups=GROUPS,
                                                          ins=[k.F_d.ap().opt()], outs=[k.Fg_d.ap().opt()]),
             r=["F_d"], w=["Fg_d"], dma="cc", inc=1)
        zero_state()
        run_chunks(1, [TL + 1, TL])
        for h in range(H):
            for rk_ in range(2):
                P.dma('sp', Fg[:], k.Fg_d.ap()[rk_ * 1024 + h * 256:rk_ * 1024 + (h + 1) * 256, :].rearrange(
                    "(c p) n -> p c n", p=128), key=Fg, r=["Fg_d"], w=[Fg])
                if rk_ == 0:
                    P.op('dve', lambda h=h: nc.vector.tensor_scalar(out=R[h][:], in0=Fg[:], scalar1=k.selt[:, 0:1],
                                                                  scalar2=None, op0=ALU.mult), r=[Fg, k.selt], w=[R[h]])
                else:
                    P.op('dve', lambda h=h: nc.vector.scalar_tensor_tensor(out=R[h][:], in0=Fg[:], scalar=k.selt[:, 1:2],
                                                                         in1=R[h][:], op0=ALU.mult, op1=ALU.add),
                         r=[Fg, k.selt, R[h]], w=[R[h]])
            P.op('act', lambda h=h: nc.scalar.copy(out=Rb[h][:], in_=R[h][:]), r=[R[h]], w=[Rb[h]])
        run_chunks(1, list(range(TL - 1, -1, -1)))
        P.barrier()


def diff_layer(k, l, last):
    import math
    nc, P = k.nc, k.P
    i = l // 2
    lam_init = 0.8 - 0.6 * math.exp(-0.3 * l)
    NH = 8
    if not hasattr(k, "qT_d"):
        k.qT_d = nc.dram_tensor("qT_d", [8, 128, NT], BF16)
    if not hasattr(k, "kTl_d"):
        k.kTl_d = [nc.dram_tensor("kTl%d_d" % g, [512, NL], BF16) for g in range(2)]
        k.kTc_d = nc.dram_tensor("kTc_d", [1024, NCX], BF16)
        k.kTg_d = [nc.dram_tensor("kTg%d_d" % g, [1024, NL], BF16) for g in range(2)]
        k.vl_d = [nc.dram_tensor("vl%d_d" % g, [NL, 512], BF16) for g in range(2)]
        k.vc_d = nc.dram_tensor("vc_d", [NCX, 1024], BF16)
        k.vg_d = [nc.dram_tensor("vg%d_d" % g, [2 * NL, 512], BF16) for g in range(2)]
    tiles_all = list(range(TT))
    q_tiles = list(range(TL)) + ([] if last else [TL, TL + 1])
    with ExitStack() as ph:
        uT = sbt(ph, nc, "d_uT", [128, 8, NT], BF16)
        with ExitStack() as ph1:
            prenorm(k, ph1, l, 0, 0, 1, uT, tiles_all)
            P.barrier()
        cos = sbt(ph, nc, "d_cos", [128, TL, 2, 16], F32)
        sin = sbt(ph, nc, "d_sin", [128, TL, 2, 16], F32)
        P.dma('sp', cos[:].rearrange("p t r f -> p (t r f)"), k.ropeD_d.ap()[0], key=cos, w=["rope_tab"])
        P.dma('sp', sin[:].rearrange("p t r f -> p (t r f)"), k.ropeD_d.ap()[1], key=sin, w=["rope_tab"])
        tmp = [sbt(ph, nc, "d_tmp%d" % j, [128, 256], F32) for j in range(4)]
        ob = Rot([sbt(ph, nc, "d_ob%d" % j, [128, 512], BF16) for j in range(3)])
        tb = Rot([sbt(ph, nc, "d_tb%d" % j, [128, 4, 128], BF16) for j in range(3)])

        def epi_qk(isq):
            def epi(t, g, ps):
                if isq and t not in q_tiles:
                    return
                o = ob.next()
                scale = 0.125 if isq else 1.0
                if t < TL:
                    rope_apply(k, ps, o, cos[:, t], sin[:, t], 8, 16, scale, tmp)
                else:
                    P.op('act', lambda: nc.scalar.activation(out=o[:], in_=ps[:], func=AF.Copy, scale=float(scale)),
                         r=[ps], w=[o])
                pb = k.psb_rot.next()
                for j in range(4):
                    P.op('pe', lambda j=j: nc.tensor.transpose(pb[:, j * 128:(j + 1) * 128], o[:, j * 128:(j + 1) * 128],
                                                              k.ident[:]), r=[o, k.ident], w=[pb], sig=(j == 3))
                tt = tb.next()
                P.op('act', lambda: nc.scalar.copy(out=tt[:], in_=pb[:, 0:512].rearrange("p (j n) -> p j n", j=4)),
                     r=[pb], w=[tt])
                if isq:
                    P.dma('sp', k.qT_d.ap()[g * 4:(g + 1) * 4, :, t * 128:(t + 1) * 128].rearrange("j p n -> p j n"), tt[:],
                          key=tt, r=[tt], w=[("qT_d", g, t)])
                elif t < TL:
                    P.dma('sp', k.kTl_d[g].ap()[:, t * 128:(t + 1) * 128].rearrange("(j p) n -> p j n", p=128),
                          tt[:], key=tt, r=[tt], w=[("kTl_d", g, t)])
                    if t == TL - 1:
                        P.op('pool', lambda: nc.gpsimd.collective_compute(
                            "AllGather", ALU.bypass, replica_groups=GROUPS, ins=[k.kTl_d[g].ap().opt()],
                            outs=[k.kTg_d[g].ap().opt()]), r=[("kTl_d", g, t_) for t_ in range(TL)], w=[("kTg_d", g)],
                            dma="cc", inc=1)
                else:
                    P.dma('sp', k.kTc_d.ap()[g * 512:(g + 1) * 512, (t - TL) * 128:(t - TL + 1) * 128].rearrange(
                        "(j p) n -> p j n", p=128), tt[:], key=tt, r=[tt], w=[("kTc_d", g, t)])
            return epi

        def epi_v(t, g, ps):
            o = ob.next()
            P.op('act', lambda: nc.scalar.copy(out=o[:], in_=ps[:]), r=[ps], w=[o])
            if t < TL:
                P.dma('sp', k.vl_d[g][t * 128:(t + 1) * 128, :], o[:], key=o, r=[o], w=[("vl_d", g, t)])
                if t == TL - 1:
                    P.op('pool', lambda: nc.gpsimd.collective_compute(
                        "AllGather", ALU.bypass, replica_groups=GROUPS, ins=[k.vl_d[g].ap().opt()],
                        outs=[k.vg_d[g].ap().opt()]), r=[("vl_d", g, t_) for t_ in range(TL)], w=[("vg_d", g)],
                        dma="cc", inc=1)
            else:
                P.dma('sp', k.vc_d[(t - TL) * 128:(t - TL + 1) * 128, g * 512:(g + 1) * 512], o[:], key=o, r=[o],
                      w=[("vc_d", g, t)])

        wi = k.diff_w_in.ap()[i]
        proj_tm(k, ph, uT, wi[:, 1024:2048], 2, tiles_all, epi_qk(False))
        proj_tm(k, ph, uT, wi[:, 2048:3072], 2, tiles_all, epi_v)
        proj_tm(k, ph, uT, wi[:, 0:1024], 2, tiles_all, epi_qk(True))
        P.barrier()
    with ExitStack() as ph:
        def t_(name, shape, dt):
            return sbt(ph, nc, name, shape, dt)

        NK = 2 * NL + NCX
        NKT = NK // 128
        yT = t_("d_yT", [128, NH, NT], BF16)
        Wo = t_("d_Wo", [128, 8, D], BF16)
        P.dma('pool', Wo[:], k.diff_w_out.ap()[i].rearrange("(f p) n -> p f n", p=128), key=Wo, w=[Wo])
        gg = gate_tiles(k, ph, l, 1, 2)
        ones = t_("d_ones", [128, 128], BF16)
        P.op('dve', lambda: nc.vector.memset(ones[:], 1.0), w=[ones])
        lv = t_("d_lv", [128, 4, 64], F32)
        lp = t_("d_lp", [128, 2, 64], F32)
        ls = t_("d_ls", [128, 2], F32)
        nlam = t_("d_nlam", [128, 1], F32)
        gsub = t_("d_gsub", [128, 1], F32)
        P.dma('sp', lv[:].rearrange("p a b -> p (a b)"),
              k.diff_lam.ap()[i:i + 1].rearrange("o a b -> o (a b)").to_broadcast([128, 256]), key=lv, w=[lv])
        for a in range(2):
            P.op('dve', lambda a=a: nc.vector.tensor_tensor(out=lp[:, a, :], in0=lv[:, 2 * a, :], in1=lv[:, 2 * a + 1, :],
                                                          op=ALU.mult), r=[lv], w=[lp])
            P.op('dve', lambda a=a: nc.vector.reduce_sum(out=ls[:, a:a + 1], in_=lp[:, a, :], axis=AX.X), r=[lp], w=[ls])
        P.op('act', lambda: nc.scalar.activation(out=ls[:], in_=ls[:], func=AF.Exp), r=[ls], w=[ls])
        P.op('dve', lambda: nc.vector.scalar_tensor_tensor(out=nlam[:], in0=ls[:, 1:2], scalar=-float(lam_init),
                                                           in1=ls[:, 0:1], op0=ALU.add, op1=ALU.subtract),
             r=[ls], w=[nlam])
        P.dma('sp', gsub[:], k.diff_subln.ap()[i].rearrange("(p o) -> p o", o=1), key=gsub, w=[gsub],
              allow_slow_non_contiguous=True)
        P.op('dve', lambda: nc.vector.tensor_scalar(out=gsub[:], in0=gsub[:], scalar1=float(1.0 - lam_init), scalar2=None,
                                                    op0=ALU.mult), r=[gsub], w=[gsub])
        kTb = Rot([t_("d_kT%d" % j, [128, NK], BF16) for j in range(2)])
        Vb = Rot([t_("d_V%d" % j, [128, NKT, 128], BF16) for j in range(2)])
        qAb = Rot([t_("d_qA%d" % j, [128, NT], BF16) for j in range(2)])
        qBb = Rot([t_("d_qB%d" % j, [128, NT], BF16) for j in range(2)])
        for qa in qAb.items:
            P.op('pool', lambda qa=qa: nc.gpsimd.memset(qa[64:128, :], 0.0), w=[("qz", qa.name)])
        for qb_ in qBb.items:
            P.op('pool', lambda qb_=qb_: nc.gpsimd.memset(qb_[0:64, :], 0.0), w=[("qz", qb_.name)])
        eb = Rot([t_("d_e%d" % j, [128, 512], BF16) for j in range(8)])
        accD = Rot([t_("d_aD%d" % j, [128, 512], F32) for j in range(2)])
        accP = Rot([t_("d_aP%d" % j, [128, 512], F32) for j in range(2)])
        zhi = Rot([t_("d_zh%d" % j, [128, 512], BF16) for j in range(2)])
        zlo = Rot([t_("d_zl%d" % j, [128, 512], BF16) for j in range(2)])
        rzb = Rot([t_("d_rz%d" % j, [128, 512], F32) for j in range(2)])
        Onb = [Rot([t_("d_On%d_%d" % (w_, j), [128, 512], F32) for j in range(2)]) for w_ in range(2)]
        OTb = Rot([t_("d_OT%d" % j, [128, 512], F32) for j in range(2)])
        sqb = Rot([t_("d_sq%d" % j, [128, 512], BF16) for j in range(2)])
        rsb = Rot([t_("d_rs%d" % j, [128, 512], F32) for j in range(2)])
        lnb = Rot([t_("d_ln%d" % j, [128, 512], F32) for j in range(2)])
        hb = Rot([t_("d_h%d" % j, [128, D], F32) for j in range(2)])
        ob2 = Rot([t_("d_o%d" % j, [128, D], F32) for j in range(2)])
        st4 = Rot([[t_("d_s%d_%d" % (a, b), [128, 1], F32) for b in range(4)] for a in range(2)])
        class View:
            def __init__(self, t):
                self.t, self.name = t, t.name + "_f32"

            def __getitem__(self, idx):
                return self.t[:].bitcast(F32)[idx]

        ps_s = Rot(k.psf[0:3])
        ps_o = Rot(k.psf[3:5])
        ps_z = Rot([k.psf[5], View(k.psb[0])])
        ps_ln = View(k.psb[1])
        deferred = []

        def load_head(h):
            kT, V, qA, qB = kTb.next(), Vb.next(), qAb.next(), qBb.next()
            hg_, hh = h // 4, h % 4
            for rk_ in range(2):
                P.dma('sp', kT[:, rk_ * NL:(rk_ + 1) * NL],
                      k.kTg_d[hg_][rk_ * 512 + hh * 128:rk_ * 512 + (hh + 1) * 128, :],
                      key=("kT", kT.name, rk_), r=[("kTg_d", hg_)], w=[("kTp", kT.name, rk_)])
                P.dma('sp', V[:, rk_ * TL:(rk_ + 1) * TL, :],
                      k.vg_d[hg_].ap()[rk_ * NL:(rk_ + 1) * NL, hh * 128:(hh + 1) * 128].rearrange("(j p) d -> p j d", p=128),
                      key=("V", V.name, rk_), r=[("vg_d", hg_)], w=[("Vp", V.name, rk_)])
            P.dma('sp', kT[:, 2 * NL:NK], k.kTc_d[h * 128:(h + 1) * 128, :], key=("kT", kT.name, 2),
                  r=[("kTc_d", h // 4, t_) for t_ in (TL, TL + 1)], w=[("kTp", kT.name, 2)])
            P.dma('sp', V[:, 2 * TL:NKT, :], k.vc_d.ap()[:, h * 128:(h + 1) * 128].rearrange("(j p) d -> p j d", p=128),
                  key=("V", V.name, 2), r=[("vc_d", h // 4, t_) for t_ in (TL, TL + 1)], w=[("Vp", V.name, 2)])
            qr = [("qT_d", h // 4, t) for t in q_tiles]
            P.dma('sp', qA[0:64, :], k.qT_d.ap()[h, 0:64, :], key=qA, r=qr, w=[qA])
            P.dma('sp', qB[64:128, :], k.qT_d.ap()[h, 64:128, :], key=qB, r=qr, w=[qB])
            return kT, V, qA, qB

        blocks = [(qb * 512, 512, list(range(NKT))) for qb in range(4)]
        if not last:
            blocks.append((NL, NCX, [NKT - 2, NKT - 1]))
        items = []
        for h in range(NH):
            for (q0, nq, ktiles) in blocks:
                for w_ in range(2):
                    for jn, jt in enumerate(ktiles):
                        items.append(dict(h=h, q0=q0, nq=nq, w=w_, jn=jn, jt=jt, first=(jn == 0),
                                          last=(jn == len(ktiles) - 1)))
        heads = {}
        state = {}

        def emit_S(it):
            h = it["h"]
            if h not in heads:
                assert h == 0
                heads[h] = load_head(h)
            kT, V, qA, qB = heads[h]
            q = qA if it["w"] == 0 else qB
            nq, q0, jt = it["nq"], it["q0"], it["jt"]
            ps = ps_s.next()
            e = eb.next()
            it["e"] = e
            P.op('pe', lambda: nc.tensor.matmul(ps[:, 0:nq], lhsT=kT[:, jt * 128:(jt + 1) * 128], rhs=q[:, q0:q0 + nq],
                                                start=True, stop=True),
                 r=[("kTp", kT.name, min(jt // TL, 2)), q, ("qz", q.name)], w=[ps])
            P.op('act', lambda: nc.scalar.activation(out=e[:, 0:nq], in_=ps[:, 0:nq], func=AF.Exp), r=[ps], w=[e])

        def emit_PV(it):
            h, w_, nq, q0, jt, jn = it["h"], it["w"], it["nq"], it["q0"], it["jt"], it["jn"]
            kT, V, qA, qB = heads[h]
            e = it["e"]
            if it["first"] and w_ == 0 and q0 == 0 and h + 1 < NH:
                heads[h + 1] = load_head(h + 1)
            if it["first"]:
                state["pO"] = ps_o.next()
                state["aD"], state["aP"] = accD.next(), accP.next()
                state["usedP"] = False
            pO, aD, aP = state["pO"], state["aD"], state["aP"]
            P.op('pe', lambda: nc.tensor.matmul(pO[:, 0:nq], lhsT=V[:, jt, :], rhs=e[:, 0:nq], start=it["first"],
                                                stop=it["last"]),
                 r=[("Vp", V.name, min(jt // TL, 2)), e], w=[pO], sig=it["last"])
            if it["first"]:
                state["pZ"] = ps_z.next()
                state["zstart"] = True
            pZ = state["pZ"]
            if jn % 2 == 1:
                P.op('pe', lambda: nc.tensor.matmul(pZ[:, 0:nq], lhsT=ones[:], rhs=e[:, 0:nq], start=state["zstart"],
                                                    stop=False), r=[ones, e], w=[pZ], sig=True)
                state["zstart"] = False
            elif jn == 0:
                P.op('dve', lambda: nc.vector.tensor_copy(out=aD[:, 0:nq], in_=e[:, 0:nq]), r=[e], w=[aD])
            else:
                P.op('dve', lambda: nc.vector.tensor_tensor(out=aD[:, 0:nq], in0=aD[:, 0:nq], in1=e[:, 0:nq], op=ALU.add),
                     r=[e, aD], w=[aD])
            if not it["last"]:
                return
            zh, zl = zhi.next(), zlo.next()
            P.op('dve', lambda: nc.vector.tensor_copy(out=zh[:, 0:nq], in_=aD[:, 0:nq]), r=[aD], w=[zh])
            P.op('dve', lambda: nc.vector.tensor_tensor(out=zl[:, 0:nq], in0=aD[:, 0:nq], in1=zh[:, 0:nq], op=ALU.subtract),
                 r=[aD, zh], w=[zl])
            P.op('pe', lambda: nc.tensor.matmul(pZ[:, 0:nq], lhsT=ones[:], rhs=zh[:, 0:nq], start=state["zstart"],
                                                stop=False), r=[ones, zh], w=[pZ], sig=False)
            P.op('pe', lambda: nc.tensor.matmul(pZ[:, 0:nq], lhsT=ones[:], rhs=zl[:, 0:nq], start=False, stop=True),
                 r=[ones, zl], w=[pZ])
            rz = rzb.next()
            On = Onb[w_].next()
            P.op('act', lambda: nc.scalar.activation(out=rz[:, 0:nq], in_=pZ[:, 0:nq], func=AF.Ln), r=[pZ], w=[rz])
            P.op('act', lambda: nc.scalar.activation(out=rz[:, 0:nq], in_=rz[:, 0:nq], func=AF.Exp, scale=-1.0),
                 r=[rz], w=[rz])
            P.op('dve', lambda: nc.vector.tensor_tensor(out=On[:, 0:nq], in0=pO[:, 0:nq], in1=rz[:, 0:nq], op=ALU.mult),
                 r=[pO, rz], w=[On])
            state["On%d" % w_] = On
            if w_ == 0:
                return
            On0, On1 = state["On0"], state["On1"]
            OT, sq, rs, ln_ = OTb.next(), sqb.next(), rsb.next(), lnb.next()
            P.op('dve', lambda: nc.vector.scalar_tensor_tensor(out=OT[:, 0:nq], in0=On1[:, 0:nq], scalar=nlam[:, 0:1],
                                                               in1=On0[:, 0:nq], op0=ALU.mult, op1=ALU.add),
                 r=[On0, On1, nlam], w=[OT])
            P.op('act', lambda: nc.scalar.activation(out=sq[:, 0:nq], in_=OT[:, 0:nq], func=AF.Square), r=[OT], w=[sq])

            def part2():
                pS = ps_ln
                P.op('pe', lambda: nc.tensor.matmul(pS[:, 0:nq], lhsT=ones[:], rhs=sq[:, 0:nq], start=True, stop=True),
                     r=[ones, sq], w=[pS])
                P.op('act', lambda: nc.scalar.activation(out=ln_[:, 0:nq], in_=pS[:, 0:nq], func=AF.Ln, scale=1.0 / 128,
                                                         bias=k.epsb[:, 0:1]), r=[pS, k.epsb], w=[ln_])
                P.op('act', lambda: nc.scalar.activation(out=rs[:, 0:nq], in_=ln_[:, 0:nq], func=AF.Exp, scale=-0.5),
                     r=[ln_], w=[rs])
                P.op('dve', lambda: nc.vector.scalar_tensor_tensor(out=yT[:, h, q0:q0 + nq], in0=OT[:, 0:nq],
                                                                   scalar=gsub[:, 0:1], in1=rs[:, 0:nq], op0=ALU.mult,
                                                                   op1=ALU.mult), r=[OT, gsub, rs], w=[("yT", h, q0)])

            deferred.append([4, part2])

        def tick():
            for d in list(deferred):
                d[0] -= 1
                if d[0] <= 0:
                    deferred.remove(d)
                    d[1]()

        pend = []
        for it in items:
            emit_S(it)
            pend.append(it)
            if len(pend) > 2:
                emit_PV(pend.pop(0))
                tick()
        while pend:
            emit_PV(pend.pop(0))
            tick()
        while deferred:
            tick()
        for t in q_tiles:
            pss = [k.psf_rot.next(), k.psf_rot.next()]
            for cg in range(2):
                for kk in range(8):
                    P.op('pe', lambda cg=cg, kk=kk: nc.tensor.matmul(pss[cg][:], lhsT=yT[:, kk, t * 128:(t + 1) * 128],
                                                                   rhs=Wo[:, kk, cg * 512:(cg + 1) * 512],
                                                                   start=(kk == 0), stop=(kk == 7)),
                         r=[("yT", kk, (t * 128 // 512) * 512 if t < TL else NL), Wo], w=[pss[cg]], sig=(kk == 7))
            resid_epilogue(k, pss, t, gg[0 if t < TL else 1], hb.next(), ob2.next(), st4.next())
        P.barrier()

def prep_inputs(inp):
    f = lambda a: np.ascontiguousarray(np.asarray(a, dtype=np.float32))
    x, c, ctx, c_ctx = f(inp["x"]), f(inp["c"]), f(inp["ctx"]), f(inp["c_ctx"])
    shared = {n: f(inp[n]) for n in ("ada_w", "ada_b", "norm_g", "ret_w_out", "diff_w_in", "diff_w_out",
                                     "diff_subln", "ffn_w_up", "ffn_conv_b", "ffn_w_down")}
    shared["diff_lam"] = np.ascontiguousarray(np.stack([f(inp["diff_lq1"]), f(inp["diff_lk1"]), f(inp["diff_lq2"]),
                                                        f(inp["diff_lk2"])], axis=1))
    rw = f(inp["ret_w_in"])
    rw_sw = np.ascontiguousarray(np.concatenate([rw[:, :, :4096], rw[:, :, 6144:8192], rw[:, :, 4096:6144]], axis=2))
    dec = np.stack([f(inp["ret_decay_f"]), f(inp["ret_decay_b"])], axis=1)
    dec_sw = np.ascontiguousarray(dec[:, ::-1, :])
    cwv = f(inp["ffn_conv_w"])
    cw_sw = np.ascontiguousarray(cwv[:, ::-1, :])
    maps = []
    for core in range(8):
        b, half = core // 2, core % 2
        xs = x[b, half * NL:(half + 1) * NL]
        cs = ctx[b]
        pos = np.arange(half * NL, (half + 1) * NL)
        if half:
            xs, cs, pos = xs[::-1], cs[::-1], pos[::-1]
        posrc = np.stack([pos // 64, pos % 64], axis=-1).astype(np.float32)
        m = dict(shared)
        m["x"] = np.ascontiguousarray(xs)
        m["ctx"] = np.ascontiguousarray(cs)
        m["c2"] = np.ascontiguousarray(np.stack([c[b], c_ctx]))
        sel = np.zeros((128, 2), np.float32)
        sel[:, 1 - half] = 1.0
        m["sel"] = sel
        m["pos"] = np.ascontiguousarray(posrc.reshape(TL, 128, 2).transpose(1, 0, 2))
        m["ret_w_in"] = rw_sw if half else rw
        m["ret_decay"] = dec_sw if half else np.ascontiguousarray(dec)
        m["ffn_conv_w"] = cw_sw if half else cwv
        maps.append(m)
    return maps


def gather_out(results):
    out = np.empty((4, 2 * NL, D), np.float32)
    for core in range(8):
        b, half = core // 2, core % 2
        o = np.asarray(results[core]["out"])[:NL]
        out[b, half * NL:(half + 1) * NL] = o[::-1] if half else o
    return out


def kernel(**inputs):
    nc = build("full")
    maps = prep_inputs(inputs)
    res = run_bass_kernel_spmd(nc, maps, core_ids=list(range(8)))
    return gather_out(res.results)
```

```python
import numpy as np
from contextlib import ExitStack
import concourse.bass as bass
import concourse.mybir as mybir
from concourse.bass_utils import run_bass_kernel_spmd

F32 = mybir.dt.float32
BF16 = mybir.dt.bfloat16
I32 = mybir.dt.int32
AF = mybir.ActivationFunctionType
ALU = mybir.AluOpType
AX = mybir.AxisListType

D = 1024
NL = 2048
NCX = 256
NT = NL + NCX
TL, TCX, TT = 16, 2, 18
DEPTH = 4
FH = 2816
NFC = 22
EPS = 1e-6
GROUPS = [[0, 1], [2, 3], [4, 5], [6, 7]]


class Prog:
    def __init__(self, nc, stack):
        self.nc = nc
        self.stack = stack
        self.E = dict(pe=nc.tensor, act=nc.scalar, dve=nc.vector, pool=nc.gpsimd, sp=nc.sync)
        self.phys = []
        self.pcnt = []
        self.key2p = {}
        self.free = {}
        self.pclass = {}
        self.lastw = {}
        self.rd = {}
        self.dmalast = {}
        self.waited = {e: {} for e in self.E}
        self.nins = 0

    def _sem(self, k, cls="c"):
        if k not in self.key2p:
            fl = self.free.setdefault(cls, [])
            if fl:
                p = fl.pop()
            else:
                p = len(self.phys)
                self.phys.append(self.stack.enter_context(self.nc.semaphore("s%d" % p)))
                self.pcnt.append(0)
                self.pclass[p] = cls
            self.key2p[k] = p
        p = self.key2p[k]
        assert self.pclass[p] == cls, (k, cls, self.pclass[p])
        return p

    @staticmethod
    def _rk(x):
        return x if isinstance(x, (str, tuple)) else ("T", x.name)

    def op(self, eng, fn, r=(), w=(), sig=True, dma=None, inc=None):
        r = [self._rk(x) for x in r]
        w = [self._rk(x) for x in w]
        if dma is not None:
            dma = self._rk(dma)
        kind = 'd' if dma is not None else 'c'
        if dma is None:
            p = self._sem(eng)
        else:
            p = self._sem(('d', dma), "cc" if inc == 1 else ("sw" if eng == "pool" else "hw"))
        deps = {}

        def add(tok, raw):
            k, v, e, kd = tok
            if kd == 'c' and kind == 'c' and e == eng:
                if eng == 'pe' or not raw:
                    return
            if deps.get(k, 0) < v:
                deps[k] = v

        for x in r:
            if x in self.lastw:
                add(self.lastw[x], True)
        for x in w:
            if x in self.lastw:
                add(self.lastw[x], False)
            for tok in self.rd.get(x, {}).values():
                add(tok, False)
        if dma is not None and dma in self.dmalast:
            add(self.dmalast[dma], False)
        wd = self.waited[eng]
        for k, v in deps.items():
            if wd.get(k, 0) >= v:
                continue
            assert self.pcnt[k] >= v, ("wait on unsignaled op", k, v, self.pcnt[k])
            self.E[eng].wait_ge(self.phys[k], v)
            wd[k] = v
            self.nins += 1
        ins = fn()
        self.nins += 1
        if sig:
            if inc is None:
                inc = 16 if kind == 'd' else 1
            self.pcnt[p] += inc
            ins.then_inc(self.phys[p], inc)
            val = self.pcnt[p]
        else:
            val = self.pcnt[p] + 1
        tok = (p, val, eng, kind)
        for x in r:
            self.rd.setdefault(x, {})[p] = tok
        for x in w:
            self.lastw[x] = tok
            self.rd[x] = {}
        if dma is not None:
            self.dmalast[dma] = tok

    def dma(self, q, out, in_, key, r=(), w=(), **kw):
        self.op(q, lambda: self.E[q].dma_start(out=out, in_=in_, **kw), r=r, w=w, dma=key)

    def barrier(self):
        for e in self.E:
            wd = self.waited[e]
            for p, h in enumerate(self.phys):
                v = self.pcnt[p]
                if v > wd.get(p, 0):
                    self.E[e].wait_ge(h, v)
                    wd[p] = v
                    self.nins += 1
        self.lastw.clear()
        self.rd.clear()
        self.dmalast.clear()
        for k in [k for k in self.key2p if isinstance(k, tuple) and k[0] == 'd']:
            p = self.key2p.pop(k)
            self.free[self.pclass[p]].append(p)


_UNIQ = [0]


def sbt(ph, nc, name, shape, dt):
    _UNIQ[0] += 1
    return ph.enter_context(nc.sbuf_tensor("%s_%d" % (name, _UNIQ[0]), list(shape), dt))


class Rot:
    def __init__(self, items):
        self.items = list(items)
        self.i = 0

    def next(self):
        x = self.items[self.i % len(self.items)]
        self.i += 1
        return x


class K:
    pass


def build(mode="full"):
    nc = bass.Bass("TRN2", target_bir_lowering=False)
    stack = ExitStack()
    with stack:
        _build(nc, stack, mode)
    return nc


def _dram_in(nc, name, shape, dt=F32):
    return nc.dram_tensor(name, list(shape), dt, kind="ExternalInput")


def _build(nc, stack, mode):
    P = Prog(nc, stack)
    k = K()
    k.nc, k.P, k.mode = nc, P, mode
    k.x = _dram_in(nc, "x", [NL, D])
    k.ctx = _dram_in(nc, "ctx", [NCX, D])
    k.c2 = _dram_in(nc, "c2", [2, D])
    k.sel = _dram_in(nc, "sel", [128, 2])
    k.pos = _dram_in(nc, "pos", [128, TL, 2])
    k.ada_w = _dram_in(nc, "ada_w", [DEPTH, D, 6 * D])
    k.ada_b = _dram_in(nc, "ada_b", [DEPTH, 6 * D])
    k.norm_g = _dram_in(nc, "norm_g", [DEPTH, 4, D])
    k.ret_w_in = _dram_in(nc, "ret_w_in", [2, D, 8192])
    k.ret_w_out = _dram_in(nc, "ret_w_out", [2, 2048, D])
    k.ret_decay = _dram_in(nc, "ret_decay", [2, 2, 4])
    k.diff_w_in = _dram_in(nc, "diff_w_in", [2, D, 3 * D])
    k.diff_w_out = _dram_in(nc, "diff_w_out", [2, D, D])
    k.diff_lam = _dram_in(nc, "diff_lam", [2, 4, 64])
    k.diff_subln = _dram_in(nc, "diff_subln", [2, 128])
    k.ffn_w_up = _dram_in(nc, "ffn_w_up", [DEPTH, D, 2 * FH])
    k.ffn_conv_w = _dram_in(nc, "ffn_conv_w", [DEPTH, 3, FH])
    k.ffn_conv_b = _dram_in(nc, "ffn_conv_b", [DEPTH, FH])
    k.ffn_w_down = _dram_in(nc, "ffn_w_down", [DEPTH, FH, D])
    k.out = nc.dram_tensor("out", [NT, D], F32, kind="ExternalOutput")
    k.h_d = nc.dram_tensor("h_d", [NT, D], F32)
    k.mod_d = nc.dram_tensor("mod_d", [DEPTH, 2, 6 * D], F32)
    k.hx_d = nc.dram_tensor("hx_d", [128, 8], BF16)
    k.hg_d = nc.dram_tensor("hg_d", [256, 8], BF16)

    def sb(name, shape, dt):
        return stack.enter_context(nc.sbuf_tensor(name, list(shape), dt))

    k.ident = sb("ident", [128, 128], BF16)
    k.iotaf = sb("iotaf", [128, 128], F32)
    k.iotap = sb("iotap", [128, 1], F32)
    k.neghalf = sb("neghalf", [128, 1], F32)
    k.epsb = sb("epsb", [128, 1], F32)
    k.selt = sb("selt", [128, 2], F32)
    k.junk = sb("junk", [128, 1024], BF16)
    k.psf = [stack.enter_context(nc.psum_tensor("psf%d" % i, [128, 512], F32)) for i in range(6)]
    k.psb = [stack.enter_context(nc.psum_tensor("psb%d" % i, [128, 1024], BF16)) for i in range(2)]
    k.psf_rot = Rot(k.psf)
    k.psb_rot = Rot(k.psb)

    consts(k)
    ada_all(k)
    init_h(k)
    if mode == "ffn0":
        ffn_layer(k, 0, False)
    if mode in ("ret0", "L0"):
        rope_tables(k)
        ret_layer(k, 0)
        if mode == "L0":
            ffn_layer(k, 0, False)
    if mode == "diff1":
        rope_tables(k)
        diff_layer(k, 1, False)
    if mode == "full":
        rope_tables(k)
        for l in range(DEPTH):
            last = l == DEPTH - 1
            if l % 2 == 0:
                ret_layer(k, l)
            else:
                diff_layer(k, l, last)
            ffn_layer(k, l, last)
    P.barrier()
    with ExitStack() as ph:
        ob = [sbt(ph, nc, "ob%d" % i, [128, D], F32) for i in range(3)]
        rot = Rot(ob)
        for t in range(TT):
            b = rot.next()
            P.dma('sp', b[:], k.h_d[t * 128:(t + 1) * 128, :], key=b, r=[("h", t)], w=[b])
            P.dma('sp', k.out[t * 128:(t + 1) * 128, :], b[:], key=b, r=[b], w=[("out", t)])
        P.barrier()
    print("instructions:", P.nins, "sems:", len(P.phys))


def consts(k):
    nc, P = k.nc, k.P
    with ExitStack() as ph:
        ii = sbt(ph, nc, "ii", [128, 128], I32)
        pp = sbt(ph, nc, "pp", [128, 1], I32)
        P.op('pool', lambda: nc.gpsimd.iota(ii[:], pattern=[[1, 128]], base=0, channel_multiplier=0), w=[ii])
        P.op('pool', lambda: nc.gpsimd.iota(pp[:], pattern=[[0, 1]], base=0, channel_multiplier=1), w=[pp])
        P.op('dve', lambda: nc.vector.tensor_copy(out=k.iotaf[:], in_=ii[:]), r=[ii], w=[k.iotaf])
        P.op('dve', lambda: nc.vector.tensor_copy(out=k.iotap[:], in_=pp[:]), r=[pp], w=[k.iotap])
        P.op('dve', lambda: nc.vector.tensor_scalar(out=k.ident[:], in0=k.iotaf[:], scalar1=k.iotap[:, 0:1],
                                                    scalar2=None, op0=ALU.is_equal),
             r=[k.iotaf, k.iotap], w=[k.ident])
        P.op('dve', lambda: nc.vector.memset(k.neghalf[:], -0.5), w=[k.neghalf])
        P.op('dve', lambda: nc.vector.memset(k.epsb[:], EPS), w=[k.epsb])
        P.dma('sp', k.selt[:], k.sel[:, :], key=k.selt, w=[k.selt])
        P.barrier()


def init_h(k):
    P = k.P
    P.dma('sp', k.h_d[0:NL, :], k.x[:, :], key="inith0", w=[("h", t) for t in range(TL)])
    P.dma('sp', k.h_d[NL:NT, :], k.ctx[:, :], key="inith1", w=[("h", t) for t in range(TL, TT)])


def ada_all(k):
    nc, P = k.nc, k.P
    with ExitStack() as ph:
        cT = sbt(ph, nc, "cT", [128, 8, 2], F32)
        sT = sbt(ph, nc, "sT", [128, 8, 2], F32)
        wb = [sbt(ph, nc, "adaw%d" % i, [128, 8, 512], F32) for i in range(2)]
        bias = sbt(ph, nc, "adab", [2, 6 * D], F32)
        row = sbt(ph, nc, "adarow", [2, 6 * D], F32)
        wrot = Rot(wb)
        for r in range(2):
            P.dma('sp', cT[:, :, r], k.c2.ap()[r].rearrange("(k p) -> p k", p=128), key=cT, w=[cT],
                  allow_slow_non_contiguous=True)
        P.op('act', lambda: nc.scalar.activation(out=sT[:], in_=cT[:], func=AF.Silu), r=[cT], w=[sT])
        for l in range(DEPTH):
            P.dma('sp', bias[:], k.ada_b.ap()[l:l + 1, :].to_broadcast([2, 6 * D]), key=bias, w=[bias])
            for g in range(12):
                w = wrot.next()
                P.dma('sp', w[:], k.ada_w.ap()[l][:, g * 512:(g + 1) * 512].rearrange("(k p) n -> p k n", p=128),
                      key=w, w=[w])
                ps = k.psf_rot.next()
                for kk in range(8):
                    P.op('pe', lambda kk=kk, w=w, ps=ps: nc.tensor.matmul(ps[0:2, :], lhsT=sT[:, kk, :], rhs=w[:, kk, :],
                                                                     start=(kk == 0), stop=(kk == 7)),
                         r=[sT, w], w=[ps], sig=(kk == 7))
                P.op('dve', lambda g=g, ps=ps: nc.vector.tensor_tensor(out=row[:, g * 512:(g + 1) * 512], in0=ps[0:2, :],
                                                                    in1=bias[:, g * 512:(g + 1) * 512], op=ALU.add),
                     r=[ps, bias], w=[row])
            P.dma('sp', k.mod_d.ap()[l], row[:], key=row, r=[row], w=[("mod", l)])
        P.barrier()


def bcast_row(k, q, dst, src_row, key, r=(), w=()):
    n = dst.shape[-1]
    k.P.dma(q, dst, src_row.to_broadcast([128, n]), key=key, r=r, w=w)


def rstd_from_ss(k, ss, ms, rstd, n):
    nc, P = k.nc, k.P
    P.op('act', lambda: nc.scalar.activation(out=ms[:], in_=ss[:], func=AF.Ln, scale=1.0 / n, bias=k.epsb[:, 0:1]),
         r=[ss, k.epsb], w=[ms])
    P.op('act', lambda: nc.scalar.activation(out=rstd[:], in_=ms[:], func=AF.Exp, scale=-0.5), r=[ms], w=[rstd])


def prenorm(k, ph, l, gi, shi, sci, uT, tiles, hook=None):
    nc, P = k.nc, k.P

    def t_(name, shape, dt):
        return sbt(ph, nc, name, shape, dt)

    gb = t_("pn_g", [128, D], F32)
    gs = [t_("pn_gs%d" % r, [128, D], F32) for r in range(2)]
    sh = [t_("pn_sh%d" % r, [128, D], F32) for r in range(2)]
    hb = Rot([t_("pn_h%d" % i, [128, D], F32) for i in range(3)])
    tb = Rot([t_("pn_t%d" % i, [128, D], F32) for i in range(2)])
    ub = Rot([t_("pn_u%d" % i, [128, D], BF16) for i in range(2)])
    st = Rot([[t_("pn_s%d_%d" % (i, j), [128, 1], F32) for j in range(3)] for i in range(3)])
    bcast_row(k, 'sp', gb[:], k.norm_g.ap()[l, gi:gi + 1, :], key=gb, w=[gb])
    for r in range(2):
        bcast_row(k, 'sp', gs[r][:], k.mod_d.ap()[l, r:r + 1, sci * D:(sci + 1) * D], key=gs[r], r=[("mod", l)], w=[gs[r]])
        bcast_row(k, 'sp', sh[r][:], k.mod_d.ap()[l, r:r + 1, shi * D:(shi + 1) * D], key=sh[r], r=[("mod", l)], w=[sh[r]])
        P.op('dve', lambda r=r: nc.vector.scalar_tensor_tensor(out=gs[r][:], in0=gs[r][:], scalar=1.0, in1=gb[:],
                                                           op0=ALU.add, op1=ALU.mult), r=[gs[r], gb], w=[gs[r]])
    def stage_a(t):
        h = hb.next()
        ss, ms, rstd = st.next()
        P.dma('sp', h[:], k.h_d[t * 128:(t + 1) * 128, :], key=h, r=[("h", t)], w=[h])
        P.op('act', lambda: nc.scalar.activation(out=k.junk[:], in_=h[:], func=AF.Square, accum_out=ss[:]),
             r=[h], w=[ss])
        rstd_from_ss(k, ss, ms, rstd, D)
        return h, rstd

    def stage_b(t, h, rstd):
        r = 0 if t < TL else 1
        tt = tb.next()
        u = ub.next()
        P.op('dve', lambda: nc.vector.scalar_tensor_tensor(
            out=tt[:], in0=h[:], scalar=rstd[:, 0:1], in1=gs[r][:], op0=ALU.mult, op1=ALU.mult),
            r=[h, rstd, gs[r]], w=[tt])
        P.op('dve', lambda: nc.vector.tensor_tensor(out=u[:], in0=tt[:], in1=sh[r][:], op=ALU.add),
             r=[tt, sh[r]], w=[u])
        pb = k.psb_rot.next()
        for kk in range(8):
            P.op('pe', lambda kk=kk: nc.tensor.transpose(pb[:, kk * 128:(kk + 1) * 128], u[:, kk * 128:(kk + 1) * 128],
                                                        k.ident[:]),
                 r=[u, k.ident], w=[pb], sig=(kk == 7))
        P.op('act', lambda: nc.scalar.copy(out=uT[:, :, t * 128:(t + 1) * 128],
                                           in_=pb[:].rearrange("p (k n) -> p k n", k=8)),
             r=[pb], w=[("uT", t)])
        if hook is not None:
            hook(t)

    prev = None
    for t in tiles:
        a = stage_a(t)
        if prev is not None:
            stage_b(*prev)
        prev = (t,) + a
    stage_b(*prev)


def uT_res(c, n):
    return [("uT", t) for t in range(c // 128, (c + n - 1) // 128 + 1)]


def ffn_layer(k, l, last):
    nc, P = k.nc, k.P
    tiles_all = list(range(TL)) + ([] if last else list(range(TL, TT)))
    with ExitStack() as ph:
        def t_(name, shape, dt):
            return sbt(ph, nc, name, shape, dt)

        uT = t_("f_uT", [128, 8, NT], BF16)
        Wd = t_("f_Wd", [128, NFC, D], BF16)
        cw = t_("f_cw", [128, 4, NFC], F32)
        uH = [t_("f_uH%d" % i, [128, 8, 2], BF16) for i in range(2)]
        hx = t_("f_hx", [128, 8], BF16)
        hg = t_("f_hg", [128, 2, 8], BF16)
        hgf = t_("f_hgf", [128, 8], F32)
        for half in range(2):
            f0, f1 = half * 11, (half + 1) * 11
            P.dma('pool', Wd[:, f0:f1, :], k.ffn_w_down.ap()[l][f0 * 128:f1 * 128, :].rearrange("(f p) n -> p f n", p=128),
                  key=("Wd", half), w=[("Wd", half)])
        for tap in range(3):
            P.dma('sp', cw[:, tap, :], k.ffn_conv_w.ap()[l, tap].rearrange("(f p) -> p f", p=128), key=cw, w=[cw],
                  allow_slow_non_contiguous=True)
        P.dma('sp', cw[:, 3, :], k.ffn_conv_b.ap()[l].rearrange("(f p) -> p f", p=128), key=cw, w=[cw],
              allow_slow_non_contiguous=True)
        with ExitStack() as ph1:
            order = [TL - 1] + [t for t in tiles_all if t != TL - 1]

            def hook(t):
                if t != TL - 1:
                    return
                P.op('dve', lambda: nc.vector.tensor_copy(out=hx[:], in_=uT[:, :, NL - 1]), r=[("uT", TL - 1)], w=[hx])
                P.dma('pool', k.hx_d[:, :], hx[:], key=hx, r=[hx], w=["hx_d"])
                P.op('pool', lambda: nc.gpsimd.collective_compute("AllGather", ALU.bypass, replica_groups=GROUPS,
                                                                  ins=[k.hx_d.ap().opt()], outs=[k.hg_d.ap().opt()]),
                     r=["hx_d"], w=["hg_d"], dma="cc", inc=1)

            prenorm(k, ph1, l, 2, 3, 4, uT, order, hook)
            P.dma('sp', hg[:], k.hg_d.ap().rearrange("(r p) k -> p r k", p=128), key=hg, r=["hg_d"], w=[hg])
            P.op('dve', lambda: nc.vector.tensor_scalar(out=hgf[:], in0=hg[:, 0, :], scalar1=k.selt[:, 0:1], scalar2=None,
                                                        op0=ALU.mult), r=[hg, k.selt], w=[hgf])
            P.op('dve', lambda: nc.vector.scalar_tensor_tensor(out=uH[1][:, :, 1], in0=hg[:, 1, :], scalar=k.selt[:, 1:2],
                                                               in1=hgf[:], op0=ALU.mult, op1=ALU.add),
                 r=[hg, hgf, k.selt], w=[uH[1]])
            P.op('dve', lambda: nc.vector.tensor_copy(out=uH[1][:, :, 0], in_=uT[:, :, NL // 2 - 1]), r=[("uT", 7)], w=[uH[1]])
            P.op('dve', lambda: nc.vector.tensor_copy(out=uH[0][:, :, 0], in_=uT[:, :, NL // 2]), r=[("uT", 8)], w=[uH[0]])
            P.op('dve', lambda: nc.vector.tensor_copy(out=uH[0][:, :, 1], in_=uT[:, :, NL // 2]), r=[("uT", 8)], w=[uH[0]])
            P.barrier()
        segs = [dict(tiles=list(range(0, 8)), subs=[(0, 1024, False, True)], uH=uH[0]),
                dict(tiles=list(range(8, 16)) + ([] if last else [16, 17]),
                     subs=[(1024, 1024, True, True)] + ([] if last else [(2048, 256, False, False)]), uH=uH[1])]
        with ExitStack() as ph2:
            def t2(name, shape, dt):
                return sbt(ph2, nc, name, shape, dt)

            NCOL = 1280
            aT = t2("f_aT", [128, NFC, NCOL], BF16)
            Wv = Rot([t2("f_Wv%d" % i, [128, 8, 256], BF16) for i in range(2)])
            Wg = Rot([t2("f_Wg%d" % i, [128, 8, 256], BF16) for i in range(2)])
            GW = NCOL + 4
            Gb = Rot([t2("f_G%d" % i, [128, GW], F32) for i in range(2)])
            Vb = Rot([t2("f_V%d" % i, [128, NCOL], F32) for i in range(2)])
            Tb = Rot([t2("f_T%d" % i, [128, NCOL], F32) for i in range(1)])
            gg = [t2("f_gg%d" % r, [128, D], F32) for r in range(2)]
            g3 = t2("f_g3", [128, D], F32)
            hb = Rot([t2("f_h%d" % i, [128, D], F32) for i in range(2)])
            ob = Rot([t2("f_o%d" % i, [128, D], F32) for i in range(2)])
            st = Rot([[t2("f_s%d_%d" % (i, j), [128, 1], F32) for j in range(4)] for i in range(2)])
            for G in Gb.items:
                P.op('pool', lambda G=G: nc.gpsimd.memset(G[:], 0.0), w=[G])
            bcast_row(k, 'sp', g3[:], k.norm_g.ap()[l, 3:4, :], key=g3, w=[g3])
            for r in range(2):
                bcast_row(k, 'sp', gg[r][:], k.mod_d.ap()[l, r:r + 1, 5 * D:6 * D], key=gg[r], r=[("mod", l)], w=[gg[r]])
                P.op('dve', lambda r=r: nc.vector.tensor_tensor(out=gg[r][:], in0=gg[r][:], in1=g3[:], op=ALU.mult),
                     r=[gg[r], g3], w=[gg[r]])
            wup = k.ffn_w_up.ap()[l]

            def load_w(fcg):
                wv, wg = Wv.next(), Wg.next()
                P.dma('pool', wv[:], wup[:, fcg * 256:(fcg + 1) * 256].rearrange("(k p) n -> p k n", p=128), key=wv, w=[wv])
                P.dma('pool', wg[:], wup[:, FH + fcg * 256:FH + (fcg + 1) * 256].rearrange("(k p) n -> p k n", p=128),
                      key=wg, w=[wg])
                return wv, wg

            for seg in segs:
                subs = seg["subs"]
                uHs = seg["uH"]
                offs, goffs = [], []
                o, go = 0, 0
                for (c0, n, lh, rh) in subs:
                    offs.append(o)
                    goffs.append(go)
                    o += n
                    go += n + 2
                nxt = load_w(0)
                for fcg in range(11):
                    wv, wg = nxt
                    if fcg + 1 < 11:
                        nxt = load_w(fcg + 1)
                    for j in range(2):
                        fc = fcg * 2 + j
                        G, V, T = Gb.next(), Vb.next(), Tb.next()
                        for si, (c0, n, lh, rh) in enumerate(subs):
                            for b0 in range(0, n, 512):
                                nb = min(512, n - b0)
                                for (wt, dst, doff) in ((wv, V, offs[si] + b0), (wg, G, goffs[si] + 1 + b0)):
                                    ps = k.psf_rot.next()
                                    for kk in range(8):
                                        P.op('pe', lambda kk=kk, wt=wt, ps=ps, c=c0 + b0, nb=nb, j=j: nc.tensor.matmul(
                                            ps[:, 0:nb], lhsT=wt[:, kk, j * 128:(j + 1) * 128], rhs=uT[:, kk, c:c + nb],
                                            start=(kk == 0), stop=(kk == 7)),
                                            r=[wt] + uT_res(c0 + b0, nb), w=[ps], sig=(kk == 7))
                                    P.op('act', lambda ps=ps, dst=dst, doff=doff, nb=nb: nc.scalar.copy(
                                        out=dst[:, doff:doff + nb], in_=ps[:, 0:nb]), r=[ps], w=[dst])
                        ps = k.psf_rot.next()
                        for kk in range(8):
                            P.op('pe', lambda kk=kk, ps=ps, j=j, wg=wg: nc.tensor.matmul(
                                ps[:, 0:2], lhsT=wg[:, kk, j * 128:(j + 1) * 128], rhs=uHs[:, kk, :],
                                start=(kk == 0), stop=(kk == 7)), r=[wg, uHs], w=[ps], sig=(kk == 7))
                        for si, (c0, n, lh, rh) in enumerate(subs):
                            go = goffs[si]
                            if lh:
                                P.op('act', lambda ps=ps, G=G, go=go: nc.scalar.copy(out=G[:, go:go + 1], in_=ps[:, 0:1]),
                                     r=[ps], w=[G])
                            if rh:
                                P.op('act', lambda ps=ps, G=G, go=go, n=n: nc.scalar.copy(
                                    out=G[:, go + n + 1:go + n + 2], in_=ps[:, 1:2]), r=[ps], w=[G])
                        for si, (c0, n, lh, rh) in enumerate(subs):
                            go, o = goffs[si], offs[si]
                            P.op('dve', lambda G=G, T=T, go=go, o=o, n=n, fc=fc: nc.vector.tensor_scalar(
                                out=T[:, o:o + n], in0=G[:, go + 1:go + 1 + n], scalar1=cw[:, 1, fc:fc + 1],
                                scalar2=cw[:, 3, fc:fc + 1], op0=ALU.mult, op1=ALU.add), r=[G, cw], w=[T])
                            P.op('dve', lambda G=G, T=T, go=go, o=o, n=n, fc=fc: nc.vector.scalar_tensor_tensor(
                                out=T[:, o:o + n], in0=G[:, go:go + n], scalar=cw[:, 0, fc:fc + 1], in1=T[:, o:o + n],
                                op0=ALU.mult, op1=ALU.add), r=[G, cw, T], w=[T])
                            P.op('dve', lambda G=G, T=T, go=go, o=o, n=n, fc=fc: nc.vector.scalar_tensor_tensor(
                                out=T[:, o:o + n], in0=G[:, go + 2:go + 2 + n], scalar=cw[:, 2, fc:fc + 1], in1=T[:, o:o + n],
                                op0=ALU.mult, op1=ALU.add), r=[G, cw, T], w=[T])
                            P.op('act', lambda T=T, o=o, n=n: nc.scalar.activation(out=T[:, o:o + n], in_=T[:, o:o + n],
                                                                                func=AF.Silu), r=[T], w=[T])
                            P.op('dve', lambda T=T, V=V, o=o, n=n, fc=fc: nc.vector.tensor_tensor(
                                out=aT[:, fc, o:o + n], in0=T[:, o:o + n], in1=V[:, o:o + n], op=ALU.mult),
                                r=[T, V], w=[("aT", fc)])
                col = 0
                for t in seg["tiles"]:
                    r = 0 if t < TL else 1
                    pss = [k.psf_rot.next(), k.psf_rot.next()]
                    for cg in range(2):
                        for fc in range(NFC):
                            P.op('pe', lambda fc=fc, cg=cg, col=col: nc.tensor.matmul(
                                pss[cg][:], lhsT=aT[:, fc, col:col + 128], rhs=Wd[:, fc, cg * 512:(cg + 1) * 512],
                                start=(fc == 0), stop=(fc == NFC - 1)),
                                r=[("aT", fc), ("Wd", fc // 11)], w=[pss[cg]], sig=(fc == NFC - 1))
                    h, o_ = hb.next(), ob.next()
                    s0, s1, ms, rstd = st.next()
                    P.dma('sp', h[:], k.h_d[t * 128:(t + 1) * 128, :], key=h, r=[("h", t)], w=[h])
                    for cg, s in ((0, s0), (1, s1)):
                        P.op('act', lambda cg=cg, s=s: nc.scalar.activation(out=k.junk[:, 0:512], in_=pss[cg][:], func=AF.Square,
                                                                        accum_out=s[:]), r=[pss[cg]], w=[s])
                    P.op('dve', lambda s0=s0, s1=s1: nc.vector.tensor_tensor(out=s0[:], in0=s0[:], in1=s1[:], op=ALU.add),
                         r=[s0, s1], w=[s0])
                    rstd_from_ss(k, s0, ms, rstd, D)
                    for cg in range(2):
                        P.op('dve', lambda cg=cg, o_=o_, rstd=rstd, r=r: nc.vector.scalar_tensor_tensor(
                            out=o_[:, cg * 512:(cg + 1) * 512], in0=pss[cg][:], scalar=rstd[:, 0:1],
                            in1=gg[r][:, cg * 512:(cg + 1) * 512], op0=ALU.mult, op1=ALU.mult),
                            r=[pss[cg], rstd, gg[r]], w=[o_])
                    P.op('dve', lambda o_=o_, h=h: nc.vector.tensor_tensor(out=o_[:], in0=o_[:], in1=h[:], op=ALU.add),
                         r=[o_, h], w=[o_])
                    P.dma('sp', k.h_d[t * 128:(t + 1) * 128, :], o_[:], key=o_, r=[o_], w=[("h", t)])
                    col += 128
            P.barrier()


def resid_epilogue(k, pss, t, ggr, h, o_, stt4):
    nc, P = k.nc, k.P
    s0, s1, ms, rstd = stt4
    P.dma('sp', h[:], k.h_d[t * 128:(t + 1) * 128, :], key=h, r=[("h", t)], w=[h])
    for cg, s_ in ((0, s0), (1, s1)):
        P.op('act', lambda cg=cg, s_=s_: nc.scalar.activation(out=k.junk[:, 0:512], in_=pss[cg][:], func=AF.Square,
                                                            accum_out=s_[:]), r=[pss[cg]], w=[s_])
    P.op('dve', lambda: nc.vector.tensor_tensor(out=s0[:], in0=s0[:], in1=s1[:], op=ALU.add), r=[s0, s1], w=[s0])
    rstd_from_ss(k, s0, ms, rstd, D)
    for cg in range(2):
        P.op('dve', lambda cg=cg: nc.vector.scalar_tensor_tensor(
            out=o_[:, cg * 512:(cg + 1) * 512], in0=pss[cg][:], scalar=rstd[:, 0:1],
            in1=ggr[:, cg * 512:(cg + 1) * 512], op0=ALU.mult, op1=ALU.mult), r=[pss[cg], rstd, ggr], w=[o_])
    P.op('dve', lambda: nc.vector.tensor_tensor(out=o_[:], in0=o_[:], in1=h[:], op=ALU.add), r=[o_, h], w=[o_])
    P.dma('pool', k.h_d[t * 128:(t + 1) * 128, :], o_[:], key=o_, r=[o_], w=[("h", t)])


def gate_tiles(k, ph, l, gi, gatei):
    nc, P = k.nc, k.P
    gg = [sbt(ph, nc, "gg%d" % r, [128, D], F32) for r in range(2)]
    g3 = sbt(ph, nc, "ggn", [128, D], F32)
    bcast_row(k, 'sp', g3[:], k.norm_g.ap()[l, gi:gi + 1, :], key=g3, w=[g3])
    for r in range(2):
        bcast_row(k, 'sp', gg[r][:], k.mod_d.ap()[l, r:r + 1, gatei * D:(gatei + 1) * D], key=gg[r], r=[("mod", l)],
                  w=[gg[r]])
        P.op('dve', lambda r=r: nc.vector.tensor_tensor(out=gg[r][:], in0=gg[r][:], in1=g3[:], op=ALU.mult),
             r=[gg[r], g3], w=[gg[r]])
    return gg


def proj_tm(k, ph, uT, w_ap, ngroups, tiles, epilogue):
    nc, P = k.nc, k.P
    Wb = Rot([sbt(ph, nc, "pj_w%d" % i, [128, 8, 512], BF16) for i in range(2)])

    def load(g):
        w = Wb.next()
        P.dma('pool', w[:], w_ap[:, g * 512:(g + 1) * 512].rearrange("(k p) n -> p k n", p=128), key=w, w=[w])
        return w

    pending = [None]
    nxt = load(0)
    for g in range(ngroups):
        w = nxt
        if g + 1 < ngroups:
            nxt = load(g + 1)
        for t in tiles:
            ps = k.psf_rot.next()
            for kk in range(8):
                P.op('pe', lambda kk=kk: nc.tensor.matmul(ps[:], lhsT=uT[:, kk, t * 128:(t + 1) * 128], rhs=w[:, kk, :],
                                                        start=(kk == 0), stop=(kk == 7)),
                     r=[w, ("uT", t)], w=[ps], sig=(kk == 7))
            d = epilogue(t, g, ps)
            if pending[0] is not None:
                pending[0]()
            pending[0] = d
    if pending[0] is not None:
        pending[0]()


def rope_apply(k, ps, outb, cos, sin, nh, nf, scale, tmp):
    nc, P = k.nc, k.P
    n = nh * 4 * nf
    pv = ps[:, 0:n].rearrange("p (h r x f) -> p h r x f", h=nh, r=2, x=2)
    ov = outb[:, 0:n].rearrange("p (h r x f) -> p h r x f", h=nh, r=2, x=2)
    x1, x2 = pv[:, :, :, 0, :], pv[:, :, :, 1, :]
    cb = cos.unsqueeze(1).to_broadcast([128, nh, 2, nf])
    sb_ = sin.unsqueeze(1).to_broadcast([128, nh, 2, nf])
    tv = [t_[:, 0:nh * 2 * nf].rearrange("p (h r f) -> p h r f", h=nh, r=2) for t_ in tmp]
    for (dst, xin, tab) in ((0, x1, cb), (1, x2, sb_), (2, x1, sb_), (3, x2, cb)):
        P.op('dve', lambda dst=dst, xin=xin, tab=tab: nc.vector.scalar_tensor_tensor(
            out=tv[dst], in0=xin, scalar=float(scale), in1=tab, op0=ALU.mult, op1=ALU.mult),
            r=[ps, "rope_tab"], w=[tmp[dst]])
    P.op('dve', lambda: nc.vector.tensor_tensor(out=ov[:, :, :, 0, :], in0=tv[0], in1=tv[1], op=ALU.subtract),
         r=[tmp[0], tmp[1]], w=[outb])
    P.op('dve', lambda: nc.vector.tensor_tensor(out=ov[:, :, :, 1, :], in0=tv[2], in1=tv[3], op=ALU.add),
         r=[tmp[2], tmp[3]], w=[outb])


def rope_tables(k):
    nc, P = k.nc, k.P
    PI = float(np.pi)
    k.ropeR_d = nc.dram_tensor("ropeR_d", [2, 128, TL * 2 * 64], F32)
    k.ropeD_d = nc.dram_tensor("ropeD_d", [2, 128, TL * 2 * 16], F32)
    with ExitStack() as ph:
        pos = sbt(ph, nc, "pos", [128, TL, 2], F32)
        P.dma('sp', pos[:], k.pos[:, :, :], key=pos, w=[pos])
        for (nf, d2, dst) in ((64, 128, k.ropeR_d), (16, 32, k.ropeD_d)):
            n = TL * 2 * nf
            inv = sbt(ph, nc, "inv", [128, nf], F32)
            ang = sbt(ph, nc, "ang", [128, n], F32)
            xx = sbt(ph, nc, "xx", [128, n], F32)
            ki = sbt(ph, nc, "ki", [128, n], I32)
            kf = sbt(ph, nc, "kf", [128, n], F32)
            mm = sbt(ph, nc, "mm", [128, n], F32)
            res = sbt(ph, nc, "res", [128, n], F32)
            P.op('act', lambda: nc.scalar.activation(out=inv[:], in_=k.iotaf[:, 0:nf], func=AF.Exp,
                                                     scale=-float(np.log(10000.0)) * 2.0 / d2), r=[k.iotaf], w=[inv])
            for t in range(TL):
                for rc in range(2):
                    o = (t * 2 + rc) * nf
                    P.op('dve', lambda t=t, rc=rc, o=o: nc.vector.tensor_scalar(
                        out=ang[:, o:o + nf], in0=inv[:], scalar1=pos[:, t, rc:rc + 1], scalar2=None, op0=ALU.mult),
                        r=[inv, pos], w=[ang])
            for ci, off in ((0, PI / 2), (1, 0.0)):
                P.op('dve', lambda: nc.vector.tensor_scalar(out=xx[:], in0=ang[:], scalar1=float(off), scalar2=None,
                                                            op0=ALU.add), r=[ang], w=[xx])
                P.op('dve', lambda: nc.vector.tensor_scalar(out=ki[:], in0=xx[:], scalar1=float(1.0 / (2 * PI)),
                                                            scalar2=None, op0=ALU.mult), r=[xx], w=[ki])
                P.op('dve', lambda: nc.vector.tensor_copy(out=kf[:], in_=ki[:]), r=[ki], w=[kf])
                P.op('dve', lambda: nc.vector.scalar_tensor_tensor(out=xx[:], in0=kf[:], scalar=-2 * PI, in1=xx[:],
                                                                   op0=ALU.mult, op1=ALU.add), r=[kf, xx], w=[xx])
                P.op('dve', lambda: nc.vector.tensor_scalar(out=mm[:], in0=xx[:], scalar1=PI, scalar2=None,
                                                            op0=ALU.is_gt), r=[xx], w=[mm])
                P.op('dve', lambda: nc.vector.scalar_tensor_tensor(out=xx[:], in0=mm[:], scalar=-2 * PI, in1=xx[:],
                                                                   op0=ALU.mult, op1=ALU.add), r=[mm, xx], w=[xx])
                P.op('dve', lambda: nc.vector.tensor_scalar(out=mm[:], in0=xx[:], scalar1=-PI, scalar2=None,
                                                            op0=ALU.is_lt), r=[xx], w=[mm])
                P.op('dve', lambda: nc.vector.scalar_tensor_tensor(out=xx[:], in0=mm[:], scalar=2 * PI, in1=xx[:],
                                                                   op0=ALU.mult, op1=ALU.add), r=[mm, xx], w=[xx])
                P.op('act', lambda: nc.scalar.activation(out=res[:], in_=xx[:], func=AF.Sin), r=[xx], w=[res])
                P.dma('sp', dst.ap()[ci], res[:], key=res, r=[res], w=[("ropetab", nf, ci)])
        P.barrier()


def ret_layer(k, l):
    nc, P = k.nc, k.P
    i = l // 2
    H, DK, DV = 4, 256, 512
    if not hasattr(k, "rqT_d"):
        k.rqT_d = nc.dram_tensor("rqT_d", [TT, 128, 8, 128], BF16)
        k.rkT_d = nc.dram_tensor("rkT_d", [TT, 128, 8, 128], BF16)
        k.k_d = nc.dram_tensor("k_d", [NT, 1024], BF16)
        k.v_d = nc.dram_tensor("v_d", [NT, 2048], BF16)
        k.g_d = [nc.dram_tensor("g%d_d" % p, [NT, 2048], BF16) for p in range(2)]
        k.y_d = nc.dram_tensor("y_d", [NT, 2048], F32)
        k.F_d = nc.dram_tensor("F_d", [1024, 512], F32)
        k.Fg_d = nc.dram_tensor("Fg_d", [2048, 512], F32)
    tiles_all = list(range(TT))
    with ExitStack() as ph:
        uT = sbt(ph, nc, "r_uT", [128, 8, NT], BF16)
        with ExitStack() as ph1:
            prenorm(k, ph1, l, 0, 0, 1, uT, tiles_all)
            P.barrier()
        cos = sbt(ph, nc, "r_cos", [128, TL, 2, 64], F32)
        sin = sbt(ph, nc, "r_sin", [128, TL, 2, 64], F32)
        P.dma('sp', cos[:].rearrange("p t r f -> p (t r f)"), k.ropeR_d.ap()[0], key=cos, w=["rope_tab"])
        P.dma('sp', sin[:].rearrange("p t r f -> p (t r f)"), k.ropeR_d.ap()[1], key=sin, w=["rope_tab"])
        tmp = [sbt(ph, nc, "r_tmp%d" % j, [128, 256], F32) for j in range(4)]
        ob = Rot([sbt(ph, nc, "r_ob%d" % j, [128, 512], BF16) for j in range(3)])
        tb = Rot([sbt(ph, nc, "r_tb%d" % j, [128, 4, 128], BF16) for j in range(3)])

        def epi(t, g, ps):
            o = ob.next()
            if g < 4:
                scale = 1.0 if g < 2 else DK ** -0.5
                if t < TL:
                    rope_apply(k, ps, o, cos[:, t], sin[:, t], 2, 64, scale, tmp)
                else:
                    P.op('act', lambda: nc.scalar.activation(out=o[:], in_=ps[:], func=AF.Copy, scale=float(scale)),
                         r=[ps], w=[o])
                if g >= 2:
                    P.dma('sp', k.k_d[t * 128:(t + 1) * 128, (g - 2) * 512:(g - 1) * 512], o[:], key=o, r=[o],
                          w=[("k_d", t)])
                def part_b():
                    pb = k.psb_rot.next()
                    for j in range(4):
                        P.op('pe', lambda j=j: nc.tensor.transpose(pb[:, j * 128:(j + 1) * 128],
                                                                  o[:, j * 128:(j + 1) * 128], k.ident[:]),
                             r=[o, k.ident], w=[pb], sig=(j == 3))
                    tt = tb.next()
                    P.op('act', lambda: nc.scalar.copy(out=tt[:], in_=pb[:, 0:512].rearrange("p (j n) -> p j n", j=4)),
                         r=[pb], w=[tt])
                    dst = k.rqT_d if g < 2 else k.rkT_d
                    gg_ = g % 2
                    P.dma('sp', dst.ap()[t][:, gg_ * 4:(gg_ + 1) * 4, :], tt[:], key=tt, r=[tt], w=[("qkT_d", g, t)])

                return part_b
            elif g < 8:
                P.op('act', lambda: nc.scalar.copy(out=o[:], in_=ps[:]), r=[ps], w=[o])
                P.dma('sp', k.v_d[t * 128:(t + 1) * 128, (g - 4) * 512:(g - 3) * 512], o[:], key=o, r=[o], w=[("v_d", t)])
            else:
                p_ = (g - 8) // 4
                gc = (g - 8) % 4
                P.op('act', lambda: nc.scalar.activation(out=o[:], in_=ps[:], func=AF.Silu), r=[ps], w=[o])
                P.dma('sp', k.g_d[p_][t * 128:(t + 1) * 128, gc * 512:(gc + 1) * 512], o[:], key=o, r=[o],
                      w=[("g_d", p_, t)])

        proj_tm(k, ph, uT, k.ret_w_in.ap()[i], 16, tiles_all, epi)
        P.barrier()
    with ExitStack() as ph:
        def t_(name, shape, dt):
            return sbt(ph, nc, name, shape, dt)

        lg = t_("lg", [128, 8], F32)
        dm = t_("dm", [128, 128], F32)
        msk = [t_("msk%d" % j, [128, 128], F32) for j in range(8)]
        xi = [t_("xi%d" % j, [128, 128], BF16) for j in range(8)]
        zeta = t_("zeta", [128, 8], F32)
        cd = t_("cd", [128, 8], F32)
        c128 = t_("c128", [128, 1], F32)
        tA = t_("tA", [128, 128], F32)
        tB = t_("tB", [128, 128], F32)
        P.dma('sp', lg[:], k.ret_decay.ap()[i:i + 1].rearrange("o p h -> o (p h)").to_broadcast([128, 8]), key=lg, w=[lg])
        P.op('act', lambda: nc.scalar.activation(out=lg[:], in_=lg[:], func=AF.Exp, scale=-1.0), r=[lg], w=[lg])
        P.op('dve', lambda: nc.vector.tensor_scalar(out=lg[:], in0=lg[:], scalar1=1.0, scalar2=None, op0=ALU.add),
             r=[lg], w=[lg])
        P.op('act', lambda: nc.scalar.activation(out=lg[:], in_=lg[:], func=AF.Ln), r=[lg], w=[lg])
        P.op('dve', lambda: nc.vector.tensor_scalar(out=lg[:], in0=lg[:], scalar1=-1.0, scalar2=None, op0=ALU.mult),
             r=[lg], w=[lg])
        P.op('dve', lambda: nc.vector.memset(c128[:], 128.0), w=[c128])
        P.op('dve', lambda: nc.vector.tensor_scalar(out=dm[:], in0=k.iotaf[:], scalar1=k.iotap[:, 0:1], scalar2=None,
                                                    op0=ALU.subtract), r=[k.iotaf, k.iotap], w=[dm])
        for p_ in range(2):
            sgn = 1.0 if p_ == 0 else -1.0
            for h in range(H):
                j = p_ * 4 + h
                lgc = lg[:, j:j + 1]
                P.op('dve', lambda: nc.vector.tensor_scalar(out=tA[:], in0=dm[:], scalar1=sgn, scalar2=0.0, op0=ALU.mult,
                                                            op1=ALU.max), r=[dm], w=[tA])
                P.op('act', lambda lgc=lgc: nc.scalar.activation(out=tA[:], in_=tA[:], func=AF.Exp, scale=lgc),
                     r=[tA, lg], w=[tA])
                P.op('dve', lambda: nc.vector.tensor_scalar(out=tB[:], in0=dm[:], scalar1=sgn, scalar2=0.0, op0=ALU.mult,
                                                            op1=ALU.is_ge), r=[dm], w=[tB])
                P.op('dve', lambda j=j: nc.vector.tensor_tensor(out=msk[j][:], in0=tA[:], in1=tB[:], op=ALU.mult),
                     r=[tA, tB], w=[msk[j]])
                if p_ == 0:
                    P.op('dve', lambda: nc.vector.tensor_scalar(out=tA[:], in0=k.iotaf[:], scalar1=1.0, scalar2=None,
                                                                op0=ALU.add), r=[k.iotaf], w=[tA])
                else:
                    P.op('dve', lambda: nc.vector.tensor_scalar(out=tA[:], in0=k.iotaf[:], scalar1=-1.0, scalar2=128.0,
                                                                op0=ALU.mult, op1=ALU.add), r=[k.iotaf], w=[tA])
                P.op('act', lambda j=j, lgc=lgc: nc.scalar.activation(out=xi[j][:], in_=tA[:], func=AF.Exp, scale=lgc),
                     r=[tA, lg], w=[xi[j]])
                if p_ == 0:
                    P.op('dve', lambda: nc.vector.tensor_scalar(out=tB[:, 0:1], in0=k.iotap[:], scalar1=-1.0, scalar2=127.0,
                                                                op0=ALU.mult, op1=ALU.add), r=[k.iotap], w=[tB])
                else:
                    P.op('dve', lambda: nc.vector.tensor_copy(out=tB[:, 0:1], in_=k.iotap[:]), r=[k.iotap], w=[tB])
                P.op('act', lambda j=j, lgc=lgc: nc.scalar.activation(out=zeta[:, j:j + 1], in_=tB[:, 0:1], func=AF.Exp,
                                                                     scale=lgc), r=[tB, lg], w=[zeta])
                P.op('act', lambda j=j, lgc=lgc: nc.scalar.activation(out=cd[:, j:j + 1], in_=c128[:], func=AF.Exp,
                                                                     scale=lgc), r=[c128, lg], w=[cd])
        Wo = t_("r_Wo", [128, 16, D], BF16)
        for half in range(2):
            P.dma('pool', Wo[:, half * 8:(half + 1) * 8, :],
                  k.ret_w_out.ap()[i][half * 1024:(half + 1) * 1024, :].rearrange("(f p) n -> p f n", p=128),
                  key=("Wo", half), w=[("Wo", half)])
        gg = gate_tiles(k, ph, l, 1, 2)
        R = [t_("R%d" % h, [128, 2, DV], F32) for h in range(H)]
        Rb = [t_("Rb%d" % h, [128, 2, DV], BF16) for h in range(H)]
        Fg = t_("Fg", [128, 2, DV], F32)
        qTb = Rot([t_("qTc%d" % j, [128, 8, 128], BF16) for j in range(3)])
        kTb = Rot([t_("kTc%d" % j, [128, 8, 128], BF16) for j in range(3)])
        kb = Rot([t_("kc%d" % j, [128, 1024], BF16) for j in range(3)])
        vb = Rot([t_("vc%d" % j, [128, 2048], BF16) for j in range(3)])
        gb = Rot([t_("gc%d" % j, [128, 2048], BF16) for j in range(3)])
        psr = Rot(k.psf[0:6])
        yb = Rot([t_("y%d" % j, [128, 2048], F32) for j in range(2)])
        y1b = Rot([t_("y1_%d" % j, [128, 2048], F32) for j in range(3)])
        ybf = Rot([t_("ybf%d" % j, [128, 2048], BF16) for j in range(2)])
        yTb = Rot([t_("yT%d" % j, [128, 16, 128], BF16) for j in range(2)])
        sTb = Rot([t_("sT%d" % j, [128, 128], BF16) for j in range(8)])
        qxb = Rot([t_("qx%d" % j, [128, 2, 128], BF16) for j in range(8)])
        kzb = Rot([t_("kz%d" % j, [128, 256], BF16) for j in range(8)])
        hb = Rot([t_("r_h%d" % j, [128, D], F32) for j in range(2)])
        ob2 = Rot([t_("r_o%d" % j, [128, D], F32) for j in range(2)])
        st4 = Rot([[t_("r_s%d_%d" % (a, b), [128, 1], F32) for b in range(4)] for a in range(2)])
        st3 = Rot([[t_("r_n%d_%d" % (a, b), [128, 1], F32) for b in range(3)] for a in range(8)])

        def zero_state():
            for h in range(H):
                P.op('pool', lambda h=h: nc.gpsimd.memset(R[h][:], 0.0), w=[R[h]])
                P.op('pool', lambda h=h: nc.gpsimd.memset(Rb[h][:], 0.0), w=[Rb[h]])

        def load_chunk(p_, t):
            qTc, kTc, kc, vc, gc = qTb.next(), kTb.next(), kb.next(), vb.next(), gb.next()
            cs = slice(t * 128, (t + 1) * 128)
            P.dma('sp', qTc[:], k.rqT_d.ap()[t], key=qTc, r=[("qkT_d", g, t) for g in (0, 1)], w=[qTc])
            P.dma('sp', kTc[:], k.rkT_d.ap()[t], key=kTc, r=[("qkT_d", g, t) for g in (2, 3)], w=[kTc])
            P.dma('sp', kc[:], k.k_d[cs, :], key=kc, r=[("k_d", t)], w=[kc])
            P.dma('sp', vc[:], k.v_d[cs, :], key=vc, r=[("v_d", t)], w=[vc])
            P.dma('sp', gc[:], k.g_d[p_][cs, :], key=gc, r=[("g_d", p_, t)], w=[gc])
            y1 = None
            if p_ == 1:
                y1 = y1b.next()
                P.dma('sp', y1[:], k.y_d[cs, :], key=y1, r=[("y_d", t)], w=[y1])
            return qTc, kTc, kc, vc, gc, y1

        def stage_ab(p_, t, bufs):
            qTc, kTc, kc, vc, gc, y1 = bufs
            loc = dict(sT=[], qx=[], kz=[])
            for h in range(H):
                j = p_ * 4 + h
                kz, qx = kzb.next(), qxb.next()
                P.op('act', lambda: nc.scalar.activation(out=kz[:], in_=kc[:, h * DK:(h + 1) * DK], func=AF.Copy,
                                                         scale=zeta[:, j:j + 1]), r=[kc, zeta], w=[kz])
                P.op('pool', lambda: nc.gpsimd.tensor_tensor(
                    out=qx[:], in0=qTc[:, h * 2:h * 2 + 2, :], in1=xi[j][:].unsqueeze(1).to_broadcast([128, 2, 128]),
                    op=ALU.mult), r=[qTc, xi[j]], w=[qx])
                loc["kz"].append(kz)
                loc["qx"].append(qx)
            for h in range(H):
                j = p_ * 4 + h
                ps_s = psr.next()
                for c2 in range(2):
                    P.op('pe', lambda: nc.tensor.matmul(ps_s[:, 0:128], lhsT=kTc[:, h * 2 + c2, :],
                                                        rhs=qTc[:, h * 2 + c2, :], start=(c2 == 0), stop=(c2 == 1)),
                         r=[kTc, qTc], w=[ps_s], sig=(c2 == 1))
                sT = sTb.next()
                P.op('dve', lambda: nc.vector.tensor_tensor(out=sT[:], in0=ps_s[:, 0:128], in1=msk[j][:],
                                                            op=ALU.mult), r=[ps_s, msk[j]], w=[sT])
                loc["sT"].append(sT)
            return loc

        def stage_cd(p_, t, bufs, loc):
            qTc, kTc, kc, vc, gc, y1 = bufs
            cs = slice(t * 128, (t + 1) * 128)
            y = yb.next()
            if p_ == 1:
                yf = ybf.next()
            for h in range(H):
                j = p_ * 4 + h
                hs = slice(h * DV, (h + 1) * DV)
                kz = loc["kz"][h]
                for c2 in range(2):
                    ps_k = psr.next()
                    P.op('pe', lambda: nc.tensor.matmul(ps_k[:], lhsT=kz[:, c2 * 128:(c2 + 1) * 128], rhs=vc[:, hs],
                                                        start=True, stop=True), r=[kz, vc], w=[ps_k])
                    P.op('dve', lambda: nc.vector.scalar_tensor_tensor(
                        out=R[h][:, c2, :], in0=R[h][:, c2, :], scalar=cd[:, j:j + 1], in1=ps_k[:], op0=ALU.mult,
                        op1=ALU.add), r=[R[h], cd, ps_k], w=[R[h]])
            for h in range(H):
                hs = slice(h * DV, (h + 1) * DV)
                sT, qx = loc["sT"][h], loc["qx"][h]
                ss, ms, rstd = st3.next()
                ps_o = psr.next()
                P.op('pe', lambda: nc.tensor.matmul(ps_o[:], lhsT=sT[:], rhs=vc[:, hs], start=True, stop=False),
                     r=[sT, vc], w=[ps_o], sig=False)
                for c2 in range(2):
                    P.op('pe', lambda: nc.tensor.matmul(ps_o[:], lhsT=qx[:, c2, :], rhs=Rb[h][:, c2, :], start=False,
                                                        stop=(c2 == 1)), r=[qx, Rb[h]], w=[ps_o], sig=(c2 == 1))
                P.op('act', lambda: nc.scalar.activation(out=k.junk[:, 0:512], in_=ps_o[:], func=AF.Square, accum_out=ss[:]),
                     r=[ps_o], w=[ss])
                rstd_from_ss(k, ss, ms, rstd, DV)
                P.op('dve', lambda: nc.vector.scalar_tensor_tensor(out=y[:, hs], in0=ps_o[:], scalar=rstd[:, 0:1],
                                                                   in1=gc[:, hs], op0=ALU.mult, op1=ALU.mult),
                     r=[ps_o, rstd, gc], w=[("y", y.name, h)])
                if p_ == 1:
                    P.op('pool', lambda: nc.gpsimd.tensor_tensor(out=yf[:, hs], in0=y[:, hs], in1=y1[:, hs], op=ALU.add),
                         r=[("y", y.name, h), y1], w=[("yf", yf.name, h)])
            for h in range(H):
                P.op('act', lambda: nc.scalar.copy(out=Rb[h][:], in_=R[h][:]), r=[R[h]], w=[Rb[h]])
            if p_ == 0:
                P.dma('pool', k.y_d[cs, :], y[:], key=y, r=[("y", y.name, h) for h in range(H)], w=[("y_d", t)])
            else:
                yT = yTb.next()
                for half in range(2):
                    pb = k.psb_rot.next()
                    for jj in range(8):
                        kk = half * 8 + jj
                        P.op('pe', lambda: nc.tensor.transpose(pb[:, jj * 128:(jj + 1) * 128],
                                                               yf[:, kk * 128:(kk + 1) * 128], k.ident[:]),
                             r=[("yf", yf.name, kk // 4), k.ident], w=[pb], sig=(jj == 7))
                    P.op('act', lambda: nc.scalar.copy(out=yT[:, half * 8:(half + 1) * 8, :],
                                                       in_=pb[:].rearrange("p (j n) -> p j n", j=8)),
                         r=[pb], w=[("yTh", yT.name, half)])
                pss = [psr.next(), psr.next()]
                for cg in range(2):
                    for kk in range(16):
                        P.op('pe', lambda: nc.tensor.matmul(pss[cg][:], lhsT=yT[:, kk, :],
                                                            rhs=Wo[:, kk, cg * 512:(cg + 1) * 512],
                                                            start=(kk == 0), stop=(kk == 15)),
                             r=[("yTh", yT.name, kk // 8), ("Wo", kk // 8)], w=[pss[cg]], sig=(kk == 15))
                resid_epilogue(k, pss, t, gg[0 if t < TL else 1], hb.next(), ob2.next(), st4.next())

        def run_chunks(p_, ts):
            bufs = {0: load_chunk(p_, ts[0])}
            if len(ts) > 1:
                bufs[1] = load_chunk(p_, ts[1])
            locs = {0: stage_ab(p_, ts[0], bufs[0])}
            for n_, t in enumerate(ts):
                if n_ + 2 < len(ts):
                    bufs[n_ + 2] = load_chunk(p_, ts[n_ + 2])
                if n_ + 1 < len(ts):
                    locs[n_ + 1] = stage_ab(p_, ts[n_ + 1], bufs[n_ + 1])
                stage_cd(p_, t, bufs.pop(n_), locs.pop(n_))

        zero_state()
        run_chunks(0, [TL, TL + 1] + list(range(TL)))
        for h in range(H):
            P.dma('sp', k.F_d.ap()[h * 256:(h + 1) * 256, :].rearrange("(c p) n -> p c n", p=128), R[h][:], key=R[h],
                  r=[R[h]], w=["F_d"])
        P.op('pool', lambda: nc.gpsimd.collective_compute("AllGather", ALU.bypass, replica_groups=GROUPS,
                                                          ins=[k.F_d.ap().opt()], outs=[k.Fg_d.ap().opt()]),
             r=["F_d"], w=["Fg_d"], dma="cc", inc=1)
        zero_state()
        run_chunks(1, [TL + 1, TL])
        for h in range(H):
            for rk_ in range(2):
                P.dma('sp', Fg[:], k.Fg_d.ap()[rk_ * 1024 + h * 256:rk_ * 1024 + (h + 1) * 256, :].rearrange(
                    "(c p) n -> p c n", p=128), key=Fg, r=["Fg_d"], w=[Fg])
                if rk_ == 0:
                    P.op('dve', lambda h=h: nc.vector.tensor_scalar(out=R[h][:], in0=Fg[:], scalar1=k.selt[:, 0:1],
                                                                  scalar2=None, op0=ALU.mult), r=[Fg, k.selt], w=[R[h]])
                else:
                    P.op('dve', lambda h=h: nc.vector.scalar_tensor_tensor(out=R[h][:], in0=Fg[:], scalar=k.selt[:, 1:2],
                                                                         in1=R[h][:], op0=ALU.mult, op1=ALU.add),
                         r=[Fg, k.selt, R[h]], w=[R[h]])
            P.op('act', lambda h=h: nc.scalar.copy(out=Rb[h][:], in_=R[h][:]), r=[R[h]], w=[Rb[h]])
        run_chunks(1, list(range(TL - 1, -1, -1)))
        P.barrier()


def diff_layer(k, l, last):
    import math
    nc, P = k.nc, k.P
    i = l // 2
    lam_init = 0.8 - 0.6 * math.exp(-0.3 * l)
    NH = 8
    if not hasattr(k, "qT_d"):
        k.qT_d = nc.dram_tensor("qT_d", [8, 128, NT], BF16)
    if not hasattr(k, "kTl_d"):
        k.kTl_d = [nc.dram_tensor("kTl%d_d" % g, [512, NL], BF16) for g in range(2)]
        k.kTc_d = nc.dram_tensor("kTc_d", [1024, NCX], BF16)
        k.kTg_d = [nc.dram_tensor("kTg%d_d" % g, [1024, NL], BF16) for g in range(2)]
        k.vl_d = [nc.dram_tensor("vl%d_d" % g, [NL, 512], BF16) for g in range(2)]
        k.vc_d = nc.dram_tensor("vc_d", [NCX, 1024], BF16)
        k.vg_d = [nc.dram_tensor("vg%d_d" % g, [2 * NL, 512], BF16) for g in range(2)]
    tiles_all = list(range(TT))
    q_tiles = list(range(TL)) + ([] if last else [TL, TL + 1])
    with ExitStack() as ph:
        uT = sbt(ph, nc, "d_uT", [128, 8, NT], BF16)
        with ExitStack() as ph1:
            prenorm(k, ph1, l, 0, 0, 1, uT, tiles_all)
            P.barrier()
        cos = sbt(ph, nc, "d_cos", [128, TL, 2, 16], F32)
        sin = sbt(ph, nc, "d_sin", [128, TL, 2, 16], F32)
        P.dma('sp', cos[:].rearrange("p t r f -> p (t r f)"), k.ropeD_d.ap()[0], key=cos, w=["rope_tab"])
        P.dma('sp', sin[:].rearrange("p t r f -> p (t r f)"), k.ropeD_d.ap()[1], key=sin, w=["rope_tab"])
        tmp = [sbt(ph, nc, "d_tmp%d" % j, [128, 256], F32) for j in range(4)]
        ob = Rot([sbt(ph, nc, "d_ob%d" % j, [128, 512], BF16) for j in range(3)])
        tb = Rot([sbt(ph, nc, "d_tb%d" % j, [128, 4, 128], BF16) for j in range(3)])

        def epi_qk(isq):
            def epi(t, g, ps):
                if isq and t not in q_tiles:
                    return
                o = ob.next()
                scale = 0.125 if isq else 1.0
                if t < TL:
                    rope_apply(k, ps, o, cos[:, t], sin[:, t], 8, 16, scale, tmp)
                else:
                    P.op('act', lambda: nc.scalar.activation(out=o[:], in_=ps[:], func=AF.Copy, scale=float(scale)),
                         r=[ps], w=[o])
                def part_b():
                    pb = k.psb_rot.next()
                    for j in range(4):
                        P.op('pe', lambda j=j: nc.tensor.transpose(pb[:, j * 128:(j + 1) * 128], o[:, j * 128:(j + 1) * 128],
                                                                  k.ident[:]), r=[o, k.ident], w=[pb], sig=(j == 3))
                    tt = tb.next()
                    P.op('act', lambda: nc.scalar.copy(out=tt[:], in_=pb[:, 0:512].rearrange("p (j n) -> p j n", j=4)),
                         r=[pb], w=[tt])
                    if isq:
                        P.dma('sp', k.qT_d.ap()[g * 4:(g + 1) * 4, :, t * 128:(t + 1) * 128].rearrange("j p n -> p j n"), tt[:],
                              key=tt, r=[tt], w=[("qT_d", g, t)])
                    elif t < TL:
                        P.dma('sp', k.kTl_d[g].ap()[:, t * 128:(t + 1) * 128].rearrange("(j p) n -> p j n", p=128),
                              tt[:], key=tt, r=[tt], w=[("kTl_d", g, t)])
                        if t == TL - 1:
                            P.op('pool', lambda: nc.gpsimd.collective_compute(
                                "AllGather", ALU.bypass, replica_groups=GROUPS, ins=[k.kTl_d[g].ap().opt()],
                                outs=[k.kTg_d[g].ap().opt()]), r=[("kTl_d", g, t_) for t_ in range(TL)], w=[("kTg_d", g)],
                                dma="cc", inc=1)
                    else:
                        P.dma('sp', k.kTc_d.ap()[g * 512:(g + 1) * 512, (t - TL) * 128:(t - TL + 1) * 128].rearrange(
                            "(j p) n -> p j n", p=128), tt[:], key=tt, r=[tt], w=[("kTc_d", g, t)])

                return part_b
            return epi

        def epi_v(t, g, ps):
            o = ob.next()
            P.op('act', lambda: nc.scalar.copy(out=o[:], in_=ps[:]), r=[ps], w=[o])
            if t < TL:
                P.dma('sp', k.vl_d[g][t * 128:(t + 1) * 128, :], o[:], key=o, r=[o], w=[("vl_d", g, t)])
                if t == TL - 1:
                    P.op('pool', lambda: nc.gpsimd.collective_compute(
                        "AllGather", ALU.bypass, replica_groups=GROUPS, ins=[k.vl_d[g].ap().opt()],
                        outs=[k.vg_d[g].ap().opt()]), r=[("vl_d", g, t_) for t_ in range(TL)], w=[("vg_d", g)],
                        dma="cc", inc=1)
            else:
                P.dma('sp', k.vc_d[(t - TL) * 128:(t - TL + 1) * 128, g * 512:(g + 1) * 512], o[:], key=o, r=[o],
                      w=[("vc_d", g, t)])

        wi = k.diff_w_in.ap()[i]
        proj_tm(k, ph, uT, wi[:, 1024:2048], 2, tiles_all, epi_qk(False))
        proj_tm(k, ph, uT, wi[:, 2048:3072], 2, tiles_all, epi_v)
        proj_tm(k, ph, uT, wi[:, 0:1024], 2, tiles_all, epi_qk(True))
        P.barrier()
    with ExitStack() as ph:
        def t_(name, shape, dt):
            return sbt(ph, nc, name, shape, dt)

        NK = 2 * NL + NCX
        NKT = NK // 128
        yT = t_("d_yT", [128, NH, NT], BF16)
        Wo = t_("d_Wo", [128, 8, D], BF16)
        P.dma('pool', Wo[:], k.diff_w_out.ap()[i].rearrange("(f p) n -> p f n", p=128), key=Wo, w=[Wo])
        gg = gate_tiles(k, ph, l, 1, 2)
        ones = t_("d_ones", [128, 128], BF16)
        P.op('dve', lambda: nc.vector.memset(ones[:], 1.0), w=[ones])
        lv = t_("d_lv", [128, 4, 64], F32)
        lp = t_("d_lp", [128, 2, 64], F32)
        ls = t_("d_ls", [128, 2], F32)
        nlam = t_("d_nlam", [128, 1], F32)
        gsub = t_("d_gsub", [128, 1], F32)
        P.dma('sp', lv[:].rearrange("p a b -> p (a b)"),
              k.diff_lam.ap()[i:i + 1].rearrange("o a b -> o (a b)").to_broadcast([128, 256]), key=lv, w=[lv])
        for a in range(2):
            P.op('dve', lambda a=a: nc.vector.tensor_tensor(out=lp[:, a, :], in0=lv[:, 2 * a, :], in1=lv[:, 2 * a + 1, :],
                                                          op=ALU.mult), r=[lv], w=[lp])
            P.op('dve', lambda a=a: nc.vector.reduce_sum(out=ls[:, a:a + 1], in_=lp[:, a, :], axis=AX.X), r=[lp], w=[ls])
        P.op('act', lambda: nc.scalar.activation(out=ls[:], in_=ls[:], func=AF.Exp), r=[ls], w=[ls])
        P.op('dve', lambda: nc.vector.scalar_tensor_tensor(out=nlam[:], in0=ls[:, 1:2], scalar=-float(lam_init),
                                                           in1=ls[:, 0:1], op0=ALU.add, op1=ALU.subtract),
             r=[ls], w=[nlam])
        P.dma('sp', gsub[:], k.diff_subln.ap()[i].rearrange("(p o) -> p o", o=1), key=gsub, w=[gsub],
              allow_slow_non_contiguous=True)
        P.op('dve', lambda: nc.vector.tensor_scalar(out=gsub[:], in0=gsub[:], scalar1=float(1.0 - lam_init), scalar2=None,
                                                    op0=ALU.mult), r=[gsub], w=[gsub])
        kTb = Rot([t_("d_kT%d" % j, [128, NK], BF16) for j in range(2)])
        Vb = Rot([t_("d_V%d" % j, [128, NKT, 128], BF16) for j in range(2)])
        qAb = Rot([t_("d_qA%d" % j, [128, NT], BF16) for j in range(2)])
        qBb = Rot([t_("d_qB%d" % j, [128, NT], BF16) for j in range(2)])
        for qa in qAb.items:
            P.op('pool', lambda qa=qa: nc.gpsimd.memset(qa[64:128, :], 0.0), w=[("qz", qa.name)])
        for qb_ in qBb.items:
            P.op('pool', lambda qb_=qb_: nc.gpsimd.memset(qb_[0:64, :], 0.0), w=[("qz", qb_.name)])
        eb = Rot([t_("d_e%d" % j, [128, 512], BF16) for j in range(8)])
        accD = Rot([t_("d_aD%d" % j, [128, 512], F32) for j in range(2)])
        accP = Rot([t_("d_aP%d" % j, [128, 512], F32) for j in range(2)])
        zhi = Rot([t_("d_zh%d" % j, [128, 512], BF16) for j in range(2)])
        zlo = Rot([t_("d_zl%d" % j, [128, 512], BF16) for j in range(2)])
        rzb = Rot([t_("d_rz%d" % j, [128, 512], F32) for j in range(2)])
        Onb = [Rot([t_("d_On%d_%d" % (w_, j), [128, 512], F32) for j in range(2)]) for w_ in range(2)]
        OTb = Rot([t_("d_OT%d" % j, [128, 512], F32) for j in range(2)])
        sqb = Rot([t_("d_sq%d" % j, [128, 512], BF16) for j in range(2)])
        rsb = Rot([t_("d_rs%d" % j, [128, 512], F32) for j in range(2)])
        lnb = Rot([t_("d_ln%d" % j, [128, 512], F32) for j in range(2)])
        hb = Rot([t_("d_h%d" % j, [128, D], F32) for j in range(2)])
        ob2 = Rot([t_("d_o%d" % j, [128, D], F32) for j in range(2)])
        st4 = Rot([[t_("d_s%d_%d" % (a, b), [128, 1], F32) for b in range(4)] for a in range(2)])
        class View:
            def __init__(self, t):
                self.t, self.name = t, t.name + "_f32"

            def __getitem__(self, idx):
                return self.t[:].bitcast(F32)[idx]

        ps_s = Rot(k.psf[0:3])
        ps_o = Rot(k.psf[3:5])
        ps_z = Rot([k.psf[5], View(k.psb[0])])
        ps_ln = View(k.psb[1])
        deferred = []

        def load_head(h):
            kT, V, qA, qB = kTb.next(), Vb.next(), qAb.next(), qBb.next()
            hg_, hh = h // 4, h % 4
            for rk_ in range(2):
                P.dma('sp', kT[:, rk_ * NL:(rk_ + 1) * NL],
                      k.kTg_d[hg_][rk_ * 512 + hh * 128:rk_ * 512 + (hh + 1) * 128, :],
                      key=("kT", kT.name, rk_), r=[("kTg_d", hg_)], w=[("kTp", kT.name, rk_)])
                P.dma('sp', V[:, rk_ * TL:(rk_ + 1) * TL, :],
                      k.vg_d[hg_].ap()[rk_ * NL:(rk_ + 1) * NL, hh * 128:(hh + 1) * 128].rearrange("(j p) d -> p j d", p=128),
                      key=("V", V.name, rk_), r=[("vg_d", hg_)], w=[("Vp", V.name, rk_)])
            P.dma('sp', kT[:, 2 * NL:NK], k.kTc_d[h * 128:(h + 1) * 128, :], key=("kT", kT.name, 2),
                  r=[("kTc_d", h // 4, t_) for t_ in (TL, TL + 1)], w=[("kTp", kT.name, 2)])
            P.dma('sp', V[:, 2 * TL:NKT, :], k.vc_d.ap()[:, h * 128:(h + 1) * 128].rearrange("(j p) d -> p j d", p=128),
                  key=("V", V.name, 2), r=[("vc_d", h // 4, t_) for t_ in (TL, TL + 1)], w=[("Vp", V.name, 2)])
            qr = [("qT_d", h // 4, t) for t in q_tiles]
            P.dma('sp', qA[0:64, :], k.qT_d.ap()[h, 0:64, :], key=qA, r=qr, w=[qA])
            P.dma('sp', qB[64:128, :], k.qT_d.ap()[h, 64:128, :], key=qB, r=qr, w=[qB])
            return kT, V, qA, qB

        blocks = [(qb * 512, 512, list(range(NKT))) for qb in range(4)]
        if not last:
            blocks.append((NL, NCX, [NKT - 2, NKT - 1]))
        items = []
        for h in range(NH):
            for (q0, nq, ktiles) in blocks:
                for w_ in range(2):
                    for jn, jt in enumerate(ktiles):
                        items.append(dict(h=h, q0=q0, nq=nq, w=w_, jn=jn, jt=jt, first=(jn == 0),
                                          last=(jn == len(ktiles) - 1)))
        heads = {}
        state = {}

        def emit_S(it):
            h = it["h"]
            if h not in heads:
                assert h == 0
                heads[h] = load_head(h)
            kT, V, qA, qB = heads[h]
            q = qA if it["w"] == 0 else qB
            nq, q0, jt = it["nq"], it["q0"], it["jt"]
            ps = ps_s.next()
            e = eb.next()
            it["e"] = e
            P.op('pe', lambda: nc.tensor.matmul(ps[:, 0:nq], lhsT=kT[:, jt * 128:(jt + 1) * 128], rhs=q[:, q0:q0 + nq],
                                                start=True, stop=True),
                 r=[("kTp", kT.name, min(jt // TL, 2)), q, ("qz", q.name)], w=[ps])
            P.op('act', lambda: nc.scalar.activation(out=e[:, 0:nq], in_=ps[:, 0:nq], func=AF.Exp), r=[ps], w=[e])

        def emit_PV(it):
            h, w_, nq, q0, jt, jn = it["h"], it["w"], it["nq"], it["q0"], it["jt"], it["jn"]
            kT, V, qA, qB = heads[h]
            e = it["e"]
            if it["first"] and w_ == 0 and q0 == 0 and h + 1 < NH:
                heads[h + 1] = load_head(h + 1)
            if it["first"]:
                state["pO"] = ps_o.next()
                state["aD"], state["aP"] = accD.next(), accP.next()
                state["usedP"] = False
            pO, aD, aP = state["pO"], state["aD"], state["aP"]
            P.op('pe', lambda: nc.tensor.matmul(pO[:, 0:nq], lhsT=V[:, jt, :], rhs=e[:, 0:nq], start=it["first"],
                                                stop=it["last"]),
                 r=[("Vp", V.name, min(jt // TL, 2)), e], w=[pO], sig=it["last"])
            if it["first"]:
                state["pZ"] = ps_z.next()
                state["zstart"] = True
            pZ = state["pZ"]
            if jn % 2 == 1:
                P.op('pe', lambda: nc.tensor.matmul(pZ[:, 0:nq], lhsT=ones[:], rhs=e[:, 0:nq], start=state["zstart"],
                                                    stop=False), r=[ones, e], w=[pZ], sig=True)
                state["zstart"] = False
            elif jn == 0:
                P.op('dve', lambda: nc.vector.tensor_copy(out=aD[:, 0:nq], in_=e[:, 0:nq]), r=[e], w=[aD])
            else:
                P.op('dve', lambda: nc.vector.tensor_tensor(out=aD[:, 0:nq], in0=aD[:, 0:nq], in1=e[:, 0:nq], op=ALU.add),
                     r=[e, aD], w=[aD])
            if not it["last"]:
                return
            zh, zl = zhi.next(), zlo.next()
            P.op('dve', lambda: nc.vector.tensor_copy(out=zh[:, 0:nq], in_=aD[:, 0:nq]), r=[aD], w=[zh])
            P.op('dve', lambda: nc.vector.tensor_tensor(out=zl[:, 0:nq], in0=aD[:, 0:nq], in1=zh[:, 0:nq], op=ALU.subtract),
                 r=[aD, zh], w=[zl])
            P.op('pe', lambda: nc.tensor.matmul(pZ[:, 0:nq], lhsT=ones[:], rhs=zh[:, 0:nq], start=state["zstart"],
                                                stop=False), r=[ones, zh], w=[pZ], sig=False)
            P.op('pe', lambda: nc.tensor.matmul(pZ[:, 0:nq], lhsT=ones[:], rhs=zl[:, 0:nq], start=False, stop=True),
                 r=[ones, zl], w=[pZ])
            rz = rzb.next()
            On = Onb[w_].next()
            P.op('act', lambda: nc.scalar.activation(out=rz[:, 0:nq], in_=pZ[:, 0:nq], func=AF.Ln), r=[pZ], w=[rz])
            P.op('act', lambda: nc.scalar.activation(out=rz[:, 0:nq], in_=rz[:, 0:nq], func=AF.Exp, scale=-1.0),
                 r=[rz], w=[rz])
            P.op('dve', lambda: nc.vector.tensor_tensor(out=On[:, 0:nq], in0=pO[:, 0:nq], in1=rz[:, 0:nq], op=ALU.mult),
                 r=[pO, rz], w=[On])
            state["On%d" % w_] = On
            if w_ == 0:
                return
            On0, On1 = state["On0"], state["On1"]
            OT, sq, rs, ln_ = OTb.next(), sqb.next(), rsb.next(), lnb.next()
            P.op('dve', lambda: nc.vector.scalar_tensor_tensor(out=OT[:, 0:nq], in0=On1[:, 0:nq], scalar=nlam[:, 0:1],
                                                               in1=On0[:, 0:nq], op0=ALU.mult, op1=ALU.add),
                 r=[On0, On1, nlam], w=[OT])
            P.op('act', lambda: nc.scalar.activation(out=sq[:, 0:nq], in_=OT[:, 0:nq], func=AF.Square), r=[OT], w=[sq])

            def part2():
                pS = ps_ln
                P.op('pe', lambda: nc.tensor.matmul(pS[:, 0:nq], lhsT=ones[:], rhs=sq[:, 0:nq], start=True, stop=True),
                     r=[ones, sq], w=[pS])
                P.op('act', lambda: nc.scalar.activation(out=ln_[:, 0:nq], in_=pS[:, 0:nq], func=AF.Ln, scale=1.0 / 128,
                                                         bias=k.epsb[:, 0:1]), r=[pS, k.epsb], w=[ln_])
                P.op('act', lambda: nc.scalar.activation(out=rs[:, 0:nq], in_=ln_[:, 0:nq], func=AF.Exp, scale=-0.5),
                     r=[ln_], w=[rs])
                P.op('dve', lambda: nc.vector.scalar_tensor_tensor(out=yT[:, h, q0:q0 + nq], in0=OT[:, 0:nq],
                                                                   scalar=gsub[:, 0:1], in1=rs[:, 0:nq], op0=ALU.mult,
                                                                   op1=ALU.mult), r=[OT, gsub, rs], w=[("yT", h, q0)])

            deferred.append([4, part2])

        def tick():
            for d in list(deferred):
                d[0] -= 1
                if d[0] <= 0:
                    deferred.remove(d)
                    d[1]()

        pend = []
        for it in items:
            emit_S(it)
            pend.append(it)
            if len(pend) > 2:
                emit_PV(pend.pop(0))
                tick()
        while pend:
            emit_PV(pend.pop(0))
            tick()
        while deferred:
            tick()
        for t in q_tiles:
            pss = [k.psf_rot.next(), k.psf_rot.next()]
            for cg in range(2):
                for kk in range(8):
                    P.op('pe', lambda cg=cg, kk=kk: nc.tensor.matmul(pss[cg][:], lhsT=yT[:, kk, t * 128:(t + 1) * 128],
                                                                   rhs=Wo[:, kk, cg * 512:(cg + 1) * 512],
                                                                   start=(kk == 0), stop=(kk == 7)),
                         r=[("yT", kk, (t * 128 // 512) * 512 if t < TL else NL), Wo], w=[pss[cg]], sig=(kk == 7))
            resid_epilogue(k, pss, t, gg[0 if t < TL else 1], hb.next(), ob2.next(), st4.next())
        P.barrier()

def prep_inputs(inp):
    f = lambda a: np.ascontiguousarray(np.asarray(a, dtype=np.float32))
    x, c, ctx, c_ctx = f(inp["x"]), f(inp["c"]), f(inp["ctx"]), f(inp["c_ctx"])
    shared = {n: f(inp[n]) for n in ("ada_w", "ada_b", "norm_g", "ret_w_out", "diff_w_in", "diff_w_out",
                                     "diff_subln", "ffn_w_up", "ffn_conv_b", "ffn_w_down")}
    shared["diff_lam"] = np.ascontiguousarray(np.stack([f(inp["diff_lq1"]), f(inp["diff_lk1"]), f(inp["diff_lq2"]),
                                                        f(inp["diff_lk2"])], axis=1))
    rw = f(inp["ret_w_in"])
    rw_sw = np.ascontiguousarray(np.concatenate([rw[:, :, :4096], rw[:, :, 6144:8192], rw[:, :, 4096:6144]], axis=2))
    dec = np.stack([f(inp["ret_decay_f"]), f(inp["ret_decay_b"])], axis=1)
    dec_sw = np.ascontiguousarray(dec[:, ::-1, :])
    cwv = f(inp["ffn_conv_w"])
    cw_sw = np.ascontiguousarray(cwv[:, ::-1, :])
    maps = []
    for core in range(8):
        b, half = core // 2, core % 2
        xs = x[b, half * NL:(half + 1) * NL]
        cs = ctx[b]
        pos = np.arange(half * NL, (half + 1) * NL)
        if half:
            xs, cs, pos = xs[::-1], cs[::-1], pos[::-1]
        posrc = np.stack([pos // 64, pos % 64], axis=-1).astype(np.float32)
        m = dict(shared)
        m["x"] = np.ascontiguousarray(xs)
        m["ctx"] = np.ascontiguousarray(cs)
        m["c2"] = np.ascontiguousarray(np.stack([c[b], c_ctx]))
        sel = np.zeros((128, 2), np.float32)
        sel[:, 1 - half] = 1.0
        m["sel"] = sel
        m["pos"] = np.ascontiguousarray(posrc.reshape(TL, 128, 2).transpose(1, 0, 2))
        m["ret_w_in"] = rw_sw if half else rw
        m["ret_decay"] = dec_sw if half else np.ascontiguousarray(dec)
        m["ffn_conv_w"] = cw_sw if half else cwv
        maps.append(m)
    return maps


def gather_out(results):
    out = np.empty((4, 2 * NL, D), np.float32)
    for core in range(8):
        b, half = core // 2, core % 2
        o = np.asarray(results[core]["out"])[:NL]
        out[b, half * NL:(half + 1) * NL] = o[::-1] if half else o
    return out


def kernel(**inputs):
    nc = build("full")
    maps = prep_inputs(inputs)
    res = run_bass_kernel_spmd(nc, maps, core_ids=list(range(8)))
    return gather_out(res.results)
```
